# Optimizing a Trainium2 kernel written in Bass

```python
import math
import jax
import jax.numpy as jnp
from jax import lax
import numpy as np


D_MODEL = 1024
BATCH = 8
SEQ = 2048
DEPTH = 2

GRID_W = 64
SSD_WIDTH = D_MODEL
SSD_HEAD_DIM = 64
SSD_HEADS = SSD_WIDTH // SSD_HEAD_DIM
SSD_GROUPS = 2
SSD_STATE = 128
SSD_CONV = 5
SSD_CHUNK = 128
SSD_XBC = SSD_WIDTH + 2 * SSD_GROUPS * SSD_STATE
SSD_IN = SSD_WIDTH + SSD_XBC + 2 * SSD_HEADS
DIFF_WIDTH = D_MODEL // 2
DIFF_HEAD_DIM = 64
DIFF_HEADS = DIFF_WIDTH // (2 * DIFF_HEAD_DIM)
DIFF_IN = 4 * DIFF_WIDTH
Q_BLOCK = 128
NA_WIDTH = D_MODEL // 2
NA_HEAD_DIM = 64
NA_HEADS = NA_WIDTH // NA_HEAD_DIM
NA_KH_MAX = 8
NA_KW = 16
NA_IN = 4 * NA_WIDTH

MIX_WIDTH = SSD_WIDTH + DIFF_WIDTH + NA_WIDTH
IN_WIDTH = SSD_IN + DIFF_IN + NA_IN
ROPE_THETA = 10000.0
EPS = 1e-6

kernel_name = 'hybrid_ssd_diffattn_natten_encoder'


def _rmsnorm(x, w):
    xf = x.astype(jnp.float32)
    y = xf * lax.rsqrt(jnp.mean(xf * xf, axis=-1, keepdims=True) + EPS)
    return (y * w.astype(jnp.float32)).astype(x.dtype)


def _rope(x, cos, sin):
    half = x.shape[-1] // 2
    x1, x2 = x[..., :half], x[..., half:]
    cos = cos.astype(x.dtype)
    sin = sin.astype(x.dtype)
    return jnp.concatenate([x1 * cos - x2 * sin, x2 * cos + x1 * sin], axis=-1)


def _dwconv_centred(x, w, b):
    k = w.shape[0]
    y = lax.conv_general_dilated(
        x, w[:, None, :].astype(x.dtype), window_strides=(1,),
        padding=[(k // 2, k // 2)], dimension_numbers=('NWC', 'WIO', 'NWC'),
        feature_group_count=x.shape[-1])
    return y + b.astype(x.dtype)


def _segsum(a):
    t = a.shape[-1]
    cs = jnp.cumsum(a, axis=-1)
    seg = cs[..., :, None] - cs[..., None, :]
    mask = jnp.tril(jnp.ones((t, t), dtype=bool))
    return jnp.where(mask, seg, -jnp.inf)


def _ssd_chunked(x, a, bm, cm):
    b, L, h, p = x.shape
    nc = L // SSD_CHUNK
    x = x.reshape(b, nc, SSD_CHUNK, h, p)
    bm = bm.reshape(b, nc, SSD_CHUNK, h, -1)
    cm = cm.reshape(b, nc, SSD_CHUNK, h, -1)
    a = a.reshape(b, nc, SSD_CHUNK, h).transpose(0, 3, 1, 2)
    a_cs = jnp.cumsum(a, axis=-1)
    lmat = jnp.exp(_segsum(a))
    scores = jnp.einsum('bclhn,bcshn->bhcls', cm, bm) * lmat
    y_diag = jnp.einsum('bhcls,bcshp->bclhp', scores, x)
    decay_states = jnp.exp(a_cs[..., -1:] - a_cs).transpose(0, 2, 3, 1)
    states = jnp.einsum('bclhn,bclhp->bchpn', bm * decay_states[..., None], x)
    states = jnp.concatenate([jnp.zeros_like(states[:, :1]), states], axis=1)
    decay_chunk = jnp.exp(_segsum(jnp.pad(a_cs[..., -1], ((0, 0), (0, 0), (1, 0)))))
    entering = jnp.einsum('bhzc,bchpn->bzhpn', decay_chunk, states)[:, :-1]
    state_decay_out = jnp.exp(a_cs).transpose(0, 2, 3, 1)[..., None]
    y_off = jnp.einsum('bclhn,bchpn->bclhp', cm, entering) * state_decay_out
    return (y_diag + y_off).reshape(b, L, h, p)


def _ssd_branch(u, conv_w, conv_b, a_log, dt_bias, d_skip, norm_w):
    b, L, _ = u.shape
    z = u[..., :SSD_WIDTH]
    xbc = u[..., SSD_WIDTH:SSD_WIDTH + SSD_XBC]
    dt = u[..., SSD_WIDTH + SSD_XBC:]
    xbc = jax.nn.silu(_dwconv_centred(xbc, conv_w, conv_b)).astype(jnp.float32)
    xs = xbc[..., :SSD_WIDTH].reshape(b, L, SSD_HEADS, SSD_HEAD_DIM)
    gs = SSD_GROUPS * SSD_STATE
    rep = SSD_HEADS // SSD_GROUPS
    bm = jnp.repeat(xbc[..., SSD_WIDTH:SSD_WIDTH + gs].reshape(b, L, SSD_GROUPS, SSD_STATE), rep, axis=2)
    cm = jnp.repeat(xbc[..., SSD_WIDTH + gs:].reshape(b, L, SSD_GROUPS, SSD_STATE), rep, axis=2)
    dt = jax.nn.softplus(dt.reshape(b, L, 2, SSD_HEADS).astype(jnp.float32) + dt_bias.astype(jnp.float32))
    a = -jnp.exp(a_log.astype(jnp.float32))
    dta = dt * a
    flip = lambda t: jnp.flip(t, axis=1)
    y_f = _ssd_chunked(xs * dt[:, :, 0, :, None], dta[:, :, 0], bm, cm)
    y_b = flip(_ssd_chunked(flip(xs * dt[:, :, 1, :, None]), flip(dta[:, :, 1]), flip(bm), flip(cm)))
    dsk = d_skip.astype(jnp.float32)
    y = y_f + y_b + (dsk[0] + dsk[1])[:, None] * xs
    y = y.reshape(b, L, SSD_WIDTH) * jax.nn.silu(z.astype(jnp.float32))
    yg = y.reshape(b, L, SSD_GROUPS, SSD_WIDTH // SSD_GROUPS)
    yg = yg * lax.rsqrt(jnp.mean(yg * yg, axis=-1, keepdims=True) + EPS)
    y = yg.reshape(b, L, SSD_WIDTH) * norm_w.astype(jnp.float32)
    return y.astype(u.dtype)


def _diff_branch(u, qk_norm_w, lam, subln_w, lam_init, cos, sin):
    b, L, _ = u.shape
    h, d = DIFF_HEADS, DIFF_HEAD_DIM
    q, k, v, g = jnp.split(u, 4, axis=-1)
    q = q.reshape(b, L, h, 2, d).transpose(0, 2, 3, 1, 4)
    k = k.reshape(b, L, h, 2, d).transpose(0, 2, 3, 1, 4)
    q = _rope(_rmsnorm(q, qk_norm_w[0]), cos, sin) * (d ** -0.5)
    k = _rope(_rmsnorm(k, qk_norm_w[1]), cos, sin)
    v = v.reshape(b, L, h, 2 * d).transpose(0, 2, 1, 3)
    lf = lam.astype(jnp.float32)
    lam_full = jnp.exp(jnp.sum(lf[0] * lf[1])) - jnp.exp(jnp.sum(lf[2] * lf[3])) + lam_init
    nb = L // Q_BLOCK
    qb = q.reshape(b, h, 2, nb, Q_BLOCK, d).transpose(3, 0, 1, 2, 4, 5)

    def one_block(qblk):
        s = jnp.einsum('bhcqd,bhckd->bhcqk', qblk, k).astype(jnp.float32)
        p = jax.nn.softmax(s, axis=-1)
        attn = p[:, :, 0] - lam_full * p[:, :, 1]
        return jnp.einsum('bhqk,bhkd->bhqd', attn.astype(v.dtype), v)

    o = lax.map(one_block, qb)
    o = o.transpose(1, 0, 3, 2, 4).reshape(b, L, h, 2 * d)
    o = _rmsnorm(o, subln_w) * (1.0 - lam_init)
    return o.reshape(b, L, DIFF_WIDTH) * jax.nn.silu(g)


def _na_branch(u, qk_norm_w, rpb):
    b, L, _ = u.shape
    h, d = NA_HEADS, NA_HEAD_DIM
    rows = L // GRID_W
    kh = min(NA_KH_MAX, rows)
    q, k, v, g = jnp.split(u, 4, axis=-1)
    to_grid = lambda t: t.reshape(b, rows, GRID_W, h, d).transpose(0, 3, 1, 2, 4)
    qg = to_grid(_rmsnorm(q.reshape(b, L, h, d), qk_norm_w[0]) * (d ** -0.5))
    kg = to_grid(_rmsnorm(k.reshape(b, L, h, d), qk_norm_w[1]))
    vg = to_grid(v)
    r = jnp.arange(rows)
    row_start = jnp.clip(r - kh // 2, 0, rows - kh)
    row_idx = row_start[:, None] + jnp.arange(kh)
    kb = kg[:, :, row_idx]
    vb = vg[:, :, row_idx]
    s = jnp.einsum('bhrqd,bhrjkd->bhrqjk', qg, kb).astype(jnp.float32)
    c = jnp.arange(GRID_W)
    col_start = jnp.clip(c - NA_KW // 2, 0, GRID_W - NA_KW)
    col_ok = (c[None, :] >= col_start[:, None]) & (c[None, :] < col_start[:, None] + NA_KW)
    dr_i = row_idx - r[:, None] + (NA_KH_MAX - 1)
    dc_i = jnp.clip(c[None, :] - c[:, None] + (NA_KW - 1), 0, 2 * NA_KW - 2)
    bias = rpb.astype(jnp.float32)[:, dr_i[:, None, :, None], dc_i[None, :, None, :]]
    s = jnp.where(col_ok[:, None, :], s + bias[None], -jnp.inf)
    p = jax.nn.softmax(s.reshape(s.shape[:4] + (kh * GRID_W,)), axis=-1).reshape(s.shape)
    o = jnp.einsum('bhrqjk,bhrjkd->bhrqd', p.astype(vb.dtype), vb)
    o = o.transpose(0, 2, 3, 1, 4).reshape(b, L, NA_WIDTH)
    return o * jax.nn.silu(g)


def setup_inputs(seed: int = 0) -> dict:
    key = jax.random.key(seed)
    ks = jax.random.split(key, 20)
    f32 = jnp.float32
    x = jax.random.normal(ks[0], (BATCH, SEQ, D_MODEL), f32)
    norm_w = 1.0 + 0.02 * jax.random.normal(ks[1], (DEPTH, D_MODEL), f32)
    w_in = jax.random.normal(ks[2], (DEPTH, D_MODEL, IN_WIDTH), f32) * (D_MODEL ** -0.5)
    conv_w = jax.random.normal(ks[3], (DEPTH, SSD_CONV, SSD_XBC), f32) * (SSD_CONV ** -0.5)
    conv_b = 0.01 * jax.random.normal(ks[4], (DEPTH, SSD_XBC), f32)
    a_log = jnp.log(jax.random.uniform(ks[5], (DEPTH, 2, SSD_HEADS), f32, 1.0, 16.0))
    dt0 = jnp.exp(jax.random.uniform(ks[6], (DEPTH, 2, SSD_HEADS), f32, math.log(1e-3), math.log(1e-1)))
    dt_bias = dt0 + jnp.log(-jnp.expm1(-dt0))
    d_skip = 1.0 + 0.1 * jax.random.normal(ks[7], (DEPTH, 2, SSD_HEADS), f32)
    ssd_norm_w = 1.0 + 0.02 * jax.random.normal(ks[8], (DEPTH, SSD_WIDTH), f32)
    diff_qk_norm = 1.0 + 0.02 * jax.random.normal(ks[9], (DEPTH, 2, DIFF_HEAD_DIM), f32)
    diff_lambda = 0.1 * jax.random.normal(ks[10], (DEPTH, 4, DIFF_HEAD_DIM), f32)
    diff_subln = 1.0 + 0.02 * jax.random.normal(ks[11], (DEPTH, 2 * DIFF_HEAD_DIM), f32)
    na_qk_norm = 1.0 + 0.02 * jax.random.normal(ks[12], (DEPTH, 2, NA_HEAD_DIM), f32)
    na_rpb = 0.02 * jax.random.normal(ks[13], (DEPTH, NA_HEADS, 2 * NA_KH_MAX - 1, 2 * NA_KW - 1), f32)
    w_out = jax.random.normal(ks[14], (DEPTH, MIX_WIDTH, D_MODEL), f32) * (MIX_WIDTH ** -0.5)
    return {'x': x, 'norm_w': norm_w, 'w_in': w_in, 'conv_w': conv_w, 'conv_b': conv_b,
            'a_log': a_log, 'dt_bias': dt_bias, 'd_skip': d_skip, 'ssd_norm_w': ssd_norm_w,
            'diff_qk_norm': diff_qk_norm, 'diff_lambda': diff_lambda, 'diff_subln': diff_subln,
            'na_qk_norm': na_qk_norm, 'na_rpb': na_rpb, 'w_out': w_out}


def reference(x, norm_w, w_in, conv_w, conv_b, a_log, dt_bias, d_skip, ssd_norm_w,
              diff_qk_norm, diff_lambda, diff_subln, na_qk_norm, na_rpb, w_out):
    L = x.shape[1]
    inv_freq = ROPE_THETA ** (-jnp.arange(0, DIFF_HEAD_DIM, 2, dtype=jnp.float32) / DIFF_HEAD_DIM)
    ang = jnp.arange(L, dtype=jnp.float32)[:, None] * inv_freq[None, :]
    cos, sin = jnp.cos(ang), jnp.sin(ang)
    for i in range(DEPTH):
        lam_init = 0.8 - 0.6 * math.exp(-0.3 * i)
        hdn = _rmsnorm(x, norm_w[i])
        u = jnp.einsum('bld,de->ble', hdn, w_in[i].astype(hdn.dtype))
        u_ssd = u[..., :SSD_IN]
        u_diff = u[..., SSD_IN:SSD_IN + DIFF_IN]
        u_na = u[..., SSD_IN + DIFF_IN:]
        y_ssd = _ssd_branch(u_ssd, conv_w[i], conv_b[i], a_log[i], dt_bias[i], d_skip[i], ssd_norm_w[i])
        y_diff = _diff_branch(u_diff, diff_qk_norm[i], diff_lambda[i], diff_subln[i], lam_init, cos, sin)
        y_na = _na_branch(u_na, na_qk_norm[i], na_rpb[i])
        y = jnp.concatenate([y_ssd, y_diff, y_na], axis=-1)
        x = x + jnp.einsum('ble,ed->bld', y, w_out[i].astype(y.dtype))
    return x
```

```python
import contextlib
import math
import numpy as np
import concourse.bass as bass
import concourse.mybir as mybir
from concourse.bass_utils import run_bass_kernel_spmd

F32 = mybir.dt.float32
BF16 = mybir.dt.bfloat16
AF = mybir.ActivationFunctionType
ALU = mybir.AluOpType
AX = mybir.AxisListType

L = 2048
DM = 1024
NT = 16
INW = 6688
EPS = 1e-6
NEG = -30000.0
C_Z, C_XBC, C_DT, C_DIFF, C_NA = 0, 1024, 2560, 2592, 4640


class T:
    __slots__ = ("w", "r", "excl")

    def __init__(self, excl=False):
        self.w = None
        self.r = []
        self.excl = excl


class Op:
    __slots__ = ("eng", "fn", "deps", "idx", "sig", "isdma", "semid", "semval", "prev_same_sem")


class Prog:
    COMPUTE = ("pe", "act", "dve", "pool")
    DMAQ = ("sp", "actq", "poolq")
    STREAM = {"pe": "pe", "act": "act", "dve": "dve", "pool": "pool", "sp": "sp", "actq": "act", "poolq": "pool"}
    STREAMS = ("pe", "act", "dve", "pool", "sp")

    def __init__(self, nc, n_dma_sems=12):
        self.nc = nc
        self.ops = []
        self.n_dma_sems = n_dma_sems
        self.last = {s: None for s in self.STREAMS}
        self.recent_dma = {q: [] for q in self.DMAQ}
        self.frontier = []
        self.synced = {s: True for s in self.STREAMS}

    def barrier(self):
        fr = [o for o in self.last.values() if o is not None]
        for q in self.DMAQ:
            fr.extend(self.recent_dma[q])
        self.frontier = fr
        self.synced = {s: False for s in self.STREAMS}

    def op(self, eng, fn, reads=(), writes=()):
        o = Op()
        o.eng = eng
        o.fn = fn
        o.isdma = eng in self.DMAQ
        o.idx = len(self.ops)
        o.prev_same_sem = None
        deps = {}
        if any(t.excl for t in reads):
            writes = list(writes) + [t for t in reads if t.excl and t not in writes]
            reads = [t for t in reads if not t.excl]
        for t in reads:
            if t.w is not None:
                deps[t.w.idx] = ("raw", t.w)
        for t in writes:
            if t.w is not None and t.w.idx not in deps:
                deps[t.w.idx] = ("waw", t.w)
            for r in t.r:
                if r.idx not in deps:
                    deps[r.idx] = ("war", r)
        st = self.STREAM[eng]
        if not self.synced[st]:
            for p in self.frontier:
                if p.idx not in deps:
                    deps[p.idx] = ("bar", p)
            self.synced[st] = True
        for t in writes:
            t.w = o
            t.r = []
        for t in reads:
            if t.w is not o:
                t.r.append(o)
        o.deps = deps
        o.sig = False
        self.ops.append(o)
        self.last[st] = o
        if o.isdma:
            lst = self.recent_dma[eng]
            lst.append(o)
            if len(lst) > self.n_dma_sems:
                lst.pop(0)
        return o

    def emit(self, final_waits=()):
        nc = self.nc
        streams = {s: [] for s in self.STREAMS}
        for o in self.ops:
            streams[self.STREAM[o.eng]].append(o)
        pos = {}
        for s, lst in streams.items():
            for i, o in enumerate(lst):
                pos[o.idx] = i
        need = {}
        for o in self.ops:
            lst = []
            so = self.STREAM[o.eng]
            for (kind, p) in o.deps.values():
                sp_ = self.STREAM[p.eng]
                if p.isdma or o.isdma:
                    lst.append(p)
                elif sp_ != so:
                    lst.append(p)
                else:
                    if so == "pe":
                        continue
                    lst.append(p)
            need[o.idx] = lst
            for p in lst:
                p.sig = True
        for o in final_waits:
            o.sig = True
        stack = contextlib.ExitStack()
        esem = {e: stack.enter_context(nc.semaphore("s_" + e)) for e in self.COMPUTE}
        ecount = {e: 0 for e in self.COMPUTE}
        dsems = {q: [stack.enter_context(nc.semaphore("d_%s_%d" % (q, i))) for i in range(self.n_dma_sems)]
                 for q in self.DMAQ}
        duse = {q: [0] * self.n_dma_sems for q in self.DMAQ}
        dlast = {q: [None] * self.n_dma_sems for q in self.DMAQ}
        dnext = {q: 0 for q in self.DMAQ}
        for s in self.STREAMS:
            for o in streams[s]:
                if o.isdma:
                    q = o.eng
                    j = dnext[q]
                    dnext[q] = (j + 1) % self.n_dma_sems
                    duse[q][j] += 1
                    o.semid = (q, j)
                    o.semval = 16 * duse[q][j]
                    o.prev_same_sem = dlast[q][j]
                    dlast[q][j] = o
                elif o.sig:
                    ecount[o.eng] += 1
                    o.semid = o.eng
                    o.semval = ecount[o.eng]
        self.n_waits = 0

        def sem_of(p):
            if p.isdma:
                return dsems[p.semid[0]][p.semid[1]]
            return esem[p.semid]

        def emit_stream(s, engobj):
            known = {}
            for o in streams[s]:
                waits = {}
                cand = list(need[o.idx])
                if o.isdma and o.prev_same_sem is not None:
                    cand.append(o.prev_same_sem)
                for p in cand:
                    k = p.semid
                    if known.get(k, 0) >= p.semval:
                        continue
                    if waits.get(k, (0, None))[0] < p.semval:
                        waits[k] = (p.semval, p)
                for k, (v, p) in waits.items():
                    engobj.wait_ge(sem_of(p), v)
                    known[k] = v
                    self.n_waits += 1
                ins = o.fn(engobj)
                if o.isdma:
                    ins.then_inc(dsems[o.semid[0]][o.semid[1]], 16)
                elif o.sig:
                    ins.then_inc(esem[o.semid], 1)
            if s == "sp":
                for o in final_waits:
                    engobj.wait_ge(sem_of(o), o.semval)

        with nc.Block() as block:
            @block.tensor
            def _(e):
                emit_stream("pe", e)

            @block.scalar
            def _(e):
                emit_stream("act", e)

            @block.vector
            def _(e):
                emit_stream("dve", e)

            @block.gpsimd
            def _(e):
                emit_stream("pool", e)

            @block.sync
            def _(e):
                emit_stream("sp", e)
        stack.close()


def ACT(out, in_, func, **kw):
    return lambda e: e.activation(out=out, in_=in_, func=func, **kw)


def TT(out, in0, in1, op):
    return lambda e: e.tensor_tensor(out=out, in0=in0, in1=in1, op=op)


def TS(out, in0, s1, op0, s2=None, op1=None):
    if op1 is None:
        return lambda e: e.tensor_scalar(out=out, in0=in0, scalar1=s1, scalar2=None, op0=op0)
    return lambda e: e.tensor_scalar(out=out, in0=in0, scalar1=s1, scalar2=s2, op0=op0, op1=op1)


def STT(out, in0, scalar, in1, op0, op1):
    return lambda e: e.scalar_tensor_tensor(out=out, in0=in0, scalar=scalar, in1=in1, op0=op0, op1=op1)


def CP(out, in_):
    return lambda e: e.tensor_copy(out=out, in_=in_)


def MM(out, lhsT, rhs, start=True, stop=True, skip=False):
    if skip:
        return lambda e: e.matmul(out, lhsT, rhs, start=start, stop=stop, skip_group_check=True)
    return lambda e: e.matmul(out, lhsT, rhs, start=start, stop=stop)


def TR(out, in_, ident):
    return lambda e: e.transpose(out, in_, ident)


def DMA(out, in_):
    return lambda e: e.dma_start(out=out, in_=in_)


def RED(out, in_, op=None):
    return lambda e: e.tensor_reduce(out=out, in_=in_, axis=AX.X, op=(op or ALU.add))


def RECIP(out, in_):
    return lambda e: e.reciprocal(out=out, in_=in_)


def MEMSET(ap, v):
    return lambda e: e.memset(ap, v)


class Arena:
    def __init__(self, nc, words):
        self.t = nc.alloc_sbuf_tensor("arena", [128, words], F32)
        self.words = words
        self.off = 0
        self.peak = 0

    def alloc(self, shape, dt):
        n = int(np.prod(shape[1:]))
        nw = n if dt == F32 else (n + 1) // 2
        nw = (nw + 7) // 8 * 8
        assert self.off + nw <= self.words, ("SBUF arena overflow", self.off, nw, self.words)
        v = self.t[:, self.off:self.off + nw]
        self.off += nw
        self.peak = max(self.peak, self.off)
        if dt != F32:
            v = v.bitcast(dt)
        v = v[:, 0:n]
        if len(shape) == 3:
            v = v.rearrange("p (a b) -> p a b", b=shape[2])
        elif len(shape) == 4:
            v = v.rearrange("p (a b c) -> p a b c", b=shape[2], c=shape[3])
        return v

    def mark(self):
        return self.off

    def release(self, m):
        self.off = m


def bc(ap, shape):
    return ap.to_broadcast(shape)


class Ctx:
    pass


def build(n_layers=2, phases=("diff", "na", "ssd"), dbg=False):
    nc = bass.Bass("TRN2", target_bir_lowering=False)
    c = Ctx()
    c.nc = nc
    c.dbg = dbg

    def din(name, shape, dt=F32):
        return nc.dram_tensor(name, shape, dt, kind="ExternalInput").ap()

    c.x_in = din("x", [L, DM])
    c.norm_w = din("norm_w", [2, DM])
    c.w_in = din("w_in", [2, DM, INW])
    c.conv_w = din("conv_w", [2, 5, 1536])
    c.conv_b = din("conv_b", [2, 1536])
    c.a_log = din("a_log", [2, 32])
    c.dt_bias = din("dt_bias", [2, 32])
    c.d_skip = din("d_skip", [2, 32])
    c.ssd_norm_w = din("ssd_norm_w", [2, 1024])
    c.diff_qk_norm = din("diff_qk_norm", [2, 128])
    c.diff_lambda = din("diff_lambda", [2, 256])
    c.diff_subln = din("diff_subln", [2, 128])
    c.na_qk_norm = din("na_qk_norm", [2, 128])
    c.na_tab = din("na_tab", [2, 8, 128, NA_NCLS * 128])
    c.w_out = din("w_out", [2, 2048, DM])
    c.cs_tab = din("cs_tab", [L, 128])
    c.consts = din("consts", [128, 4 * 128])
    c.out = nc.dram_tensor("out", [L, DM], F32, kind="ExternalOutput").ap()
    kind_scr = "ExternalOutput" if dbg else "Internal"
    c.x1 = nc.dram_tensor("x1", [L, DM], F32, kind=kind_scr).ap()
    c.yT = [nc.dram_tensor("yT%d" % i, [2048, L], BF16, kind=kind_scr).ap() for i in range(n_layers)]
    c.csT_d = nc.dram_tensor("csT_d", [NT, 32, 128], F32, kind="Internal").ap()
    c.sb_d = nc.dram_tensor("sb_d", [2, NT, 128, 512], BF16, kind="Internal").ap()
    c.sf_d = nc.dram_tensor("sf_d", [2, NT, 128, 512], BF16, kind="Internal").ap()

    c.dumps = []

    def dump(name, ap, shape, dt, tiles):
        if not dbg:
            return
        d = nc.dram_tensor("dbg_" + name, list(shape), dt, kind="ExternalOutput").ap()
        c.dumps.append(c.P.op("sp", DMA(d, ap), reads=tiles))
    c.dump = dump
    A = Arena(nc, 51000)
    c.A = A
    P = Prog(nc)
    c.P = P
    c.psall = nc.alloc_psum_tensor("psall", [128, 8 * 512], F32)[:, :]
    c.ps = [c.psall[:, i * 512:(i + 1) * 512] for i in range(7)]
    c.tps = [T(excl=True) for _ in range(7)]
    c.psb = c.psall[:, 7 * 512:8 * 512].bitcast(BF16)
    _tb = T(excl=True)
    c.tpsb = [_tb, _tb]

    c.cst = A.alloc([128, 512], F32)
    c.t_cst = T()
    P.op("sp", DMA(c.cst, c.consts), writes=[c.t_cst])
    c.identf = c.cst[:, 0:128]
    c.U = c.cst[:, 128:256]
    c.Ur = c.cst[:, 256:384]
    c.onesf = c.cst[:, 384:512]
    c.ident = A.alloc([128, 128], BF16)
    c.t_ident = T()
    P.op("dve", CP(c.ident, c.identf), reads=[c.t_cst], writes=[c.t_ident])
    c.epsc = A.alloc([128, 1], F32)
    c.t_eps = T()
    P.op("pool", MEMSET(c.epsc, EPS), writes=[c.t_eps])
    c.onec = A.alloc([128, 1], F32)
    P.op("pool", MEMSET(c.onec, 1.0), writes=[c.t_eps])
    c.hdnT = A.alloc([128, 8, L], BF16)
    c.t_hdnT = [T() for _ in range(NT)]
    c.Wbuf = A.alloc([128, 16384], BF16)
    c.t_Wq = [T() for _ in range(4)]
    c.W8 = c.Wbuf.rearrange("p (k c) -> p k c", c=2048)
    c.Wo = c.Wbuf.rearrange("p (k c) -> p k c", c=1024)
    c.Wz = [c.Wbuf[:, 3 * 4096:4 * 4096].rearrange("p (k c) -> p k c", c=512),
            c.Wbuf[:, 2 * 4096:3 * 4096].rearrange("p (k c) -> p k c", c=512)]
    c.pf = {}
    c.phases = phases

    finals = []
    for l in range(n_layers):
        xsrc = c.x_in if l == 0 else c.x1
        xdst = c.x1 if l < n_layers - 1 or dbg and n_layers == 1 else c.out
        if l == n_layers - 1:
            xdst = c.out
        P.barrier()
        if "diff" in phases:
            pf_qkvg(c, l, "diff")
        if l == 0:
            phase_hdn(c, l, xsrc)
        if "diff" in phases:
            P.barrier()
            phase_diff(c, l)
        if "na" in phases:
            P.barrier()
            phase_na(c, l)
        if "ssd" in phases:
            P.barrier()
            phase_ssd(c, l)
        P.barrier()
        fin = phase_out(c, l, xsrc, xdst, fuse_next=(l + 1 < n_layers))
        if l == n_layers - 1:
            finals = fin
    P.barrier()
    finals = list(finals)
    if dbg:
        finals += [o for o in P.ops if o.isdma][-36:] + c.dumps
    P.emit(final_waits=finals)
    c.peak = A.peak
    return nc, c


def phase_hdn(c, l, xsrc):
    P, A = c.P, c.A
    m = A.mark()
    nwb = A.alloc([128, DM], F32)
    t_nwb = T()
    P.op("sp", DMA(nwb, c.norm_w[l:l + 1, :].partition_broadcast(128)), writes=[t_nwb])
    NX = 4
    xt = [A.alloc([128, DM], F32) for _ in range(NX)]
    t_xt = [T() for _ in range(NX)]
    sq = A.alloc([128, DM], F32)
    t_sq = T()
    ss = A.alloc([128, NT], F32)
    t_ss = [T() for _ in range(NT)]
    hb = [A.alloc([128, DM], BF16) for _ in range(2)]
    t_hb = [T(), T()]

    def load_x(tt):
        P.op("sp", DMA(xt[tt % NX], xsrc[tt * 128:(tt + 1) * 128, :]), writes=[t_xt[tt % NX]])
    for tt in range(min(NX - 1, NT)):
        load_x(tt)
    for tt in range(NT):
        b = tt % 2
        xb, t_xb = xt[tt % NX], t_xt[tt % NX]
        s1 = ss[:, tt:tt + 1]
        if tt + NX - 1 < NT:
            load_x(tt + NX - 1)
        P.op("act", ACT(sq, xb, AF.Square, accum_out=s1), reads=[t_xb], writes=[t_sq, t_ss[tt]])
        P.op("act", ACT(s1, s1, AF.Ln, scale=1.0 / DM, bias=c.epsc), reads=[t_ss[tt], c.t_eps], writes=[t_ss[tt]])
        P.op("act", ACT(s1, s1, AF.Exp, scale=-0.5), reads=[t_ss[tt]], writes=[t_ss[tt]])
        P.op("dve", STT(hb[b], xb, s1, nwb, ALU.mult, ALU.mult), reads=[t_xb, t_ss[tt], t_nwb], writes=[t_hb[b]])
        for k in range(8):
            P.op("pe", TR(c.psb[:, k * 128:(k + 1) * 128], hb[b][:, k * 128:(k + 1) * 128], c.ident),
                 reads=[t_hb[b], c.t_ident], writes=[c.tpsb[k // 4]])
        P.op("act", ACT(c.hdnT[:, :, tt * 128:(tt + 1) * 128], c.psb[:, :].rearrange("p (k t) -> p k t", t=128), AF.Copy),
             reads=[c.tpsb[0], c.tpsb[1]], writes=[c.t_hdnT[tt]])
    A.release(m)


def load_w(c, l, dst, col0, ncols, tiles, step=512, extra=()):
    P = c.P
    j = 0
    for c0 in range(0, ncols, step):
        n = min(step, ncols - c0)
        src = c.w_in[l, :, col0 + c0:col0 + c0 + n].rearrange("(k p) c -> p k c", p=128)
        P.op("poolq", DMA(dst[:, :, c0:c0 + n], src), writes=[tiles[j]] + list(extra))
        j += 1


def pf_qkvg(c, l, which):
    key = (which, l)
    if key not in c.pf:
        tW = [T() for _ in range(4)]
        load_w(c, l, c.W8, C_DIFF if which == "diff" else C_NA, 2048, tW, extra=c.t_Wq)
        c.pf[key] = tW
    return c.pf[key]


def pf_ssd(c, l):
    key = ("ssd", l)
    if key not in c.pf:
        tWz = [[T()], [T()]]
        load_w(c, l, c.Wz[0], C_Z, 512, tWz[0], extra=[c.t_Wq[3]])
        load_w(c, l, c.Wz[1], C_Z + 512, 512, tWz[1], extra=[c.t_Wq[2]])
        c.pf[key] = tWz
    return c.pf[key]


def pf_out(c, l, quarters):
    key = ("out", l)
    if key not in c.pf:
        c.pf[key] = [None] * 4
    tWo = c.pf[key]
    for j in quarters:
        if tWo[j] is None:
            tWo[j] = T()
            src = c.w_out[l, j * 512:(j + 1) * 512, :].rearrange("(k p) c -> p k c", p=128)
            c.P.op("poolq", DMA(c.Wo[:, j * 4:(j + 1) * 4, :], src), writes=[tWo[j], c.t_Wq[j]])
    return tWo


def qk_norm_rope(c, src_ps, t_src, sq, t_sq, ss8, t_ss8, tbuf, t_tb, ubuf, t_ub, TC, TS_, t_tab, tt, dst, t_dst, rope):
    P = c.P
    P.op("act", ACT(sq, src_ps, AF.Square), reads=[t_src], writes=[t_sq])
    P.op("dve", RED(ss8, sq.rearrange("p (g d) -> p g d", d=64)), reads=[t_sq], writes=[t_ss8])
    P.op("act", ACT(ss8, ss8, AF.Ln, scale=1.0 / 64, bias=c.epsc), reads=[t_ss8, c.t_eps], writes=[t_ss8])
    P.op("act", ACT(ss8, ss8, AF.Exp, scale=-0.5), reads=[t_ss8], writes=[t_ss8])
    t3 = tbuf.rearrange("p (g d) -> p g d", d=64)
    P.op("dve", TT(t3, src_ps.rearrange("p (g d) -> p g d", d=64), bc(ss8.unsqueeze(2), [128, 8, 64]), ALU.mult),
         reads=[t_src, t_ss8], writes=[t_tb])
    if rope:
        u3 = ubuf.rearrange("p (g d) -> p g d", d=64)
        P.op("pool", TT(u3[:, :, 0:32], t3[:, :, 32:64], bc(TS_[:, tt, 0:32].unsqueeze(1), [128, 8, 32]), ALU.mult),
             reads=[t_tb, t_tab], writes=[t_ub])
        P.op("pool", TT(u3[:, :, 32:64], t3[:, :, 0:32], bc(TS_[:, tt, 32:64].unsqueeze(1), [128, 8, 32]), ALU.mult),
             reads=[t_tb, t_tab], writes=[t_ub])
        P.op("dve", TT(t3, t3, bc(TC[:, tt, :].unsqueeze(1), [128, 8, 64]), ALU.mult), reads=[t_tb, t_tab], writes=[t_tb])
        P.op("dve", TT(dst, tbuf, ubuf, ALU.add), reads=[t_tb, t_ub], writes=[t_dst])
    else:
        P.op("dve", TT(dst.rearrange("p (g d) -> p g d", d=64), t3, bc(TC.unsqueeze(1), [128, 8, 64]), ALU.mult),
             reads=[t_tb, t_tab], writes=[t_dst])


def prep_qkvg(c, W, tW, qT, t_qT, kT, t_kT, vcopy, t_V, G, t_G, tabs, t_tab, rope):
    P, A = c.P, c.A
    ps, tps = c.ps, c.tps
    NB = 2
    raw = [[A.alloc([128, 512], F32) for _ in range(NB)] for _ in range(2)]
    t_raw = [[T() for _ in range(NB)] for _ in range(2)]
    sq = [A.alloc([128, 512], F32) for _ in range(2)]
    t_sq = [T(), T()]
    ss8 = [[A.alloc([128, 8], F32) for _ in range(NB)] for _ in range(2)]
    t_ss8 = [[T() for _ in range(NB)] for _ in range(2)]
    tbuf = [[A.alloc([128, 512], F32) for _ in range(NB)] for _ in range(2)]
    t_tb = [[T() for _ in range(NB)] for _ in range(2)]
    ubuf = [[A.alloc([128, 512], F32) for _ in range(NB)] for _ in range(2)] if rope else None
    t_ub = [[T() for _ in range(NB)] for _ in range(2)]
    rbuf = [[A.alloc([128, 512], BF16) for _ in range(NB)] for _ in range(2)]
    t_rb = [[T() for _ in range(NB)] for _ in range(2)]
    dsts = ((qT, t_qT), (kT, t_kT))
    nbank = 0
    gbank = {}

    def chain(tt, i):
        b = tt % NB
        rw, t_rw = raw[i][b], t_raw[i][b]
        s8, t_s8 = ss8[i][b], t_ss8[i][b]
        tb_, t_tb_ = tbuf[i][b], t_tb[i][b]
        P.op("act", ACT(sq[i], rw, AF.Square), reads=[t_rw], writes=[t_sq[i]])
        P.op("dve", RED(s8, sq[i].rearrange("p (g d) -> p g d", d=64)), reads=[t_sq[i]], writes=[t_s8])
        P.op("act", ACT(s8, s8, AF.Ln, scale=1.0 / 64, bias=c.epsc), reads=[t_s8, c.t_eps], writes=[t_s8])
        P.op("act", ACT(s8, s8, AF.Exp, scale=-0.5), reads=[t_s8], writes=[t_s8])
        t3 = tb_.rearrange("p (g d) -> p g d", d=64)
        P.op("dve", TT(t3, rw.rearrange("p (g d) -> p g d", d=64), bc(s8.unsqueeze(2), [128, 8, 64]), ALU.mult),
             reads=[t_rw, t_s8], writes=[t_tb_])
        dst, t_dst = rbuf[i][b], t_rb[i][b]
        TC, TS_ = tabs[2 * i], tabs[2 * i + 1]
        if rope:
            ub, t_ub_ = ubuf[i][b], t_ub[i][b]
            u3 = ub.rearrange("p (g d) -> p g d", d=64)
            P.op("pool", TT(u3[:, :, 0:32], t3[:, :, 32:64], bc(TS_[:, tt, 0:32].unsqueeze(1), [128, 8, 32]), ALU.mult),
                 reads=[t_tb_, t_tab], writes=[t_ub_])
            P.op("pool", TT(u3[:, :, 32:64], t3[:, :, 0:32], bc(TS_[:, tt, 32:64].unsqueeze(1), [128, 8, 32]), ALU.mult),
                 reads=[t_tb_, t_tab], writes=[t_ub_])
            P.op("dve", TT(t3, t3, bc(TC[:, tt, :].unsqueeze(1), [128, 8, 64]), ALU.mult), reads=[t_tb_, t_tab], writes=[t_tb_])
            P.op("dve", TT(dst, tb_, ub, ALU.add), reads=[t_tb_, t_ub_], writes=[t_dst])
        else:
            P.op("dve", TT(dst.rearrange("p (g d) -> p g d", d=64), t3, bc(TC.unsqueeze(1), [128, 8, 64]), ALU.mult),
                 reads=[t_tb_, t_tab], writes=[t_dst])

    def transposes(tt):
        b = tt % NB
        tok = slice(tt * 128, (tt + 1) * 128)
        for i in range(2):
            for h in range(4):
                P.op("pe", TR(c.psb[:, i * 512 + h * 128:i * 512 + (h + 1) * 128], rbuf[i][b][:, h * 128:(h + 1) * 128], c.ident),
                     reads=[t_rb[i][b], c.t_ident], writes=[c.tpsb[i]])
        for i in range(2):
            dstT, t_dstT = dsts[i]
            P.op("dve", CP(dstT[:, :, tok], c.psb[:, i * 512:(i + 1) * 512].rearrange("p (h t) -> p h t", t=128)),
                 reads=[c.tpsb[i]], writes=[t_dstT[tt]])

    for tt in range(NT + 1):
        if tt < NT:
            tok = slice(tt * 128, (tt + 1) * 128)
            b = tt % NB
            banks = []
            for j in range(4):
                bk = nbank % 7
                nbank += 1
                banks.append(bk)
                for k in range(8):
                    P.op("pe", MM(ps[bk], c.hdnT[:, k, tok], W[:, k, j * 512:(j + 1) * 512], start=(k == 0), stop=(k == 7)),
                         reads=[c.t_hdnT[tt], tW[j]] + c.t_Wq, writes=[tps[bk]])
            for i in range(2):
                P.op("dve", CP(raw[i][b], ps[banks[i]]), reads=[tps[banks[i]]], writes=[t_raw[i][b]])
            vcopy(tt, ps[banks[2]], tps[banks[2]])
            gbank[tt] = banks[3]
            for i in range(2):
                chain(tt, i)
            if tt % 2 == 1 or tt == NT - 1:
                for t2 in ([tt - 1, tt] if tt % 2 == 1 else [tt]):
                    P.op("act", ACT(G[:, t2, :], ps[gbank[t2]], AF.Silu), reads=[tps[gbank[t2]]], writes=[t_G[t2]])
        if tt >= 1:
            transposes(tt - 1)


def phase_diff(c, l):
    P, A = c.P, c.A
    m = A.mark()
    lam_init = 0.8 - 0.6 * math.exp(-0.3 * l)
    W = c.W8
    tW = pf_qkvg(c, l, "diff")
    qT = A.alloc([128, 4, L], BF16)
    kT = A.alloc([128, 4, L], BF16)
    t_qT = [T() for _ in range(NT)]
    t_kT = [T() for _ in range(NT)]
    V = A.alloc([128, NT, 4, 129], BF16)
    t_V = [T() for _ in range(NT)]
    G = A.alloc([128, NT, 512], BF16)
    t_G = [T() for _ in range(NT)]
    t_misc = T()
    P.op("pool", MEMSET(V[:, :, :, 128:129], 1.0), writes=t_V)
    wqk = A.alloc([128, 128], F32)
    P.op("sp", DMA(wqk, c.diff_qk_norm[l:l + 1, :].partition_broadcast(128)), writes=[t_misc])
    P.op("dve", TS(wqk[:, 0:64], wqk[:, 0:64], 0.125, ALU.mult), reads=[t_misc], writes=[t_misc])
    tabs = A.alloc([128, 4, NT * 64], F32)
    t_tab = T()
    m_cs = A.mark()
    c.cs_sb = A.alloc([128, NT, 128], F32)
    c.t_cs = T()
    P.op("sp", DMA(c.cs_sb, c.cs_tab.rearrange("(t p) c -> p t c", p=128)), writes=[c.t_cs])
    for i, w in enumerate((wqk[:, 0:64], wqk[:, 64:128])):
        TCv = tabs[:, 2 * i, :].rearrange("p (t d) -> p t d", d=64)
        TSv = tabs[:, 2 * i + 1, :].rearrange("p (t d) -> p t d", d=64)
        P.op("dve", TT(TCv, c.cs_sb[:, :, 0:64], bc(w.unsqueeze(1), [128, NT, 64]), ALU.mult), reads=[c.t_cs, t_misc], writes=[t_tab])
        P.op("dve", TT(TSv[:, :, 0:32], c.cs_sb[:, :, 64:96], bc(w[:, 32:64].unsqueeze(1), [128, NT, 32]), ALU.mult),
             reads=[c.t_cs, t_misc], writes=[t_tab])
        P.op("dve", TT(TSv[:, :, 32:64], c.cs_sb[:, :, 96:128], bc(w[:, 0:32].unsqueeze(1), [128, NT, 32]), ALU.mult),
             reads=[c.t_cs, t_misc], writes=[t_tab])
    P.barrier()
    A.release(m_cs)
    TCq = tabs[:, 0, :].rearrange("p (t d) -> p t d", d=64)
    TSq = tabs[:, 1, :].rearrange("p (t d) -> p t d", d=64)
    TCk = tabs[:, 2, :].rearrange("p (t d) -> p t d", d=64)
    TSk = tabs[:, 3, :].rearrange("p (t d) -> p t d", d=64)
    lamb = A.alloc([128, 256], F32)
    t_lam = T()
    P.op("sp", DMA(lamb, c.diff_lambda[l:l + 1, :].partition_broadcast(128)), writes=[t_lam])
    lsc = A.alloc([128, 4], F32)
    lam3 = lamb.rearrange("p (a b d) -> p a b d", a=2, b=2)
    prod = A.alloc([128, 2, 64], F32)
    P.op("dve", TT(prod, lam3[:, :, 0, :], lam3[:, :, 1, :], ALU.mult), reads=[t_lam], writes=[t_lam])
    P.op("dve", RED(lsc[:, 0:2], prod), reads=[t_lam], writes=[t_lam])
    P.op("act", ACT(lsc[:, 0:2], lsc[:, 0:2], AF.Exp), reads=[t_lam], writes=[t_lam])
    P.op("dve", TT(lsc[:, 2:3], lsc[:, 1:2], lsc[:, 0:1], ALU.subtract), reads=[t_lam], writes=[t_lam])
    P.op("dve", TS(lsc[:, 3:4], lsc[:, 2:3], -lam_init, ALU.add), reads=[t_lam], writes=[t_lam])
    neglam = lsc[:, 3:4]
    swb = A.alloc([128, 128], F32)
    t_swb = T()
    P.op("sp", DMA(swb, c.diff_subln[l:l + 1, :].partition_broadcast(128)), writes=[t_swb])
    P.op("dve", TS(swb, swb, 1.0 - lam_init, ALU.mult), reads=[t_swb], writes=[t_swb])

    ps, tps = c.ps, c.tps
    m_prep = A.mark()

    def vcopy(tt, bank, t_bank):
        P.op("act", ACT(V[:, tt, :, 0:128], bank.rearrange("p (h d) -> p h d", d=128), AF.Copy), reads=[t_bank], writes=[t_V[tt]])
    prep_qkvg(c, W, tW, qT, t_qT, kT, t_kT, vcopy, t_V, G, t_G, (TCq, TSq, TCk, TSk), t_tab, True)
    P.barrier()
    A.release(m_prep)
    if "na" in c.phases:
        pf_qkvg(c, l, "na")
    elif "ssd" in c.phases:
        pf_ssd(c, l)
    c.dump("qT%d" % l, qT, [128, 4, L], BF16, t_qT)
    c.dump("kT%d" % l, kT, [128, 4, L], BF16, t_kT)
    c.dump("V%d" % l, V, [128, NT, 4, 129], BF16, t_V)
    c.dump("G%d" % l, G, [128, NT, 512], BF16, t_G)
    c.dump("hdnT%d" % l, c.hdnT, [128, 8, L], BF16, c.t_hdnT)
    NPT = 3
    Pt = [A.alloc([128, 1024], BF16) for _ in range(NPT)]
    t_Pt = [T() for _ in range(NPT)]
    SP = [c.psall[:, 0:1024], c.psall[:, 1024:2048]]
    t_SP = [[tps[0], tps[1]], [tps[2], tps[3]]]
    Ob3 = [ps[4], ps[5], ps[6]]
    t_O3 = [tps[4], tps[5], tps[6]]

    def acc(a):
        return a // 3, (a % 3) * 129
    Osb = A.alloc([128, 3, 512], F32)
    t_Osb = T()
    rr = [A.alloc([128, 16], F32) for _ in range(2)]
    t_rr = [T(), T()]
    ob4 = [A.alloc([128, 4, 128], F32) for _ in range(2)]
    t_ob4 = [T(), T()]
    junk = A.alloc([128, 128], F32)
    t_junk = T()
    yb4 = [A.alloc([128, 512], BF16) for _ in range(2)]
    t_yb4 = [T(), T()]
    yst = [A.alloc([128, 512], BF16) for _ in range(2)]
    t_yst = [T(), T()]
    iters = [(h, qb, kt) for h in range(4) for qb in range(4) for kt in range(NT)]
    NI = len(iters)
    deferred = []

    def emit_S(i):
        h, qb, kt = iters[i]
        for cc in range(2):
            pr = slice(cc * 64, (cc + 1) * 64)
            P.op("pe", MM(SP[i % 2][:, cc * 512:(cc + 1) * 512], kT[pr, h, kt * 128:(kt + 1) * 128], qT[pr, h, qb * 512:(qb + 1) * 512]),
                 reads=[t_kT[kt]] + [t_qT[qb * 4 + j] for j in range(4)], writes=[t_SP[i % 2][cc]])

    def Oslice(a):
        bk, off = acc(a)
        return Osb[:, bk, off:off + 129]

    def fin_stage1(blk, h, qb):
        b = blk % 2
        r_, t_r = rr[b], t_rr[b]
        ob, t_ob = ob4[b], t_ob4[b]
        for j in range(4):
            O0, O1 = Oslice(j), Oslice(4 + j)
            P.op("dve", RECIP(r_[:, 4 * j:4 * j + 1], O0[:, 128:129]), reads=[t_Osb], writes=[t_r])
            P.op("dve", RECIP(r_[:, 4 * j + 1:4 * j + 2], O1[:, 128:129]), reads=[t_Osb], writes=[t_r])
            P.op("dve", TT(r_[:, 4 * j + 2:4 * j + 3], r_[:, 4 * j + 1:4 * j + 2], neglam, ALU.mult), reads=[t_r, t_lam], writes=[t_r])
            P.op("dve", TS(ob[:, j, :], O0[:, 0:128], r_[:, 4 * j:4 * j + 1], ALU.mult), reads=[t_Osb, t_r], writes=[t_ob])
            P.op("dve", STT(ob[:, j, :], O1[:, 0:128], r_[:, 4 * j + 2:4 * j + 3], ob[:, j, :], ALU.mult, ALU.add),
                 reads=[t_Osb, t_r, t_ob], writes=[t_ob])

    def fin_stage2(blk, h, qb):
        b = blk % 2
        r_, t_r = rr[b], t_rr[b]
        ob, t_ob = ob4[b], t_ob4[b]
        for j in range(4):
            P.op("act", ACT(junk, ob[:, j, :], AF.Square, accum_out=r_[:, 4 * j + 3:4 * j + 4]), reads=[t_ob], writes=[t_junk, t_r])
        r3 = r_.rearrange("p (j k) -> p j k", k=4)[:, :, 3]
        P.op("act", ACT(r3, r3, AF.Ln, scale=1.0 / 128, bias=c.epsc), reads=[t_r, c.t_eps], writes=[t_r])
        P.op("act", ACT(r3, r3, AF.Exp, scale=-0.5), reads=[t_r], writes=[t_r])

    def fin_stage3(blk, h, qb):
        b = blk % 2
        r_, t_r = rr[b], t_rr[b]
        ob, t_ob = ob4[b], t_ob4[b]
        yb, t_yb = yb4[b], t_yb4[b]
        for j in range(4):
            tt = qb * 4 + j
            P.op("dve", STT(ob[:, j, :], ob[:, j, :], r_[:, 4 * j + 3:4 * j + 4], swb, ALU.mult, ALU.mult), reads=[t_ob, t_r, t_swb], writes=[t_ob])
            P.op("dve", TT(yb[:, j * 128:(j + 1) * 128], ob[:, j, :], G[:, tt, h * 128:(h + 1) * 128], ALU.mult),
                 reads=[t_ob, t_G[tt]], writes=[t_yb])
        for j in range(4):
            P.op("pe", TR(c.psb[:, j * 128:(j + 1) * 128], yb[:, j * 128:(j + 1) * 128], c.ident),
                 reads=[t_yb, c.t_ident], writes=[c.tpsb[0]])

    def fin_stage4(blk, h, qb):
        b = blk % 2
        ys, t_ys = yst[b], t_yst[b]
        P.op("dve", CP(ys, c.psb[:, 0:512]), reads=[c.tpsb[0]], writes=[t_ys])
        P.op("sp", DMA(c.yT[l][1024 + h * 128:1024 + (h + 1) * 128, qb * 512:(qb + 1) * 512], ys), reads=[t_ys])

    emit_S(0)
    blk = 0
    for i in range(NI):
        h, qb, kt = iters[i]
        if i + 1 < NI:
            emit_S(i + 1)
        pt, t_pt = Pt[i % NPT], t_Pt[i % NPT]
        P.op("act", ACT(pt, SP[i % 2], AF.Exp), reads=t_SP[i % 2], writes=[t_pt])
        for cc in range(2):
            for j in range(4):
                bk, off = acc(cc * 4 + j)
                P.op("pe", MM(Ob3[bk][:, off:off + 129], pt[:, cc * 512 + j * 128:cc * 512 + (j + 1) * 128],
                              V[:, kt, h, :], start=(kt == 0 and off == 0), stop=(kt == NT - 1), skip=True),
                     reads=[t_pt, t_V[kt]], writes=[t_O3[bk]])
        for (due, fn) in [d for d in deferred if d[0] <= i]:
            fn()
        deferred = [d for d in deferred if d[0] > i]
        if kt == NT - 1:
            for k3 in range(3):
                P.op("dve", CP(Osb[:, k3, 0:387], Ob3[k3][:, 0:387]), reads=[t_O3[k3]], writes=[t_Osb])
            fin_stage1(blk, h, qb)
            deferred.append((i + 2, (lambda b_=blk, h_=h, q_=qb: fin_stage2(b_, h_, q_))))
            deferred.append((i + 4, (lambda b_=blk, h_=h, q_=qb: fin_stage3(b_, h_, q_))))
            deferred.append((i + 6, (lambda b_=blk, h_=h, q_=qb: fin_stage4(b_, h_, q_))))
            blk += 1
    for (due, fn) in deferred:
        fn()
    A.release(m)


def _na_classes():
    rows, W, kh, kw = 32, 64, 8, 16
    rs = lambda r: min(max(r - kh // 2, 0), rows - kh)
    types = {}
    keys = []
    plan = []
    for i in range(16):
        qrows = [2 * i, 2 * i + 1]
        lo = min(rs(r) for r in qrows)
        hi = max(rs(r) + kh - 1 for r in qrows)
        tkeys = []
        kbs = list(range(lo // 2, hi // 2 + 1))
        for kb in kbs:
            key = []
            for b in range(2):
                for a in range(2):
                    kr, qr = 2 * kb + b, 2 * i + a
                    ok = rs(qr) <= kr < rs(qr) + kh
                    key.append((kr - qr + 7) if ok else -1)
            tkeys.append(tuple(key))
        tkeys = tuple(tkeys)
        if tkeys not in types:
            types[tkeys] = len(keys)
            keys.extend(tkeys)
        base = types[tkeys]
        plan.append([(kb, base + bi) for bi, kb in enumerate(kbs)])
    return keys, plan


NA_KEYS, NA_PLAN = _na_classes()
NA_NCLS = len(NA_KEYS)


def _na_tables(rpb):
    W, kw = 64, 16
    cidx = np.arange(W)
    col_start = np.clip(cidx - kw // 2, 0, W - kw)
    col_ok = (cidx[None, :] >= col_start[:, None]) & (cidx[None, :] < col_start[:, None] + kw)
    dc = np.clip(cidx[None, :] - cidx[:, None] + (kw - 1), 0, 2 * kw - 2)
    out = np.full((2, 8, 128, NA_NCLS, 128), NEG, dtype=np.float32)
    for ci, key in enumerate(NA_KEYS):
        n = 0
        for b in range(2):
            for a in range(2):
                dr = key[n]
                n += 1
                if dr < 0:
                    continue
                blkv = rpb[:, :, dr, :][:, :, dc]
                blkv = np.where(col_ok[None, None], blkv, np.float32(NEG))
                out[:, :, b * 64:(b + 1) * 64, ci, a * 64:(a + 1) * 64] = np.transpose(blkv, (0, 1, 3, 2))
    return np.ascontiguousarray(out.reshape(2, 8, 128, NA_NCLS * 128))


def phase_na(c, l):
    P, A = c.P, c.A
    m = A.mark()
    W = c.W8
    tW = pf_qkvg(c, l, "na")
    qT = A.alloc([128, 4, L], BF16)
    kT = A.alloc([128, 4, L], BF16)
    t_qT = [T() for _ in range(NT)]
    t_kT = [T() for _ in range(NT)]
    V = A.alloc([128, NT, 8, 65], BF16)
    t_V = [T() for _ in range(NT)]
    G = A.alloc([128, NT, 512], BF16)
    t_G = [T() for _ in range(NT)]
    Y = A.alloc([128, NT, 512], BF16)
    t_Y = [T() for _ in range(NT)]
    P.op("pool", MEMSET(V[:, :, :, 64:65], 1.0), writes=t_V)
    t_misc = T()
    wqk = A.alloc([128, 128], F32)
    P.op("sp", DMA(wqk, c.na_qk_norm[l:l + 1, :].partition_broadcast(128)), writes=[t_misc])
    P.op("dve", TS(wqk[:, 0:64], wqk[:, 0:64], 0.125, ALU.mult), reads=[t_misc], writes=[t_misc])
    ps, tps = c.ps, c.tps
    m_prep = A.mark()

    def vcopy(tt, bank, t_bank):
        P.op("act", ACT(V[:, tt, :, 0:64], bank.rearrange("p (h d) -> p h d", d=64), AF.Copy), reads=[t_bank], writes=[t_V[tt]])
    prep_qkvg(c, W, tW, qT, t_qT, kT, t_kT, vcopy, t_V, G, t_G, (wqk[:, 0:64], None, wqk[:, 64:128], None), t_misc, False)
    P.barrier()
    A.release(m_prep)
    if "ssd" in c.phases:
        pf_ssd(c, l)
    pf_out(c, l, [0, 1])
    tab = [A.alloc([128, NA_NCLS * 128], F32) for _ in range(2)]
    t_tab = [T(), T()]
    Tb = [A.alloc([128, 640], F32) for _ in range(2)]
    t_Tb = [T(), T()]
    Pb = [A.alloc([128, 640], BF16) for _ in range(3)]
    t_Pb = [T(), T(), T()]
    rr = [A.alloc([128, 2], F32) for _ in range(2)]
    t_rr = [T(), T()]
    its = [(hh, i) for hh in range(8) for i in range(NT)]
    NI = len(its)
    loaded = set()

    def load_tab(hh):
        if hh in loaded or hh >= 8:
            return
        loaded.add(hh)
        P.op("sp", DMA(tab[hh % 2], c.na_tab[l, hh, :, :]), writes=[t_tab[hh % 2]])
        P.op("act", ACT(tab[hh % 2], tab[hh % 2], AF.Exp), reads=[], writes=[t_tab[hh % 2]])

    def emit_S(n):
        hh, i = its[n]
        jb, e = hh // 2, hh % 2
        pr = slice(e * 64, (e + 1) * 64)
        S0, S1 = ps[(n % 2) * 2], ps[(n % 2) * 2 + 1]
        tS = [tps[(n % 2) * 2], tps[(n % 2) * 2 + 1]]
        for bi, (kb, cls) in enumerate(NA_PLAN[i]):
            dstS = (S0 if bi < 4 else S1)[:, (bi % 4) * 128:(bi % 4 + 1) * 128]
            P.op("pe", MM(dstS, kT[pr, jb, kb * 128:(kb + 1) * 128], qT[pr, jb, i * 128:(i + 1) * 128]),
                 reads=[t_kT[kb], t_qT[i]], writes=[tS[bi // 4]])

    def emit_exp_mul(n):
        hh, i = its[n]
        plan = NA_PLAN[i]
        nb = len(plan)
        base = plan[0][1]
        tS = [tps[(n % 2) * 2], tps[(n % 2) * 2 + 1]]
        Sboth = c.psall[:, (n % 2) * 1024:(n % 2) * 1024 + nb * 128]
        T_, t_T = Tb[n % 2], t_Tb[n % 2]
        P_, t_P = Pb[n % 3], t_Pb[n % 3]
        P.op("act", ACT(T_[:, 0:nb * 128], Sboth, AF.Exp), reads=(tS if nb > 4 else tS[0:1]), writes=[t_T])
        P.op("dve", TT(P_[:, 0:nb * 128], T_[:, 0:nb * 128], tab[hh % 2][:, base * 128:(base + nb) * 128], ALU.mult),
             reads=[t_T, t_tab[hh % 2]], writes=[t_P])

    load_tab(0)
    emit_S(0)
    if NI > 1:
        emit_S(1)
    emit_exp_mul(0)
    for n, (hh, i) in enumerate(its):
        if i == 2:
            load_tab(hh + 1)
        plan = NA_PLAN[i]
        nb = len(plan)
        Ob, t_Ob = ps[4 + n % 2], tps[4 + n % 2]
        P_, t_P = Pb[n % 3], t_Pb[n % 3]
        for bi, (kb, cls) in enumerate(plan):
            P.op("pe", MM(Ob[:, 0:65], P_[:, bi * 128:(bi + 1) * 128], V[:, kb, hh, :], start=(bi == 0), stop=(bi == nb - 1)),
                 reads=[t_P, t_V[kb]], writes=[t_Ob])
        if n + 2 < NI:
            emit_S(n + 2)
        if n + 1 < NI:
            emit_exp_mul(n + 1)
        r_, t_r = rr[n % 2], t_rr[n % 2]
        P.op("dve", RECIP(r_[:, 0:1], Ob[:, 64:65]), reads=[t_Ob], writes=[t_r])
        P.op("dve", STT(Y[:, i, hh * 64:(hh + 1) * 64], Ob[:, 0:64], r_[:, 0:1], G[:, i, hh * 64:(hh + 1) * 64], ALU.mult, ALU.mult),
             reads=[t_Ob, t_r, t_G[i]], writes=[t_Y[i]])
    yst = [A.alloc([128, 512], BF16) for _ in range(2)]
    t_yst = [T(), T()]
    blk = 0
    for qb in range(4):
        for j4 in range(4):
            for j in range(4):
                tt = qb * 4 + j
                P.op("pe", TR(c.psb[:, j * 128:(j + 1) * 128], Y[:, tt, j4 * 128:(j4 + 1) * 128], c.ident),
                     reads=[t_Y[tt], c.t_ident], writes=[c.tpsb[0]])
            ys, t_ys = yst[blk % 2], t_yst[blk % 2]
            blk += 1
            P.op("act", ACT(ys, c.psb[:, 0:512], AF.Copy), reads=[c.tpsb[0]], writes=[t_ys])
            P.op("sp", DMA(c.yT[l][1536 + j4 * 128:1536 + (j4 + 1) * 128, qb * 512:(qb + 1) * 512], ys), reads=[t_ys])
    A.release(m)


def phase_ssd(c, l):
    STOP = 9
    P, A = c.P, c.A
    m = A.mark()
    ps, tps = c.ps, c.tps
    nc = c.nc
    Wdt = A.alloc([128, 8, 32], BF16)
    tWdt = [T()]
    load_w(c, l, Wdt, C_DT, 32, tWdt)
    t_sm = T()
    dtb = A.alloc([128, 32], F32)
    alog = A.alloc([128, 32], F32)
    dsk = A.alloc([128, 32], F32)
    P.op("sp", DMA(dtb, c.dt_bias[l:l + 1, :].partition_broadcast(128)), writes=[t_sm])
    t_al = T()
    P.op("sp", DMA(alog, c.a_log[l:l + 1, :].partition_broadcast(128)), writes=[t_al])
    t_dsk = T()
    P.op("sp", DMA(dsk, c.d_skip[l:l + 1, :].partition_broadcast(128)), writes=[t_dsk])
    dsum = A.alloc([128, 16], F32)
    P.op("dve", TT(dsum, dsk[:, 0:16], dsk[:, 16:32], ALU.add), reads=[t_dsk], writes=[t_dsk])
    P.op("act", ACT(alog, alog, AF.Exp), reads=[t_al], writes=[t_al])
    P.op("dve", TS(alog, alog, -1.0, ALU.mult), reads=[t_al], writes=[t_al])
    snw = A.alloc([128, 1024], F32)
    t_snw = T()
    P.op("sp", DMA(snw, c.ssd_norm_w[l:l + 1, :].partition_broadcast(128)), writes=[t_snw])
    cbb = A.alloc([128, 1536], F32)
    t_cbb = T()
    P.op("sp", DMA(cbb, c.conv_b[l:l + 1, :].partition_broadcast(128)), writes=[t_cbb])
    cwT = A.alloc([128, 12, 6], F32)
    t_cwT = T()

    def a3(shape=(128, NT, 32)):
        return A.alloc(list(shape), F32)
    dt = a3()
    dta = a3()
    cs = a3()
    dcb = a3()
    dout = a3()
    dend = a3()
    cXd = a3()
    m_tmp = A.mark()
    v = a3()
    tmp3 = a3()
    cw6 = A.alloc([128, 1536], F32)
    t_cw6 = T()
    P.op("sp", DMA(cw6[0:5, :], c.conv_w[l, :, :]), writes=[t_cw6])
    P.op("sp", DMA(cw6[5:6, :], c.conv_b[l:l + 1, :]), writes=[t_cw6])
    for j in range(12):
        P.op("pe", TR(ps[6][:, j * 6:(j + 1) * 6], cw6[0:6, j * 128:(j + 1) * 128], c.identf[0:6, 0:6]),
             reads=[t_cw6, c.t_cst], writes=[tps[6]])
    P.op("dve", CP(cwT, ps[6][:, 0:72].rearrange("p (j k) -> p j k", k=6)), reads=[tps[6]], writes=[t_cwT])
    t_dt, t_dta, t_cs, t_dcb, t_dout, t_dend, t_cXd, t_v, t_tmp3 = [T() for _ in range(9)]
    if STOP <= 0.1:
        A.release(m)
        return
    p0 = ps[0].rearrange("p (t h) -> p t h", h=32)
    for tt in range(NT):
        for k in range(8):
            P.op("pe", MM(ps[0][:, tt * 32:(tt + 1) * 32], c.hdnT[:, k, tt * 128:(tt + 1) * 128], Wdt[:, k, :],
                          start=(k == 0), stop=(k == 7), skip=True),
                 reads=[c.t_hdnT[tt], tWdt[0]], writes=[tps[0]])
    P.op("dve", TT(v, p0, bc(dtb.unsqueeze(1), [128, NT, 32]), ALU.add), reads=[tps[0], t_sm], writes=[t_v])
    P.op("act", ACT(tmp3, v, AF.Abs), reads=[t_v], writes=[t_tmp3])
    P.op("act", ACT(tmp3, tmp3, AF.Exp, scale=-1.0), reads=[t_tmp3], writes=[t_tmp3])
    P.op("act", ACT(tmp3, tmp3, AF.Ln, bias=c.onec, scale=1.0), reads=[t_tmp3, c.t_eps], writes=[t_tmp3])
    P.op("dve", STT(dt, v, 0.0, tmp3, ALU.max, ALU.add), reads=[t_v, t_tmp3], writes=[t_dt])
    P.op("dve", TT(dta, dt, bc(alog.unsqueeze(1), [128, NT, 32]), ALU.mult), reads=[t_dt, t_al], writes=[t_dta])
    if STOP <= 0.2:
        A.release(m)
        return
    for tt in range(NT):
        P.op("pe", MM(ps[1][:, tt * 32:tt * 32 + 16], c.U, dta[:, tt, 0:16], skip=True), reads=[t_dta, c.t_cst], writes=[tps[1]])
        P.op("pe", MM(ps[1][:, tt * 32 + 16:tt * 32 + 32], c.Ur, dta[:, tt, 16:32], skip=True), reads=[t_dta, c.t_cst], writes=[tps[1]])
    P.op("act", ACT(cs, ps[1].rearrange("p (t h) -> p t h", h=32), AF.Copy), reads=[tps[1]], writes=[t_cs])
    if STOP <= 0.3:
        A.release(m)
        return
    for tt in range(NT):
        P.op("pe", MM(ps[2][:, tt * 32:(tt + 1) * 32], c.onesf, dta[:, tt, :], skip=True), reads=[t_dta, c.t_cst], writes=[tps[2]])
    p2 = ps[2].rearrange("p (t h) -> p t h", h=32)
    P.op("act", ACT(tmp3, p2, AF.Copy), reads=[tps[2]], writes=[t_tmp3])
    P.op("act", ACT(dcb, tmp3, AF.Exp), reads=[t_tmp3], writes=[t_dcb])
    if STOP <= 0.31:
        A.release(m)
        return
    P.op("act", ACT(dout, cs, AF.Exp), reads=[t_cs], writes=[t_dout])
    if STOP <= 0.32:
        A.release(m)
        return
    P.op("dve", TT(dend, tmp3, cs, ALU.subtract), reads=[t_tmp3, t_cs], writes=[t_dend])
    if STOP <= 0.33:
        A.release(m)
        return
    P.op("act", ACT(dend, dend, AF.Exp), reads=[t_dend], writes=[t_dend])
    if STOP <= 0.34:
        A.release(m)
        return
    P.op("dve", TT(cXd, dt, dend, ALU.mult), reads=[t_dt, t_dend], writes=[t_cXd])
    if STOP <= 0.4:
        A.release(m)
        return
    csT = A.alloc([128, NT, 128], F32)
    t_csT = T()
    for q in range(4):
        for j in range(4):
            tt = q * 4 + j
            P.op("pe", TR(ps[3][0:32, j * 128:(j + 1) * 128], cs[:, tt, :], c.identf), reads=[t_cs, c.t_cst], writes=[tps[3]])
        P.op("act", ACT(csT[0:32, q * 4:(q + 1) * 4, :], ps[3][0:32, :].rearrange("p (j t) -> p j t", t=128), AF.Copy),
             reads=[tps[3]], writes=[t_csT])
    if STOP <= 0.5:
        A.release(m)
        return
    t_csd = T()
    P.op("sp", DMA(c.csT_d.rearrange("t h l -> h t l"), csT[0:32, :, :]), reads=[t_csT], writes=[t_csd])
    csT_v = c.csT_d.rearrange("t (d h) l -> t d h l", d=2)
    P.barrier()
    A.release(m_tmp)
    if STOP <= 1:
        A.release(m)
        return

    for g in range(2):
        mg = A.mark()
        Wz = c.Wz[g]
        tWz = pf_ssd(c, l)[g]
        t_Wzq = c.t_Wq[3 - g]
        xs = A.alloc([128, NT, 512], F32)
        t_xs = [T() for _ in range(NT)]
        Btok = A.alloc([128, NT, 128], BF16)
        t_Btok = [T() for _ in range(NT)]
        BT = A.alloc([128, L], BF16)
        t_BT = [T() for _ in range(4)]
        CT = A.alloc([128, L], BF16)
        t_CT = [T() for _ in range(4)]
        m_conv = A.mark()
        pre = [A.alloc([128, L + 4], BF16) for _ in range(2)]
        t_pre = [[T() for _ in range(4)] for _ in range(2)]
        t_halo = [T(), T()]
        for b in range(2):
            P.op("pool", MEMSET(pre[b][:, 0:2], 0.0), writes=[t_halo[b]])
            P.op("pool", MEMSET(pre[b][:, L + 2:L + 4], 0.0), writes=[t_halo[b]])
        Wc = [A.alloc([128, 8, 128], BF16) for _ in range(2)]
        tWc = [[T()], [T()]]
        dg = [A.alloc([128, 5, 128], BF16) for _ in range(2)]
        t_dg = [T(), T()]
        ctmp = [A.alloc([128, 512], F32) for _ in range(2)]
        t_ctmp = [T(), T()]
        chunks = [("B", 8 + g), ("C", 10 + g)] + [("x%d" % j, 4 * g + j) for j in range(4)]
        pbank = 0
        for n, (kind, ci) in enumerate(chunks):
            b = n % 2
            load_w(c, l, Wc[b], C_XBC + ci * 128, 128, tWc[b])
            P.op("dve", TT(dg[b], bc(c.identf.unsqueeze(1), [128, 5, 128]), bc(cwT[:, ci, 0:5].unsqueeze(2), [128, 5, 128]), ALU.mult),
                 reads=[c.t_cst, t_cwT], writes=[t_dg[b]])
            for tb in range(4):
                bank, tbk = ps[pbank % 4], tps[pbank % 4]
                pbank += 1
                for k in range(8):
                    P.op("pe", MM(bank, Wc[b][:, k, :], c.hdnT[:, k, tb * 512:(tb + 1) * 512], start=(k == 0), stop=(k == 7)),
                         reads=[tWc[b][0]] + c.t_hdnT[tb * 4:(tb + 1) * 4], writes=[tbk])
                P.op("act", ACT(pre[b][:, 2 + tb * 512:2 + (tb + 1) * 512], bank, AF.Copy), reads=[tbk], writes=[t_pre[b][tb]])
            allpre = t_pre[b] + [t_halo[b]]
            if kind in ("B", "C"):
                dstT, t_dstT = (BT, t_BT) if kind == "B" else (CT, t_CT)
                for tb in range(4):
                    bank, tbk = ps[pbank % 4], tps[pbank % 4]
                    pbank += 1
                    for k in range(5):
                        P.op("pe", MM(bank, dg[b][:, k, :], pre[b][:, tb * 512 + k:tb * 512 + k + 512], start=(k == 0), stop=(k == 4)),
                             reads=[t_dg[b]] + allpre, writes=[tbk])
                    P.op("act", ACT(dstT[:, tb * 512:(tb + 1) * 512], bank, AF.Silu, bias=cwT[:, ci, 5:6], scale=1.0),
                         reads=[tbk, t_cwT], writes=[t_dstT[tb]])
            if kind != "C":
                for q in range(4):
                    bank, tbk = ps[pbank % 4], tps[pbank % 4]
                    pbank += 1
                    for j in range(4):
                        tt = q * 4 + j
                        for k in range(5):
                            P.op("pe", MM(bank[:, j * 128:(j + 1) * 128], pre[b][:, tt * 128 + k:tt * 128 + k + 128], dg[b][:, k, :],
                                          start=(k == 0 and j == 0), stop=(k == 4), skip=True),
                                 reads=[t_dg[b]] + allpre, writes=[tbk])
                    ct, t_ct = ctmp[q % 2], t_ctmp[q % 2]
                    P.op("dve", TT(ct.rearrange("p (j c) -> p j c", c=128), bank.rearrange("p (j c) -> p j c", c=128),
                                   bc(cbb[:, ci * 128:(ci + 1) * 128].unsqueeze(1), [128, 4, 128]), ALU.add),
                         reads=[tbk, t_cbb], writes=[t_ct])
                    if kind == "B":
                        P.op("act", ACT(Btok[:, q * 4:(q + 1) * 4, :], ct.rearrange("p (j c) -> p j c", c=128), AF.Silu),
                             reads=[t_ct], writes=t_Btok[q * 4:(q + 1) * 4])
                    else:
                        jx = int(kind[1])
                        P.op("act", ACT(xs[:, q * 4:(q + 1) * 4, jx * 128:(jx + 1) * 128], ct.rearrange("p (j c) -> p j c", c=128), AF.Silu),
                             reads=[t_ct], writes=t_xs[q * 4:(q + 1) * 4])

        P.barrier()
        A.release(m_conv)
        if STOP <= 2:
            A.release(mg)
            continue
        hb0 = 16 + 8 * g
        hf0 = 8 * g
        m_passA = A.mark()
        Sst = [A.alloc([128, 512], F32) for _ in range(2)]
        t_Sst = [T(), T()]
        for d in range(2):
            P.op("pool", MEMSET(Sst[d], 0.0), writes=[t_Sst[d]])
        stg = [[A.alloc([128, 512], BF16) for _ in range(2)] for _ in range(2)]
        t_stg = [[T(), T()], [T(), T()]]
        Xd = [[A.alloc([128, 512], BF16) for _ in range(2)] for _ in range(2)]
        t_Xd = [[T(), T()], [T(), T()]]
        t_sd = [[T() for _ in range(NT)] for _ in range(2)]
        sdram = [c.sf_d, c.sb_d]
        h0s = [hf0, hb0]

        def bh(ap3, h0):
            return bc(ap3[:, h0:h0 + 8].unsqueeze(2), [128, 8, 64])

        def v8(ap):
            return ap.rearrange("p (h d) -> p h d", d=64)
        for k in range(NT):
            for d in range(2):
                ci_ = k if d == 0 else NT - 1 - k
                last = (ci_ == NT - 1) if d == 0 else (ci_ == 0)
                sg, t_sg = stg[d][k % 2], t_stg[d][k % 2]
                P.op("act", ACT(sg, Sst[d], AF.Copy), reads=[t_Sst[d]], writes=[t_sg])
                P.op("sp", DMA(sdram[d][g, ci_], sg), reads=[t_sg], writes=[t_sd[d][ci_]])
                if not last:
                    xd, t_xd = Xd[d][k % 2], t_Xd[d][k % 2]
                    P.op("pool", TT(v8(xd), v8(xs[:, ci_, :]), bh(cXd[:, ci_, :], h0s[d]), ALU.mult), reads=[t_xs[ci_], t_cXd], writes=[t_xd])
                    bank, tbk = ps[4 + d], tps[4 + d]
                    P.op("pe", MM(bank, Btok[:, ci_, :], xd), reads=[t_Btok[ci_], t_xd], writes=[tbk])
                    P.op("dve", TT(v8(Sst[d]), v8(Sst[d]), bh(dcb[:, ci_, :], h0s[d]), ALU.mult), reads=[t_Sst[d], t_dcb], writes=[t_Sst[d]])
                    P.op("dve", TT(Sst[d], Sst[d], bank, ALU.add), reads=[t_Sst[d], tbk], writes=[t_Sst[d]])
        P.barrier()
        A.release(m_passA)
        if STOP <= 3:
            A.release(mg)
            continue
        R = [A.alloc([128, 2, 8, 128], F32) for _ in range(2)]
        t_R = [T(), T()]
        SFc = [A.alloc([128, 512], BF16) for _ in range(2)]
        t_SFc = [T(), T()]
        SBc = [A.alloc([128, 512], BF16) for _ in range(2)]
        t_SBc = [T(), T()]
        E = A.alloc([128, 2, 8, 128], BF16)
        t_E = T()
        Mt = [A.alloc([128, 2, 8, 128], BF16) for _ in range(2)]
        t_Mt = [T(), T()]
        Gm = [A.alloc([128, 2, 128], BF16) for _ in range(2)]
        t_Gm = [T(), T()]
        Xt = [A.alloc([128, 3, 512], BF16) for _ in range(2)]
        t_Xt = [T(), T()]
        sz = [A.alloc([128, 512], F32) for _ in range(2)]
        t_sz = [T(), T()]
        ya = A.alloc([128, 512], F32)
        yb_ = A.alloc([128, 512], F32)
        yy = A.alloc([128, 512], F32)
        t_ya, t_yb, t_yy = T(), T(), T()
        ssn = A.alloc([128, 2], F32)
        t_ssn = T()
        yo = [A.alloc([128, 512], BF16) for _ in range(2)]
        t_yo = [T(), T()]
        yst = [A.alloc([128, 4, 128], BF16) for _ in range(2)]
        t_yst = [T(), T()]

        def loadR(ci_):
            b = ci_ % 2
            P.op("sp", DMA(R[b].rearrange("p d h l -> p d (h l)"),
                           csT_v[ci_, :, 8 * g:8 * g + 8, :].rearrange("d h l -> d (h l)").partition_broadcast(128)),
                 reads=[t_csd], writes=[t_R[b]])

        def loadS(ci_):
            b = ci_ % 2
            P.op("sp", DMA(SFc[b], c.sf_d[g, ci_]), reads=[t_sd[0][ci_]], writes=[t_SFc[b]])
            P.op("sp", DMA(SBc[b], c.sb_d[g, ci_]), reads=[t_sd[1][ci_]], writes=[t_SBc[b]])

        def iteration(cn, cc_):
            if cn is not None:
                bn = cn % 2
                tokn = slice(cn * 128, (cn + 1) * 128)
                P.op("pe", MM(ps[4][:, 0:128], BT[:, tokn], CT[:, tokn]), reads=[t_BT[cn // 4], t_CT[cn // 4]], writes=[tps[4]])
                for k in range(8):
                    P.op("pe", MM(ps[3], c.hdnT[:, k, tokn], Wz[:, k, :], start=(k == 0), stop=(k == 7)),
                         reads=[c.t_hdnT[cn], tWz[0], t_Wzq], writes=[tps[3]])
                P.op("dve", TT(Gm[bn][:, 0, :], ps[4][:, 0:128], c.U, ALU.mult), reads=[tps[4], c.t_cst], writes=[t_Gm[bn]])
                P.op("dve", TT(Gm[bn][:, 1, :], ps[4][:, 0:128], c.Ur, ALU.mult), reads=[tps[4], c.t_cst], writes=[t_Gm[bn]])
                csv = cs[:, cn, :].rearrange("p (d h) -> p d h", d=2)[:, :, 8 * g:8 * g + 8]
                Dd, t_D = R[bn], t_R[bn]
                P.op("dve", TT(Dd, Dd, bc(csv.unsqueeze(3), [128, 2, 8, 128]), ALU.subtract), reads=[t_cs], writes=[t_D])
                P.op("act", ACT(Dd, Dd, AF.Relu, scale=-1.0), reads=[], writes=[t_D])
                P.op("act", ACT(E, Dd, AF.Exp, scale=-1.0), reads=[t_D], writes=[t_E])
                P.op("pool", TT(v8(Xt[bn][:, 0, :]), v8(xs[:, cn, :]), bh(dt[:, cn, :], hf0), ALU.mult), reads=[t_xs[cn], t_dt], writes=[t_Xt[bn]])
                P.op("pool", TT(v8(Xt[bn][:, 1, :]), v8(xs[:, cn, :]), bh(dt[:, cn, :], hb0), ALU.mult), reads=[t_xs[cn], t_dt], writes=[t_Xt[bn]])
                P.op("pool", TT(v8(Xt[bn][:, 2, :]), v8(xs[:, cn, :]), bh(dsum, 8 * g), ALU.mult), reads=[t_xs[cn], t_dsk], writes=[t_Xt[bn]])
            if cc_ is not None:
                b = cc_ % 2
                tok = slice(cc_ * 128, (cc_ + 1) * 128)
                Yb_, t_Y = (ps[0], tps[0]) if b == 0 else (ps[5], tps[5])
                P.op("pe", MM(Yb_, c.ident, Xt[b][:, 2, :], start=True, stop=False, skip=True), reads=[c.t_ident, t_Xt[b]], writes=[t_Y])
                for h in range(8):
                    for d in range(2):
                        P.op("pe", MM(Yb_[:, h * 64:(h + 1) * 64], Mt[b][:, d, h, :], Xt[b][:, d, h * 64:(h + 1) * 64],
                                      start=False, stop=(d == 1), skip=True),
                             reads=[t_Mt[b], t_Xt[b]], writes=[t_Y])
                P.op("pe", MM(ps[1], CT[:, tok], SFc[b]), reads=[t_CT[cc_ // 4], t_SFc[b]], writes=[tps[1]])
                P.op("pe", MM(ps[2], CT[:, tok], SBc[b]), reads=[t_CT[cc_ // 4], t_SBc[b]], writes=[tps[2]])
                P.op("dve", TT(v8(ya), v8(ps[1]), bh(dout[:, cc_, :], hf0), ALU.mult), reads=[tps[1], t_dout], writes=[t_ya])
                P.op("dve", TT(v8(yb_), v8(ps[2]), bh(dout[:, cc_, :], hb0), ALU.mult), reads=[tps[2], t_dout], writes=[t_yb])
                P.op("pool", TT(ya, ya, yb_, ALU.add), reads=[t_ya, t_yb], writes=[t_ya])
                P.op("dve", TT(yy, Yb_, ya, ALU.add), reads=[t_Y, t_ya], writes=[t_yy])
                P.op("pool", TT(yy, yy, sz[b], ALU.mult), reads=[t_yy, t_sz[b]], writes=[t_yy])
            if cn is not None:
                for d in range(2):
                    P.op("dve", TT(Mt[bn][:, d], E[:, d], bc(Gm[bn][:, d, :].unsqueeze(1), [128, 8, 128]), ALU.mult),
                         reads=[t_E, t_Gm[bn]], writes=[t_Mt[bn]])
                P.op("act", ACT(sz[bn], ps[3], AF.Silu), reads=[tps[3]], writes=[t_sz[bn]])
            if cc_ is not None:
                P.op("act", ACT(ya, yy, AF.Square, accum_out=ssn[:, 0:1]), reads=[t_yy], writes=[t_ya, t_ssn])
                P.op("act", ACT(ssn[:, 0:1], ssn[:, 0:1], AF.Ln, scale=1.0 / 512, bias=c.epsc), reads=[t_ssn, c.t_eps], writes=[t_ssn])
                P.op("act", ACT(ssn[:, 0:1], ssn[:, 0:1], AF.Exp, scale=-0.5), reads=[t_ssn], writes=[t_ssn])
                P.op("dve", STT(yo[b], yy, ssn[:, 0:1], snw[:, g * 512:(g + 1) * 512], ALU.mult, ALU.mult),
                     reads=[t_yy, t_ssn, t_snw], writes=[t_yo[b]])

        def stageC(ci_):
            b = ci_ % 2
            tok = slice(ci_ * 128, (ci_ + 1) * 128)
            for j in range(4):
                P.op("pe", TR(c.psb[:, j * 128:(j + 1) * 128], yo[b][:, j * 128:(j + 1) * 128], c.ident),
                     reads=[t_yo[b], c.t_ident], writes=[c.tpsb[0]])
            P.op("act", ACT(yst[b], c.psb[:, 0:512].rearrange("p (j t) -> p j t", t=128), AF.Copy), reads=[c.tpsb[0]], writes=[t_yst[b]])
            P.op("sp", DMA(c.yT[l][g * 512:(g + 1) * 512, tok].rearrange("(j p) t -> p j t", p=128), yst[b]), reads=[t_yst[b]])

        loadR(0)
        loadS(0)
        loadR(1)
        iteration(0, None)
        for ci_ in range(NT):
            if ci_ + 1 < NT:
                loadS(ci_ + 1)
                if ci_ + 2 < NT:
                    loadR(ci_ + 2)
            iteration(ci_ + 1 if ci_ + 1 < NT else None, ci_)
            if ci_ >= 1:
                stageC(ci_ - 1)
        stageC(NT - 1)
        A.release(mg)
    A.release(m)


def phase_out(c, l, xsrc, xdst, fuse_next=False):
    P, A = c.P, c.A
    m = A.mark()
    if fuse_next:
        nwb = A.alloc([128, DM], F32)
        t_nwb = T()
        P.op("sp", DMA(nwb, c.norm_w[l + 1:l + 2, :].partition_broadcast(128)), writes=[t_nwb])
        sqj = A.alloc([128, DM], F32)
        t_sqj = T()
        ssx = A.alloc([128, NT], F32)
        t_ssx = [T() for _ in range(NT)]
        hb = [A.alloc([128, DM], BF16) for _ in range(2)]
        t_hb = [T(), T()]
    Wo = c.Wo
    tWo = pf_out(c, l, [0, 1, 2, 3])
    yt = [A.alloc([128, 16, 512], BF16) for _ in range(2)]
    t_yt = [T(), T()]
    xt = [A.alloc([128, DM], F32) for _ in range(3)]
    t_xt = [T(), T(), T()]
    ot = [A.alloc([128, DM], F32) for _ in range(2)]
    t_ot = [T(), T()]
    ps, tps = c.ps, c.tps
    fin = []

    def load_y(qb):
        P.op("sp", DMA(yt[qb % 2], c.yT[l][:, qb * 512:(qb + 1) * 512].rearrange("(k p) t -> p k t", p=128)), writes=[t_yt[qb % 2]])

    def load_x(tt):
        P.op("sp", DMA(xt[tt % 3], xsrc[tt * 128:(tt + 1) * 128, :]), writes=[t_xt[tt % 3]])
    load_y(0)
    load_x(0)
    load_x(1)
    nb = 0
    for tt in range(NT):
        b = tt % 2
        qb, j = tt // 4, tt % 4
        tok = slice(tt * 128, (tt + 1) * 128)
        if j == 0 and qb + 1 < 4:
            load_y(qb + 1)
        if tt + 2 < NT:
            load_x(tt + 2)
        for n in range(2):
            bank, tb = ps[nb % 6], tps[nb % 6]
            nb += 1
            for k in range(16):
                P.op("pe", MM(bank, yt[qb % 2][:, k, j * 128:(j + 1) * 128], Wo[:, k, n * 512:(n + 1) * 512], start=(k == 0), stop=(k == 15)),
                     reads=[t_yt[qb % 2], tWo[k // 4], c.t_Wq[k // 4]], writes=[tb])
            P.op("dve", TT(ot[b][:, n * 512:(n + 1) * 512], bank, xt[tt % 3][:, n * 512:(n + 1) * 512], ALU.add),
                 reads=[tb, t_xt[tt % 3]], writes=[t_ot[b]])
        fin.append(P.op("sp", DMA(xdst[tok, :], ot[b]), reads=[t_ot[b]]))
        if fuse_next:
            s1 = ssx[:, tt:tt + 1]
            P.op("act", ACT(sqj, ot[b], AF.Square, accum_out=s1), reads=[t_ot[b]], writes=[t_sqj, t_ssx[tt]])
            P.op("act", ACT(s1, s1, AF.Ln, scale=1.0 / DM, bias=c.epsc), reads=[t_ssx[tt], c.t_eps], writes=[t_ssx[tt]])
            P.op("act", ACT(s1, s1, AF.Exp, scale=-0.5), reads=[t_ssx[tt]], writes=[t_ssx[tt]])
            P.op("dve", STT(hb[b], ot[b], s1, nwb, ALU.mult, ALU.mult), reads=[t_ot[b], t_ssx[tt], t_nwb], writes=[t_hb[b]])
            for k in range(8):
                P.op("pe", TR(c.psb[:, k * 128:(k + 1) * 128], hb[b][:, k * 128:(k + 1) * 128], c.ident),
                     reads=[t_hb[b], c.t_ident], writes=[c.tpsb[0]])
            P.op("act", ACT(c.hdnT[:, :, tok], c.psb[:, :].rearrange("p (k t) -> p k t", t=128), AF.Copy),
                 reads=[c.tpsb[0]], writes=[c.t_hdnT[tt]])
    A.release(m)
    return fin


def _host_consts():
    inv_freq = (10000.0 ** (-(np.arange(0, 64, 2, dtype=np.float32)) / np.float32(64))).astype(np.float32)
    ang = (np.arange(L, dtype=np.float32)[:, None] * inv_freq[None, :]).astype(np.float32)
    cos, sin = np.cos(ang).astype(np.float32), np.sin(ang).astype(np.float32)
    cs = np.concatenate([cos, cos, -sin, sin], axis=1).astype(np.float32)
    k = np.arange(128)
    ident = np.eye(128, dtype=np.float32)
    U = (k[:, None] <= k[None, :]).astype(np.float32)
    Ur = (k[:, None] >= k[None, :]).astype(np.float32)
    ones = np.ones((128, 128), np.float32)
    consts = np.concatenate([ident, U, Ur, ones], axis=1)
    return np.ascontiguousarray(cs), np.ascontiguousarray(consts)


_CACHE = {}


def make_in_maps(inputs, n_cores=8):
    cs, consts = _host_consts()
    f = lambda a: np.ascontiguousarray(np.asarray(a, dtype=np.float32))
    shared = {
        "norm_w": f(inputs["norm_w"]), "w_in": f(inputs["w_in"]), "conv_w": f(inputs["conv_w"]),
        "conv_b": f(inputs["conv_b"]), "a_log": f(inputs["a_log"]).reshape(2, 32),
        "dt_bias": f(inputs["dt_bias"]).reshape(2, 32), "d_skip": f(inputs["d_skip"]).reshape(2, 32),
        "ssd_norm_w": f(inputs["ssd_norm_w"]), "diff_qk_norm": f(inputs["diff_qk_norm"]).reshape(2, 128),
        "diff_lambda": f(inputs["diff_lambda"]).reshape(2, 256), "diff_subln": f(inputs["diff_subln"]),
        "na_qk_norm": f(inputs["na_qk_norm"]).reshape(2, 128), "na_tab": _na_tables(f(inputs["na_rpb"])),
        "w_out": f(inputs["w_out"]), "cs_tab": cs, "consts": consts,
    }
    x = f(inputs["x"])
    return [dict(shared, x=x[b]) for b in range(n_cores)]


def kernel(**inputs):
    if "nc" not in _CACHE:
        _CACHE["nc"] = build()[0]
    nc = _CACHE["nc"]
    in_maps = make_in_maps(inputs)
    res = run_bass_kernel_spmd(nc, in_maps, core_ids=list(range(8)))
    return np.stack([np.asarray(r["out"], dtype=np.float32) for r in res.results], axis=0)
```

```python
import contextlib
import math
import numpy as np
import concourse.bass as bass
import concourse.mybir as mybir
from concourse.bass_utils import run_bass_kernel_spmd

F32 = mybir.dt.float32
BF16 = mybir.dt.bfloat16
AF = mybir.ActivationFunctionType
ALU = mybir.AluOpType
AX = mybir.AxisListType

L = 2048
DM = 1024
NT = 16
INW = 6688
EPS = 1e-6
NEG = -30000.0
C_Z, C_XBC, C_DT, C_DIFF, C_NA = 0, 1024, 2560, 2592, 4640


class T:
    __slots__ = ("w", "r", "excl")

    def __init__(self, excl=False):
        self.w = None
        self.r = []
        self.excl = excl


class Op:
    __slots__ = ("eng", "fn", "deps", "idx", "sig", "isdma", "semid", "semval", "prev_same_sem")


class Prog:
    COMPUTE = ("pe", "act", "dve", "pool")
    DMAQ = ("sp", "actq", "poolq")
    STREAM = {"pe": "pe", "act": "act", "dve": "dve", "pool": "pool", "sp": "sp", "actq": "act", "poolq": "pool"}
    STREAMS = ("pe", "act", "dve", "pool", "sp")

    def __init__(self, nc, n_dma_sems=12):
        self.nc = nc
        self.ops = []
        self.n_dma_sems = n_dma_sems
        self.last = {s: None for s in self.STREAMS}
        self.recent_dma = {q: [] for q in self.DMAQ}
        self.frontier = []
        self.synced = {s: True for s in self.STREAMS}

    def barrier(self):
        fr = [o for o in self.last.values() if o is not None]
        for q in self.DMAQ:
            fr.extend(self.recent_dma[q])
        self.frontier = fr
        self.synced = {s: False for s in self.STREAMS}

    def op(self, eng, fn, reads=(), writes=()):
        o = Op()
        o.eng = eng
        o.fn = fn
        o.isdma = eng in self.DMAQ
        o.idx = len(self.ops)
        o.prev_same_sem = None
        deps = {}
        if any(t.excl for t in reads):
            writes = list(writes) + [t for t in reads if t.excl and t not in writes]
            reads = [t for t in reads if not t.excl]
        for t in reads:
            if t.w is not None:
                deps[t.w.idx] = ("raw", t.w)
        for t in writes:
            if t.w is not None and t.w.idx not in deps:
                deps[t.w.idx] = ("waw", t.w)
            for r in t.r:
                if r.idx not in deps:
                    deps[r.idx] = ("war", r)
        st = self.STREAM[eng]
        if not self.synced[st]:
            for p in self.frontier:
                if p.idx not in deps:
                    deps[p.idx] = ("bar", p)
            self.synced[st] = True
        for t in writes:
            t.w = o
            t.r = []
        for t in reads:
            if t.w is not o:
                t.r.append(o)
        o.deps = deps
        o.sig = False
        self.ops.append(o)
        self.last[st] = o
        if o.isdma:
            lst = self.recent_dma[eng]
            lst.append(o)
            if len(lst) > self.n_dma_sems:
                lst.pop(0)
        return o

    def emit(self, final_waits=()):
        nc = self.nc
        streams = {s: [] for s in self.STREAMS}
        for o in self.ops:
            streams[self.STREAM[o.eng]].append(o)
        pos = {}
        for s, lst in streams.items():
            for i, o in enumerate(lst):
                pos[o.idx] = i
        need = {}
        for o in self.ops:
            lst = []
            so = self.STREAM[o.eng]
            for (kind, p) in o.deps.values():
                sp_ = self.STREAM[p.eng]
                if p.isdma or o.isdma:
                    lst.append(p)
                elif sp_ != so:
                    lst.append(p)
                else:
                    if so == "pe":
                        continue
                    lst.append(p)
            need[o.idx] = lst
            for p in lst:
                p.sig = True
        for o in final_waits:
            o.sig = True
        stack = contextlib.ExitStack()
        esem = {e: stack.enter_context(nc.semaphore("s_" + e)) for e in self.COMPUTE}
        ecount = {e: 0 for e in self.COMPUTE}
        dsems = {q: [stack.enter_context(nc.semaphore("d_%s_%d" % (q, i))) for i in range(self.n_dma_sems)]
                 for q in self.DMAQ}
        duse = {q: [0] * self.n_dma_sems for q in self.DMAQ}
        dlast = {q: [None] * self.n_dma_sems for q in self.DMAQ}
        dnext = {q: 0 for q in self.DMAQ}
        for s in self.STREAMS:
            for o in streams[s]:
                if o.isdma:
                    q = o.eng
                    j = dnext[q]
                    dnext[q] = (j + 1) % self.n_dma_sems
                    duse[q][j] += 1
                    o.semid = (q, j)
                    o.semval = 16 * duse[q][j]
                    o.prev_same_sem = dlast[q][j]
                    dlast[q][j] = o
                elif o.sig:
                    ecount[o.eng] += 1
                    o.semid = o.eng
                    o.semval = ecount[o.eng]
        self.n_waits = 0

        def sem_of(p):
            if p.isdma:
                return dsems[p.semid[0]][p.semid[1]]
            return esem[p.semid]

        def emit_stream(s, engobj):
            known = {}
            for o in streams[s]:
                waits = {}
                cand = list(need[o.idx])
                if o.isdma and o.prev_same_sem is not None:
                    cand.append(o.prev_same_sem)
                for p in cand:
                    k = p.semid
                    if known.get(k, 0) >= p.semval:
                        continue
                    if waits.get(k, (0, None))[0] < p.semval:
                        waits[k] = (p.semval, p)
                for k, (v, p) in waits.items():
                    engobj.wait_ge(sem_of(p), v)
                    known[k] = v
                    self.n_waits += 1
                ins = o.fn(engobj)
                if o.isdma:
                    ins.then_inc(dsems[o.semid[0]][o.semid[1]], 16)
                elif o.sig:
                    ins.then_inc(esem[o.semid], 1)
            if s == "sp":
                for o in final_waits:
                    engobj.wait_ge(sem_of(o), o.semval)

        with nc.Block() as block:
            @block.tensor
            def _(e):
                emit_stream("pe", e)

            @block.scalar
            def _(e):
                emit_stream("act", e)

            @block.vector
            def _(e):
                emit_stream("dve", e)

            @block.gpsimd
            def _(e):
                emit_stream("pool", e)

            @block.sync
            def _(e):
                emit_stream("sp", e)
        stack.close()


def ACT(out, in_, func, **kw):
    return lambda e: e.activation(out=out, in_=in_, func=func, **kw)


def TT(out, in0, in1, op):
    return lambda e: e.tensor_tensor(out=out, in0=in0, in1=in1, op=op)


def TS(out, in0, s1, op0, s2=None, op1=None):
    if op1 is None:
        return lambda e: e.tensor_scalar(out=out, in0=in0, scalar1=s1, scalar2=None, op0=op0)
    return lambda e: e.tensor_scalar(out=out, in0=in0, scalar1=s1, scalar2=s2, op0=op0, op1=op1)


def STT(out, in0, scalar, in1, op0, op1):
    return lambda e: e.scalar_tensor_tensor(out=out, in0=in0, scalar=scalar, in1=in1, op0=op0, op1=op1)


def CP(out, in_):
    return lambda e: e.tensor_copy(out=out, in_=in_)


def MM(out, lhsT, rhs, start=True, stop=True, skip=False):
    if skip:
        return lambda e: e.matmul(out, lhsT, rhs, start=start, stop=stop, skip_group_check=True)
    return lambda e: e.matmul(out, lhsT, rhs, start=start, stop=stop)


def TR(out, in_, ident):
    return lambda e: e.transpose(out, in_, ident)


def DMA(out, in_):
    return lambda e: e.dma_start(out=out, in_=in_)


def RED(out, in_, op=None):
    return lambda e: e.tensor_reduce(out=out, in_=in_, axis=AX.X, op=(op or ALU.add))


def RECIP(out, in_):
    return lambda e: e.reciprocal(out=out, in_=in_)


def MEMSET(ap, v):
    return lambda e: e.memset(ap, v)


class Arena:
    def __init__(self, nc, words):
        self.t = nc.alloc_sbuf_tensor("arena", [128, words], F32)
        self.words = words
        self.off = 0
        self.peak = 0

    def alloc(self, shape, dt):
        n = int(np.prod(shape[1:]))
        nw = n if dt == F32 else (n + 1) // 2
        nw = (nw + 7) // 8 * 8
        assert self.off + nw <= self.words, ("SBUF arena overflow", self.off, nw, self.words)
        v = self.t[:, self.off:self.off + nw]
        self.off += nw
        self.peak = max(self.peak, self.off)
        if dt != F32:
            v = v.bitcast(dt)
        v = v[:, 0:n]
        if len(shape) == 3:
            v = v.rearrange("p (a b) -> p a b", b=shape[2])
        elif len(shape) == 4:
            v = v.rearrange("p (a b c) -> p a b c", b=shape[2], c=shape[3])
        return v

    def mark(self):
        return self.off

    def release(self, m):
        self.off = m


def bc(ap, shape):
    return ap.to_broadcast(shape)


class Ctx:
    pass


def build(n_layers=2, phases=("diff", "na", "ssd"), dbg=False):
    nc = bass.Bass("TRN2", target_bir_lowering=False)
    c = Ctx()
    c.nc = nc
    c.dbg = dbg

    def din(name, shape, dt=F32):
        return nc.dram_tensor(name, shape, dt, kind="ExternalInput").ap()

    c.x_in = din("x", [L, DM])
    c.norm_w = din("norm_w", [2, DM])
    c.w_in = din("w_in", [2, DM, INW])
    c.conv_w = din("conv_w", [2, 5, 1536])
    c.conv_b = din("conv_b", [2, 1536])
    c.a_log = din("a_log", [2, 32])
    c.dt_bias = din("dt_bias", [2, 32])
    c.d_skip = din("d_skip", [2, 32])
    c.ssd_norm_w = din("ssd_norm_w", [2, 1024])
    c.diff_qk_norm = din("diff_qk_norm", [2, 128])
    c.diff_lambda = din("diff_lambda", [2, 256])
    c.diff_subln = din("diff_subln", [2, 128])
    c.na_qk_norm = din("na_qk_norm", [2, 128])
    c.na_tab = din("na_tab", [2, 8, 128, NA_NCLS * 128])
    c.w_out = din("w_out", [2, 2048, DM])
    c.cs_tab = din("cs_tab", [L, 128])
    c.consts = din("consts", [128, 4 * 128])
    c.out = nc.dram_tensor("out", [L, DM], F32, kind="ExternalOutput").ap()
    kind_scr = "ExternalOutput" if dbg else "Internal"
    c.x1 = nc.dram_tensor("x1", [L, DM], F32, kind=kind_scr).ap()
    c.yT = [nc.dram_tensor("yT%d" % i, [2048, L], BF16, kind=kind_scr).ap() for i in range(n_layers)]
    c.csT_d = nc.dram_tensor("csT_d", [NT, 32, 128], F32, kind="Internal").ap()
    c.sb_d = nc.dram_tensor("sb_d", [2, NT, 128, 512], BF16, kind="Internal").ap()
    c.sf_d = nc.dram_tensor("sf_d", [2, NT, 128, 512], BF16, kind="Internal").ap()

    c.dumps = []

    def dump(name, ap, shape, dt, tiles):
        if not dbg:
            return
        d = nc.dram_tensor("dbg_" + name, list(shape), dt, kind="ExternalOutput").ap()
        c.dumps.append(c.P.op("sp", DMA(d, ap), reads=tiles))
    c.dump = dump
    A = Arena(nc, 51000)
    c.A = A
    P = Prog(nc)
    c.P = P
    c.psall = nc.alloc_psum_tensor("psall", [128, 8 * 512], F32)[:, :]
    c.ps = [c.psall[:, i * 512:(i + 1) * 512] for i in range(7)]
    c.tps = [T(excl=True) for _ in range(7)]
    c.psb = c.psall[:, 7 * 512:8 * 512].bitcast(BF16)
    _tb = T(excl=True)
    c.tpsb = [_tb, _tb]

    c.cst = A.alloc([128, 512], F32)
    c.t_cst = T()
    P.op("sp", DMA(c.cst, c.consts), writes=[c.t_cst])
    c.identf = c.cst[:, 0:128]
    c.U = c.cst[:, 128:256]
    c.Ur = c.cst[:, 256:384]
    c.onesf = c.cst[:, 384:512]
    c.ident = A.alloc([128, 128], BF16)
    c.t_ident = T()
    P.op("dve", CP(c.ident, c.identf), reads=[c.t_cst], writes=[c.t_ident])
    c.epsc = A.alloc([128, 1], F32)
    c.t_eps = T()
    P.op("pool", MEMSET(c.epsc, EPS), writes=[c.t_eps])
    c.onec = A.alloc([128, 1], F32)
    P.op("pool", MEMSET(c.onec, 1.0), writes=[c.t_eps])
    c.hdnT = A.alloc([128, 8, L], BF16)
    c.t_hdnT = [T() for _ in range(NT)]
    c.Wbuf = A.alloc([128, 16384], BF16)
    c.t_Wq = [T() for _ in range(4)]
    c.W8 = c.Wbuf.rearrange("p (k c) -> p k c", c=2048)
    c.Wo = c.Wbuf.rearrange("p (k c) -> p k c", c=1024)
    c.Wz = [c.Wbuf[:, 3 * 4096:4 * 4096].rearrange("p (k c) -> p k c", c=512),
            c.Wbuf[:, 2 * 4096:3 * 4096].rearrange("p (k c) -> p k c", c=512)]
    c.pf = {}
    c.phases = phases

    finals = []
    for l in range(n_layers):
        xsrc = c.x_in if l == 0 else c.x1
        xdst = c.x1 if l < n_layers - 1 or dbg and n_layers == 1 else c.out
        if l == n_layers - 1:
            xdst = c.out
        P.barrier()
        if "diff" in phases:
            pf_qkvg(c, l, "diff")
        if l == 0:
            phase_hdn(c, l, xsrc)
        if "diff" in phases:
            P.barrier()
            phase_diff(c, l)
        if "na" in phases:
            P.barrier()
            phase_na(c, l)
        if "ssd" in phases:
            P.barrier()
            phase_ssd(c, l)
        P.barrier()
        fin = phase_out(c, l, xsrc, xdst, fuse_next=(l + 1 < n_layers))
        if l == n_layers - 1:
            finals = fin
    P.barrier()
    finals = list(finals)
    if dbg:
        finals += [o for o in P.ops if o.isdma][-36:] + c.dumps
    P.emit(final_waits=finals)
    c.peak = A.peak
    return nc, c


def phase_hdn(c, l, xsrc):
    P, A = c.P, c.A
    m = A.mark()
    nwb = A.alloc([128, DM], F32)
    t_nwb = T()
    P.op("sp", DMA(nwb, c.norm_w[l:l + 1, :].partition_broadcast(128)), writes=[t_nwb])
    NX = 4
    xt = [A.alloc([128, DM], F32) for _ in range(NX)]
    t_xt = [T() for _ in range(NX)]
    sq = A.alloc([128, DM], F32)
    t_sq = T()
    ss = A.alloc([128, NT], F32)
    t_ss = [T() for _ in range(NT)]
    hb = [A.alloc([128, DM], BF16) for _ in range(2)]
    t_hb = [T(), T()]

    def load_x(tt):
        P.op("sp", DMA(xt[tt % NX], xsrc[tt * 128:(tt + 1) * 128, :]), writes=[t_xt[tt % NX]])
    for tt in range(min(NX - 1, NT)):
        load_x(tt)
    for tt in range(NT):
        b = tt % 2
        xb, t_xb = xt[tt % NX], t_xt[tt % NX]
        s1 = ss[:, tt:tt + 1]
        if tt + NX - 1 < NT:
            load_x(tt + NX - 1)
        P.op("act", ACT(sq, xb, AF.Square, accum_out=s1), reads=[t_xb], writes=[t_sq, t_ss[tt]])
        P.op("act", ACT(s1, s1, AF.Ln, scale=1.0 / DM, bias=c.epsc), reads=[t_ss[tt], c.t_eps], writes=[t_ss[tt]])
        P.op("act", ACT(s1, s1, AF.Exp, scale=-0.5), reads=[t_ss[tt]], writes=[t_ss[tt]])
        P.op("dve", STT(hb[b], xb, s1, nwb, ALU.mult, ALU.mult), reads=[t_xb, t_ss[tt], t_nwb], writes=[t_hb[b]])
        for k in range(8):
            P.op("pe", TR(c.psb[:, k * 128:(k + 1) * 128], hb[b][:, k * 128:(k + 1) * 128], c.ident),
                 reads=[t_hb[b], c.t_ident], writes=[c.tpsb[k // 4]])
        P.op("act", ACT(c.hdnT[:, :, tt * 128:(tt + 1) * 128], c.psb[:, :].rearrange("p (k t) -> p k t", t=128), AF.Copy),
             reads=[c.tpsb[0], c.tpsb[1]], writes=[c.t_hdnT[tt]])
    A.release(m)


def load_w(c, l, dst, col0, ncols, tiles, step=512, extra=()):
    P = c.P
    j = 0
    for c0 in range(0, ncols, step):
        n = min(step, ncols - c0)
        src = c.w_in[l, :, col0 + c0:col0 + c0 + n].rearrange("(k p) c -> p k c", p=128)
        P.op("poolq", DMA(dst[:, :, c0:c0 + n], src), writes=[tiles[j]] + list(extra))
        j += 1


def pf_qkvg(c, l, which):
    key = (which, l)
    if key not in c.pf:
        tW = [T() for _ in range(4)]
        load_w(c, l, c.W8, C_DIFF if which == "diff" else C_NA, 2048, tW, extra=c.t_Wq)
        c.pf[key] = tW
    return c.pf[key]


def pf_ssd(c, l):
    key = ("ssd", l)
    if key not in c.pf:
        tWz = [[T()], [T()]]
        load_w(c, l, c.Wz[0], C_Z, 512, tWz[0], extra=[c.t_Wq[3]])
        load_w(c, l, c.Wz[1], C_Z + 512, 512, tWz[1], extra=[c.t_Wq[2]])
        c.pf[key] = tWz
    return c.pf[key]


def pf_out(c, l, quarters):
    key = ("out", l)
    if key not in c.pf:
        c.pf[key] = [None] * 4
    tWo = c.pf[key]
    for j in quarters:
        if tWo[j] is None:
            tWo[j] = T()
            src = c.w_out[l, j * 512:(j + 1) * 512, :].rearrange("(k p) c -> p k c", p=128)
            c.P.op("poolq", DMA(c.Wo[:, j * 4:(j + 1) * 4, :], src), writes=[tWo[j], c.t_Wq[j]])
    return tWo


def qk_norm_rope(c, src_ps, t_src, sq, t_sq, ss8, t_ss8, tbuf, t_tb, ubuf, t_ub, TC, TS_, t_tab, tt, dst, t_dst, rope):
    P = c.P
    P.op("act", ACT(sq, src_ps, AF.Square), reads=[t_src], writes=[t_sq])
    P.op("dve", RED(ss8, sq.rearrange("p (g d) -> p g d", d=64)), reads=[t_sq], writes=[t_ss8])
    P.op("act", ACT(ss8, ss8, AF.Ln, scale=1.0 / 64, bias=c.epsc), reads=[t_ss8, c.t_eps], writes=[t_ss8])
    P.op("act", ACT(ss8, ss8, AF.Exp, scale=-0.5), reads=[t_ss8], writes=[t_ss8])
    t3 = tbuf.rearrange("p (g d) -> p g d", d=64)
    P.op("dve", TT(t3, src_ps.rearrange("p (g d) -> p g d", d=64), bc(ss8.unsqueeze(2), [128, 8, 64]), ALU.mult),
         reads=[t_src, t_ss8], writes=[t_tb])
    if rope:
        u3 = ubuf.rearrange("p (g d) -> p g d", d=64)
        P.op("pool", TT(u3[:, :, 0:32], t3[:, :, 32:64], bc(TS_[:, tt, 0:32].unsqueeze(1), [128, 8, 32]), ALU.mult),
             reads=[t_tb, t_tab], writes=[t_ub])
        P.op("pool", TT(u3[:, :, 32:64], t3[:, :, 0:32], bc(TS_[:, tt, 32:64].unsqueeze(1), [128, 8, 32]), ALU.mult),
             reads=[t_tb, t_tab], writes=[t_ub])
        P.op("dve", TT(t3, t3, bc(TC[:, tt, :].unsqueeze(1), [128, 8, 64]), ALU.mult), reads=[t_tb, t_tab], writes=[t_tb])
        P.op("dve", TT(dst, tbuf, ubuf, ALU.add), reads=[t_tb, t_ub], writes=[t_dst])
    else:
        P.op("dve", TT(dst.rearrange("p (g d) -> p g d", d=64), t3, bc(TC.unsqueeze(1), [128, 8, 64]), ALU.mult),
             reads=[t_tb, t_tab], writes=[t_dst])


def prep_qkvg(c, W, tW, qT, t_qT, kT, t_kT, vcopy, t_V, G, t_G, tabs, t_tab, rope):
    P, A = c.P, c.A
    ps, tps = c.ps, c.tps
    NB = 3 if rope else 2
    LAG = NB - 1
    raw = [[A.alloc([128, 512], F32) for _ in range(NB)] for _ in range(2)]
    t_raw = [[T() for _ in range(NB)] for _ in range(2)]
    sq = [A.alloc([128, 512], F32) for _ in range(2)]
    t_sq = [T(), T()]
    ss8 = [[A.alloc([128, 8], F32) for _ in range(NB)] for _ in range(2)]
    t_ss8 = [[T() for _ in range(NB)] for _ in range(2)]
    tbuf = [[A.alloc([128, 512], F32) for _ in range(NB)] for _ in range(2)]
    t_tb = [[T() for _ in range(NB)] for _ in range(2)]
    ubuf = [[A.alloc([128, 512], F32) for _ in range(NB)] for _ in range(2)] if rope else None
    t_ub = [[T() for _ in range(NB)] for _ in range(2)]
    rbuf = [[A.alloc([128, 512], BF16) for _ in range(NB)] for _ in range(2)]
    t_rb = [[T() for _ in range(NB)] for _ in range(2)]
    dsts = ((qT, t_qT), (kT, t_kT))
    nbank = 0
    gbank = {}

    def chain_a(tt, i):
        b = tt % NB
        P.op("act", ACT(sq[i], raw[i][b], AF.Square), reads=[t_raw[i][b]], writes=[t_sq[i]])
        P.op("dve", RED(ss8[i][b], sq[i].rearrange("p (g d) -> p g d", d=64)), reads=[t_sq[i]], writes=[t_ss8[i][b]])

    def chain_b(tt, i):
        b = tt % NB
        s8, t_s8 = ss8[i][b], t_ss8[i][b]
        P.op("act", ACT(s8, s8, AF.Ln, scale=1.0 / 64, bias=c.epsc), reads=[t_s8, c.t_eps], writes=[t_s8])
        P.op("act", ACT(s8, s8, AF.Exp, scale=-0.5), reads=[t_s8], writes=[t_s8])

    def chain_c(tt, i):
        b = tt % NB
        rw, t_rw = raw[i][b], t_raw[i][b]
        s8, t_s8 = ss8[i][b], t_ss8[i][b]
        tb_, t_tb_ = tbuf[i][b], t_tb[i][b]
        t3 = tb_.rearrange("p (g d) -> p g d", d=64)
        P.op("dve", TT(t3, rw.rearrange("p (g d) -> p g d", d=64), bc(s8.unsqueeze(2), [128, 8, 64]), ALU.mult),
             reads=[t_rw, t_s8], writes=[t_tb_])
        dst, t_dst = rbuf[i][b], t_rb[i][b]
        TC, TS_ = tabs[2 * i], tabs[2 * i + 1]
        if rope:
            ub, t_ub_ = ubuf[i][b], t_ub[i][b]
            u3 = ub.rearrange("p (g d) -> p g d", d=64)
            eng_u = "pool" if i == 1 else "dve"
            P.op(eng_u, TT(u3[:, :, 0:32], t3[:, :, 32:64], bc(TS_[:, tt, 0:32].unsqueeze(1), [128, 8, 32]), ALU.mult),
                 reads=[t_tb_, t_tab], writes=[t_ub_])
            P.op(eng_u, TT(u3[:, :, 32:64], t3[:, :, 0:32], bc(TS_[:, tt, 32:64].unsqueeze(1), [128, 8, 32]), ALU.mult),
                 reads=[t_tb_, t_tab], writes=[t_ub_])
            P.op("dve", TT(t3, t3, bc(TC[:, tt, :].unsqueeze(1), [128, 8, 64]), ALU.mult), reads=[t_tb_, t_tab], writes=[t_tb_])
            P.op("dve", TT(dst, tb_, ub, ALU.add), reads=[t_tb_, t_ub_], writes=[t_dst])
        else:
            P.op("dve", TT(dst.rearrange("p (g d) -> p g d", d=64), t3, bc(TC.unsqueeze(1), [128, 8, 64]), ALU.mult),
                 reads=[t_tb_, t_tab], writes=[t_dst])

    def transposes(tt):
        b = tt % NB
        tok = slice(tt * 128, (tt + 1) * 128)
        for i in range(2):
            for h in range(4):
                P.op("pe", TR(c.psb[:, i * 512 + h * 128:i * 512 + (h + 1) * 128], rbuf[i][b][:, h * 128:(h + 1) * 128], c.ident),
                     reads=[t_rb[i][b], c.t_ident], writes=[c.tpsb[i]])
        for i in range(2):
            dstT, t_dstT = dsts[i]
            P.op("dve", CP(dstT[:, :, tok], c.psb[:, i * 512:(i + 1) * 512].rearrange("p (h t) -> p h t", t=128)),
                 reads=[c.tpsb[i]], writes=[t_dstT[tt]])

    for tt in range(NT + LAG):
        if tt < NT:
            tok = slice(tt * 128, (tt + 1) * 128)
            b = tt % NB
            banks = []
            for j in range(4):
                bk = nbank % 7
                nbank += 1
                banks.append(bk)
                for k in range(8):
                    P.op("pe", MM(ps[bk], c.hdnT[:, k, tok], W[:, k, j * 512:(j + 1) * 512], start=(k == 0), stop=(k == 7)),
                         reads=[c.t_hdnT[tt], tW[j]] + c.t_Wq, writes=[tps[bk]])
            for i in range(2):
                if rope:
                    P.op("act", ACT(raw[i][b], ps[banks[i]], AF.Copy), reads=[tps[banks[i]]], writes=[t_raw[i][b]])
                else:
                    P.op("dve", CP(raw[i][b], ps[banks[i]]), reads=[tps[banks[i]]], writes=[t_raw[i][b]])
            vcopy(tt, ps[banks[2]], tps[banks[2]])
            gbank[tt] = banks[3]
            for stage in (chain_a, chain_b, chain_c):
                for i in range(2):
                    stage(tt, i)
            if tt % 2 == 1 or tt == NT - 1:
                for t2 in ([tt - 1, tt] if tt % 2 == 1 else [tt]):
                    P.op("act", ACT(G[:, t2, :], ps[gbank[t2]], AF.Silu), reads=[tps[gbank[t2]]], writes=[t_G[t2]])
        if tt >= LAG:
            transposes(tt - LAG)


def phase_diff(c, l):
    P, A = c.P, c.A
    m = A.mark()
    lam_init = 0.8 - 0.6 * math.exp(-0.3 * l)
    W = c.W8
    tW = pf_qkvg(c, l, "diff")
    qT = A.alloc([128, 4, L], BF16)
    kT = A.alloc([128, 4, L], BF16)
    t_qT = [T() for _ in range(NT)]
    t_kT = [T() for _ in range(NT)]
    V = A.alloc([128, NT, 4, 129], BF16)
    t_V = [T() for _ in range(NT)]
    G = A.alloc([128, NT, 512], BF16)
    t_G = [T() for _ in range(NT)]
    t_misc = T()
    P.op("pool", MEMSET(V[:, :, :, 128:129], 1.0), writes=t_V)
    wqk = A.alloc([128, 128], F32)
    P.op("sp", DMA(wqk, c.diff_qk_norm[l:l + 1, :].partition_broadcast(128)), writes=[t_misc])
    P.op("dve", TS(wqk[:, 0:64], wqk[:, 0:64], 0.125, ALU.mult), reads=[t_misc], writes=[t_misc])
    tabs = A.alloc([128, 4, NT * 64], F32)
    t_tab = T()
    m_cs = A.mark()
    c.cs_sb = A.alloc([128, NT, 128], F32)
    c.t_cs = T()
    P.op("sp", DMA(c.cs_sb, c.cs_tab.rearrange("(t p) c -> p t c", p=128)), writes=[c.t_cs])
    for i, w in enumerate((wqk[:, 0:64], wqk[:, 64:128])):
        TCv = tabs[:, 2 * i, :].rearrange("p (t d) -> p t d", d=64)
        TSv = tabs[:, 2 * i + 1, :].rearrange("p (t d) -> p t d", d=64)
        P.op("dve", TT(TCv, c.cs_sb[:, :, 0:64], bc(w.unsqueeze(1), [128, NT, 64]), ALU.mult), reads=[c.t_cs, t_misc], writes=[t_tab])
        P.op("dve", TT(TSv[:, :, 0:32], c.cs_sb[:, :, 64:96], bc(w[:, 32:64].unsqueeze(1), [128, NT, 32]), ALU.mult),
             reads=[c.t_cs, t_misc], writes=[t_tab])
        P.op("dve", TT(TSv[:, :, 32:64], c.cs_sb[:, :, 96:128], bc(w[:, 0:32].unsqueeze(1), [128, NT, 32]), ALU.mult),
             reads=[c.t_cs, t_misc], writes=[t_tab])
    P.barrier()
    A.release(m_cs)
    TCq = tabs[:, 0, :].rearrange("p (t d) -> p t d", d=64)
    TSq = tabs[:, 1, :].rearrange("p (t d) -> p t d", d=64)
    TCk = tabs[:, 2, :].rearrange("p (t d) -> p t d", d=64)
    TSk = tabs[:, 3, :].rearrange("p (t d) -> p t d", d=64)
    lamb = A.alloc([128, 256], F32)
    t_lam = T()
    P.op("sp", DMA(lamb, c.diff_lambda[l:l + 1, :].partition_broadcast(128)), writes=[t_lam])
    lsc = A.alloc([128, 4], F32)
    lam3 = lamb.rearrange("p (a b d) -> p a b d", a=2, b=2)
    prod = A.alloc([128, 2, 64], F32)
    P.op("dve", TT(prod, lam3[:, :, 0, :], lam3[:, :, 1, :], ALU.mult), reads=[t_lam], writes=[t_lam])
    P.op("dve", RED(lsc[:, 0:2], prod), reads=[t_lam], writes=[t_lam])
    P.op("act", ACT(lsc[:, 0:2], lsc[:, 0:2], AF.Exp), reads=[t_lam], writes=[t_lam])
    P.op("dve", TT(lsc[:, 2:3], lsc[:, 1:2], lsc[:, 0:1], ALU.subtract), reads=[t_lam], writes=[t_lam])
    P.op("dve", TS(lsc[:, 3:4], lsc[:, 2:3], -lam_init, ALU.add), reads=[t_lam], writes=[t_lam])
    neglam = lsc[:, 3:4]
    swb = A.alloc([128, 128], F32)
    t_swb = T()
    P.op("sp", DMA(swb, c.diff_subln[l:l + 1, :].partition_broadcast(128)), writes=[t_swb])
    P.op("dve", TS(swb, swb, 1.0 - lam_init, ALU.mult), reads=[t_swb], writes=[t_swb])

    ps, tps = c.ps, c.tps
    m_prep = A.mark()

    def vcopy(tt, bank, t_bank):
        P.op("act", ACT(V[:, tt, :, 0:128], bank.rearrange("p (h d) -> p h d", d=128), AF.Copy), reads=[t_bank], writes=[t_V[tt]])
    prep_qkvg(c, W, tW, qT, t_qT, kT, t_kT, vcopy, t_V, G, t_G, (TCq, TSq, TCk, TSk), t_tab, True)
    P.barrier()
    A.release(m_prep)
    if "na" in c.phases:
        pf_qkvg(c, l, "na")
    elif "ssd" in c.phases:
        pf_ssd(c, l)
    c.dump("qT%d" % l, qT, [128, 4, L], BF16, t_qT)
    c.dump("kT%d" % l, kT, [128, 4, L], BF16, t_kT)
    c.dump("V%d" % l, V, [128, NT, 4, 129], BF16, t_V)
    c.dump("G%d" % l, G, [128, NT, 512], BF16, t_G)
    c.dump("hdnT%d" % l, c.hdnT, [128, 8, L], BF16, c.t_hdnT)
    NPT = 3
    Pt = [A.alloc([128, 1024], BF16) for _ in range(NPT)]
    t_Pt = [T() for _ in range(NPT)]
    SP = [c.psall[:, 0:1024], c.psall[:, 1024:2048]]
    t_SP = [[tps[0], tps[1]], [tps[2], tps[3]]]
    Ob3 = [ps[4], ps[5], ps[6]]
    t_O3 = [tps[4], tps[5], tps[6]]

    def acc(a):
        return a // 3, (a % 3) * 129
    Osb = A.alloc([128, 3, 512], F32)
    t_Osb = T()
    rr = [A.alloc([128, 16], F32) for _ in range(2)]
    t_rr = [T(), T()]
    ob4 = [A.alloc([128, 4, 128], F32) for _ in range(2)]
    t_ob4 = [T(), T()]
    junk = A.alloc([128, 128], F32)
    t_junk = T()
    yb4 = [A.alloc([128, 512], BF16) for _ in range(2)]
    t_yb4 = [T(), T()]
    yst = [A.alloc([128, 512], BF16) for _ in range(2)]
    t_yst = [T(), T()]
    iters = [(h, qb, kt) for h in range(4) for qb in range(4) for kt in range(NT)]
    NI = len(iters)
    deferred = []

    def emit_S(i):
        h, qb, kt = iters[i]
        for cc in range(2):
            pr = slice(cc * 64, (cc + 1) * 64)
            P.op("pe", MM(SP[i % 2][:, cc * 512:(cc + 1) * 512], kT[pr, h, kt * 128:(kt + 1) * 128], qT[pr, h, qb * 512:(qb + 1) * 512]),
                 reads=[t_kT[kt]] + [t_qT[qb * 4 + j] for j in range(4)], writes=[t_SP[i % 2][cc]])

    def Oslice(a):
        bk, off = acc(a)
        return Osb[:, bk, off:off + 129]

    def fin_stage1(blk, h, qb):
        b = blk % 2
        r_, t_r = rr[b], t_rr[b]
        ob, t_ob = ob4[b], t_ob4[b]
        for j in range(4):
            O0, O1 = Oslice(j), Oslice(4 + j)
            P.op("dve", RECIP(r_[:, 4 * j:4 * j + 1], O0[:, 128:129]), reads=[t_Osb], writes=[t_r])
            P.op("dve", RECIP(r_[:, 4 * j + 1:4 * j + 2], O1[:, 128:129]), reads=[t_Osb], writes=[t_r])
            P.op("dve", TT(r_[:, 4 * j + 2:4 * j + 3], r_[:, 4 * j + 1:4 * j + 2], neglam, ALU.mult), reads=[t_r, t_lam], writes=[t_r])
            P.op("dve", TS(ob[:, j, :], O0[:, 0:128], r_[:, 4 * j:4 * j + 1], ALU.mult), reads=[t_Osb, t_r], writes=[t_ob])
            P.op("dve", STT(ob[:, j, :], O1[:, 0:128], r_[:, 4 * j + 2:4 * j + 3], ob[:, j, :], ALU.mult, ALU.add),
                 reads=[t_Osb, t_r, t_ob], writes=[t_ob])

    def fin_stage2(blk, h, qb):
        b = blk % 2
        r_, t_r = rr[b], t_rr[b]
        ob, t_ob = ob4[b], t_ob4[b]
        for j in range(4):
            P.op("act", ACT(junk, ob[:, j, :], AF.Square, accum_out=r_[:, 4 * j + 3:4 * j + 4]), reads=[t_ob], writes=[t_junk, t_r])
        r3 = r_.rearrange("p (j k) -> p j k", k=4)[:, :, 3]
        P.op("act", ACT(r3, r3, AF.Ln, scale=1.0 / 128, bias=c.epsc), reads=[t_r, c.t_eps], writes=[t_r])
        P.op("act", ACT(r3, r3, AF.Exp, scale=-0.5), reads=[t_r], writes=[t_r])

    def fin_stage3(blk, h, qb):
        b = blk % 2
        r_, t_r = rr[b], t_rr[b]
        ob, t_ob = ob4[b], t_ob4[b]
        yb, t_yb = yb4[b], t_yb4[b]
        for j in range(4):
            tt = qb * 4 + j
            P.op("dve", STT(ob[:, j, :], ob[:, j, :], r_[:, 4 * j + 3:4 * j + 4], swb, ALU.mult, ALU.mult), reads=[t_ob, t_r, t_swb], writes=[t_ob])
            P.op("dve", TT(yb[:, j * 128:(j + 1) * 128], ob[:, j, :], G[:, tt, h * 128:(h + 1) * 128], ALU.mult),
                 reads=[t_ob, t_G[tt]], writes=[t_yb])
        for j in range(4):
            P.op("pe", TR(c.psb[:, j * 128:(j + 1) * 128], yb[:, j * 128:(j + 1) * 128], c.ident),
                 reads=[t_yb, c.t_ident], writes=[c.tpsb[0]])

    def fin_stage4(blk, h, qb):
        b = blk % 2
        ys, t_ys = yst[b], t_yst[b]
        P.op("dve", CP(ys, c.psb[:, 0:512]), reads=[c.tpsb[0]], writes=[t_ys])
        P.op("sp", DMA(c.yT[l][1024 + h * 128:1024 + (h + 1) * 128, qb * 512:(qb + 1) * 512], ys), reads=[t_ys])

    emit_S(0)
    blk = 0
    for i in range(NI):
        h, qb, kt = iters[i]
        if i + 1 < NI:
            emit_S(i + 1)
        pt, t_pt = Pt[i % NPT], t_Pt[i % NPT]
        P.op("act", ACT(pt, SP[i % 2], AF.Exp), reads=t_SP[i % 2], writes=[t_pt])
        for cc in range(2):
            for j in range(4):
                bk, off = acc(cc * 4 + j)
                P.op("pe", MM(Ob3[bk][:, off:off + 129], pt[:, cc * 512 + j * 128:cc * 512 + (j + 1) * 128],
                              V[:, kt, h, :], start=(kt == 0 and off == 0), stop=(kt == NT - 1), skip=True),
                     reads=[t_pt, t_V[kt]], writes=[t_O3[bk]])
        for (due, fn) in [d for d in deferred if d[0] <= i]:
            fn()
        deferred = [d for d in deferred if d[0] > i]
        if kt == NT - 1:
            for k3 in range(3):
                P.op("dve", CP(Osb[:, k3, 0:387], Ob3[k3][:, 0:387]), reads=[t_O3[k3]], writes=[t_Osb])
            fin_stage1(blk, h, qb)
            deferred.append((i + 2, (lambda b_=blk, h_=h, q_=qb: fin_stage2(b_, h_, q_))))
            deferred.append((i + 4, (lambda b_=blk, h_=h, q_=qb: fin_stage3(b_, h_, q_))))
            deferred.append((i + 6, (lambda b_=blk, h_=h, q_=qb: fin_stage4(b_, h_, q_))))
            blk += 1
    for (due, fn) in deferred:
        fn()
    A.release(m)


def _na_classes():
    rows, W, kh, kw = 32, 64, 8, 16
    rs = lambda r: min(max(r - kh // 2, 0), rows - kh)
    types = {}
    keys = []
    plan = []
    for i in range(16):
        qrows = [2 * i, 2 * i + 1]
        lo = min(rs(r) for r in qrows)
        hi = max(rs(r) + kh - 1 for r in qrows)
        tkeys = []
        kbs = list(range(lo // 2, hi // 2 + 1))
        for kb in kbs:
            key = []
            for b in range(2):
                for a in range(2):
                    kr, qr = 2 * kb + b, 2 * i + a
                    ok = rs(qr) <= kr < rs(qr) + kh
                    key.append((kr - qr + 7) if ok else -1)
            tkeys.append(tuple(key))
        tkeys = tuple(tkeys)
        if tkeys not in types:
            types[tkeys] = len(keys)
            keys.extend(tkeys)
        base = types[tkeys]
        plan.append([(kb, base + bi) for bi, kb in enumerate(kbs)])
    return keys, plan


NA_KEYS, NA_PLAN = _na_classes()
NA_NCLS = len(NA_KEYS)


def _na_tables(rpb):
    W, kw = 64, 16
    cidx = np.arange(W)
    col_start = np.clip(cidx - kw // 2, 0, W - kw)
    col_ok = (cidx[None, :] >= col_start[:, None]) & (cidx[None, :] < col_start[:, None] + kw)
    dc = np.clip(cidx[None, :] - cidx[:, None] + (kw - 1), 0, 2 * kw - 2)
    out = np.full((2, 8, 128, NA_NCLS, 128), NEG, dtype=np.float32)
    for ci, key in enumerate(NA_KEYS):
        n = 0
        for b in range(2):
            for a in range(2):
                dr = key[n]
                n += 1
                if dr < 0:
                    continue
                blkv = rpb[:, :, dr, :][:, :, dc]
                blkv = np.where(col_ok[None, None], blkv, np.float32(NEG))
                out[:, :, b * 64:(b + 1) * 64, ci, a * 64:(a + 1) * 64] = np.transpose(blkv, (0, 1, 3, 2))
    return np.ascontiguousarray(out.reshape(2, 8, 128, NA_NCLS * 128))


def phase_na(c, l):
    P, A = c.P, c.A
    m = A.mark()
    W = c.W8
    tW = pf_qkvg(c, l, "na")
    qT = A.alloc([128, 4, L], BF16)
    kT = A.alloc([128, 4, L], BF16)
    t_qT = [T() for _ in range(NT)]
    t_kT = [T() for _ in range(NT)]
    V = A.alloc([128, NT, 8, 65], BF16)
    t_V = [T() for _ in range(NT)]
    G = A.alloc([128, NT, 512], BF16)
    t_G = [T() for _ in range(NT)]
    Y = A.alloc([128, NT, 512], BF16)
    t_Y = [T() for _ in range(NT)]
    P.op("pool", MEMSET(V[:, :, :, 64:65], 1.0), writes=t_V)
    t_misc = T()
    wqk = A.alloc([128, 128], F32)
    P.op("sp", DMA(wqk, c.na_qk_norm[l:l + 1, :].partition_broadcast(128)), writes=[t_misc])
    P.op("dve", TS(wqk[:, 0:64], wqk[:, 0:64], 0.125, ALU.mult), reads=[t_misc], writes=[t_misc])
    ps, tps = c.ps, c.tps
    m_prep = A.mark()

    def vcopy(tt, bank, t_bank):
        P.op("act", ACT(V[:, tt, :, 0:64], bank.rearrange("p (h d) -> p h d", d=64), AF.Copy), reads=[t_bank], writes=[t_V[tt]])
    prep_qkvg(c, W, tW, qT, t_qT, kT, t_kT, vcopy, t_V, G, t_G, (wqk[:, 0:64], None, wqk[:, 64:128], None), t_misc, False)
    P.barrier()
    A.release(m_prep)
    if "ssd" in c.phases:
        pf_ssd(c, l)
    pf_out(c, l, [0, 1])
    tab = [A.alloc([128, NA_NCLS * 128], F32) for _ in range(2)]
    t_tab = [T(), T()]
    Tb = [A.alloc([128, 640], F32) for _ in range(2)]
    t_Tb = [T(), T()]
    Pb = [A.alloc([128, 640], BF16) for _ in range(3)]
    t_Pb = [T(), T(), T()]
    rr = [A.alloc([128, 2], F32) for _ in range(2)]
    t_rr = [T(), T()]
    its = [(hh, i) for hh in range(8) for i in range(NT)]
    NI = len(its)
    loaded = set()

    def load_tab(hh):
        if hh in loaded or hh >= 8:
            return
        loaded.add(hh)
        P.op("sp", DMA(tab[hh % 2], c.na_tab[l, hh, :, :]), writes=[t_tab[hh % 2]])
        P.op("act", ACT(tab[hh % 2], tab[hh % 2], AF.Exp), reads=[], writes=[t_tab[hh % 2]])

    def emit_S(n):
        hh, i = its[n]
        jb, e = hh // 2, hh % 2
        pr = slice(e * 64, (e + 1) * 64)
        S0, S1 = ps[(n % 2) * 2], ps[(n % 2) * 2 + 1]
        tS = [tps[(n % 2) * 2], tps[(n % 2) * 2 + 1]]
        for bi, (kb, cls) in enumerate(NA_PLAN[i]):
            dstS = (S0 if bi < 4 else S1)[:, (bi % 4) * 128:(bi % 4 + 1) * 128]
            P.op("pe", MM(dstS, kT[pr, jb, kb * 128:(kb + 1) * 128], qT[pr, jb, i * 128:(i + 1) * 128]),
                 reads=[t_kT[kb], t_qT[i]], writes=[tS[bi // 4]])

    def emit_exp_mul(n):
        hh, i = its[n]
        plan = NA_PLAN[i]
        nb = len(plan)
        base = plan[0][1]
        tS = [tps[(n % 2) * 2], tps[(n % 2) * 2 + 1]]
        Sboth = c.psall[:, (n % 2) * 1024:(n % 2) * 1024 + nb * 128]
        T_, t_T = Tb[n % 2], t_Tb[n % 2]
        P_, t_P = Pb[n % 3], t_Pb[n % 3]
        P.op("act", ACT(T_[:, 0:nb * 128], Sboth, AF.Exp), reads=(tS if nb > 4 else tS[0:1]), writes=[t_T])
        P.op("dve", TT(P_[:, 0:nb * 128], T_[:, 0:nb * 128], tab[hh % 2][:, base * 128:(base + nb) * 128], ALU.mult),
             reads=[t_T, t_tab[hh % 2]], writes=[t_P])

    load_tab(0)
    emit_S(0)
    if NI > 1:
        emit_S(1)
    emit_exp_mul(0)
    for n, (hh, i) in enumerate(its):
        if i == 2:
            load_tab(hh + 1)
        plan = NA_PLAN[i]
        nb = len(plan)
        Ob, t_Ob = ps[4 + n % 2], tps[4 + n % 2]
        P_, t_P = Pb[n % 3], t_Pb[n % 3]
        for bi, (kb, cls) in enumerate(plan):
            P.op("pe", MM(Ob[:, 0:65], P_[:, bi * 128:(bi + 1) * 128], V[:, kb, hh, :], start=(bi == 0), stop=(bi == nb - 1)),
                 reads=[t_P, t_V[kb]], writes=[t_Ob])
        if n + 2 < NI:
            emit_S(n + 2)
        if n + 1 < NI:
            emit_exp_mul(n + 1)
        r_, t_r = rr[n % 2], t_rr[n % 2]
        P.op("dve", RECIP(r_[:, 0:1], Ob[:, 64:65]), reads=[t_Ob], writes=[t_r])
        P.op("dve", STT(Y[:, i, hh * 64:(hh + 1) * 64], Ob[:, 0:64], r_[:, 0:1], G[:, i, hh * 64:(hh + 1) * 64], ALU.mult, ALU.mult),
             reads=[t_Ob, t_r, t_G[i]], writes=[t_Y[i]])
    yst = [A.alloc([128, 512], BF16) for _ in range(2)]
    t_yst = [T(), T()]
    blk = 0
    for qb in range(4):
        for j4 in range(4):
            for j in range(4):
                tt = qb * 4 + j
                P.op("pe", TR(c.psb[:, j * 128:(j + 1) * 128], Y[:, tt, j4 * 128:(j4 + 1) * 128], c.ident),
                     reads=[t_Y[tt], c.t_ident], writes=[c.tpsb[0]])
            ys, t_ys = yst[blk % 2], t_yst[blk % 2]
            blk += 1
            P.op("act", ACT(ys, c.psb[:, 0:512], AF.Copy), reads=[c.tpsb[0]], writes=[t_ys])
            P.op("sp", DMA(c.yT[l][1536 + j4 * 128:1536 + (j4 + 1) * 128, qb * 512:(qb + 1) * 512], ys), reads=[t_ys])
    A.release(m)


def phase_ssd(c, l):
    STOP = 9
    P, A = c.P, c.A
    m = A.mark()
    ps, tps = c.ps, c.tps
    nc = c.nc
    Wdt = A.alloc([128, 8, 32], BF16)
    tWdt = [T()]
    load_w(c, l, Wdt, C_DT, 32, tWdt)
    t_sm = T()
    dtb = A.alloc([128, 32], F32)
    alog = A.alloc([128, 32], F32)
    dsk = A.alloc([128, 32], F32)
    P.op("sp", DMA(dtb, c.dt_bias[l:l + 1, :].partition_broadcast(128)), writes=[t_sm])
    t_al = T()
    P.op("sp", DMA(alog, c.a_log[l:l + 1, :].partition_broadcast(128)), writes=[t_al])
    t_dsk = T()
    P.op("sp", DMA(dsk, c.d_skip[l:l + 1, :].partition_broadcast(128)), writes=[t_dsk])
    dsum = A.alloc([128, 16], F32)
    P.op("dve", TT(dsum, dsk[:, 0:16], dsk[:, 16:32], ALU.add), reads=[t_dsk], writes=[t_dsk])
    P.op("act", ACT(alog, alog, AF.Exp), reads=[t_al], writes=[t_al])
    P.op("dve", TS(alog, alog, -1.0, ALU.mult), reads=[t_al], writes=[t_al])
    snw = A.alloc([128, 1024], F32)
    t_snw = T()
    P.op("sp", DMA(snw, c.ssd_norm_w[l:l + 1, :].partition_broadcast(128)), writes=[t_snw])
    cbb = A.alloc([128, 1536], F32)
    t_cbb = T()
    P.op("sp", DMA(cbb, c.conv_b[l:l + 1, :].partition_broadcast(128)), writes=[t_cbb])
    cwT = A.alloc([128, 12, 6], F32)
    t_cwT = T()

    def a3(shape=(128, NT, 32)):
        return A.alloc(list(shape), F32)
    dt = a3()
    dta = a3()
    cs = a3()
    dcb = a3()
    dout = a3()
    dend = a3()
    cXd = a3()
    m_tmp = A.mark()
    v = a3()
    tmp3 = a3()
    cw6 = A.alloc([128, 1536], F32)
    t_cw6 = T()
    P.op("sp", DMA(cw6[0:5, :], c.conv_w[l, :, :]), writes=[t_cw6])
    P.op("sp", DMA(cw6[5:6, :], c.conv_b[l:l + 1, :]), writes=[t_cw6])
    for j in range(12):
        P.op("pe", TR(ps[6][:, j * 6:(j + 1) * 6], cw6[0:6, j * 128:(j + 1) * 128], c.identf[0:6, 0:6]),
             reads=[t_cw6, c.t_cst], writes=[tps[6]])
    P.op("dve", CP(cwT, ps[6][:, 0:72].rearrange("p (j k) -> p j k", k=6)), reads=[tps[6]], writes=[t_cwT])
    t_dt, t_dta, t_cs, t_dcb, t_dout, t_dend, t_cXd, t_v, t_tmp3 = [T() for _ in range(9)]
    if STOP <= 0.1:
        A.release(m)
        return
    p0 = ps[0].rearrange("p (t h) -> p t h", h=32)
    for tt in range(NT):
        for k in range(8):
            P.op("pe", MM(ps[0][:, tt * 32:(tt + 1) * 32], c.hdnT[:, k, tt * 128:(tt + 1) * 128], Wdt[:, k, :],
                          start=(k == 0), stop=(k == 7), skip=True),
                 reads=[c.t_hdnT[tt], tWdt[0]], writes=[tps[0]])
    P.op("dve", TT(v, p0, bc(dtb.unsqueeze(1), [128, NT, 32]), ALU.add), reads=[tps[0], t_sm], writes=[t_v])
    P.op("act", ACT(tmp3, v, AF.Abs), reads=[t_v], writes=[t_tmp3])
    P.op("act", ACT(tmp3, tmp3, AF.Exp, scale=-1.0), reads=[t_tmp3], writes=[t_tmp3])
    P.op("act", ACT(tmp3, tmp3, AF.Ln, bias=c.onec, scale=1.0), reads=[t_tmp3, c.t_eps], writes=[t_tmp3])
    P.op("dve", STT(dt, v, 0.0, tmp3, ALU.max, ALU.add), reads=[t_v, t_tmp3], writes=[t_dt])
    P.op("dve", TT(dta, dt, bc(alog.unsqueeze(1), [128, NT, 32]), ALU.mult), reads=[t_dt, t_al], writes=[t_dta])
    if STOP <= 0.2:
        A.release(m)
        return
    for tt in range(NT):
        P.op("pe", MM(ps[1][:, tt * 32:tt * 32 + 16], c.U, dta[:, tt, 0:16], skip=True), reads=[t_dta, c.t_cst], writes=[tps[1]])
        P.op("pe", MM(ps[1][:, tt * 32 + 16:tt * 32 + 32], c.Ur, dta[:, tt, 16:32], skip=True), reads=[t_dta, c.t_cst], writes=[tps[1]])
    P.op("act", ACT(cs, ps[1].rearrange("p (t h) -> p t h", h=32), AF.Copy), reads=[tps[1]], writes=[t_cs])
    if STOP <= 0.3:
        A.release(m)
        return
    for tt in range(NT):
        P.op("pe", MM(ps[2][:, tt * 32:(tt + 1) * 32], c.onesf, dta[:, tt, :], skip=True), reads=[t_dta, c.t_cst], writes=[tps[2]])
    p2 = ps[2].rearrange("p (t h) -> p t h", h=32)
    P.op("act", ACT(tmp3, p2, AF.Copy), reads=[tps[2]], writes=[t_tmp3])
    P.op("act", ACT(dcb, tmp3, AF.Exp), reads=[t_tmp3], writes=[t_dcb])
    if STOP <= 0.31:
        A.release(m)
        return
    P.op("act", ACT(dout, cs, AF.Exp), reads=[t_cs], writes=[t_dout])
    if STOP <= 0.32:
        A.release(m)
        return
    P.op("dve", TT(dend, tmp3, cs, ALU.subtract), reads=[t_tmp3, t_cs], writes=[t_dend])
    if STOP <= 0.33:
        A.release(m)
        return
    P.op("act", ACT(dend, dend, AF.Exp), reads=[t_dend], writes=[t_dend])
    if STOP <= 0.34:
        A.release(m)
        return
    P.op("dve", TT(cXd, dt, dend, ALU.mult), reads=[t_dt, t_dend], writes=[t_cXd])
    if STOP <= 0.4:
        A.release(m)
        return
    csT = A.alloc([128, NT, 128], F32)
    t_csT = T()
    for q in range(4):
        for j in range(4):
            tt = q * 4 + j
            P.op("pe", TR(ps[3][0:32, j * 128:(j + 1) * 128], cs[:, tt, :], c.identf), reads=[t_cs, c.t_cst], writes=[tps[3]])
        P.op("act", ACT(csT[0:32, q * 4:(q + 1) * 4, :], ps[3][0:32, :].rearrange("p (j t) -> p j t", t=128), AF.Copy),
             reads=[tps[3]], writes=[t_csT])
    if STOP <= 0.5:
        A.release(m)
        return
    t_csd = T()
    P.op("sp", DMA(c.csT_d.rearrange("t h l -> h t l"), csT[0:32, :, :]), reads=[t_csT], writes=[t_csd])
    csT_v = c.csT_d.rearrange("t (d h) l -> t d h l", d=2)
    P.barrier()
    A.release(m_tmp)
    if STOP <= 1:
        A.release(m)
        return

    for g in range(2):
        mg = A.mark()
        Wz = c.Wz[g]
        tWz = pf_ssd(c, l)[g]
        t_Wzq = c.t_Wq[3 - g]
        xs = A.alloc([128, NT, 512], F32)
        t_xs = [T() for _ in range(NT)]
        Btok = A.alloc([128, NT, 128], BF16)
        t_Btok = [T() for _ in range(NT)]
        BT = A.alloc([128, L], BF16)
        t_BT = [T() for _ in range(4)]
        CT = A.alloc([128, L], BF16)
        t_CT = [T() for _ in range(4)]
        m_conv = A.mark()
        pre = [A.alloc([128, L + 4], BF16) for _ in range(2)]
        t_pre = [[T() for _ in range(4)] for _ in range(2)]
        t_halo = [T(), T()]
        for b in range(2):
            P.op("pool", MEMSET(pre[b][:, 0:2], 0.0), writes=[t_halo[b]])
            P.op("pool", MEMSET(pre[b][:, L + 2:L + 4], 0.0), writes=[t_halo[b]])
        Wc = [A.alloc([128, 8, 128], BF16) for _ in range(2)]
        tWc = [[T()], [T()]]
        dg = [A.alloc([128, 5, 128], BF16) for _ in range(2)]
        t_dg = [T(), T()]
        ctmp = [A.alloc([128, 512], F32) for _ in range(2)]
        t_ctmp = [T(), T()]
        chunks = [("B", 8 + g), ("C", 10 + g)] + [("x%d" % j, 4 * g + j) for j in range(4)]
        pbank = 0
        for n, (kind, ci) in enumerate(chunks):
            b = n % 2
            load_w(c, l, Wc[b], C_XBC + ci * 128, 128, tWc[b])
            P.op("dve", TT(dg[b], bc(c.identf.unsqueeze(1), [128, 5, 128]), bc(cwT[:, ci, 0:5].unsqueeze(2), [128, 5, 128]), ALU.mult),
                 reads=[c.t_cst, t_cwT], writes=[t_dg[b]])
            for tb in range(4):
                bank, tbk = ps[pbank % 4], tps[pbank % 4]
                pbank += 1
                for k in range(8):
                    P.op("pe", MM(bank, Wc[b][:, k, :], c.hdnT[:, k, tb * 512:(tb + 1) * 512], start=(k == 0), stop=(k == 7)),
                         reads=[tWc[b][0]] + c.t_hdnT[tb * 4:(tb + 1) * 4], writes=[tbk])
                P.op("act", ACT(pre[b][:, 2 + tb * 512:2 + (tb + 1) * 512], bank, AF.Copy), reads=[tbk], writes=[t_pre[b][tb]])
            allpre = t_pre[b] + [t_halo[b]]
            if kind in ("B", "C"):
                dstT, t_dstT = (BT, t_BT) if kind == "B" else (CT, t_CT)
                for tb in range(4):
                    bank, tbk = ps[pbank % 4], tps[pbank % 4]
                    pbank += 1
                    for k in range(5):
                        P.op("pe", MM(bank, dg[b][:, k, :], pre[b][:, tb * 512 + k:tb * 512 + k + 512], start=(k == 0), stop=(k == 4)),
                             reads=[t_dg[b]] + allpre, writes=[tbk])
                    P.op("act", ACT(dstT[:, tb * 512:(tb + 1) * 512], bank, AF.Silu, bias=cwT[:, ci, 5:6], scale=1.0),
                         reads=[tbk, t_cwT], writes=[t_dstT[tb]])
            if kind != "C":
                for q in range(4):
                    bank, tbk = ps[pbank % 4], tps[pbank % 4]
                    pbank += 1
                    for j in range(4):
                        tt = q * 4 + j
                        for k in range(5):
                            P.op("pe", MM(bank[:, j * 128:(j + 1) * 128], pre[b][:, tt * 128 + k:tt * 128 + k + 128], dg[b][:, k, :],
                                          start=(k == 0 and j == 0), stop=(k == 4), skip=True),
                                 reads=[t_dg[b]] + allpre, writes=[tbk])
                    ct, t_ct = ctmp[q % 2], t_ctmp[q % 2]
                    P.op("dve", TT(ct.rearrange("p (j c) -> p j c", c=128), bank.rearrange("p (j c) -> p j c", c=128),
                                   bc(cbb[:, ci * 128:(ci + 1) * 128].unsqueeze(1), [128, 4, 128]), ALU.add),
                         reads=[tbk, t_cbb], writes=[t_ct])
                    if kind == "B":
                        P.op("act", ACT(Btok[:, q * 4:(q + 1) * 4, :], ct.rearrange("p (j c) -> p j c", c=128), AF.Silu),
                             reads=[t_ct], writes=t_Btok[q * 4:(q + 1) * 4])
                    else:
                        jx = int(kind[1])
                        P.op("act", ACT(xs[:, q * 4:(q + 1) * 4, jx * 128:(jx + 1) * 128], ct.rearrange("p (j c) -> p j c", c=128), AF.Silu),
                             reads=[t_ct], writes=t_xs[q * 4:(q + 1) * 4])

        P.barrier()
        A.release(m_conv)
        if STOP <= 2:
            A.release(mg)
            continue
        hb0 = 16 + 8 * g
        hf0 = 8 * g
        m_passA = A.mark()
        Sst = [A.alloc([128, 512], F32) for _ in range(2)]
        t_Sst = [T(), T()]
        for d in range(2):
            P.op("pool", MEMSET(Sst[d], 0.0), writes=[t_Sst[d]])
        stg = [[A.alloc([128, 512], BF16) for _ in range(2)] for _ in range(2)]
        t_stg = [[T(), T()], [T(), T()]]
        Xd = [[A.alloc([128, 512], BF16) for _ in range(2)] for _ in range(2)]
        t_Xd = [[T(), T()], [T(), T()]]
        t_sd = [[T() for _ in range(NT)] for _ in range(2)]
        sdram = [c.sf_d, c.sb_d]
        h0s = [hf0, hb0]

        def bh(ap3, h0):
            return bc(ap3[:, h0:h0 + 8].unsqueeze(2), [128, 8, 64])

        def v8(ap):
            return ap.rearrange("p (h d) -> p h d", d=64)
        for k in range(NT):
            for d in range(2):
                ci_ = k if d == 0 else NT - 1 - k
                last = (ci_ == NT - 1) if d == 0 else (ci_ == 0)
                sg, t_sg = stg[d][k % 2], t_stg[d][k % 2]
                P.op("act", ACT(sg, Sst[d], AF.Copy), reads=[t_Sst[d]], writes=[t_sg])
                P.op("sp", DMA(sdram[d][g, ci_], sg), reads=[t_sg], writes=[t_sd[d][ci_]])
                if not last:
                    xd, t_xd = Xd[d][k % 2], t_Xd[d][k % 2]
                    P.op("pool", TT(v8(xd), v8(xs[:, ci_, :]), bh(cXd[:, ci_, :], h0s[d]), ALU.mult), reads=[t_xs[ci_], t_cXd], writes=[t_xd])
                    bank, tbk = ps[4 + d], tps[4 + d]
                    P.op("pe", MM(bank, Btok[:, ci_, :], xd), reads=[t_Btok[ci_], t_xd], writes=[tbk])
                    P.op("dve", TT(v8(Sst[d]), v8(Sst[d]), bh(dcb[:, ci_, :], h0s[d]), ALU.mult), reads=[t_Sst[d], t_dcb], writes=[t_Sst[d]])
                    P.op("dve", TT(Sst[d], Sst[d], bank, ALU.add), reads=[t_Sst[d], tbk], writes=[t_Sst[d]])
        P.barrier()
        A.release(m_passA)
        if STOP <= 3:
            A.release(mg)
            continue
        R = [A.alloc([128, 2, 8, 128], F32) for _ in range(2)]
        t_R = [T(), T()]
        SFc = [A.alloc([128, 512], BF16) for _ in range(2)]
        t_SFc = [T(), T()]
        SBc = [A.alloc([128, 512], BF16) for _ in range(2)]
        t_SBc = [T(), T()]
        E = A.alloc([128, 2, 8, 128], BF16)
        t_E = T()
        Mt = [A.alloc([128, 2, 8, 128], BF16) for _ in range(2)]
        t_Mt = [T(), T()]
        Gm = [A.alloc([128, 2, 128], BF16) for _ in range(2)]
        t_Gm = [T(), T()]
        Xt = [A.alloc([128, 3, 512], BF16) for _ in range(2)]
        t_Xt = [T(), T()]
        sz = [A.alloc([128, 512], F32) for _ in range(2)]
        t_sz = [T(), T()]
        ya = A.alloc([128, 512], F32)
        yb_ = A.alloc([128, 512], F32)
        yy = A.alloc([128, 512], F32)
        t_ya, t_yb, t_yy = T(), T(), T()
        ssn = A.alloc([128, 2], F32)
        t_ssn = T()
        yo = [A.alloc([128, 512], BF16) for _ in range(2)]
        t_yo = [T(), T()]
        yst = [A.alloc([128, 4, 128], BF16) for _ in range(2)]
        t_yst = [T(), T()]

        def loadR(ci_):
            b = ci_ % 2
            P.op("sp", DMA(R[b].rearrange("p d h l -> p d (h l)"),
                           csT_v[ci_, :, 8 * g:8 * g + 8, :].rearrange("d h l -> d (h l)").partition_broadcast(128)),
                 reads=[t_csd], writes=[t_R[b]])

        def loadS(ci_):
            b = ci_ % 2
            P.op("sp", DMA(SFc[b], c.sf_d[g, ci_]), reads=[t_sd[0][ci_]], writes=[t_SFc[b]])
            P.op("sp", DMA(SBc[b], c.sb_d[g, ci_]), reads=[t_sd[1][ci_]], writes=[t_SBc[b]])

        def iteration(cn, cc_):
            if cn is not None:
                bn = cn % 2
                tokn = slice(cn * 128, (cn + 1) * 128)
                P.op("pe", MM(ps[4][:, 0:128], BT[:, tokn], CT[:, tokn]), reads=[t_BT[cn // 4], t_CT[cn // 4]], writes=[tps[4]])
                for k in range(8):
                    P.op("pe", MM(ps[3], c.hdnT[:, k, tokn], Wz[:, k, :], start=(k == 0), stop=(k == 7)),
                         reads=[c.t_hdnT[cn], tWz[0], t_Wzq], writes=[tps[3]])
                P.op("dve", TT(Gm[bn][:, 0, :], ps[4][:, 0:128], c.U, ALU.mult), reads=[tps[4], c.t_cst], writes=[t_Gm[bn]])
                P.op("dve", TT(Gm[bn][:, 1, :], ps[4][:, 0:128], c.Ur, ALU.mult), reads=[tps[4], c.t_cst], writes=[t_Gm[bn]])
                csv = cs[:, cn, :].rearrange("p (d h) -> p d h", d=2)[:, :, 8 * g:8 * g + 8]
                Dd, t_D = R[bn], t_R[bn]
                P.op("dve", TT(Dd, Dd, bc(csv.unsqueeze(3), [128, 2, 8, 128]), ALU.subtract), reads=[t_cs], writes=[t_D])
                P.op("act", ACT(Dd, Dd, AF.Relu, scale=-1.0), reads=[], writes=[t_D])
                P.op("act", ACT(E, Dd, AF.Exp, scale=-1.0), reads=[t_D], writes=[t_E])
                P.op("pool", TT(v8(Xt[bn][:, 0, :]), v8(xs[:, cn, :]), bh(dt[:, cn, :], hf0), ALU.mult), reads=[t_xs[cn], t_dt], writes=[t_Xt[bn]])
                P.op("pool", TT(v8(Xt[bn][:, 1, :]), v8(xs[:, cn, :]), bh(dt[:, cn, :], hb0), ALU.mult), reads=[t_xs[cn], t_dt], writes=[t_Xt[bn]])
                P.op("pool", TT(v8(Xt[bn][:, 2, :]), v8(xs[:, cn, :]), bh(dsum, 8 * g), ALU.mult), reads=[t_xs[cn], t_dsk], writes=[t_Xt[bn]])
            if cc_ is not None:
                b = cc_ % 2
                tok = slice(cc_ * 128, (cc_ + 1) * 128)
                Yb_, t_Y = (ps[0], tps[0]) if b == 0 else (ps[5], tps[5])
                P.op("pe", MM(Yb_, c.ident, Xt[b][:, 2, :], start=True, stop=False, skip=True), reads=[c.t_ident, t_Xt[b]], writes=[t_Y])
                for h in range(8):
                    for d in range(2):
                        P.op("pe", MM(Yb_[:, h * 64:(h + 1) * 64], Mt[b][:, d, h, :], Xt[b][:, d, h * 64:(h + 1) * 64],
                                      start=False, stop=(d == 1), skip=True),
                             reads=[t_Mt[b], t_Xt[b]], writes=[t_Y])
                P.op("pe", MM(ps[1], CT[:, tok], SFc[b]), reads=[t_CT[cc_ // 4], t_SFc[b]], writes=[tps[1]])
                P.op("pe", MM(ps[2], CT[:, tok], SBc[b]), reads=[t_CT[cc_ // 4], t_SBc[b]], writes=[tps[2]])
                P.op("dve", TT(v8(ya), v8(ps[1]), bh(dout[:, cc_, :], hf0), ALU.mult), reads=[tps[1], t_dout], writes=[t_ya])
                P.op("dve", TT(v8(yb_), v8(ps[2]), bh(dout[:, cc_, :], hb0), ALU.mult), reads=[tps[2], t_dout], writes=[t_yb])
                P.op("pool", TT(ya, ya, yb_, ALU.add), reads=[t_ya, t_yb], writes=[t_ya])
                P.op("dve", TT(yy, Yb_, ya, ALU.add), reads=[t_Y, t_ya], writes=[t_yy])
                P.op("pool", TT(yy, yy, sz[b], ALU.mult), reads=[t_yy, t_sz[b]], writes=[t_yy])
            if cn is not None:
                for d in range(2):
                    P.op("dve", TT(Mt[bn][:, d], E[:, d], bc(Gm[bn][:, d, :].unsqueeze(1), [128, 8, 128]), ALU.mult),
                         reads=[t_E, t_Gm[bn]], writes=[t_Mt[bn]])
                P.op("act", ACT(sz[bn], ps[3], AF.Silu), reads=[tps[3]], writes=[t_sz[bn]])
            if cc_ is not None:
                P.op("act", ACT(ya, yy, AF.Square, accum_out=ssn[:, 0:1]), reads=[t_yy], writes=[t_ya, t_ssn])
                P.op("act", ACT(ssn[:, 0:1], ssn[:, 0:1], AF.Ln, scale=1.0 / 512, bias=c.epsc), reads=[t_ssn, c.t_eps], writes=[t_ssn])
                P.op("act", ACT(ssn[:, 0:1], ssn[:, 0:1], AF.Exp, scale=-0.5), reads=[t_ssn], writes=[t_ssn])
                P.op("dve", STT(yo[b], yy, ssn[:, 0:1], snw[:, g * 512:(g + 1) * 512], ALU.mult, ALU.mult),
                     reads=[t_yy, t_ssn, t_snw], writes=[t_yo[b]])

        def stageC(ci_):
            b = ci_ % 2
            tok = slice(ci_ * 128, (ci_ + 1) * 128)
            for j in range(4):
                P.op("pe", TR(c.psb[:, j * 128:(j + 1) * 128], yo[b][:, j * 128:(j + 1) * 128], c.ident),
                     reads=[t_yo[b], c.t_ident], writes=[c.tpsb[0]])
            P.op("act", ACT(yst[b], c.psb[:, 0:512].rearrange("p (j t) -> p j t", t=128), AF.Copy), reads=[c.tpsb[0]], writes=[t_yst[b]])
            P.op("sp", DMA(c.yT[l][g * 512:(g + 1) * 512, tok].rearrange("(j p) t -> p j t", p=128), yst[b]), reads=[t_yst[b]])

        loadR(0)
        loadS(0)
        loadR(1)
        iteration(0, None)
        for ci_ in range(NT):
            if ci_ + 1 < NT:
                loadS(ci_ + 1)
                if ci_ + 2 < NT:
                    loadR(ci_ + 2)
            iteration(ci_ + 1 if ci_ + 1 < NT else None, ci_)
            if ci_ >= 1:
                stageC(ci_ - 1)
        stageC(NT - 1)
        A.release(mg)
    A.release(m)


def phase_out(c, l, xsrc, xdst, fuse_next=False):
    P, A = c.P, c.A
    m = A.mark()
    if fuse_next:
        nwb = A.alloc([128, DM], F32)
        t_nwb = T()
        P.op("sp", DMA(nwb, c.norm_w[l + 1:l + 2, :].partition_broadcast(128)), writes=[t_nwb])
        sqj = A.alloc([128, DM], F32)
        t_sqj = T()
        ssx = A.alloc([128, NT], F32)
        t_ssx = [T() for _ in range(NT)]
        hb = [A.alloc([128, DM], BF16) for _ in range(2)]
        t_hb = [T(), T()]
    Wo = c.Wo
    tWo = pf_out(c, l, [0, 1, 2, 3])
    yt = [A.alloc([128, 16, 512], BF16) for _ in range(2)]
    t_yt = [T(), T()]
    xt = [A.alloc([128, DM], F32) for _ in range(3)]
    t_xt = [T(), T(), T()]
    ot = [A.alloc([128, DM], F32) for _ in range(2)]
    t_ot = [T(), T()]
    ps, tps = c.ps, c.tps
    fin = []

    def load_y(qb):
        P.op("sp", DMA(yt[qb % 2], c.yT[l][:, qb * 512:(qb + 1) * 512].rearrange("(k p) t -> p k t", p=128)), writes=[t_yt[qb % 2]])

    def load_x(tt):
        P.op("sp", DMA(xt[tt % 3], xsrc[tt * 128:(tt + 1) * 128, :]), writes=[t_xt[tt % 3]])
    load_y(0)
    load_x(0)
    load_x(1)
    nb = 0
    for tt in range(NT):
        b = tt % 2
        qb, j = tt // 4, tt % 4
        tok = slice(tt * 128, (tt + 1) * 128)
        if j == 0 and qb + 1 < 4:
            load_y(qb + 1)
        if tt + 2 < NT:
            load_x(tt + 2)
        for n in range(2):
            bank, tb = ps[nb % 6], tps[nb % 6]
            nb += 1
            for k in range(16):
                P.op("pe", MM(bank, yt[qb % 2][:, k, j * 128:(j + 1) * 128], Wo[:, k, n * 512:(n + 1) * 512], start=(k == 0), stop=(k == 15)),
                     reads=[t_yt[qb % 2], tWo[k // 4], c.t_Wq[k // 4]], writes=[tb])
            P.op("dve", TT(ot[b][:, n * 512:(n + 1) * 512], bank, xt[tt % 3][:, n * 512:(n + 1) * 512], ALU.add),
                 reads=[tb, t_xt[tt % 3]], writes=[t_ot[b]])
        fin.append(P.op("sp", DMA(xdst[tok, :], ot[b]), reads=[t_ot[b]]))
        if fuse_next:
            s1 = ssx[:, tt:tt + 1]
            P.op("act", ACT(sqj, ot[b], AF.Square, accum_out=s1), reads=[t_ot[b]], writes=[t_sqj, t_ssx[tt]])
            P.op("act", ACT(s1, s1, AF.Ln, scale=1.0 / DM, bias=c.epsc), reads=[t_ssx[tt], c.t_eps], writes=[t_ssx[tt]])
            P.op("act", ACT(s1, s1, AF.Exp, scale=-0.5), reads=[t_ssx[tt]], writes=[t_ssx[tt]])
            P.op("dve", STT(hb[b], ot[b], s1, nwb, ALU.mult, ALU.mult), reads=[t_ot[b], t_ssx[tt], t_nwb], writes=[t_hb[b]])
            for k in range(8):
                P.op("pe", TR(c.psb[:, k * 128:(k + 1) * 128], hb[b][:, k * 128:(k + 1) * 128], c.ident),
                     reads=[t_hb[b], c.t_ident], writes=[c.tpsb[0]])
            P.op("act", ACT(c.hdnT[:, :, tok], c.psb[:, :].rearrange("p (k t) -> p k t", t=128), AF.Copy),
                 reads=[c.tpsb[0]], writes=[c.t_hdnT[tt]])
    A.release(m)
    return fin


def _host_consts():
    inv_freq = (10000.0 ** (-(np.arange(0, 64, 2, dtype=np.float32)) / np.float32(64))).astype(np.float32)
    ang = (np.arange(L, dtype=np.float32)[:, None] * inv_freq[None, :]).astype(np.float32)
    cos, sin = np.cos(ang).astype(np.float32), np.sin(ang).astype(np.float32)
    cs = np.concatenate([cos, cos, -sin, sin], axis=1).astype(np.float32)
    k = np.arange(128)
    ident = np.eye(128, dtype=np.float32)
    U = (k[:, None] <= k[None, :]).astype(np.float32)
    Ur = (k[:, None] >= k[None, :]).astype(np.float32)
    ones = np.ones((128, 128), np.float32)
    consts = np.concatenate([ident, U, Ur, ones], axis=1)
    return np.ascontiguousarray(cs), np.ascontiguousarray(consts)


_CACHE = {}


def make_in_maps(inputs, n_cores=8):
    cs, consts = _host_consts()
    f = lambda a: np.ascontiguousarray(np.asarray(a, dtype=np.float32))
    shared = {
        "norm_w": f(inputs["norm_w"]), "w_in": f(inputs["w_in"]), "conv_w": f(inputs["conv_w"]),
        "conv_b": f(inputs["conv_b"]), "a_log": f(inputs["a_log"]).reshape(2, 32),
        "dt_bias": f(inputs["dt_bias"]).reshape(2, 32), "d_skip": f(inputs["d_skip"]).reshape(2, 32),
        "ssd_norm_w": f(inputs["ssd_norm_w"]), "diff_qk_norm": f(inputs["diff_qk_norm"]).reshape(2, 128),
        "diff_lambda": f(inputs["diff_lambda"]).reshape(2, 256), "diff_subln": f(inputs["diff_subln"]),
        "na_qk_norm": f(inputs["na_qk_norm"]).reshape(2, 128), "na_tab": _na_tables(f(inputs["na_rpb"])),
        "w_out": f(inputs["w_out"]), "cs_tab": cs, "consts": consts,
    }
    x = f(inputs["x"])
    return [dict(shared, x=x[b]) for b in range(n_cores)]


def kernel(**inputs):
    if "nc" not in _CACHE:
        _CACHE["nc"] = build()[0]
    nc = _CACHE["nc"]
    in_maps = make_in_maps(inputs)
    res = run_bass_kernel_spmd(nc, in_maps, core_ids=list(range(8)))
    return np.stack([np.asarray(r["out"], dtype=np.float32) for r in res.results], axis=0)
```

```python
import contextlib
import math
import numpy as np
import concourse.bass as bass
import concourse.mybir as mybir
from concourse.bass_utils import run_bass_kernel_spmd

F32 = mybir.dt.float32
BF16 = mybir.dt.bfloat16
AF = mybir.ActivationFunctionType
ALU = mybir.AluOpType
AX = mybir.AxisListType

L = 2048
DM = 1024
NT = 16
INW = 6688
EPS = 1e-6
NEG = -30000.0
C_Z, C_XBC, C_DT, C_DIFF, C_NA = 0, 1024, 2560, 2592, 4640


class T:
    __slots__ = ("w", "r", "excl")

    def __init__(self, excl=False):
        self.w = None
        self.r = []
        self.excl = excl


class Op:
    __slots__ = ("eng", "fn", "deps", "idx", "sig", "isdma", "semid", "semval", "prev_same_sem")


class Prog:
    COMPUTE = ("pe", "act", "dve", "pool")
    DMAQ = ("sp", "actq", "poolq")
    STREAM = {"pe": "pe", "act": "act", "dve": "dve", "pool": "pool", "sp": "sp", "actq": "act", "poolq": "pool"}
    STREAMS = ("pe", "act", "dve", "pool", "sp")

    def __init__(self, nc, n_dma_sems=12):
        self.nc = nc
        self.ops = []
        self.n_dma_sems = n_dma_sems
        self.last = {s: None for s in self.STREAMS}
        self.recent_dma = {q: [] for q in self.DMAQ}
        self.frontier = []
        self.synced = {s: True for s in self.STREAMS}

    def barrier(self):
        fr = [o for o in self.last.values() if o is not None]
        for q in self.DMAQ:
            fr.extend(self.recent_dma[q])
        self.frontier = fr
        self.synced = {s: False for s in self.STREAMS}

    def op(self, eng, fn, reads=(), writes=()):
        o = Op()
        o.eng = eng
        o.fn = fn
        o.isdma = eng in self.DMAQ
        o.idx = len(self.ops)
        o.prev_same_sem = None
        deps = {}
        if any(t.excl for t in reads):
            writes = list(writes) + [t for t in reads if t.excl and t not in writes]
            reads = [t for t in reads if not t.excl]
        for t in reads:
            if t.w is not None:
                deps[t.w.idx] = ("raw", t.w)
        for t in writes:
            if t.w is not None and t.w.idx not in deps:
                deps[t.w.idx] = ("waw", t.w)
            for r in t.r:
                if r.idx not in deps:
                    deps[r.idx] = ("war", r)
        st = self.STREAM[eng]
        if not self.synced[st]:
            for p in self.frontier:
                if p.idx not in deps:
                    deps[p.idx] = ("bar", p)
            self.synced[st] = True
        for t in writes:
            t.w = o
            t.r = []
        for t in reads:
            if t.w is not o:
                t.r.append(o)
        o.deps = deps
        o.sig = False
        self.ops.append(o)
        self.last[st] = o
        if o.isdma:
            lst = self.recent_dma[eng]
            lst.append(o)
            if len(lst) > self.n_dma_sems:
                lst.pop(0)
        return o

    def emit(self, final_waits=()):
        nc = self.nc
        streams = {s: [] for s in self.STREAMS}
        for o in self.ops:
            streams[self.STREAM[o.eng]].append(o)
        pos = {}
        for s, lst in streams.items():
            for i, o in enumerate(lst):
                pos[o.idx] = i
        need = {}
        for o in self.ops:
            lst = []
            so = self.STREAM[o.eng]
            for (kind, p) in o.deps.values():
                sp_ = self.STREAM[p.eng]
                if p.isdma or o.isdma:
                    lst.append(p)
                elif sp_ != so:
                    lst.append(p)
                else:
                    if so == "pe":
                        continue
                    lst.append(p)
            need[o.idx] = lst
            for p in lst:
                p.sig = True
        for o in final_waits:
            o.sig = True
        stack = contextlib.ExitStack()
        esem = {e: stack.enter_context(nc.semaphore("s_" + e)) for e in self.COMPUTE}
        ecount = {e: 0 for e in self.COMPUTE}
        dsems = {q: [stack.enter_context(nc.semaphore("d_%s_%d" % (q, i))) for i in range(self.n_dma_sems)]
                 for q in self.DMAQ}
        duse = {q: [0] * self.n_dma_sems for q in self.DMAQ}
        dlast = {q: [None] * self.n_dma_sems for q in self.DMAQ}
        dnext = {q: 0 for q in self.DMAQ}
        for s in self.STREAMS:
            for o in streams[s]:
                if o.isdma:
                    q = o.eng
                    j = dnext[q]
                    dnext[q] = (j + 1) % self.n_dma_sems
                    duse[q][j] += 1
                    o.semid = (q, j)
                    o.semval = 16 * duse[q][j]
                    o.prev_same_sem = dlast[q][j]
                    dlast[q][j] = o
                elif o.sig:
                    ecount[o.eng] += 1
                    o.semid = o.eng
                    o.semval = ecount[o.eng]
        self.n_waits = 0

        def sem_of(p):
            if p.isdma:
                return dsems[p.semid[0]][p.semid[1]]
            return esem[p.semid]

        def emit_stream(s, engobj):
            known = {}
            for o in streams[s]:
                waits = {}
                cand = list(need[o.idx])
                if o.isdma and o.prev_same_sem is not None:
                    cand.append(o.prev_same_sem)
                for p in cand:
                    k = p.semid
                    if known.get(k, 0) >= p.semval:
                        continue
                    if waits.get(k, (0, None))[0] < p.semval:
                        waits[k] = (p.semval, p)
                for k, (v, p) in waits.items():
                    engobj.wait_ge(sem_of(p), v)
                    known[k] = v
                    self.n_waits += 1
                ins = o.fn(engobj)
                if o.isdma:
                    ins.then_inc(dsems[o.semid[0]][o.semid[1]], 16)
                elif o.sig:
                    ins.then_inc(esem[o.semid], 1)
            if s == "sp":
                for o in final_waits:
                    engobj.wait_ge(sem_of(o), o.semval)

        with nc.Block() as block:
            @block.tensor
            def _(e):
                emit_stream("pe", e)

            @block.scalar
            def _(e):
                emit_stream("act", e)

            @block.vector
            def _(e):
                emit_stream("dve", e)

            @block.gpsimd
            def _(e):
                emit_stream("pool", e)

            @block.sync
            def _(e):
                emit_stream("sp", e)
        stack.close()


def ACT(out, in_, func, **kw):
    return lambda e: e.activation(out=out, in_=in_, func=func, **kw)


def TT(out, in0, in1, op):
    return lambda e: e.tensor_tensor(out=out, in0=in0, in1=in1, op=op)


def TS(out, in0, s1, op0, s2=None, op1=None):
    if op1 is None:
        return lambda e: e.tensor_scalar(out=out, in0=in0, scalar1=s1, scalar2=None, op0=op0)
    return lambda e: e.tensor_scalar(out=out, in0=in0, scalar1=s1, scalar2=s2, op0=op0, op1=op1)


def STT(out, in0, scalar, in1, op0, op1):
    return lambda e: e.scalar_tensor_tensor(out=out, in0=in0, scalar=scalar, in1=in1, op0=op0, op1=op1)


def CP(out, in_):
    return lambda e: e.tensor_copy(out=out, in_=in_)


def MM(out, lhsT, rhs, start=True, stop=True, skip=False):
    if skip:
        return lambda e: e.matmul(out, lhsT, rhs, start=start, stop=stop, skip_group_check=True)
    return lambda e: e.matmul(out, lhsT, rhs, start=start, stop=stop)


def TR(out, in_, ident):
    return lambda e: e.transpose(out, in_, ident)


def DMA(out, in_):
    return lambda e: e.dma_start(out=out, in_=in_)


def RED(out, in_, op=None):
    return lambda e: e.tensor_reduce(out=out, in_=in_, axis=AX.X, op=(op or ALU.add))


def RECIP(out, in_):
    return lambda e: e.reciprocal(out=out, in_=in_)


def MEMSET(ap, v):
    return lambda e: e.memset(ap, v)


class Arena:
    def __init__(self, nc, words):
        self.t = nc.alloc_sbuf_tensor("arena", [128, words], F32)
        self.words = words
        self.off = 0
        self.peak = 0

    def alloc(self, shape, dt):
        n = int(np.prod(shape[1:]))
        nw = n if dt == F32 else (n + 1) // 2
        nw = (nw + 7) // 8 * 8
        assert self.off + nw <= self.words, ("SBUF arena overflow", self.off, nw, self.words)
        v = self.t[:, self.off:self.off + nw]
        self.off += nw
        self.peak = max(self.peak, self.off)
        if dt != F32:
            v = v.bitcast(dt)
        v = v[:, 0:n]
        if len(shape) == 3:
            v = v.rearrange("p (a b) -> p a b", b=shape[2])
        elif len(shape) == 4:
            v = v.rearrange("p (a b c) -> p a b c", b=shape[2], c=shape[3])
        return v

    def mark(self):
        return self.off

    def release(self, m):
        self.off = m


def bc(ap, shape):
    return ap.to_broadcast(shape)


class Ctx:
    pass


def build(n_layers=2, phases=("diff", "na", "ssd"), dbg=False):
    nc = bass.Bass("TRN2", target_bir_lowering=False)
    c = Ctx()
    c.nc = nc
    c.dbg = dbg

    def din(name, shape, dt=F32):
        return nc.dram_tensor(name, shape, dt, kind="ExternalInput").ap()

    c.x_in = din("x", [L, DM])
    c.norm_w = din("norm_w", [2, DM])
    c.w_in = din("w_in", [2, DM, INW])
    c.conv_w = din("conv_w", [2, 5, 1536])
    c.conv_b = din("conv_b", [2, 1536])
    c.a_log = din("a_log", [2, 32])
    c.dt_bias = din("dt_bias", [2, 32])
    c.d_skip = din("d_skip", [2, 32])
    c.ssd_norm_w = din("ssd_norm_w", [2, 1024])
    c.diff_qk_norm = din("diff_qk_norm", [2, 128])
    c.diff_lambda = din("diff_lambda", [2, 256])
    c.diff_subln = din("diff_subln", [2, 128])
    c.na_qk_norm = din("na_qk_norm", [2, 128])
    c.na_tab = din("na_tab", [2, 8, 128, NA_NCLS * 128])
    c.w_out = din("w_out", [2, 2048, DM])
    c.cs_tab = din("cs_tab", [L, 128])
    c.consts = din("consts", [128, 4 * 128])
    c.out = nc.dram_tensor("out", [L, DM], F32, kind="ExternalOutput").ap()
    kind_scr = "ExternalOutput" if dbg else "Internal"
    c.x1 = nc.dram_tensor("x1", [L, DM], F32, kind=kind_scr).ap()
    c.yT = [nc.dram_tensor("yT%d" % i, [2048, L], BF16, kind=kind_scr).ap() for i in range(n_layers)]
    c.csT_d = nc.dram_tensor("csT_d", [NT, 32, 128], F32, kind="Internal").ap()
    c.sb_d = nc.dram_tensor("sb_d", [2, NT, 128, 512], BF16, kind="Internal").ap()
    c.sf_d = nc.dram_tensor("sf_d", [2, NT, 128, 512], BF16, kind="Internal").ap()

    c.dumps = []

    def dump(name, ap, shape, dt, tiles):
        if not dbg:
            return
        d = nc.dram_tensor("dbg_" + name, list(shape), dt, kind="ExternalOutput").ap()
        c.dumps.append(c.P.op("sp", DMA(d, ap), reads=tiles))
    c.dump = dump
    A = Arena(nc, 51000)
    c.A = A
    P = Prog(nc)
    c.P = P
    c.psall = nc.alloc_psum_tensor("psall", [128, 8 * 512], F32)[:, :]
    c.ps = [c.psall[:, i * 512:(i + 1) * 512] for i in range(7)]
    c.tps = [T(excl=True) for _ in range(7)]
    c.psb = c.psall[:, 7 * 512:8 * 512].bitcast(BF16)
    _tb = T(excl=True)
    c.tpsb = [_tb, _tb]

    c.cst = A.alloc([128, 512], F32)
    c.t_cst = T()
    P.op("sp", DMA(c.cst, c.consts), writes=[c.t_cst])
    c.identf = c.cst[:, 0:128]
    c.U = c.cst[:, 128:256]
    c.Ur = c.cst[:, 256:384]
    c.onesf = c.cst[:, 384:512]
    c.ident = A.alloc([128, 128], BF16)
    c.t_ident = T()
    P.op("dve", CP(c.ident, c.identf), reads=[c.t_cst], writes=[c.t_ident])
    c.epsc = A.alloc([128, 1], F32)
    c.t_eps = T()
    P.op("pool", MEMSET(c.epsc, EPS), writes=[c.t_eps])
    c.onec = A.alloc([128, 1], F32)
    P.op("pool", MEMSET(c.onec, 1.0), writes=[c.t_eps])
    c.hdnT = A.alloc([128, 8, L], BF16)
    c.t_hdnT = [T() for _ in range(NT)]
    c.Wbuf = A.alloc([128, 16384], BF16)
    c.t_Wq = [T() for _ in range(4)]
    c.W8 = c.Wbuf.rearrange("p (k c) -> p k c", c=2048)
    c.Wo = c.Wbuf.rearrange("p (k c) -> p k c", c=1024)
    c.Wz = [c.Wbuf[:, 3 * 4096:4 * 4096].rearrange("p (k c) -> p k c", c=512),
            c.Wbuf[:, 2 * 4096:3 * 4096].rearrange("p (k c) -> p k c", c=512)]
    c.pf = {}
    c.phases = phases

    finals = []
    for l in range(n_layers):
        xsrc = c.x_in if l == 0 else c.x1
        xdst = c.x1 if l < n_layers - 1 or dbg and n_layers == 1 else c.out
        if l == n_layers - 1:
            xdst = c.out
        P.barrier()
        if "diff" in phases:
            pf_qkvg(c, l, "diff")
        if l == 0:
            phase_hdn(c, l, xsrc)
        if "diff" in phases:
            P.barrier()
            phase_diff(c, l)
        if "na" in phases:
            P.barrier()
            phase_na(c, l)
        if "ssd" in phases:
            P.barrier()
            phase_ssd(c, l)
        P.barrier()
        fin = phase_out(c, l, xsrc, xdst, fuse_next=(l + 1 < n_layers))
        if l == n_layers - 1:
            finals = fin
    P.barrier()
    finals = list(finals)
    if dbg:
        finals += [o for o in P.ops if o.isdma][-36:] + c.dumps
    P.emit(final_waits=finals)
    c.peak = A.peak
    return nc, c


def phase_hdn(c, l, xsrc):
    P, A = c.P, c.A
    m = A.mark()
    nwb = A.alloc([128, DM], F32)
    t_nwb = T()
    P.op("sp", DMA(nwb, c.norm_w[l:l + 1, :].partition_broadcast(128)), writes=[t_nwb])
    NX = 4
    xt = [A.alloc([128, DM], F32) for _ in range(NX)]
    t_xt = [T() for _ in range(NX)]
    sq = A.alloc([128, DM], F32)
    t_sq = T()
    ss = A.alloc([128, NT], F32)
    t_ss = [T() for _ in range(NT)]
    hb = [A.alloc([128, DM], BF16) for _ in range(2)]
    t_hb = [T(), T()]

    def load_x(tt):
        P.op("sp", DMA(xt[tt % NX], xsrc[tt * 128:(tt + 1) * 128, :]), writes=[t_xt[tt % NX]])
    for tt in range(min(NX - 1, NT)):
        load_x(tt)
    for tt in range(NT):
        b = tt % 2
        xb, t_xb = xt[tt % NX], t_xt[tt % NX]
        s1 = ss[:, tt:tt + 1]
        if tt + NX - 1 < NT:
            load_x(tt + NX - 1)
        P.op("act", ACT(sq, xb, AF.Square, accum_out=s1), reads=[t_xb], writes=[t_sq, t_ss[tt]])
        P.op("act", ACT(s1, s1, AF.Ln, scale=1.0 / DM, bias=c.epsc), reads=[t_ss[tt], c.t_eps], writes=[t_ss[tt]])
        P.op("act", ACT(s1, s1, AF.Exp, scale=-0.5), reads=[t_ss[tt]], writes=[t_ss[tt]])
        P.op("dve", STT(hb[b], xb, s1, nwb, ALU.mult, ALU.mult), reads=[t_xb, t_ss[tt], t_nwb], writes=[t_hb[b]])
        for k in range(8):
            P.op("pe", TR(c.psb[:, k * 128:(k + 1) * 128], hb[b][:, k * 128:(k + 1) * 128], c.ident),
                 reads=[t_hb[b], c.t_ident], writes=[c.tpsb[k // 4]])
        P.op("act", ACT(c.hdnT[:, :, tt * 128:(tt + 1) * 128], c.psb[:, :].rearrange("p (k t) -> p k t", t=128), AF.Copy),
             reads=[c.tpsb[0], c.tpsb[1]], writes=[c.t_hdnT[tt]])
    A.release(m)


def load_w(c, l, dst, col0, ncols, tiles, step=512, extra=()):
    P = c.P
    j = 0
    for c0 in range(0, ncols, step):
        n = min(step, ncols - c0)
        src = c.w_in[l, :, col0 + c0:col0 + c0 + n].rearrange("(k p) c -> p k c", p=128)
        P.op("poolq", DMA(dst[:, :, c0:c0 + n], src), writes=[tiles[j]] + list(extra))
        j += 1


def pf_qkvg(c, l, which):
    key = (which, l)
    if key not in c.pf:
        tW = [T() for _ in range(4)]
        load_w(c, l, c.W8, C_DIFF if which == "diff" else C_NA, 2048, tW, extra=c.t_Wq)
        c.pf[key] = tW
    return c.pf[key]


def pf_ssd(c, l):
    key = ("ssd", l)
    if key not in c.pf:
        tWz = [[T()], [T()]]
        load_w(c, l, c.Wz[0], C_Z, 512, tWz[0], extra=[c.t_Wq[3]])
        load_w(c, l, c.Wz[1], C_Z + 512, 512, tWz[1], extra=[c.t_Wq[2]])
        c.pf[key] = tWz
    return c.pf[key]


def pf_out(c, l, quarters):
    key = ("out", l)
    if key not in c.pf:
        c.pf[key] = [None] * 4
    tWo = c.pf[key]
    for j in quarters:
        if tWo[j] is None:
            tWo[j] = T()
            src = c.w_out[l, j * 512:(j + 1) * 512, :].rearrange("(k p) c -> p k c", p=128)
            c.P.op("poolq", DMA(c.Wo[:, j * 4:(j + 1) * 4, :], src), writes=[tWo[j], c.t_Wq[j]])
    return tWo


def qk_norm_rope(c, src_ps, t_src, sq, t_sq, ss8, t_ss8, tbuf, t_tb, ubuf, t_ub, TC, TS_, t_tab, tt, dst, t_dst, rope):
    P = c.P
    P.op("act", ACT(sq, src_ps, AF.Square), reads=[t_src], writes=[t_sq])
    P.op("dve", RED(ss8, sq.rearrange("p (g d) -> p g d", d=64)), reads=[t_sq], writes=[t_ss8])
    P.op("act", ACT(ss8, ss8, AF.Ln, scale=1.0 / 64, bias=c.epsc), reads=[t_ss8, c.t_eps], writes=[t_ss8])
    P.op("act", ACT(ss8, ss8, AF.Exp, scale=-0.5), reads=[t_ss8], writes=[t_ss8])
    t3 = tbuf.rearrange("p (g d) -> p g d", d=64)
    P.op("dve", TT(t3, src_ps.rearrange("p (g d) -> p g d", d=64), bc(ss8.unsqueeze(2), [128, 8, 64]), ALU.mult),
         reads=[t_src, t_ss8], writes=[t_tb])
    if rope:
        u3 = ubuf.rearrange("p (g d) -> p g d", d=64)
        P.op("pool", TT(u3[:, :, 0:32], t3[:, :, 32:64], bc(TS_[:, tt, 0:32].unsqueeze(1), [128, 8, 32]), ALU.mult),
             reads=[t_tb, t_tab], writes=[t_ub])
        P.op("pool", TT(u3[:, :, 32:64], t3[:, :, 0:32], bc(TS_[:, tt, 32:64].unsqueeze(1), [128, 8, 32]), ALU.mult),
             reads=[t_tb, t_tab], writes=[t_ub])
        P.op("dve", TT(t3, t3, bc(TC[:, tt, :].unsqueeze(1), [128, 8, 64]), ALU.mult), reads=[t_tb, t_tab], writes=[t_tb])
        P.op("dve", TT(dst, tbuf, ubuf, ALU.add), reads=[t_tb, t_ub], writes=[t_dst])
    else:
        P.op("dve", TT(dst.rearrange("p (g d) -> p g d", d=64), t3, bc(TC.unsqueeze(1), [128, 8, 64]), ALU.mult),
             reads=[t_tb, t_tab], writes=[t_dst])


def prep_qkvg(c, W, tW, qT, t_qT, kT, t_kT, vcopy, t_V, G, t_G, tabs, t_tab, rope):
    P, A = c.P, c.A
    ps, tps = c.ps, c.tps
    NB = 3 if rope else 2
    LAG = NB - 1
    raw = [[A.alloc([128, 512], F32) for _ in range(NB)] for _ in range(2)]
    t_raw = [[T() for _ in range(NB)] for _ in range(2)]
    sq = [A.alloc([128, 512], F32) for _ in range(2)]
    t_sq = [T(), T()]
    ss8 = [[A.alloc([128, 8], F32) for _ in range(NB)] for _ in range(2)]
    t_ss8 = [[T() for _ in range(NB)] for _ in range(2)]
    tbuf = [[A.alloc([128, 512], F32) for _ in range(NB)] for _ in range(2)]
    t_tb = [[T() for _ in range(NB)] for _ in range(2)]
    ubuf = [[A.alloc([128, 512], F32) for _ in range(NB)] for _ in range(2)] if rope else None
    t_ub = [[T() for _ in range(NB)] for _ in range(2)]
    rbuf = [[A.alloc([128, 512], BF16) for _ in range(NB)] for _ in range(2)]
    t_rb = [[T() for _ in range(NB)] for _ in range(2)]
    dsts = ((qT, t_qT), (kT, t_kT))
    nbank = 0
    gbank = {}

    def chain_a(tt, i):
        b = tt % NB
        P.op("act", ACT(sq[i], raw[i][b], AF.Square), reads=[t_raw[i][b]], writes=[t_sq[i]])
        P.op("dve", RED(ss8[i][b], sq[i].rearrange("p (g d) -> p g d", d=64)), reads=[t_sq[i]], writes=[t_ss8[i][b]])

    def chain_b(tt, i):
        b = tt % NB
        s8, t_s8 = ss8[i][b], t_ss8[i][b]
        P.op("act", ACT(s8, s8, AF.Ln, scale=1.0 / 64, bias=c.epsc), reads=[t_s8, c.t_eps], writes=[t_s8])
        P.op("act", ACT(s8, s8, AF.Exp, scale=-0.5), reads=[t_s8], writes=[t_s8])

    def chain_c(tt, i):
        b = tt % NB
        rw, t_rw = raw[i][b], t_raw[i][b]
        s8, t_s8 = ss8[i][b], t_ss8[i][b]
        tb_, t_tb_ = tbuf[i][b], t_tb[i][b]
        t3 = tb_.rearrange("p (g d) -> p g d", d=64)
        P.op("dve", TT(t3, rw.rearrange("p (g d) -> p g d", d=64), bc(s8.unsqueeze(2), [128, 8, 64]), ALU.mult),
             reads=[t_rw, t_s8], writes=[t_tb_])
        dst, t_dst = rbuf[i][b], t_rb[i][b]
        TC, TS_ = tabs[2 * i], tabs[2 * i + 1]
        if rope:
            ub, t_ub_ = ubuf[i][b], t_ub[i][b]
            u3 = ub.rearrange("p (g d) -> p g d", d=64)
            eng_u = "pool" if i == 1 else "dve"
            P.op(eng_u, TT(u3[:, :, 0:32], t3[:, :, 32:64], bc(TS_[:, tt, 0:32].unsqueeze(1), [128, 8, 32]), ALU.mult),
                 reads=[t_tb_, t_tab], writes=[t_ub_])
            P.op(eng_u, TT(u3[:, :, 32:64], t3[:, :, 0:32], bc(TS_[:, tt, 32:64].unsqueeze(1), [128, 8, 32]), ALU.mult),
                 reads=[t_tb_, t_tab], writes=[t_ub_])
            P.op("dve", TT(t3, t3, bc(TC[:, tt, :].unsqueeze(1), [128, 8, 64]), ALU.mult), reads=[t_tb_, t_tab], writes=[t_tb_])
            P.op("dve", TT(dst, tb_, ub, ALU.add), reads=[t_tb_, t_ub_], writes=[t_dst])
        else:
            P.op("dve", TT(dst.rearrange("p (g d) -> p g d", d=64), t3, bc(TC.unsqueeze(1), [128, 8, 64]), ALU.mult),
                 reads=[t_tb_, t_tab], writes=[t_dst])

    def transposes(tt):
        b = tt % NB
        tok = slice(tt * 128, (tt + 1) * 128)
        for i in range(2):
            for h in range(4):
                P.op("pe", TR(c.psb[:, i * 512 + h * 128:i * 512 + (h + 1) * 128], rbuf[i][b][:, h * 128:(h + 1) * 128], c.ident),
                     reads=[t_rb[i][b], c.t_ident], writes=[c.tpsb[i]])
        for i in range(2):
            dstT, t_dstT = dsts[i]
            P.op("dve", CP(dstT[:, :, tok], c.psb[:, i * 512:(i + 1) * 512].rearrange("p (h t) -> p h t", t=128)),
                 reads=[c.tpsb[i]], writes=[t_dstT[tt]])

    for tt in range(NT + LAG):
        if tt < NT:
            tok = slice(tt * 128, (tt + 1) * 128)
            b = tt % NB
            banks = []
            for j in range(4):
                bk = nbank % 7
                nbank += 1
                banks.append(bk)
                for k in range(8):
                    P.op("pe", MM(ps[bk], c.hdnT[:, k, tok], W[:, k, j * 512:(j + 1) * 512], start=(k == 0), stop=(k == 7)),
                         reads=[c.t_hdnT[tt], tW[j]] + c.t_Wq, writes=[tps[bk]])
            for i in range(2):
                if rope:
                    P.op("act", ACT(raw[i][b], ps[banks[i]], AF.Copy), reads=[tps[banks[i]]], writes=[t_raw[i][b]])
                else:
                    P.op("dve", CP(raw[i][b], ps[banks[i]]), reads=[tps[banks[i]]], writes=[t_raw[i][b]])
            vcopy(tt, ps[banks[2]], tps[banks[2]])
            gbank[tt] = banks[3]
            for stage in (chain_a, chain_b, chain_c):
                for i in range(2):
                    stage(tt, i)
            if tt % 2 == 1 or tt == NT - 1:
                for t2 in ([tt - 1, tt] if tt % 2 == 1 else [tt]):
                    P.op("act", ACT(G[:, t2, :], ps[gbank[t2]], AF.Silu), reads=[tps[gbank[t2]]], writes=[t_G[t2]])
        if tt >= LAG:
            transposes(tt - LAG)


def phase_diff(c, l):
    P, A = c.P, c.A
    m = A.mark()
    lam_init = 0.8 - 0.6 * math.exp(-0.3 * l)
    W = c.W8
    tW = pf_qkvg(c, l, "diff")
    qT = A.alloc([128, 4, L], BF16)
    kT = A.alloc([128, 4, L], BF16)
    t_qT = [T() for _ in range(NT)]
    t_kT = [T() for _ in range(NT)]
    V = A.alloc([128, NT, 4, 129], BF16)
    t_V = [T() for _ in range(NT)]
    G = A.alloc([128, NT, 512], BF16)
    t_G = [T() for _ in range(NT)]
    t_misc = T()
    P.op("pool", MEMSET(V[:, :, :, 128:129], 1.0), writes=t_V)
    wqk = A.alloc([128, 128], F32)
    P.op("sp", DMA(wqk, c.diff_qk_norm[l:l + 1, :].partition_broadcast(128)), writes=[t_misc])
    P.op("dve", TS(wqk[:, 0:64], wqk[:, 0:64], 0.125, ALU.mult), reads=[t_misc], writes=[t_misc])
    tabs = A.alloc([128, 4, NT * 64], F32)
    t_tab = T()
    m_cs = A.mark()
    c.cs_sb = A.alloc([128, NT, 128], F32)
    c.t_cs = T()
    P.op("sp", DMA(c.cs_sb, c.cs_tab.rearrange("(t p) c -> p t c", p=128)), writes=[c.t_cs])
    for i, w in enumerate((wqk[:, 0:64], wqk[:, 64:128])):
        TCv = tabs[:, 2 * i, :].rearrange("p (t d) -> p t d", d=64)
        TSv = tabs[:, 2 * i + 1, :].rearrange("p (t d) -> p t d", d=64)
        P.op("dve", TT(TCv, c.cs_sb[:, :, 0:64], bc(w.unsqueeze(1), [128, NT, 64]), ALU.mult), reads=[c.t_cs, t_misc], writes=[t_tab])
        P.op("dve", TT(TSv[:, :, 0:32], c.cs_sb[:, :, 64:96], bc(w[:, 32:64].unsqueeze(1), [128, NT, 32]), ALU.mult),
             reads=[c.t_cs, t_misc], writes=[t_tab])
        P.op("dve", TT(TSv[:, :, 32:64], c.cs_sb[:, :, 96:128], bc(w[:, 0:32].unsqueeze(1), [128, NT, 32]), ALU.mult),
             reads=[c.t_cs, t_misc], writes=[t_tab])
    P.barrier()
    A.release(m_cs)
    TCq = tabs[:, 0, :].rearrange("p (t d) -> p t d", d=64)
    TSq = tabs[:, 1, :].rearrange("p (t d) -> p t d", d=64)
    TCk = tabs[:, 2, :].rearrange("p (t d) -> p t d", d=64)
    TSk = tabs[:, 3, :].rearrange("p (t d) -> p t d", d=64)
    lamb = A.alloc([128, 256], F32)
    t_lam = T()
    P.op("sp", DMA(lamb, c.diff_lambda[l:l + 1, :].partition_broadcast(128)), writes=[t_lam])
    lsc = A.alloc([128, 4], F32)
    lam3 = lamb.rearrange("p (a b d) -> p a b d", a=2, b=2)
    prod = A.alloc([128, 2, 64], F32)
    P.op("dve", TT(prod, lam3[:, :, 0, :], lam3[:, :, 1, :], ALU.mult), reads=[t_lam], writes=[t_lam])
    P.op("dve", RED(lsc[:, 0:2], prod), reads=[t_lam], writes=[t_lam])
    P.op("act", ACT(lsc[:, 0:2], lsc[:, 0:2], AF.Exp), reads=[t_lam], writes=[t_lam])
    P.op("dve", TT(lsc[:, 2:3], lsc[:, 1:2], lsc[:, 0:1], ALU.subtract), reads=[t_lam], writes=[t_lam])
    P.op("dve", TS(lsc[:, 3:4], lsc[:, 2:3], -lam_init, ALU.add), reads=[t_lam], writes=[t_lam])
    neglam = lsc[:, 3:4]
    swb = A.alloc([128, 128], F32)
    t_swb = T()
    P.op("sp", DMA(swb, c.diff_subln[l:l + 1, :].partition_broadcast(128)), writes=[t_swb])
    P.op("dve", TS(swb, swb, 1.0 - lam_init, ALU.mult), reads=[t_swb], writes=[t_swb])

    ps, tps = c.ps, c.tps
    m_prep = A.mark()

    def vcopy(tt, bank, t_bank):
        P.op("act", ACT(V[:, tt, :, 0:128], bank.rearrange("p (h d) -> p h d", d=128), AF.Copy), reads=[t_bank], writes=[t_V[tt]])
    prep_qkvg(c, W, tW, qT, t_qT, kT, t_kT, vcopy, t_V, G, t_G, (TCq, TSq, TCk, TSk), t_tab, True)
    P.barrier()
    A.release(m_prep)
    if "na" in c.phases:
        pf_qkvg(c, l, "na")
    elif "ssd" in c.phases:
        pf_ssd(c, l)
    c.dump("qT%d" % l, qT, [128, 4, L], BF16, t_qT)
    c.dump("kT%d" % l, kT, [128, 4, L], BF16, t_kT)
    c.dump("V%d" % l, V, [128, NT, 4, 129], BF16, t_V)
    c.dump("G%d" % l, G, [128, NT, 512], BF16, t_G)
    c.dump("hdnT%d" % l, c.hdnT, [128, 8, L], BF16, c.t_hdnT)
    NPT = 3
    Pt = [A.alloc([128, 1024], BF16) for _ in range(NPT)]
    t_Pt = [T() for _ in range(NPT)]
    SP = [c.psall[:, 0:1024], c.psall[:, 1024:2048]]
    t_SP = [[tps[0], tps[1]], [tps[2], tps[3]]]
    Ob3 = [ps[4], ps[5], ps[6]]
    t_O3 = [tps[4], tps[5], tps[6]]

    def acc(a):
        return a // 3, (a % 3) * 129
    Osb = A.alloc([128, 3, 512], F32)
    t_Osb = T()
    rr = [A.alloc([128, 16], F32) for _ in range(2)]
    t_rr = [T(), T()]
    ob4 = [A.alloc([128, 4, 128], F32) for _ in range(2)]
    t_ob4 = [T(), T()]
    junk = A.alloc([128, 128], F32)
    t_junk = T()
    yb4 = [A.alloc([128, 512], BF16) for _ in range(2)]
    t_yb4 = [T(), T()]
    yst = [A.alloc([128, 512], BF16) for _ in range(2)]
    t_yst = [T(), T()]
    iters = [(h, qb, kt) for h in range(4) for qb in range(4) for kt in range(NT)]
    NI = len(iters)
    deferred = []

    def emit_S(i):
        h, qb, kt = iters[i]
        for cc in range(2):
            pr = slice(cc * 64, (cc + 1) * 64)
            P.op("pe", MM(SP[i % 2][:, cc * 512:(cc + 1) * 512], kT[pr, h, kt * 128:(kt + 1) * 128], qT[pr, h, qb * 512:(qb + 1) * 512]),
                 reads=[t_kT[kt]] + [t_qT[qb * 4 + j] for j in range(4)], writes=[t_SP[i % 2][cc]])

    def Oslice(a):
        bk, off = acc(a)
        return Osb[:, bk, off:off + 129]

    def fin_stage1(blk, h, qb):
        b = blk % 2
        r_, t_r = rr[b], t_rr[b]
        ob, t_ob = ob4[b], t_ob4[b]
        for j in range(4):
            O0, O1 = Oslice(j), Oslice(4 + j)
            P.op("dve", RECIP(r_[:, 4 * j:4 * j + 1], O0[:, 128:129]), reads=[t_Osb], writes=[t_r])
            P.op("dve", RECIP(r_[:, 4 * j + 1:4 * j + 2], O1[:, 128:129]), reads=[t_Osb], writes=[t_r])
            P.op("dve", TT(r_[:, 4 * j + 2:4 * j + 3], r_[:, 4 * j + 1:4 * j + 2], neglam, ALU.mult), reads=[t_r, t_lam], writes=[t_r])
            P.op("dve", TS(ob[:, j, :], O0[:, 0:128], r_[:, 4 * j:4 * j + 1], ALU.mult), reads=[t_Osb, t_r], writes=[t_ob])
            P.op("dve", STT(ob[:, j, :], O1[:, 0:128], r_[:, 4 * j + 2:4 * j + 3], ob[:, j, :], ALU.mult, ALU.add),
                 reads=[t_Osb, t_r, t_ob], writes=[t_ob])

    def fin_stage2(blk, h, qb):
        b = blk % 2
        r_, t_r = rr[b], t_rr[b]
        ob, t_ob = ob4[b], t_ob4[b]
        for j in range(4):
            P.op("act", ACT(junk, ob[:, j, :], AF.Square, accum_out=r_[:, 4 * j + 3:4 * j + 4]), reads=[t_ob], writes=[t_junk, t_r])
        r3 = r_.rearrange("p (j k) -> p j k", k=4)[:, :, 3]
        P.op("act", ACT(r3, r3, AF.Ln, scale=1.0 / 128, bias=c.epsc), reads=[t_r, c.t_eps], writes=[t_r])
        P.op("act", ACT(r3, r3, AF.Exp, scale=-0.5), reads=[t_r], writes=[t_r])

    def fin_stage3(blk, h, qb):
        b = blk % 2
        r_, t_r = rr[b], t_rr[b]
        ob, t_ob = ob4[b], t_ob4[b]
        yb, t_yb = yb4[b], t_yb4[b]
        for j in range(4):
            tt = qb * 4 + j
            P.op("dve", STT(ob[:, j, :], ob[:, j, :], r_[:, 4 * j + 3:4 * j + 4], swb, ALU.mult, ALU.mult), reads=[t_ob, t_r, t_swb], writes=[t_ob])
            P.op("dve", TT(yb[:, j * 128:(j + 1) * 128], ob[:, j, :], G[:, tt, h * 128:(h + 1) * 128], ALU.mult),
                 reads=[t_ob, t_G[tt]], writes=[t_yb])
        for j in range(4):
            P.op("pe", TR(c.psb[:, j * 128:(j + 1) * 128], yb[:, j * 128:(j + 1) * 128], c.ident),
                 reads=[t_yb, c.t_ident], writes=[c.tpsb[0]])

    def fin_stage4(blk, h, qb):
        b = blk % 2
        ys, t_ys = yst[b], t_yst[b]
        P.op("dve", CP(ys, c.psb[:, 0:512]), reads=[c.tpsb[0]], writes=[t_ys])
        P.op("sp", DMA(c.yT[l][1024 + h * 128:1024 + (h + 1) * 128, qb * 512:(qb + 1) * 512], ys), reads=[t_ys])

    emit_S(0)
    blk = 0
    for i in range(NI):
        h, qb, kt = iters[i]
        if i + 1 < NI:
            emit_S(i + 1)
        pt, t_pt = Pt[i % NPT], t_Pt[i % NPT]
        P.op("act", ACT(pt, SP[i % 2], AF.Exp), reads=t_SP[i % 2], writes=[t_pt])
        for cc in range(2):
            for j in range(4):
                bk, off = acc(cc * 4 + j)
                P.op("pe", MM(Ob3[bk][:, off:off + 129], pt[:, cc * 512 + j * 128:cc * 512 + (j + 1) * 128],
                              V[:, kt, h, :], start=(kt == 0 and off == 0), stop=(kt == NT - 1), skip=True),
                     reads=[t_pt, t_V[kt]], writes=[t_O3[bk]])
        for (due, fn) in [d for d in deferred if d[0] <= i]:
            fn()
        deferred = [d for d in deferred if d[0] > i]
        if kt == NT - 1:
            for k3 in range(3):
                P.op("dve", CP(Osb[:, k3, 0:387], Ob3[k3][:, 0:387]), reads=[t_O3[k3]], writes=[t_Osb])
            fin_stage1(blk, h, qb)
            deferred.append((i + 2, (lambda b_=blk, h_=h, q_=qb: fin_stage2(b_, h_, q_))))
            deferred.append((i + 4, (lambda b_=blk, h_=h, q_=qb: fin_stage3(b_, h_, q_))))
            deferred.append((i + 6, (lambda b_=blk, h_=h, q_=qb: fin_stage4(b_, h_, q_))))
            blk += 1
    for (due, fn) in deferred:
        fn()
    A.release(m)


def _na_classes():
    rows, W, kh, kw = 32, 64, 8, 16
    rs = lambda r: min(max(r - kh // 2, 0), rows - kh)
    types = {}
    keys = []
    plan = []
    for i in range(16):
        qrows = [2 * i, 2 * i + 1]
        lo = min(rs(r) for r in qrows)
        hi = max(rs(r) + kh - 1 for r in qrows)
        tkeys = []
        kbs = list(range(lo // 2, hi // 2 + 1))
        for kb in kbs:
            key = []
            for b in range(2):
                for a in range(2):
                    kr, qr = 2 * kb + b, 2 * i + a
                    ok = rs(qr) <= kr < rs(qr) + kh
                    key.append((kr - qr + 7) if ok else -1)
            tkeys.append(tuple(key))
        tkeys = tuple(tkeys)
        if tkeys not in types:
            types[tkeys] = len(keys)
            keys.extend(tkeys)
        base = types[tkeys]
        plan.append([(kb, base + bi) for bi, kb in enumerate(kbs)])
    return keys, plan


NA_KEYS, NA_PLAN = _na_classes()
NA_NCLS = len(NA_KEYS)


def _na_tables(rpb):
    W, kw = 64, 16
    cidx = np.arange(W)
    col_start = np.clip(cidx - kw // 2, 0, W - kw)
    col_ok = (cidx[None, :] >= col_start[:, None]) & (cidx[None, :] < col_start[:, None] + kw)
    dc = np.clip(cidx[None, :] - cidx[:, None] + (kw - 1), 0, 2 * kw - 2)
    out = np.full((2, 8, 128, NA_NCLS, 128), NEG, dtype=np.float32)
    for ci, key in enumerate(NA_KEYS):
        n = 0
        for b in range(2):
            for a in range(2):
                dr = key[n]
                n += 1
                if dr < 0:
                    continue
                blkv = rpb[:, :, dr, :][:, :, dc]
                blkv = np.where(col_ok[None, None], blkv, np.float32(NEG))
                out[:, :, b * 64:(b + 1) * 64, ci, a * 64:(a + 1) * 64] = np.transpose(blkv, (0, 1, 3, 2))
    return np.ascontiguousarray(out.reshape(2, 8, 128, NA_NCLS * 128))


def phase_na(c, l):
    P, A = c.P, c.A
    m = A.mark()
    W = c.W8
    tW = pf_qkvg(c, l, "na")
    qT = A.alloc([128, 4, L], BF16)
    kT = A.alloc([128, 4, L], BF16)
    t_qT = [T() for _ in range(NT)]
    t_kT = [T() for _ in range(NT)]
    V = A.alloc([128, NT, 8, 65], BF16)
    t_V = [T() for _ in range(NT)]
    G = A.alloc([128, NT, 512], BF16)
    t_G = [T() for _ in range(NT)]
    Y = A.alloc([128, NT, 512], BF16)
    t_Y = [T() for _ in range(NT)]
    P.op("pool", MEMSET(V[:, :, :, 64:65], 1.0), writes=t_V)
    t_misc = T()
    wqk = A.alloc([128, 128], F32)
    P.op("sp", DMA(wqk, c.na_qk_norm[l:l + 1, :].partition_broadcast(128)), writes=[t_misc])
    P.op("dve", TS(wqk[:, 0:64], wqk[:, 0:64], 0.125, ALU.mult), reads=[t_misc], writes=[t_misc])
    ps, tps = c.ps, c.tps
    m_prep = A.mark()

    def vcopy(tt, bank, t_bank):
        P.op("act", ACT(V[:, tt, :, 0:64], bank.rearrange("p (h d) -> p h d", d=64), AF.Copy), reads=[t_bank], writes=[t_V[tt]])
    prep_qkvg(c, W, tW, qT, t_qT, kT, t_kT, vcopy, t_V, G, t_G, (wqk[:, 0:64], None, wqk[:, 64:128], None), t_misc, False)
    P.barrier()
    A.release(m_prep)
    if "ssd" in c.phases:
        pf_ssd(c, l)
    pf_out(c, l, [0, 1])
    tab = [A.alloc([128, NA_NCLS * 128], F32) for _ in range(2)]
    t_tab = [T(), T()]
    Tb = [A.alloc([128, 640], F32) for _ in range(2)]
    t_Tb = [T(), T()]
    Pb = [A.alloc([128, 640], BF16) for _ in range(3)]
    t_Pb = [T(), T(), T()]
    rr = [A.alloc([128, 2], F32) for _ in range(2)]
    t_rr = [T(), T()]
    its = [(hh, i) for hh in range(8) for i in range(NT)]
    NI = len(its)
    loaded = set()

    def load_tab(hh):
        if hh in loaded or hh >= 8:
            return
        loaded.add(hh)
        P.op("sp", DMA(tab[hh % 2], c.na_tab[l, hh, :, :]), writes=[t_tab[hh % 2]])
        P.op("act", ACT(tab[hh % 2], tab[hh % 2], AF.Exp), reads=[], writes=[t_tab[hh % 2]])

    def emit_S(n):
        hh, i = its[n]
        jb, e = hh // 2, hh % 2
        pr = slice(e * 64, (e + 1) * 64)
        S0, S1 = ps[(n % 2) * 2], ps[(n % 2) * 2 + 1]
        tS = [tps[(n % 2) * 2], tps[(n % 2) * 2 + 1]]
        for bi, (kb, cls) in enumerate(NA_PLAN[i]):
            dstS = (S0 if bi < 4 else S1)[:, (bi % 4) * 128:(bi % 4 + 1) * 128]
            P.op("pe", MM(dstS, kT[pr, jb, kb * 128:(kb + 1) * 128], qT[pr, jb, i * 128:(i + 1) * 128]),
                 reads=[t_kT[kb], t_qT[i]], writes=[tS[bi // 4]])

    def emit_exp_mul(n):
        hh, i = its[n]
        plan = NA_PLAN[i]
        nb = len(plan)
        base = plan[0][1]
        tS = [tps[(n % 2) * 2], tps[(n % 2) * 2 + 1]]
        Sboth = c.psall[:, (n % 2) * 1024:(n % 2) * 1024 + nb * 128]
        T_, t_T = Tb[n % 2], t_Tb[n % 2]
        P_, t_P = Pb[n % 3], t_Pb[n % 3]
        P.op("act", ACT(T_[:, 0:nb * 128], Sboth, AF.Exp), reads=(tS if nb > 4 else tS[0:1]), writes=[t_T])
        P.op("dve", TT(P_[:, 0:nb * 128], T_[:, 0:nb * 128], tab[hh % 2][:, base * 128:(base + nb) * 128], ALU.mult),
             reads=[t_T, t_tab[hh % 2]], writes=[t_P])

    load_tab(0)
    emit_S(0)
    if NI > 1:
        emit_S(1)
    emit_exp_mul(0)
    for n, (hh, i) in enumerate(its):
        if i == 2:
            load_tab(hh + 1)
        plan = NA_PLAN[i]
        nb = len(plan)
        Ob, t_Ob = ps[4 + n % 2], tps[4 + n % 2]
        P_, t_P = Pb[n % 3], t_Pb[n % 3]
        for bi, (kb, cls) in enumerate(plan):
            P.op("pe", MM(Ob[:, 0:65], P_[:, bi * 128:(bi + 1) * 128], V[:, kb, hh, :], start=(bi == 0), stop=(bi == nb - 1)),
                 reads=[t_P, t_V[kb]], writes=[t_Ob])
        if n + 2 < NI:
            emit_S(n + 2)
        if n + 1 < NI:
            emit_exp_mul(n + 1)
        r_, t_r = rr[n % 2], t_rr[n % 2]
        P.op("dve", RECIP(r_[:, 0:1], Ob[:, 64:65]), reads=[t_Ob], writes=[t_r])
        P.op("dve", STT(Y[:, i, hh * 64:(hh + 1) * 64], Ob[:, 0:64], r_[:, 0:1], G[:, i, hh * 64:(hh + 1) * 64], ALU.mult, ALU.mult),
             reads=[t_Ob, t_r, t_G[i]], writes=[t_Y[i]])
    yst = [A.alloc([128, 512], BF16) for _ in range(2)]
    t_yst = [T(), T()]
    blk = 0
    for qb in range(4):
        for j4 in range(4):
            for j in range(4):
                tt = qb * 4 + j
                P.op("pe", TR(c.psb[:, j * 128:(j + 1) * 128], Y[:, tt, j4 * 128:(j4 + 1) * 128], c.ident),
                     reads=[t_Y[tt], c.t_ident], writes=[c.tpsb[0]])
            ys, t_ys = yst[blk % 2], t_yst[blk % 2]
            blk += 1
            P.op("act", ACT(ys, c.psb[:, 0:512], AF.Copy), reads=[c.tpsb[0]], writes=[t_ys])
            P.op("sp", DMA(c.yT[l][1536 + j4 * 128:1536 + (j4 + 1) * 128, qb * 512:(qb + 1) * 512], ys), reads=[t_ys])
    A.release(m)


def phase_ssd(c, l):
    STOP = 9
    P, A = c.P, c.A
    m = A.mark()
    ps, tps = c.ps, c.tps
    nc = c.nc
    Wdt = A.alloc([128, 8, 32], BF16)
    tWdt = [T()]
    load_w(c, l, Wdt, C_DT, 32, tWdt)
    t_sm = T()
    dtb = A.alloc([128, 32], F32)
    alog = A.alloc([128, 32], F32)
    dsk = A.alloc([128, 32], F32)
    P.op("sp", DMA(dtb, c.dt_bias[l:l + 1, :].partition_broadcast(128)), writes=[t_sm])
    t_al = T()
    P.op("sp", DMA(alog, c.a_log[l:l + 1, :].partition_broadcast(128)), writes=[t_al])
    t_dsk = T()
    P.op("sp", DMA(dsk, c.d_skip[l:l + 1, :].partition_broadcast(128)), writes=[t_dsk])
    dsum = A.alloc([128, 16], F32)
    P.op("dve", TT(dsum, dsk[:, 0:16], dsk[:, 16:32], ALU.add), reads=[t_dsk], writes=[t_dsk])
    P.op("act", ACT(alog, alog, AF.Exp), reads=[t_al], writes=[t_al])
    P.op("dve", TS(alog, alog, -1.0, ALU.mult), reads=[t_al], writes=[t_al])
    snw = A.alloc([128, 1024], F32)
    t_snw = T()
    P.op("sp", DMA(snw, c.ssd_norm_w[l:l + 1, :].partition_broadcast(128)), writes=[t_snw])
    cbb = A.alloc([128, 1536], F32)
    t_cbb = T()
    P.op("sp", DMA(cbb, c.conv_b[l:l + 1, :].partition_broadcast(128)), writes=[t_cbb])
    cwT = A.alloc([128, 12, 6], F32)
    t_cwT = T()

    def a3(shape=(128, NT, 32)):
        return A.alloc(list(shape), F32)
    dt = a3()
    dta = a3()
    cs = a3()
    dcb = a3()
    dout = a3()
    dend = a3()
    cXd = a3()
    t_dt, t_dta, t_cs, t_dcb, t_dout, t_dend, t_cXd, t_v, t_tmp3 = [T() for _ in range(9)]
    t_csd = T()
    csT_v = c.csT_d.rearrange("t (d h) l -> t d h l", d=2)
    m_tmp = A.mark()
    cw6 = A.alloc([128, 1536], F32)
    t_cw6 = T()
    P.op("sp", DMA(cw6[0:5, :], c.conv_w[l, :, :]), writes=[t_cw6])
    P.op("sp", DMA(cw6[5:6, :], c.conv_b[l:l + 1, :]), writes=[t_cw6])
    for j in range(12):
        P.op("pe", TR(ps[6][:, j * 6:(j + 1) * 6], cw6[0:6, j * 128:(j + 1) * 128], c.identf[0:6, 0:6]),
             reads=[t_cw6, c.t_cst], writes=[tps[6]])
    P.op("dve", CP(cwT, ps[6][:, 0:72].rearrange("p (j k) -> p j k", k=6)), reads=[tps[6]], writes=[t_cwT])
    P.barrier()
    A.release(m_tmp)

    def make_decay_stages(v, tmp3, csT):
        t_csT = T()

        def s1():
            for tt in range(NT):
                for k in range(8):
                    P.op("pe", MM(ps[4][:, tt * 32:(tt + 1) * 32], c.hdnT[:, k, tt * 128:(tt + 1) * 128], Wdt[:, k, :],
                                  start=(k == 0), stop=(k == 7), skip=True),
                         reads=[c.t_hdnT[tt], tWdt[0]], writes=[tps[4]])
            p0 = ps[4].rearrange("p (t h) -> p t h", h=32)
            P.op("dve", TT(v, p0, bc(dtb.unsqueeze(1), [128, NT, 32]), ALU.add), reads=[tps[4], t_sm], writes=[t_v])
            P.op("act", ACT(tmp3, v, AF.Abs), reads=[t_v], writes=[t_tmp3])
            P.op("act", ACT(tmp3, tmp3, AF.Exp, scale=-1.0), reads=[t_tmp3], writes=[t_tmp3])
            P.op("act", ACT(tmp3, tmp3, AF.Ln, bias=c.onec, scale=1.0), reads=[t_tmp3, c.t_eps], writes=[t_tmp3])

        def s2():
            P.op("dve", STT(dt, v, 0.0, tmp3, ALU.max, ALU.add), reads=[t_v, t_tmp3], writes=[t_dt])
            P.op("dve", TT(dta, dt, bc(alog.unsqueeze(1), [128, NT, 32]), ALU.mult), reads=[t_dt, t_al], writes=[t_dta])
            for tt in range(NT):
                P.op("pe", MM(ps[5][:, tt * 32:tt * 32 + 16], c.U, dta[:, tt, 0:16], skip=True), reads=[t_dta, c.t_cst], writes=[tps[5]])
                P.op("pe", MM(ps[5][:, tt * 32 + 16:tt * 32 + 32], c.Ur, dta[:, tt, 16:32], skip=True), reads=[t_dta, c.t_cst], writes=[tps[5]])
            for tt in range(NT):
                P.op("pe", MM(ps[6][:, tt * 32:(tt + 1) * 32], c.onesf, dta[:, tt, :], skip=True), reads=[t_dta, c.t_cst], writes=[tps[6]])
            P.op("act", ACT(cs, ps[5].rearrange("p (t h) -> p t h", h=32), AF.Copy), reads=[tps[5]], writes=[t_cs])
            P.op("act", ACT(tmp3, ps[6].rearrange("p (t h) -> p t h", h=32), AF.Copy), reads=[tps[6]], writes=[t_tmp3])

        def s3():
            P.op("act", ACT(dcb, tmp3, AF.Exp), reads=[t_tmp3], writes=[t_dcb])
            P.op("act", ACT(dout, cs, AF.Exp), reads=[t_cs], writes=[t_dout])
            P.op("dve", TT(dend, tmp3, cs, ALU.subtract), reads=[t_tmp3, t_cs], writes=[t_dend])
            P.op("act", ACT(dend, dend, AF.Exp), reads=[t_dend], writes=[t_dend])
            P.op("dve", TT(cXd, dt, dend, ALU.mult), reads=[t_dt, t_dend], writes=[t_cXd])

        def s4():
            for q in range(4):
                for j in range(4):
                    tt = q * 4 + j
                    P.op("pe", TR(ps[4][0:32, j * 128:(j + 1) * 128], cs[:, tt, :], c.identf), reads=[t_cs, c.t_cst], writes=[tps[4]])
                P.op("act", ACT(csT[0:32, q * 4:(q + 1) * 4, :], ps[4][0:32, :].rearrange("p (j t) -> p j t", t=128), AF.Copy),
                     reads=[tps[4]], writes=[t_csT])
            P.op("sp", DMA(c.csT_d.rearrange("t h l -> h t l"), csT[0:32, :, :]), reads=[t_csT], writes=[t_csd])
        return [s1, s2, s3, s4]

    for g in range(2):
        mg = A.mark()
        Wz = c.Wz[g]
        tWz = pf_ssd(c, l)[g]
        t_Wzq = c.t_Wq[3 - g]
        xs = A.alloc([128, NT, 512], F32)
        t_xs = [T() for _ in range(NT)]
        Btok = A.alloc([128, NT, 128], BF16)
        t_Btok = [T() for _ in range(NT)]
        BT = A.alloc([128, L], BF16)
        t_BT = [T() for _ in range(4)]
        CT = A.alloc([128, L], BF16)
        t_CT = [T() for _ in range(4)]
        m_conv = A.mark()
        stages = []
        if g == 0:
            stages = make_decay_stages(A.alloc([128, NT, 32], F32), A.alloc([128, NT, 32], F32), A.alloc([128, NT, 128], F32))
        pre = [A.alloc([128, L + 4], BF16) for _ in range(2)]
        t_pre = [[T() for _ in range(4)] for _ in range(2)]
        t_halo = [T(), T()]
        for b in range(2):
            P.op("pool", MEMSET(pre[b][:, 0:2], 0.0), writes=[t_halo[b]])
            P.op("pool", MEMSET(pre[b][:, L + 2:L + 4], 0.0), writes=[t_halo[b]])
        Wc = [A.alloc([128, 8, 128], BF16) for _ in range(2)]
        tWc = [[T()], [T()]]
        dg = [A.alloc([128, 5, 128], BF16) for _ in range(2)]
        t_dg = [T(), T()]
        ctmp = [A.alloc([128, 512], F32) for _ in range(2)]
        t_ctmp = [T(), T()]
        chunks = [("B", 8 + g), ("C", 10 + g)] + [("x%d" % j, 4 * g + j) for j in range(4)]
        pbank = 0
        for n, (kind, ci) in enumerate(chunks):
            b = n % 2
            if n >= 1 and stages:
                stages.pop(0)()
            load_w(c, l, Wc[b], C_XBC + ci * 128, 128, tWc[b])
            P.op("dve", TT(dg[b], bc(c.identf.unsqueeze(1), [128, 5, 128]), bc(cwT[:, ci, 0:5].unsqueeze(2), [128, 5, 128]), ALU.mult),
                 reads=[c.t_cst, t_cwT], writes=[t_dg[b]])
            for tb in range(4):
                bank, tbk = ps[pbank % 4], tps[pbank % 4]
                pbank += 1
                for k in range(8):
                    P.op("pe", MM(bank, Wc[b][:, k, :], c.hdnT[:, k, tb * 512:(tb + 1) * 512], start=(k == 0), stop=(k == 7)),
                         reads=[tWc[b][0]] + c.t_hdnT[tb * 4:(tb + 1) * 4], writes=[tbk])
                P.op("act", ACT(pre[b][:, 2 + tb * 512:2 + (tb + 1) * 512], bank, AF.Copy), reads=[tbk], writes=[t_pre[b][tb]])
            allpre = t_pre[b] + [t_halo[b]]
            if kind in ("B", "C"):
                dstT, t_dstT = (BT, t_BT) if kind == "B" else (CT, t_CT)
                for tb in range(4):
                    bank, tbk = ps[pbank % 4], tps[pbank % 4]
                    pbank += 1
                    for k in range(5):
                        P.op("pe", MM(bank, dg[b][:, k, :], pre[b][:, tb * 512 + k:tb * 512 + k + 512], start=(k == 0), stop=(k == 4)),
                             reads=[t_dg[b]] + allpre, writes=[tbk])
                    P.op("act", ACT(dstT[:, tb * 512:(tb + 1) * 512], bank, AF.Silu, bias=cwT[:, ci, 5:6], scale=1.0),
                         reads=[tbk, t_cwT], writes=[t_dstT[tb]])
            if kind != "C":
                for q in range(4):
                    bank, tbk = ps[pbank % 4], tps[pbank % 4]
                    pbank += 1
                    for j in range(4):
                        tt = q * 4 + j
                        for k in range(5):
                            P.op("pe", MM(bank[:, j * 128:(j + 1) * 128], pre[b][:, tt * 128 + k:tt * 128 + k + 128], dg[b][:, k, :],
                                          start=(k == 0 and j == 0), stop=(k == 4), skip=True),
                                 reads=[t_dg[b]] + allpre, writes=[tbk])
                    ct, t_ct = ctmp[q % 2], t_ctmp[q % 2]
                    P.op("dve", TT(ct.rearrange("p (j c) -> p j c", c=128), bank.rearrange("p (j c) -> p j c", c=128),
                                   bc(cbb[:, ci * 128:(ci + 1) * 128].unsqueeze(1), [128, 4, 128]), ALU.add),
                         reads=[tbk, t_cbb], writes=[t_ct])
                    if kind == "B":
                        P.op("act", ACT(Btok[:, q * 4:(q + 1) * 4, :], ct.rearrange("p (j c) -> p j c", c=128), AF.Silu),
                             reads=[t_ct], writes=t_Btok[q * 4:(q + 1) * 4])
                    else:
                        jx = int(kind[1])
                        P.op("act", ACT(xs[:, q * 4:(q + 1) * 4, jx * 128:(jx + 1) * 128], ct.rearrange("p (j c) -> p j c", c=128), AF.Silu),
                             reads=[t_ct], writes=t_xs[q * 4:(q + 1) * 4])

        for st_ in stages:
            st_()
        stages = []
        P.barrier()
        A.release(m_conv)
        if STOP <= 2:
            A.release(mg)
            continue
        hb0 = 16 + 8 * g
        hf0 = 8 * g
        m_passA = A.mark()
        Sst = [A.alloc([128, 512], F32) for _ in range(2)]
        t_Sst = [T(), T()]
        for d in range(2):
            P.op("pool", MEMSET(Sst[d], 0.0), writes=[t_Sst[d]])
        stg = [[A.alloc([128, 512], BF16) for _ in range(2)] for _ in range(2)]
        t_stg = [[T(), T()], [T(), T()]]
        Xd = [[A.alloc([128, 512], BF16) for _ in range(2)] for _ in range(2)]
        t_Xd = [[T(), T()], [T(), T()]]
        t_sd = [[T() for _ in range(NT)] for _ in range(2)]
        sdram = [c.sf_d, c.sb_d]
        h0s = [hf0, hb0]

        def bh(ap3, h0):
            return bc(ap3[:, h0:h0 + 8].unsqueeze(2), [128, 8, 64])

        def v8(ap):
            return ap.rearrange("p (h d) -> p h d", d=64)
        for k in range(NT):
            for d in range(2):
                ci_ = k if d == 0 else NT - 1 - k
                last = (ci_ == NT - 1) if d == 0 else (ci_ == 0)
                sg, t_sg = stg[d][k % 2], t_stg[d][k % 2]
                P.op("act", ACT(sg, Sst[d], AF.Copy), reads=[t_Sst[d]], writes=[t_sg])
                P.op("sp", DMA(sdram[d][g, ci_], sg), reads=[t_sg], writes=[t_sd[d][ci_]])
                if not last:
                    xd, t_xd = Xd[d][k % 2], t_Xd[d][k % 2]
                    P.op("pool", TT(v8(xd), v8(xs[:, ci_, :]), bh(cXd[:, ci_, :], h0s[d]), ALU.mult), reads=[t_xs[ci_], t_cXd], writes=[t_xd])
                    bank, tbk = ps[4 + d], tps[4 + d]
                    P.op("pe", MM(bank, Btok[:, ci_, :], xd), reads=[t_Btok[ci_], t_xd], writes=[tbk])
                    P.op("dve", TT(v8(Sst[d]), v8(Sst[d]), bh(dcb[:, ci_, :], h0s[d]), ALU.mult), reads=[t_Sst[d], t_dcb], writes=[t_Sst[d]])
                    P.op("dve", TT(Sst[d], Sst[d], bank, ALU.add), reads=[t_Sst[d], tbk], writes=[t_Sst[d]])
        P.barrier()
        A.release(m_passA)
        if STOP <= 3:
            A.release(mg)
            continue
        R = [A.alloc([128, 2, 8, 128], F32) for _ in range(2)]
        t_R = [T(), T()]
        SFc = [A.alloc([128, 512], BF16) for _ in range(2)]
        t_SFc = [T(), T()]
        SBc = [A.alloc([128, 512], BF16) for _ in range(2)]
        t_SBc = [T(), T()]
        E = A.alloc([128, 2, 8, 128], BF16)
        t_E = T()
        Mt = [A.alloc([128, 2, 8, 128], BF16) for _ in range(2)]
        t_Mt = [T(), T()]
        Gm = [A.alloc([128, 2, 128], BF16) for _ in range(2)]
        t_Gm = [T(), T()]
        Xt = [A.alloc([128, 3, 512], BF16) for _ in range(2)]
        t_Xt = [T(), T()]
        sz = [A.alloc([128, 512], F32) for _ in range(4)]
        t_sz = [T() for _ in range(4)]
        ya = A.alloc([128, 512], F32)
        yb_ = A.alloc([128, 512], F32)
        yy = A.alloc([128, 512], F32)
        t_ya, t_yb, t_yy = T(), T(), T()
        ssn = A.alloc([128, 2], F32)
        t_ssn = T()
        yo = [A.alloc([128, 512], BF16) for _ in range(2)]
        t_yo = [T(), T()]
        yst = [A.alloc([128, 4, 128], BF16) for _ in range(2)]
        t_yst = [T(), T()]

        def loadR(ci_):
            b = ci_ % 2
            P.op("sp", DMA(R[b].rearrange("p d h l -> p d (h l)"),
                           csT_v[ci_, :, 8 * g:8 * g + 8, :].rearrange("d h l -> d (h l)").partition_broadcast(128)),
                 reads=[t_csd], writes=[t_R[b]])

        def loadS(ci_):
            b = ci_ % 2
            P.op("sp", DMA(SFc[b], c.sf_d[g, ci_]), reads=[t_sd[0][ci_]], writes=[t_SFc[b]])
            P.op("sp", DMA(SBc[b], c.sb_d[g, ci_]), reads=[t_sd[1][ci_]], writes=[t_SBc[b]])

        def iteration(cn, cc_):
            if cn is not None:
                bn = cn % 2
                tokn = slice(cn * 128, (cn + 1) * 128)
                P.op("pe", MM(ps[4][:, 0:128], BT[:, tokn], CT[:, tokn]), reads=[t_BT[cn // 4], t_CT[cn // 4]], writes=[tps[4]])
                if cn % 2 == 0:
                    for c2 in (cn, cn + 1):
                        if c2 < NT:
                            zb, t_zb = (ps[3], tps[3]) if c2 % 2 == 0 else (ps[6], tps[6])
                            tok2 = slice(c2 * 128, (c2 + 1) * 128)
                            for k in range(8):
                                P.op("pe", MM(zb, c.hdnT[:, k, tok2], Wz[:, k, :], start=(k == 0), stop=(k == 7)),
                                     reads=[c.t_hdnT[c2], tWz[0], t_Wzq], writes=[t_zb])
                P.op("dve", TT(Gm[bn][:, 0, :], ps[4][:, 0:128], c.U, ALU.mult), reads=[tps[4], c.t_cst], writes=[t_Gm[bn]])
                P.op("dve", TT(Gm[bn][:, 1, :], ps[4][:, 0:128], c.Ur, ALU.mult), reads=[tps[4], c.t_cst], writes=[t_Gm[bn]])
                csv = cs[:, cn, :].rearrange("p (d h) -> p d h", d=2)[:, :, 8 * g:8 * g + 8]
                Dd, t_D = R[bn], t_R[bn]
                P.op("dve", TT(Dd, Dd, bc(csv.unsqueeze(3), [128, 2, 8, 128]), ALU.subtract), reads=[t_cs], writes=[t_D])
                P.op("act", ACT(Dd, Dd, AF.Relu, scale=-1.0), reads=[], writes=[t_D])
                P.op("act", ACT(E, Dd, AF.Exp, scale=-1.0), reads=[t_D], writes=[t_E])
                P.op("pool", TT(v8(Xt[bn][:, 0, :]), v8(xs[:, cn, :]), bh(dt[:, cn, :], hf0), ALU.mult), reads=[t_xs[cn], t_dt], writes=[t_Xt[bn]])
                P.op("pool", TT(v8(Xt[bn][:, 1, :]), v8(xs[:, cn, :]), bh(dt[:, cn, :], hb0), ALU.mult), reads=[t_xs[cn], t_dt], writes=[t_Xt[bn]])
                P.op("pool", TT(v8(Xt[bn][:, 2, :]), v8(xs[:, cn, :]), bh(dsum, 8 * g), ALU.mult), reads=[t_xs[cn], t_dsk], writes=[t_Xt[bn]])
            if cc_ is not None:
                b = cc_ % 2
                tok = slice(cc_ * 128, (cc_ + 1) * 128)
                Yb_, t_Y = (ps[0], tps[0]) if b == 0 else (ps[5], tps[5])
                P.op("pe", MM(Yb_, c.ident, Xt[b][:, 2, :], start=True, stop=False, skip=True), reads=[c.t_ident, t_Xt[b]], writes=[t_Y])
                for h in range(8):
                    for d in range(2):
                        P.op("pe", MM(Yb_[:, h * 64:(h + 1) * 64], Mt[b][:, d, h, :], Xt[b][:, d, h * 64:(h + 1) * 64],
                                      start=False, stop=(d == 1), skip=True),
                             reads=[t_Mt[b], t_Xt[b]], writes=[t_Y])
                P.op("pe", MM(ps[1], CT[:, tok], SFc[b]), reads=[t_CT[cc_ // 4], t_SFc[b]], writes=[tps[1]])
                P.op("pe", MM(ps[2], CT[:, tok], SBc[b]), reads=[t_CT[cc_ // 4], t_SBc[b]], writes=[tps[2]])
                P.op("dve", TT(v8(ya), v8(ps[1]), bh(dout[:, cc_, :], hf0), ALU.mult), reads=[tps[1], t_dout], writes=[t_ya])
                P.op("dve", TT(v8(yb_), v8(ps[2]), bh(dout[:, cc_, :], hb0), ALU.mult), reads=[tps[2], t_dout], writes=[t_yb])
                P.op("pool", TT(ya, ya, yb_, ALU.add), reads=[t_ya, t_yb], writes=[t_ya])
                P.op("dve", TT(yy, Yb_, ya, ALU.add), reads=[t_Y, t_ya], writes=[t_yy])
                P.op("pool", TT(yy, yy, sz[cc_ % 4], ALU.mult), reads=[t_yy, t_sz[cc_ % 4]], writes=[t_yy])
            if cn is not None:
                for d in range(2):
                    P.op("dve", TT(Mt[bn][:, d], E[:, d], bc(Gm[bn][:, d, :].unsqueeze(1), [128, 8, 128]), ALU.mult),
                         reads=[t_E, t_Gm[bn]], writes=[t_Mt[bn]])
                if cn % 2 == 0:
                    for c2 in (cn, cn + 1):
                        if c2 < NT:
                            zb, t_zb = (ps[3], tps[3]) if c2 % 2 == 0 else (ps[6], tps[6])
                            P.op("act", ACT(sz[c2 % 4], zb, AF.Silu), reads=[t_zb], writes=[t_sz[c2 % 4]])
            if cc_ is not None:
                P.op("act", ACT(ya, yy, AF.Square, accum_out=ssn[:, 0:1]), reads=[t_yy], writes=[t_ya, t_ssn])
                P.op("act", ACT(ssn[:, 0:1], ssn[:, 0:1], AF.Ln, scale=1.0 / 512, bias=c.epsc), reads=[t_ssn, c.t_eps], writes=[t_ssn])
                P.op("act", ACT(ssn[:, 0:1], ssn[:, 0:1], AF.Exp, scale=-0.5), reads=[t_ssn], writes=[t_ssn])
                P.op("dve", STT(yo[b], yy, ssn[:, 0:1], snw[:, g * 512:(g + 1) * 512], ALU.mult, ALU.mult),
                     reads=[t_yy, t_ssn, t_snw], writes=[t_yo[b]])

        def stageC(ci_):
            b = ci_ % 2
            tok = slice(ci_ * 128, (ci_ + 1) * 128)
            for j in range(4):
                P.op("pe", TR(c.psb[:, j * 128:(j + 1) * 128], yo[b][:, j * 128:(j + 1) * 128], c.ident),
                     reads=[t_yo[b], c.t_ident], writes=[c.tpsb[0]])
            P.op("dve", CP(yst[b], c.psb[:, 0:512].rearrange("p (j t) -> p j t", t=128)), reads=[c.tpsb[0]], writes=[t_yst[b]])
            P.op("sp", DMA(c.yT[l][g * 512:(g + 1) * 512, tok].rearrange("(j p) t -> p j t", p=128), yst[b]), reads=[t_yst[b]])

        loadR(0)
        loadS(0)
        loadR(1)
        iteration(0, None)
        for ci_ in range(NT):
            if ci_ + 1 < NT:
                loadS(ci_ + 1)
                if ci_ + 2 < NT:
                    loadR(ci_ + 2)
            iteration(ci_ + 1 if ci_ + 1 < NT else None, ci_)
            if ci_ >= 1:
                stageC(ci_ - 1)
        stageC(NT - 1)
        A.release(mg)
    A.release(m)


def phase_out(c, l, xsrc, xdst, fuse_next=False):
    P, A = c.P, c.A
    m = A.mark()
    if fuse_next:
        nwb = A.alloc([128, DM], F32)
        t_nwb = T()
        P.op("sp", DMA(nwb, c.norm_w[l + 1:l + 2, :].partition_broadcast(128)), writes=[t_nwb])
        sqj = A.alloc([128, DM], F32)
        t_sqj = T()
        ssx = A.alloc([128, NT], F32)
        t_ssx = [T() for _ in range(NT)]
        hb = [A.alloc([128, DM], BF16) for _ in range(2)]
        t_hb = [T(), T()]
    Wo = c.Wo
    tWo = pf_out(c, l, [0, 1, 2, 3])
    yt = [A.alloc([128, 16, 512], BF16) for _ in range(2)]
    t_yt = [T(), T()]
    xt = [A.alloc([128, DM], F32) for _ in range(3)]
    t_xt = [T(), T(), T()]
    ot = [A.alloc([128, DM], F32) for _ in range(2)]
    t_ot = [T(), T()]
    ps, tps = c.ps, c.tps
    fin = []

    def load_y(qb):
        P.op("sp", DMA(yt[qb % 2], c.yT[l][:, qb * 512:(qb + 1) * 512].rearrange("(k p) t -> p k t", p=128)), writes=[t_yt[qb % 2]])

    def load_x(tt):
        P.op("sp", DMA(xt[tt % 3], xsrc[tt * 128:(tt + 1) * 128, :]), writes=[t_xt[tt % 3]])
    load_y(0)
    load_x(0)
    load_x(1)
    nb = 0
    for tt in range(NT):
        b = tt % 2
        qb, j = tt // 4, tt % 4
        tok = slice(tt * 128, (tt + 1) * 128)
        if j == 0 and qb + 1 < 4:
            load_y(qb + 1)
        if tt + 2 < NT:
            load_x(tt + 2)
        for n in range(2):
            bank, tb = ps[nb % 6], tps[nb % 6]
            nb += 1
            for k in range(16):
                P.op("pe", MM(bank, yt[qb % 2][:, k, j * 128:(j + 1) * 128], Wo[:, k, n * 512:(n + 1) * 512], start=(k == 0), stop=(k == 15)),
                     reads=[t_yt[qb % 2], tWo[k // 4], c.t_Wq[k // 4]], writes=[tb])
            P.op("dve", TT(ot[b][:, n * 512:(n + 1) * 512], bank, xt[tt % 3][:, n * 512:(n + 1) * 512], ALU.add),
                 reads=[tb, t_xt[tt % 3]], writes=[t_ot[b]])
        fin.append(P.op("sp", DMA(xdst[tok, :], ot[b]), reads=[t_ot[b]]))
        if fuse_next:
            s1 = ssx[:, tt:tt + 1]
            P.op("act", ACT(sqj, ot[b], AF.Square, accum_out=s1), reads=[t_ot[b]], writes=[t_sqj, t_ssx[tt]])
            P.op("act", ACT(s1, s1, AF.Ln, scale=1.0 / DM, bias=c.epsc), reads=[t_ssx[tt], c.t_eps], writes=[t_ssx[tt]])
            P.op("act", ACT(s1, s1, AF.Exp, scale=-0.5), reads=[t_ssx[tt]], writes=[t_ssx[tt]])
            P.op("dve", STT(hb[b], ot[b], s1, nwb, ALU.mult, ALU.mult), reads=[t_ot[b], t_ssx[tt], t_nwb], writes=[t_hb[b]])
            for k in range(8):
                P.op("pe", TR(c.psb[:, k * 128:(k + 1) * 128], hb[b][:, k * 128:(k + 1) * 128], c.ident),
                     reads=[t_hb[b], c.t_ident], writes=[c.tpsb[0]])
            P.op("act", ACT(c.hdnT[:, :, tok], c.psb[:, :].rearrange("p (k t) -> p k t", t=128), AF.Copy),
                 reads=[c.tpsb[0]], writes=[c.t_hdnT[tt]])
    A.release(m)
    return fin


def _host_consts():
    inv_freq = (10000.0 ** (-(np.arange(0, 64, 2, dtype=np.float32)) / np.float32(64))).astype(np.float32)
    ang = (np.arange(L, dtype=np.float32)[:, None] * inv_freq[None, :]).astype(np.float32)
    cos, sin = np.cos(ang).astype(np.float32), np.sin(ang).astype(np.float32)
    cs = np.concatenate([cos, cos, -sin, sin], axis=1).astype(np.float32)
    k = np.arange(128)
    ident = np.eye(128, dtype=np.float32)
    U = (k[:, None] <= k[None, :]).astype(np.float32)
    Ur = (k[:, None] >= k[None, :]).astype(np.float32)
    ones = np.ones((128, 128), np.float32)
    consts = np.concatenate([ident, U, Ur, ones], axis=1)
    return np.ascontiguousarray(cs), np.ascontiguousarray(consts)


_CACHE = {}


def make_in_maps(inputs, n_cores=8):
    cs, consts = _host_consts()
    f = lambda a: np.ascontiguousarray(np.asarray(a, dtype=np.float32))
    shared = {
        "norm_w": f(inputs["norm_w"]), "w_in": f(inputs["w_in"]), "conv_w": f(inputs["conv_w"]),
        "conv_b": f(inputs["conv_b"]), "a_log": f(inputs["a_log"]).reshape(2, 32),
        "dt_bias": f(inputs["dt_bias"]).reshape(2, 32), "d_skip": f(inputs["d_skip"]).reshape(2, 32),
        "ssd_norm_w": f(inputs["ssd_norm_w"]), "diff_qk_norm": f(inputs["diff_qk_norm"]).reshape(2, 128),
        "diff_lambda": f(inputs["diff_lambda"]).reshape(2, 256), "diff_subln": f(inputs["diff_subln"]),
        "na_qk_norm": f(inputs["na_qk_norm"]).reshape(2, 128), "na_tab": _na_tables(f(inputs["na_rpb"])),
        "w_out": f(inputs["w_out"]), "cs_tab": cs, "consts": consts,
    }
    x = f(inputs["x"])
    return [dict(shared, x=x[b]) for b in range(n_cores)]


def kernel(**inputs):
    if "nc" not in _CACHE:
        _CACHE["nc"] = build()[0]
    nc = _CACHE["nc"]
    in_maps = make_in_maps(inputs)
    res = run_bass_kernel_spmd(nc, in_maps, core_ids=list(range(8)))
    return np.stack([np.asarray(r["out"], dtype=np.float32) for r in res.results], axis=0)
```

```python
import contextlib
import math
import numpy as np
import concourse.bass as bass
import concourse.mybir as mybir
from concourse.bass_utils import run_bass_kernel_spmd

F32 = mybir.dt.float32
BF16 = mybir.dt.bfloat16
AF = mybir.ActivationFunctionType
ALU = mybir.AluOpType
AX = mybir.AxisListType

L = 2048
DM = 1024
NT = 16
INW = 6688
EPS = 1e-6
NEG = -30000.0
C_Z, C_XBC, C_DT, C_DIFF, C_NA = 0, 1024, 2560, 2592, 4640


class T:
    __slots__ = ("w", "r", "excl")

    def __init__(self, excl=False):
        self.w = None
        self.r = []
        self.excl = excl


class Op:
    __slots__ = ("eng", "fn", "deps", "idx", "sig", "isdma", "semid", "semval", "prev_same_sem")


class Prog:
    COMPUTE = ("pe", "act", "dve", "pool")
    DMAQ = ("sp", "actq", "poolq")
    STREAM = {"pe": "pe", "act": "act", "dve": "dve", "pool": "pool", "sp": "sp", "actq": "act", "poolq": "pool"}
    STREAMS = ("pe", "act", "dve", "pool", "sp")

    def __init__(self, nc, n_dma_sems=12):
        self.nc = nc
        self.ops = []
        self.n_dma_sems = n_dma_sems
        self.last = {s: None for s in self.STREAMS}
        self.recent_dma = {q: [] for q in self.DMAQ}
        self.frontier = []
        self.synced = {s: True for s in self.STREAMS}

    def barrier(self):
        fr = [o for o in self.last.values() if o is not None]
        for q in self.DMAQ:
            fr.extend(self.recent_dma[q])
        self.frontier = fr
        self.synced = {s: False for s in self.STREAMS}

    def op(self, eng, fn, reads=(), writes=()):
        o = Op()
        o.eng = eng
        o.fn = fn
        o.isdma = eng in self.DMAQ
        o.idx = len(self.ops)
        o.prev_same_sem = None
        deps = {}
        if any(t.excl for t in reads):
            writes = list(writes) + [t for t in reads if t.excl and t not in writes]
            reads = [t for t in reads if not t.excl]
        for t in reads:
            if t.w is not None:
                deps[t.w.idx] = ("raw", t.w)
        for t in writes:
            if t.w is not None and t.w.idx not in deps:
                deps[t.w.idx] = ("waw", t.w)
            for r in t.r:
                if r.idx not in deps:
                    deps[r.idx] = ("war", r)
        st = self.STREAM[eng]
        if not self.synced[st]:
            for p in self.frontier:
                if p.idx not in deps:
                    deps[p.idx] = ("bar", p)
            self.synced[st] = True
        for t in writes:
            t.w = o
            t.r = []
        for t in reads:
            if t.w is not o:
                t.r.append(o)
        o.deps = deps
        o.sig = False
        self.ops.append(o)
        self.last[st] = o
        if o.isdma:
            lst = self.recent_dma[eng]
            lst.append(o)
            if len(lst) > self.n_dma_sems:
                lst.pop(0)
        return o

    def emit(self, final_waits=()):
        nc = self.nc
        streams = {s: [] for s in self.STREAMS}
        for o in self.ops:
            streams[self.STREAM[o.eng]].append(o)
        pos = {}
        for s, lst in streams.items():
            for i, o in enumerate(lst):
                pos[o.idx] = i
        need = {}
        for o in self.ops:
            lst = []
            so = self.STREAM[o.eng]
            for (kind, p) in o.deps.values():
                sp_ = self.STREAM[p.eng]
                if p.isdma or o.isdma:
                    lst.append(p)
                elif sp_ != so:
                    lst.append(p)
                else:
                    if so == "pe":
                        continue
                    lst.append(p)
            need[o.idx] = lst
            for p in lst:
                p.sig = True
        for o in final_waits:
            o.sig = True
        stack = contextlib.ExitStack()
        esem = {e: stack.enter_context(nc.semaphore("s_" + e)) for e in self.COMPUTE}
        ecount = {e: 0 for e in self.COMPUTE}
        dsems = {q: [stack.enter_context(nc.semaphore("d_%s_%d" % (q, i))) for i in range(self.n_dma_sems)]
                 for q in self.DMAQ}
        duse = {q: [0] * self.n_dma_sems for q in self.DMAQ}
        dlast = {q: [None] * self.n_dma_sems for q in self.DMAQ}
        dnext = {q: 0 for q in self.DMAQ}
        for s in self.STREAMS:
            for o in streams[s]:
                if o.isdma:
                    q = o.eng
                    j = dnext[q]
                    dnext[q] = (j + 1) % self.n_dma_sems
                    duse[q][j] += 1
                    o.semid = (q, j)
                    o.semval = 16 * duse[q][j]
                    o.prev_same_sem = dlast[q][j]
                    dlast[q][j] = o
                elif o.sig:
                    ecount[o.eng] += 1
                    o.semid = o.eng
                    o.semval = ecount[o.eng]
        self.n_waits = 0

        def sem_of(p):
            if p.isdma:
                return dsems[p.semid[0]][p.semid[1]]
            return esem[p.semid]

        def emit_stream(s, engobj):
            known = {}
            for o in streams[s]:
                waits = {}
                cand = list(need[o.idx])
                if o.isdma and o.prev_same_sem is not None:
                    cand.append(o.prev_same_sem)
                for p in cand:
                    k = p.semid
                    if known.get(k, 0) >= p.semval:
                        continue
                    if waits.get(k, (0, None))[0] < p.semval:
                        waits[k] = (p.semval, p)
                for k, (v, p) in waits.items():
                    engobj.wait_ge(sem_of(p), v)
                    known[k] = v
                    self.n_waits += 1
                ins = o.fn(engobj)
                if o.isdma:
                    ins.then_inc(dsems[o.semid[0]][o.semid[1]], 16)
                elif o.sig:
                    ins.then_inc(esem[o.semid], 1)
            if s == "sp":
                for o in final_waits:
                    engobj.wait_ge(sem_of(o), o.semval)

        with nc.Block() as block:
            @block.tensor
            def _(e):
                emit_stream("pe", e)

            @block.scalar
            def _(e):
                emit_stream("act", e)

            @block.vector
            def _(e):
                emit_stream("dve", e)

            @block.gpsimd
            def _(e):
                emit_stream("pool", e)

            @block.sync
            def _(e):
                emit_stream("sp", e)
        stack.close()


def ACT(out, in_, func, **kw):
    return lambda e: e.activation(out=out, in_=in_, func=func, **kw)


def TT(out, in0, in1, op):
    return lambda e: e.tensor_tensor(out=out, in0=in0, in1=in1, op=op)


def TS(out, in0, s1, op0, s2=None, op1=None):
    if op1 is None:
        return lambda e: e.tensor_scalar(out=out, in0=in0, scalar1=s1, scalar2=None, op0=op0)
    return lambda e: e.tensor_scalar(out=out, in0=in0, scalar1=s1, scalar2=s2, op0=op0, op1=op1)


def STT(out, in0, scalar, in1, op0, op1):
    return lambda e: e.scalar_tensor_tensor(out=out, in0=in0, scalar=scalar, in1=in1, op0=op0, op1=op1)


def CP(out, in_):
    return lambda e: e.tensor_copy(out=out, in_=in_)


def MM(out, lhsT, rhs, start=True, stop=True, skip=False):
    if skip:
        return lambda e: e.matmul(out, lhsT, rhs, start=start, stop=stop, skip_group_check=True)
    return lambda e: e.matmul(out, lhsT, rhs, start=start, stop=stop)


def TR(out, in_, ident):
    return lambda e: e.transpose(out, in_, ident)


def DMA(out, in_):
    return lambda e: e.dma_start(out=out, in_=in_)


def RED(out, in_, op=None):
    return lambda e: e.tensor_reduce(out=out, in_=in_, axis=AX.X, op=(op or ALU.add))


def RECIP(out, in_):
    return lambda e: e.reciprocal(out=out, in_=in_)


def MEMSET(ap, v):
    return lambda e: e.memset(ap, v)


class Arena:
    def __init__(self, nc, words):
        self.t = nc.alloc_sbuf_tensor("arena", [128, words], F32)
        self.words = words
        self.off = 0
        self.peak = 0

    def alloc(self, shape, dt):
        n = int(np.prod(shape[1:]))
        nw = n if dt == F32 else (n + 1) // 2
        nw = (nw + 7) // 8 * 8
        assert self.off + nw <= self.words, ("SBUF arena overflow", self.off, nw, self.words)
        v = self.t[:, self.off:self.off + nw]
        self.off += nw
        self.peak = max(self.peak, self.off)
        if dt != F32:
            v = v.bitcast(dt)
        v = v[:, 0:n]
        if len(shape) == 3:
            v = v.rearrange("p (a b) -> p a b", b=shape[2])
        elif len(shape) == 4:
            v = v.rearrange("p (a b c) -> p a b c", b=shape[2], c=shape[3])
        return v

    def mark(self):
        return self.off

    def release(self, m):
        self.off = m


def bc(ap, shape):
    return ap.to_broadcast(shape)


class Ctx:
    pass


def build(n_layers=2, phases=("diff", "na", "ssd"), dbg=False):
    nc = bass.Bass("TRN2", target_bir_lowering=False)
    c = Ctx()
    c.nc = nc
    c.dbg = dbg

    def din(name, shape, dt=F32):
        return nc.dram_tensor(name, shape, dt, kind="ExternalInput").ap()

    c.x_in = din("x", [L, DM])
    c.norm_w = din("norm_w", [2, DM])
    c.w_in = din("w_in", [2, DM, INW])
    c.conv_w = din("conv_w", [2, 5, 1536])
    c.conv_b = din("conv_b", [2, 1536])
    c.a_log = din("a_log", [2, 32])
    c.dt_bias = din("dt_bias", [2, 32])
    c.d_skip = din("d_skip", [2, 32])
    c.ssd_norm_w = din("ssd_norm_w", [2, 1024])
    c.diff_qk_norm = din("diff_qk_norm", [2, 128])
    c.diff_lambda = din("diff_lambda", [2, 256])
    c.diff_subln = din("diff_subln", [2, 128])
    c.na_qk_norm = din("na_qk_norm", [2, 128])
    c.na_tab = din("na_tab", [2, 8, 128, NA_NCLS * 128])
    c.w_out = din("w_out", [2, 2048, DM])
    c.cs_tab = din("cs_tab", [L, 128])
    c.consts = din("consts", [128, 4 * 128])
    c.out = nc.dram_tensor("out", [L, DM], F32, kind="ExternalOutput").ap()
    kind_scr = "ExternalOutput" if dbg else "Internal"
    c.x1 = nc.dram_tensor("x1", [L, DM], F32, kind=kind_scr).ap()
    c.yT = [nc.dram_tensor("yT%d" % i, [2048, L], BF16, kind=kind_scr).ap() for i in range(n_layers)]
    c.csT_d = nc.dram_tensor("csT_d", [NT, 32, 128], F32, kind="Internal").ap()
    c.sb_d = nc.dram_tensor("sb_d", [2, NT, 128, 512], BF16, kind="Internal").ap()
    c.sf_d = nc.dram_tensor("sf_d", [2, NT, 128, 512], BF16, kind="Internal").ap()

    c.dumps = []

    def dump(name, ap, shape, dt, tiles):
        if not dbg:
            return
        d = nc.dram_tensor("dbg_" + name, list(shape), dt, kind="ExternalOutput").ap()
        c.dumps.append(c.P.op("sp", DMA(d, ap), reads=tiles))
    c.dump = dump
    A = Arena(nc, 51000)
    c.A = A
    P = Prog(nc)
    c.P = P
    c.psall = nc.alloc_psum_tensor("psall", [128, 8 * 512], F32)[:, :]
    c.ps = [c.psall[:, i * 512:(i + 1) * 512] for i in range(7)]
    c.tps = [T(excl=True) for _ in range(7)]
    c.psb = c.psall[:, 7 * 512:8 * 512].bitcast(BF16)
    _tb = T(excl=True)
    c.tpsb = [_tb, _tb]

    c.cst = A.alloc([128, 512], F32)
    c.t_cst = T()
    P.op("sp", DMA(c.cst, c.consts), writes=[c.t_cst])
    c.identf = c.cst[:, 0:128]
    c.U = c.cst[:, 128:256]
    c.Ur = c.cst[:, 256:384]
    c.onesf = c.cst[:, 384:512]
    c.ident = A.alloc([128, 128], BF16)
    c.t_ident = T()
    P.op("dve", CP(c.ident, c.identf), reads=[c.t_cst], writes=[c.t_ident])
    c.epsc = A.alloc([128, 1], F32)
    c.t_eps = T()
    P.op("pool", MEMSET(c.epsc, EPS), writes=[c.t_eps])
    c.onec = A.alloc([128, 1], F32)
    P.op("pool", MEMSET(c.onec, 1.0), writes=[c.t_eps])
    c.hdnT = A.alloc([128, 8, L], BF16)
    c.t_hdnT = [T() for _ in range(NT)]
    c.Wbuf = A.alloc([128, 16384], BF16)
    c.t_Wq = [T() for _ in range(4)]
    c.W8 = c.Wbuf.rearrange("p (k c) -> p k c", c=2048)
    c.Wo = c.Wbuf.rearrange("p (k c) -> p k c", c=1024)
    c.Wz = [c.Wbuf[:, 3 * 4096:4 * 4096].rearrange("p (k c) -> p k c", c=512),
            c.Wbuf[:, 2 * 4096:3 * 4096].rearrange("p (k c) -> p k c", c=512)]
    c.pf = {}
    c.phases = phases

    finals = []
    for l in range(n_layers):
        xsrc = c.x_in if l == 0 else c.x1
        xdst = c.x1 if l < n_layers - 1 or dbg and n_layers == 1 else c.out
        if l == n_layers - 1:
            xdst = c.out
        P.barrier()
        if "diff" in phases:
            pf_qkvg(c, l, "diff")
        if l == 0:
            phase_hdn(c, l, xsrc)
        if "diff" in phases:
            P.barrier()
            phase_diff(c, l)
        if "na" in phases:
            P.barrier()
            phase_na(c, l)
        if "ssd" in phases:
            P.barrier()
            phase_ssd(c, l)
        P.barrier()
        fin = phase_out(c, l, xsrc, xdst, fuse_next=(l + 1 < n_layers))
        if l == n_layers - 1:
            finals = fin
    P.barrier()
    finals = list(finals)
    if dbg:
        finals += [o for o in P.ops if o.isdma][-36:] + c.dumps
    P.emit(final_waits=finals)
    c.peak = A.peak
    return nc, c


def phase_hdn(c, l, xsrc):
    P, A = c.P, c.A
    m = A.mark()
    nwb = A.alloc([128, DM], F32)
    t_nwb = T()
    P.op("sp", DMA(nwb, c.norm_w[l:l + 1, :].partition_broadcast(128)), writes=[t_nwb])
    NX = 4
    xt = [A.alloc([128, DM], F32) for _ in range(NX)]
    t_xt = [T() for _ in range(NX)]
    sq = A.alloc([128, DM], F32)
    t_sq = T()
    ss = A.alloc([128, NT], F32)
    t_ss = [T() for _ in range(NT)]
    hb = [A.alloc([128, DM], BF16) for _ in range(2)]
    t_hb = [T(), T()]

    def load_x(tt):
        P.op("sp", DMA(xt[tt % NX], xsrc[tt * 128:(tt + 1) * 128, :]), writes=[t_xt[tt % NX]])
    for tt in range(min(NX - 1, NT)):
        load_x(tt)
    for tt in range(NT):
        b = tt % 2
        xb, t_xb = xt[tt % NX], t_xt[tt % NX]
        s1 = ss[:, tt:tt + 1]
        if tt + NX - 1 < NT:
            load_x(tt + NX - 1)
        P.op("act", ACT(sq, xb, AF.Square, accum_out=s1), reads=[t_xb], writes=[t_sq, t_ss[tt]])
        P.op("act", ACT(s1, s1, AF.Ln, scale=1.0 / DM, bias=c.epsc), reads=[t_ss[tt], c.t_eps], writes=[t_ss[tt]])
        P.op("act", ACT(s1, s1, AF.Exp, scale=-0.5), reads=[t_ss[tt]], writes=[t_ss[tt]])
        P.op("dve", STT(hb[b], xb, s1, nwb, ALU.mult, ALU.mult), reads=[t_xb, t_ss[tt], t_nwb], writes=[t_hb[b]])
        for k in range(8):
            P.op("pe", TR(c.psb[:, k * 128:(k + 1) * 128], hb[b][:, k * 128:(k + 1) * 128], c.ident),
                 reads=[t_hb[b], c.t_ident], writes=[c.tpsb[k // 4]])
        P.op("act", ACT(c.hdnT[:, :, tt * 128:(tt + 1) * 128], c.psb[:, :].rearrange("p (k t) -> p k t", t=128), AF.Copy),
             reads=[c.tpsb[0], c.tpsb[1]], writes=[c.t_hdnT[tt]])
    A.release(m)


def load_w(c, l, dst, col0, ncols, tiles, step=512, extra=()):
    P = c.P
    j = 0
    for c0 in range(0, ncols, step):
        n = min(step, ncols - c0)
        src = c.w_in[l, :, col0 + c0:col0 + c0 + n].rearrange("(k p) c -> p k c", p=128)
        P.op("poolq", DMA(dst[:, :, c0:c0 + n], src), writes=[tiles[j]] + list(extra))
        j += 1


def pf_qkvg(c, l, which):
    key = (which, l)
    if key not in c.pf:
        tW = [T() for _ in range(4)]
        load_w(c, l, c.W8, C_DIFF if which == "diff" else C_NA, 2048, tW, extra=c.t_Wq)
        c.pf[key] = tW
    return c.pf[key]


def pf_ssd(c, l):
    key = ("ssd", l)
    if key not in c.pf:
        tWz = [[T()], [T()]]
        load_w(c, l, c.Wz[0], C_Z, 512, tWz[0], extra=[c.t_Wq[3]])
        load_w(c, l, c.Wz[1], C_Z + 512, 512, tWz[1], extra=[c.t_Wq[2]])
        c.pf[key] = tWz
    return c.pf[key]


def pf_out(c, l, quarters):
    key = ("out", l)
    if key not in c.pf:
        c.pf[key] = [None] * 4
    tWo = c.pf[key]
    for j in quarters:
        if tWo[j] is None:
            tWo[j] = T()
            src = c.w_out[l, j * 512:(j + 1) * 512, :].rearrange("(k p) c -> p k c", p=128)
            c.P.op("poolq", DMA(c.Wo[:, j * 4:(j + 1) * 4, :], src), writes=[tWo[j], c.t_Wq[j]])
    return tWo


def qk_norm_rope(c, src_ps, t_src, sq, t_sq, ss8, t_ss8, tbuf, t_tb, ubuf, t_ub, TC, TS_, t_tab, tt, dst, t_dst, rope):
    P = c.P
    P.op("act", ACT(sq, src_ps, AF.Square), reads=[t_src], writes=[t_sq])
    P.op("dve", RED(ss8, sq.rearrange("p (g d) -> p g d", d=64)), reads=[t_sq], writes=[t_ss8])
    P.op("act", ACT(ss8, ss8, AF.Ln, scale=1.0 / 64, bias=c.epsc), reads=[t_ss8, c.t_eps], writes=[t_ss8])
    P.op("act", ACT(ss8, ss8, AF.Exp, scale=-0.5), reads=[t_ss8], writes=[t_ss8])
    t3 = tbuf.rearrange("p (g d) -> p g d", d=64)
    P.op("dve", TT(t3, src_ps.rearrange("p (g d) -> p g d", d=64), bc(ss8.unsqueeze(2), [128, 8, 64]), ALU.mult),
         reads=[t_src, t_ss8], writes=[t_tb])
    if rope:
        u3 = ubuf.rearrange("p (g d) -> p g d", d=64)
        P.op("pool", TT(u3[:, :, 0:32], t3[:, :, 32:64], bc(TS_[:, tt, 0:32].unsqueeze(1), [128, 8, 32]), ALU.mult),
             reads=[t_tb, t_tab], writes=[t_ub])
        P.op("pool", TT(u3[:, :, 32:64], t3[:, :, 0:32], bc(TS_[:, tt, 32:64].unsqueeze(1), [128, 8, 32]), ALU.mult),
             reads=[t_tb, t_tab], writes=[t_ub])
        P.op("dve", TT(t3, t3, bc(TC[:, tt, :].unsqueeze(1), [128, 8, 64]), ALU.mult), reads=[t_tb, t_tab], writes=[t_tb])
        P.op("dve", TT(dst, tbuf, ubuf, ALU.add), reads=[t_tb, t_ub], writes=[t_dst])
    else:
        P.op("dve", TT(dst.rearrange("p (g d) -> p g d", d=64), t3, bc(TC.unsqueeze(1), [128, 8, 64]), ALU.mult),
             reads=[t_tb, t_tab], writes=[t_dst])


def prep_qkvg(c, W, tW, qT, t_qT, kT, t_kT, vcopy, t_V, G, t_G, tabs, t_tab, rope):
    P, A = c.P, c.A
    ps, tps = c.ps, c.tps
    NB = 3 if rope else 2
    LAG = NB - 1
    raw = [[A.alloc([128, 512], F32) for _ in range(NB)] for _ in range(2)]
    t_raw = [[T() for _ in range(NB)] for _ in range(2)]
    sq = [A.alloc([128, 512], F32) for _ in range(2)]
    t_sq = [T(), T()]
    ss8 = [[A.alloc([128, 8], F32) for _ in range(NB)] for _ in range(2)]
    t_ss8 = [[T() for _ in range(NB)] for _ in range(2)]
    tbuf = [[A.alloc([128, 512], F32) for _ in range(NB)] for _ in range(2)]
    t_tb = [[T() for _ in range(NB)] for _ in range(2)]
    ubuf = [[A.alloc([128, 512], F32) for _ in range(NB)] for _ in range(2)] if rope else None
    t_ub = [[T() for _ in range(NB)] for _ in range(2)]
    rbuf = [[A.alloc([128, 512], BF16) for _ in range(NB)] for _ in range(2)]
    t_rb = [[T() for _ in range(NB)] for _ in range(2)]
    dsts = ((qT, t_qT), (kT, t_kT))
    nbank = 0
    gbank = {}

    def chain_a(tt, i):
        b = tt % NB
        P.op("act", ACT(sq[i], raw[i][b], AF.Square), reads=[t_raw[i][b]], writes=[t_sq[i]])
        P.op("dve", RED(ss8[i][b], sq[i].rearrange("p (g d) -> p g d", d=64)), reads=[t_sq[i]], writes=[t_ss8[i][b]])

    def chain_b(tt, i):
        b = tt % NB
        s8, t_s8 = ss8[i][b], t_ss8[i][b]
        P.op("act", ACT(s8, s8, AF.Ln, scale=1.0 / 64, bias=c.epsc), reads=[t_s8, c.t_eps], writes=[t_s8])
        P.op("act", ACT(s8, s8, AF.Exp, scale=-0.5), reads=[t_s8], writes=[t_s8])

    def chain_c(tt, i):
        b = tt % NB
        rw, t_rw = raw[i][b], t_raw[i][b]
        s8, t_s8 = ss8[i][b], t_ss8[i][b]
        tb_, t_tb_ = tbuf[i][b], t_tb[i][b]
        t3 = tb_.rearrange("p (g d) -> p g d", d=64)
        P.op("dve", TT(t3, rw.rearrange("p (g d) -> p g d", d=64), bc(s8.unsqueeze(2), [128, 8, 64]), ALU.mult),
             reads=[t_rw, t_s8], writes=[t_tb_])
        dst, t_dst = rbuf[i][b], t_rb[i][b]
        TC, TS_ = tabs[2 * i], tabs[2 * i + 1]
        if rope:
            ub, t_ub_ = ubuf[i][b], t_ub[i][b]
            u3 = ub.rearrange("p (g d) -> p g d", d=64)
            eng_u = "pool" if i == 1 else "dve"
            P.op(eng_u, TT(u3[:, :, 0:32], t3[:, :, 32:64], bc(TS_[:, tt, 0:32].unsqueeze(1), [128, 8, 32]), ALU.mult),
                 reads=[t_tb_, t_tab], writes=[t_ub_])
            P.op(eng_u, TT(u3[:, :, 32:64], t3[:, :, 0:32], bc(TS_[:, tt, 32:64].unsqueeze(1), [128, 8, 32]), ALU.mult),
                 reads=[t_tb_, t_tab], writes=[t_ub_])
            P.op("dve", TT(t3, t3, bc(TC[:, tt, :].unsqueeze(1), [128, 8, 64]), ALU.mult), reads=[t_tb_, t_tab], writes=[t_tb_])
            P.op("dve", TT(dst, tb_, ub, ALU.add), reads=[t_tb_, t_ub_], writes=[t_dst])
        else:
            P.op("dve", TT(dst.rearrange("p (g d) -> p g d", d=64), t3, bc(TC.unsqueeze(1), [128, 8, 64]), ALU.mult),
                 reads=[t_tb_, t_tab], writes=[t_dst])

    def transposes(tt):
        b = tt % NB
        tok = slice(tt * 128, (tt + 1) * 128)
        for i in range(2):
            for h in range(4):
                P.op("pe", TR(c.psb[:, i * 512 + h * 128:i * 512 + (h + 1) * 128], rbuf[i][b][:, h * 128:(h + 1) * 128], c.ident),
                     reads=[t_rb[i][b], c.t_ident], writes=[c.tpsb[i]])
        for i in range(2):
            dstT, t_dstT = dsts[i]
            P.op("dve", CP(dstT[:, :, tok], c.psb[:, i * 512:(i + 1) * 512].rearrange("p (h t) -> p h t", t=128)),
                 reads=[c.tpsb[i]], writes=[t_dstT[tt]])

    for tt in range(NT + LAG):
        if tt < NT:
            tok = slice(tt * 128, (tt + 1) * 128)
            b = tt % NB
            banks = []
            for j in range(4):
                bk = nbank % 7
                nbank += 1
                banks.append(bk)
                for k in range(8):
                    P.op("pe", MM(ps[bk], c.hdnT[:, k, tok], W[:, k, j * 512:(j + 1) * 512], start=(k == 0), stop=(k == 7)),
                         reads=[c.t_hdnT[tt], tW[j]] + c.t_Wq, writes=[tps[bk]])
            for i in range(2):
                if rope:
                    P.op("act", ACT(raw[i][b], ps[banks[i]], AF.Copy), reads=[tps[banks[i]]], writes=[t_raw[i][b]])
                else:
                    P.op("dve", CP(raw[i][b], ps[banks[i]]), reads=[tps[banks[i]]], writes=[t_raw[i][b]])
            vcopy(tt, ps[banks[2]], tps[banks[2]])
            gbank[tt] = banks[3]
            for stage in (chain_a, chain_b, chain_c):
                for i in range(2):
                    stage(tt, i)
            if tt % 2 == 1 or tt == NT - 1:
                for t2 in ([tt - 1, tt] if tt % 2 == 1 else [tt]):
                    P.op("act", ACT(G[:, t2, :], ps[gbank[t2]], AF.Silu), reads=[tps[gbank[t2]]], writes=[t_G[t2]])
        if tt >= LAG:
            transposes(tt - LAG)


def phase_diff(c, l):
    P, A = c.P, c.A
    m = A.mark()
    lam_init = 0.8 - 0.6 * math.exp(-0.3 * l)
    W = c.W8
    tW = pf_qkvg(c, l, "diff")
    qT = A.alloc([128, 4, L], BF16)
    kT = A.alloc([128, 4, L], BF16)
    t_qT = [T() for _ in range(NT)]
    t_kT = [T() for _ in range(NT)]
    V = A.alloc([128, NT, 4, 129], BF16)
    t_V = [T() for _ in range(NT)]
    G = A.alloc([128, NT, 512], BF16)
    t_G = [T() for _ in range(NT)]
    t_misc = T()
    P.op("pool", MEMSET(V[:, :, :, 128:129], 1.0), writes=t_V)
    wqk = A.alloc([128, 128], F32)
    P.op("sp", DMA(wqk, c.diff_qk_norm[l:l + 1, :].partition_broadcast(128)), writes=[t_misc])
    P.op("dve", TS(wqk[:, 0:64], wqk[:, 0:64], 0.125, ALU.mult), reads=[t_misc], writes=[t_misc])
    tabs = A.alloc([128, 4, NT * 64], F32)
    t_tab = T()
    m_cs = A.mark()
    c.cs_sb = A.alloc([128, NT, 128], F32)
    c.t_cs = T()
    P.op("sp", DMA(c.cs_sb, c.cs_tab.rearrange("(t p) c -> p t c", p=128)), writes=[c.t_cs])
    for i, w in enumerate((wqk[:, 0:64], wqk[:, 64:128])):
        TCv = tabs[:, 2 * i, :].rearrange("p (t d) -> p t d", d=64)
        TSv = tabs[:, 2 * i + 1, :].rearrange("p (t d) -> p t d", d=64)
        P.op("dve", TT(TCv, c.cs_sb[:, :, 0:64], bc(w.unsqueeze(1), [128, NT, 64]), ALU.mult), reads=[c.t_cs, t_misc], writes=[t_tab])
        P.op("dve", TT(TSv[:, :, 0:32], c.cs_sb[:, :, 64:96], bc(w[:, 32:64].unsqueeze(1), [128, NT, 32]), ALU.mult),
             reads=[c.t_cs, t_misc], writes=[t_tab])
        P.op("dve", TT(TSv[:, :, 32:64], c.cs_sb[:, :, 96:128], bc(w[:, 0:32].unsqueeze(1), [128, NT, 32]), ALU.mult),
             reads=[c.t_cs, t_misc], writes=[t_tab])
    P.barrier()
    A.release(m_cs)
    TCq = tabs[:, 0, :].rearrange("p (t d) -> p t d", d=64)
    TSq = tabs[:, 1, :].rearrange("p (t d) -> p t d", d=64)
    TCk = tabs[:, 2, :].rearrange("p (t d) -> p t d", d=64)
    TSk = tabs[:, 3, :].rearrange("p (t d) -> p t d", d=64)
    lamb = A.alloc([128, 256], F32)
    t_lam = T()
    P.op("sp", DMA(lamb, c.diff_lambda[l:l + 1, :].partition_broadcast(128)), writes=[t_lam])
    lsc = A.alloc([128, 4], F32)
    lam3 = lamb.rearrange("p (a b d) -> p a b d", a=2, b=2)
    prod = A.alloc([128, 2, 64], F32)
    P.op("dve", TT(prod, lam3[:, :, 0, :], lam3[:, :, 1, :], ALU.mult), reads=[t_lam], writes=[t_lam])
    P.op("dve", RED(lsc[:, 0:2], prod), reads=[t_lam], writes=[t_lam])
    P.op("act", ACT(lsc[:, 0:2], lsc[:, 0:2], AF.Exp), reads=[t_lam], writes=[t_lam])
    P.op("dve", TT(lsc[:, 2:3], lsc[:, 1:2], lsc[:, 0:1], ALU.subtract), reads=[t_lam], writes=[t_lam])
    P.op("dve", TS(lsc[:, 3:4], lsc[:, 2:3], -lam_init, ALU.add), reads=[t_lam], writes=[t_lam])
    neglam = lsc[:, 3:4]
    swb = A.alloc([128, 128], F32)
    t_swb = T()
    P.op("sp", DMA(swb, c.diff_subln[l:l + 1, :].partition_broadcast(128)), writes=[t_swb])
    P.op("dve", TS(swb, swb, 1.0 - lam_init, ALU.mult), reads=[t_swb], writes=[t_swb])

    ps, tps = c.ps, c.tps
    m_prep = A.mark()

    def vcopy(tt, bank, t_bank):
        P.op("act", ACT(V[:, tt, :, 0:128], bank.rearrange("p (h d) -> p h d", d=128), AF.Copy), reads=[t_bank], writes=[t_V[tt]])
    prep_qkvg(c, W, tW, qT, t_qT, kT, t_kT, vcopy, t_V, G, t_G, (TCq, TSq, TCk, TSk), t_tab, True)
    P.barrier()
    A.release(m_prep)
    if "na" in c.phases:
        pf_qkvg(c, l, "na")
    elif "ssd" in c.phases:
        pf_ssd(c, l)
    c.dump("qT%d" % l, qT, [128, 4, L], BF16, t_qT)
    c.dump("kT%d" % l, kT, [128, 4, L], BF16, t_kT)
    c.dump("V%d" % l, V, [128, NT, 4, 129], BF16, t_V)
    c.dump("G%d" % l, G, [128, NT, 512], BF16, t_G)
    c.dump("hdnT%d" % l, c.hdnT, [128, 8, L], BF16, c.t_hdnT)
    NPT = 4
    Pt = [A.alloc([128, 1024], BF16) for _ in range(NPT)]
    t_Pt = [T() for _ in range(NPT)]
    SP = [c.psall[:, 0:1024], c.psall[:, 1024:2048]]
    t_SP = [[tps[0], tps[1]], [tps[2], tps[3]]]
    Ob3 = [ps[4], ps[5], ps[6]]
    t_O3 = [tps[4], tps[5], tps[6]]

    def acc(a):
        return a // 3, (a % 3) * 129
    Osb = A.alloc([128, 3, 512], F32)
    t_Osb = T()
    rr = [A.alloc([128, 16], F32) for _ in range(2)]
    t_rr = [T(), T()]
    ob4 = [A.alloc([128, 4, 128], F32) for _ in range(2)]
    t_ob4 = [T(), T()]
    junk = A.alloc([128, 128], F32)
    t_junk = T()
    yb4 = [A.alloc([128, 512], BF16) for _ in range(2)]
    t_yb4 = [T(), T()]
    yst = [A.alloc([128, 512], BF16) for _ in range(2)]
    t_yst = [T(), T()]
    iters = [(h, qb, kt) for h in range(4) for qb in range(4) for kt in range(NT)]
    NI = len(iters)
    deferred = []

    def emit_S(i):
        h, qb, kt = iters[i]
        for cc in range(2):
            pr = slice(cc * 64, (cc + 1) * 64)
            P.op("pe", MM(SP[i % 2][:, cc * 512:(cc + 1) * 512], kT[pr, h, kt * 128:(kt + 1) * 128], qT[pr, h, qb * 512:(qb + 1) * 512]),
                 reads=[t_kT[kt]] + [t_qT[qb * 4 + j] for j in range(4)], writes=[t_SP[i % 2][cc]])

    def Oslice(a):
        bk, off = acc(a)
        return Osb[:, bk, off:off + 129]

    def fin_stage1(blk, h, qb):
        b = blk % 2
        r_, t_r = rr[b], t_rr[b]
        ob, t_ob = ob4[b], t_ob4[b]
        for j in range(4):
            O0, O1 = Oslice(j), Oslice(4 + j)
            P.op("dve", RECIP(r_[:, 4 * j:4 * j + 1], O0[:, 128:129]), reads=[t_Osb], writes=[t_r])
            P.op("dve", RECIP(r_[:, 4 * j + 1:4 * j + 2], O1[:, 128:129]), reads=[t_Osb], writes=[t_r])
            P.op("dve", TT(r_[:, 4 * j + 2:4 * j + 3], r_[:, 4 * j + 1:4 * j + 2], neglam, ALU.mult), reads=[t_r, t_lam], writes=[t_r])
            P.op("dve", TS(ob[:, j, :], O0[:, 0:128], r_[:, 4 * j:4 * j + 1], ALU.mult), reads=[t_Osb, t_r], writes=[t_ob])
            P.op("dve", STT(ob[:, j, :], O1[:, 0:128], r_[:, 4 * j + 2:4 * j + 3], ob[:, j, :], ALU.mult, ALU.add),
                 reads=[t_Osb, t_r, t_ob], writes=[t_ob])

    def fin_stage2(blk, h, qb):
        b = blk % 2
        r_, t_r = rr[b], t_rr[b]
        ob, t_ob = ob4[b], t_ob4[b]
        for j in range(4):
            P.op("act", ACT(junk, ob[:, j, :], AF.Square, accum_out=r_[:, 4 * j + 3:4 * j + 4]), reads=[t_ob], writes=[t_junk, t_r])
        r3 = r_.rearrange("p (j k) -> p j k", k=4)[:, :, 3]
        P.op("act", ACT(r3, r3, AF.Ln, scale=1.0 / 128, bias=c.epsc), reads=[t_r, c.t_eps], writes=[t_r])
        P.op("act", ACT(r3, r3, AF.Exp, scale=-0.5), reads=[t_r], writes=[t_r])

    def fin_stage3(blk, h, qb):
        b = blk % 2
        r_, t_r = rr[b], t_rr[b]
        ob, t_ob = ob4[b], t_ob4[b]
        yb, t_yb = yb4[b], t_yb4[b]
        for j in range(4):
            tt = qb * 4 + j
            P.op("dve", STT(ob[:, j, :], ob[:, j, :], r_[:, 4 * j + 3:4 * j + 4], swb, ALU.mult, ALU.mult), reads=[t_ob, t_r, t_swb], writes=[t_ob])
            P.op("dve", TT(yb[:, j * 128:(j + 1) * 128], ob[:, j, :], G[:, tt, h * 128:(h + 1) * 128], ALU.mult),
                 reads=[t_ob, t_G[tt]], writes=[t_yb])
        for j in range(4):
            P.op("pe", TR(c.psb[:, j * 128:(j + 1) * 128], yb[:, j * 128:(j + 1) * 128], c.ident),
                 reads=[t_yb, c.t_ident], writes=[c.tpsb[0]])

    def fin_stage4(blk, h, qb):
        b = blk % 2
        ys, t_ys = yst[b], t_yst[b]
        P.op("dve", CP(ys, c.psb[:, 0:512]), reads=[c.tpsb[0]], writes=[t_ys])
        P.op("sp", DMA(c.yT[l][1024 + h * 128:1024 + (h + 1) * 128, qb * 512:(qb + 1) * 512], ys), reads=[t_ys])

    emit_S(0)
    blk = 0
    for i in range(NI):
        h, qb, kt = iters[i]
        if i + 1 < NI:
            emit_S(i + 1)
        pt, t_pt = Pt[i % NPT], t_Pt[i % NPT]
        P.op("act", ACT(pt, SP[i % 2], AF.Exp), reads=t_SP[i % 2], writes=[t_pt])
        for cc in range(2):
            for j in range(4):
                bk, off = acc(cc * 4 + j)
                P.op("pe", MM(Ob3[bk][:, off:off + 129], pt[:, cc * 512 + j * 128:cc * 512 + (j + 1) * 128],
                              V[:, kt, h, :], start=(kt == 0 and off == 0), stop=(kt == NT - 1), skip=True),
                     reads=[t_pt, t_V[kt]], writes=[t_O3[bk]])
        for (due, fn) in [d for d in deferred if d[0] <= i]:
            fn()
        deferred = [d for d in deferred if d[0] > i]
        if kt == NT - 1:
            for k3 in range(3):
                P.op("dve", CP(Osb[:, k3, 0:387], Ob3[k3][:, 0:387]), reads=[t_O3[k3]], writes=[t_Osb])
            fin_stage1(blk, h, qb)
            deferred.append((i + 2, (lambda b_=blk, h_=h, q_=qb: fin_stage2(b_, h_, q_))))
            deferred.append((i + 4, (lambda b_=blk, h_=h, q_=qb: fin_stage3(b_, h_, q_))))
            deferred.append((i + 6, (lambda b_=blk, h_=h, q_=qb: fin_stage4(b_, h_, q_))))
            blk += 1
    for (due, fn) in deferred:
        fn()
    A.release(m)


def _na_classes():
    rows, W, kh, kw = 32, 64, 8, 16
    rs = lambda r: min(max(r - kh // 2, 0), rows - kh)
    types = {}
    keys = []
    plan = []
    for i in range(16):
        qrows = [2 * i, 2 * i + 1]
        lo = min(rs(r) for r in qrows)
        hi = max(rs(r) + kh - 1 for r in qrows)
        tkeys = []
        kbs = list(range(lo // 2, hi // 2 + 1))
        for kb in kbs:
            key = []
            for b in range(2):
                for a in range(2):
                    kr, qr = 2 * kb + b, 2 * i + a
                    ok = rs(qr) <= kr < rs(qr) + kh
                    key.append((kr - qr + 7) if ok else -1)
            tkeys.append(tuple(key))
        tkeys = tuple(tkeys)
        if tkeys not in types:
            types[tkeys] = len(keys)
            keys.extend(tkeys)
        base = types[tkeys]
        plan.append([(kb, base + bi) for bi, kb in enumerate(kbs)])
    return keys, plan


NA_KEYS, NA_PLAN = _na_classes()
NA_NCLS = len(NA_KEYS)


def _na_tables(rpb):
    W, kw = 64, 16
    cidx = np.arange(W)
    col_start = np.clip(cidx - kw // 2, 0, W - kw)
    col_ok = (cidx[None, :] >= col_start[:, None]) & (cidx[None, :] < col_start[:, None] + kw)
    dc = np.clip(cidx[None, :] - cidx[:, None] + (kw - 1), 0, 2 * kw - 2)
    out = np.full((2, 8, 128, NA_NCLS, 128), NEG, dtype=np.float32)
    for ci, key in enumerate(NA_KEYS):
        n = 0
        for b in range(2):
            for a in range(2):
                dr = key[n]
                n += 1
                if dr < 0:
                    continue
                blkv = rpb[:, :, dr, :][:, :, dc]
                blkv = np.where(col_ok[None, None], blkv, np.float32(NEG))
                out[:, :, b * 64:(b + 1) * 64, ci, a * 64:(a + 1) * 64] = np.transpose(blkv, (0, 1, 3, 2))
    return np.ascontiguousarray(out.reshape(2, 8, 128, NA_NCLS * 128))


def phase_na(c, l):
    P, A = c.P, c.A
    m = A.mark()
    W = c.W8
    tW = pf_qkvg(c, l, "na")
    qT = A.alloc([128, 4, L], BF16)
    kT = A.alloc([128, 4, L], BF16)
    t_qT = [T() for _ in range(NT)]
    t_kT = [T() for _ in range(NT)]
    V = A.alloc([128, NT, 8, 65], BF16)
    t_V = [T() for _ in range(NT)]
    G = A.alloc([128, NT, 512], BF16)
    t_G = [T() for _ in range(NT)]
    Y = A.alloc([128, NT, 512], BF16)
    t_Y = [T() for _ in range(NT)]
    P.op("pool", MEMSET(V[:, :, :, 64:65], 1.0), writes=t_V)
    t_misc = T()
    wqk = A.alloc([128, 128], F32)
    P.op("sp", DMA(wqk, c.na_qk_norm[l:l + 1, :].partition_broadcast(128)), writes=[t_misc])
    P.op("dve", TS(wqk[:, 0:64], wqk[:, 0:64], 0.125, ALU.mult), reads=[t_misc], writes=[t_misc])
    ps, tps = c.ps, c.tps
    m_prep = A.mark()

    def vcopy(tt, bank, t_bank):
        P.op("act", ACT(V[:, tt, :, 0:64], bank.rearrange("p (h d) -> p h d", d=64), AF.Copy), reads=[t_bank], writes=[t_V[tt]])
    prep_qkvg(c, W, tW, qT, t_qT, kT, t_kT, vcopy, t_V, G, t_G, (wqk[:, 0:64], None, wqk[:, 64:128], None), t_misc, False)
    P.barrier()
    A.release(m_prep)
    if "ssd" in c.phases:
        pf_ssd(c, l)
    pf_out(c, l, [0, 1])
    tab = [A.alloc([128, NA_NCLS * 128], F32) for _ in range(2)]
    t_tab = [T(), T()]
    Tb = [A.alloc([128, 640], F32) for _ in range(2)]
    t_Tb = [T(), T()]
    Pb = [A.alloc([128, 640], BF16) for _ in range(3)]
    t_Pb = [T(), T(), T()]
    rr = [A.alloc([128, 2], F32) for _ in range(2)]
    t_rr = [T(), T()]
    its = [(hh, i) for hh in range(8) for i in range(NT)]
    NI = len(its)
    loaded = set()

    def load_tab(hh):
        if hh in loaded or hh >= 8:
            return
        loaded.add(hh)
        P.op("sp", DMA(tab[hh % 2], c.na_tab[l, hh, :, :]), writes=[t_tab[hh % 2]])
        P.op("act", ACT(tab[hh % 2], tab[hh % 2], AF.Exp), reads=[], writes=[t_tab[hh % 2]])

    def emit_S(n):
        hh, i = its[n]
        jb, e = hh // 2, hh % 2
        pr = slice(e * 64, (e + 1) * 64)
        S0, S1 = ps[(n % 2) * 2], ps[(n % 2) * 2 + 1]
        tS = [tps[(n % 2) * 2], tps[(n % 2) * 2 + 1]]
        for bi, (kb, cls) in enumerate(NA_PLAN[i]):
            dstS = (S0 if bi < 4 else S1)[:, (bi % 4) * 128:(bi % 4 + 1) * 128]
            P.op("pe", MM(dstS, kT[pr, jb, kb * 128:(kb + 1) * 128], qT[pr, jb, i * 128:(i + 1) * 128]),
                 reads=[t_kT[kb], t_qT[i]], writes=[tS[bi // 4]])

    def emit_exp_mul(n):
        hh, i = its[n]
        plan = NA_PLAN[i]
        nb = len(plan)
        base = plan[0][1]
        tS = [tps[(n % 2) * 2], tps[(n % 2) * 2 + 1]]
        Sboth = c.psall[:, (n % 2) * 1024:(n % 2) * 1024 + nb * 128]
        T_, t_T = Tb[n % 2], t_Tb[n % 2]
        P_, t_P = Pb[n % 3], t_Pb[n % 3]
        P.op("act", ACT(T_[:, 0:nb * 128], Sboth, AF.Exp), reads=(tS if nb > 4 else tS[0:1]), writes=[t_T])
        P.op("dve", TT(P_[:, 0:nb * 128], T_[:, 0:nb * 128], tab[hh % 2][:, base * 128:(base + nb) * 128], ALU.mult),
             reads=[t_T, t_tab[hh % 2]], writes=[t_P])

    load_tab(0)
    emit_S(0)
    if NI > 1:
        emit_S(1)
    emit_exp_mul(0)
    for n, (hh, i) in enumerate(its):
        if i == 2:
            load_tab(hh + 1)
        plan = NA_PLAN[i]
        nb = len(plan)
        Ob, t_Ob = ps[4 + n % 2], tps[4 + n % 2]
        P_, t_P = Pb[n % 3], t_Pb[n % 3]
        for bi, (kb, cls) in enumerate(plan):
            P.op("pe", MM(Ob[:, 0:65], P_[:, bi * 128:(bi + 1) * 128], V[:, kb, hh, :], start=(bi == 0), stop=(bi == nb - 1)),
                 reads=[t_P, t_V[kb]], writes=[t_Ob])
        if n + 2 < NI:
            emit_S(n + 2)
        if n + 1 < NI:
            emit_exp_mul(n + 1)
        r_, t_r = rr[n % 2], t_rr[n % 2]
        P.op("dve", RECIP(r_[:, 0:1], Ob[:, 64:65]), reads=[t_Ob], writes=[t_r])
        P.op("dve", STT(Y[:, i, hh * 64:(hh + 1) * 64], Ob[:, 0:64], r_[:, 0:1], G[:, i, hh * 64:(hh + 1) * 64], ALU.mult, ALU.mult),
             reads=[t_Ob, t_r, t_G[i]], writes=[t_Y[i]])
    yst = [A.alloc([128, 512], BF16) for _ in range(2)]
    t_yst = [T(), T()]
    blk = 0
    for qb in range(4):
        for j4 in range(4):
            for j in range(4):
                tt = qb * 4 + j
                P.op("pe", TR(c.psb[:, j * 128:(j + 1) * 128], Y[:, tt, j4 * 128:(j4 + 1) * 128], c.ident),
                     reads=[t_Y[tt], c.t_ident], writes=[c.tpsb[0]])
            ys, t_ys = yst[blk % 2], t_yst[blk % 2]
            blk += 1
            P.op("act", ACT(ys, c.psb[:, 0:512], AF.Copy), reads=[c.tpsb[0]], writes=[t_ys])
            P.op("sp", DMA(c.yT[l][1536 + j4 * 128:1536 + (j4 + 1) * 128, qb * 512:(qb + 1) * 512], ys), reads=[t_ys])
    A.release(m)


def phase_ssd(c, l):
    STOP = 9
    P, A = c.P, c.A
    m = A.mark()
    ps, tps = c.ps, c.tps
    nc = c.nc
    Wdt = A.alloc([128, 8, 32], BF16)
    tWdt = [T()]
    load_w(c, l, Wdt, C_DT, 32, tWdt)
    t_sm = T()
    dtb = A.alloc([128, 32], F32)
    alog = A.alloc([128, 32], F32)
    dsk = A.alloc([128, 32], F32)
    P.op("sp", DMA(dtb, c.dt_bias[l:l + 1, :].partition_broadcast(128)), writes=[t_sm])
    t_al = T()
    P.op("sp", DMA(alog, c.a_log[l:l + 1, :].partition_broadcast(128)), writes=[t_al])
    t_dsk = T()
    P.op("sp", DMA(dsk, c.d_skip[l:l + 1, :].partition_broadcast(128)), writes=[t_dsk])
    dsum = A.alloc([128, 16], F32)
    P.op("dve", TT(dsum, dsk[:, 0:16], dsk[:, 16:32], ALU.add), reads=[t_dsk], writes=[t_dsk])
    P.op("act", ACT(alog, alog, AF.Exp), reads=[t_al], writes=[t_al])
    P.op("dve", TS(alog, alog, -1.0, ALU.mult), reads=[t_al], writes=[t_al])
    snw = A.alloc([128, 1024], F32)
    t_snw = T()
    P.op("sp", DMA(snw, c.ssd_norm_w[l:l + 1, :].partition_broadcast(128)), writes=[t_snw])
    cbb = A.alloc([128, 1536], F32)
    t_cbb = T()
    P.op("sp", DMA(cbb, c.conv_b[l:l + 1, :].partition_broadcast(128)), writes=[t_cbb])
    cwT = A.alloc([128, 12, 6], F32)
    t_cwT = T()

    def a3(shape=(128, NT, 32)):
        return A.alloc(list(shape), F32)
    dt = a3()
    dta = a3()
    cs = a3()
    dcb = a3()
    dout = a3()
    dend = a3()
    cXd = a3()
    t_dt, t_dta, t_cs, t_dcb, t_dout, t_dend, t_cXd, t_v, t_tmp3 = [T() for _ in range(9)]
    t_csd = T()
    csT_v = c.csT_d.rearrange("t (d h) l -> t d h l", d=2)
    m_tmp = A.mark()
    cw6 = A.alloc([128, 1536], F32)
    t_cw6 = T()
    P.op("sp", DMA(cw6[0:5, :], c.conv_w[l, :, :]), writes=[t_cw6])
    P.op("sp", DMA(cw6[5:6, :], c.conv_b[l:l + 1, :]), writes=[t_cw6])
    for j in range(12):
        P.op("pe", TR(ps[6][:, j * 6:(j + 1) * 6], cw6[0:6, j * 128:(j + 1) * 128], c.identf[0:6, 0:6]),
             reads=[t_cw6, c.t_cst], writes=[tps[6]])
    P.op("dve", CP(cwT, ps[6][:, 0:72].rearrange("p (j k) -> p j k", k=6)), reads=[tps[6]], writes=[t_cwT])
    P.barrier()
    A.release(m_tmp)

    def make_decay_stages(v, tmp3, csT):
        t_csT = T()

        def s1():
            for tt in range(NT):
                for k in range(8):
                    P.op("pe", MM(ps[4][:, tt * 32:(tt + 1) * 32], c.hdnT[:, k, tt * 128:(tt + 1) * 128], Wdt[:, k, :],
                                  start=(k == 0), stop=(k == 7), skip=True),
                         reads=[c.t_hdnT[tt], tWdt[0]], writes=[tps[4]])
            p0 = ps[4].rearrange("p (t h) -> p t h", h=32)
            P.op("dve", TT(v, p0, bc(dtb.unsqueeze(1), [128, NT, 32]), ALU.add), reads=[tps[4], t_sm], writes=[t_v])
            P.op("act", ACT(tmp3, v, AF.Abs), reads=[t_v], writes=[t_tmp3])
            P.op("act", ACT(tmp3, tmp3, AF.Exp, scale=-1.0), reads=[t_tmp3], writes=[t_tmp3])
            P.op("act", ACT(tmp3, tmp3, AF.Ln, bias=c.onec, scale=1.0), reads=[t_tmp3, c.t_eps], writes=[t_tmp3])

        def s2():
            P.op("dve", STT(dt, v, 0.0, tmp3, ALU.max, ALU.add), reads=[t_v, t_tmp3], writes=[t_dt])
            P.op("dve", TT(dta, dt, bc(alog.unsqueeze(1), [128, NT, 32]), ALU.mult), reads=[t_dt, t_al], writes=[t_dta])
            for tt in range(NT):
                P.op("pe", MM(ps[5][:, tt * 32:tt * 32 + 16], c.U, dta[:, tt, 0:16], skip=True), reads=[t_dta, c.t_cst], writes=[tps[5]])
                P.op("pe", MM(ps[5][:, tt * 32 + 16:tt * 32 + 32], c.Ur, dta[:, tt, 16:32], skip=True), reads=[t_dta, c.t_cst], writes=[tps[5]])
            for tt in range(NT):
                P.op("pe", MM(ps[6][:, tt * 32:(tt + 1) * 32], c.onesf, dta[:, tt, :], skip=True), reads=[t_dta, c.t_cst], writes=[tps[6]])
            P.op("act", ACT(cs, ps[5].rearrange("p (t h) -> p t h", h=32), AF.Copy), reads=[tps[5]], writes=[t_cs])
            P.op("act", ACT(tmp3, ps[6].rearrange("p (t h) -> p t h", h=32), AF.Copy), reads=[tps[6]], writes=[t_tmp3])

        def s3():
            P.op("act", ACT(dcb, tmp3, AF.Exp), reads=[t_tmp3], writes=[t_dcb])
            P.op("act", ACT(dout, cs, AF.Exp), reads=[t_cs], writes=[t_dout])
            P.op("dve", TT(dend, tmp3, cs, ALU.subtract), reads=[t_tmp3, t_cs], writes=[t_dend])
            P.op("act", ACT(dend, dend, AF.Exp), reads=[t_dend], writes=[t_dend])
            P.op("dve", TT(cXd, dt, dend, ALU.mult), reads=[t_dt, t_dend], writes=[t_cXd])

        def s4():
            for q in range(4):
                for j in range(4):
                    tt = q * 4 + j
                    P.op("pe", TR(ps[4][0:32, j * 128:(j + 1) * 128], cs[:, tt, :], c.identf), reads=[t_cs, c.t_cst], writes=[tps[4]])
                P.op("act", ACT(csT[0:32, q * 4:(q + 1) * 4, :], ps[4][0:32, :].rearrange("p (j t) -> p j t", t=128), AF.Copy),
                     reads=[tps[4]], writes=[t_csT])
            P.op("sp", DMA(c.csT_d.rearrange("t h l -> h t l"), csT[0:32, :, :]), reads=[t_csT], writes=[t_csd])
        return [s1, s2, s3, s4]

    for g in range(2):
        mg = A.mark()
        Wz = c.Wz[g]
        tWz = pf_ssd(c, l)[g]
        t_Wzq = c.t_Wq[3 - g]
        xs = A.alloc([128, NT, 512], F32)
        t_xs = [T() for _ in range(NT)]
        Btok = A.alloc([128, NT, 128], BF16)
        t_Btok = [T() for _ in range(NT)]
        BT = A.alloc([128, L], BF16)
        t_BT = [T() for _ in range(4)]
        CT = A.alloc([128, L], BF16)
        t_CT = [T() for _ in range(4)]
        m_conv = A.mark()
        stages = []
        if g == 0:
            stages = make_decay_stages(A.alloc([128, NT, 32], F32), A.alloc([128, NT, 32], F32), A.alloc([128, NT, 128], F32))
        pre = [A.alloc([128, L + 4], BF16) for _ in range(2)]
        t_pre = [[T() for _ in range(4)] for _ in range(2)]
        t_halo = [T(), T()]
        for b in range(2):
            P.op("pool", MEMSET(pre[b][:, 0:2], 0.0), writes=[t_halo[b]])
            P.op("pool", MEMSET(pre[b][:, L + 2:L + 4], 0.0), writes=[t_halo[b]])
        Wc = [A.alloc([128, 8, 128], BF16) for _ in range(2)]
        tWc = [[T()], [T()]]
        dg = [A.alloc([128, 5, 128], BF16) for _ in range(2)]
        t_dg = [T(), T()]
        ctmp = [A.alloc([128, 512], F32) for _ in range(2)]
        t_ctmp = [T(), T()]
        chunks = [("B", 8 + g), ("C", 10 + g)] + [("x%d" % j, 4 * g + j) for j in range(4)]
        pbank = 0
        for n, (kind, ci) in enumerate(chunks):
            b = n % 2
            if n >= 1 and stages:
                stages.pop(0)()
            load_w(c, l, Wc[b], C_XBC + ci * 128, 128, tWc[b])
            P.op("dve", TT(dg[b], bc(c.identf.unsqueeze(1), [128, 5, 128]), bc(cwT[:, ci, 0:5].unsqueeze(2), [128, 5, 128]), ALU.mult),
                 reads=[c.t_cst, t_cwT], writes=[t_dg[b]])
            for tb in range(4):
                bank, tbk = ps[pbank % 4], tps[pbank % 4]
                pbank += 1
                for k in range(8):
                    P.op("pe", MM(bank, Wc[b][:, k, :], c.hdnT[:, k, tb * 512:(tb + 1) * 512], start=(k == 0), stop=(k == 7)),
                         reads=[tWc[b][0]] + c.t_hdnT[tb * 4:(tb + 1) * 4], writes=[tbk])
                P.op("act", ACT(pre[b][:, 2 + tb * 512:2 + (tb + 1) * 512], bank, AF.Copy), reads=[tbk], writes=[t_pre[b][tb]])
            allpre = t_pre[b] + [t_halo[b]]
            if kind in ("B", "C"):
                dstT, t_dstT = (BT, t_BT) if kind == "B" else (CT, t_CT)
                for tb in range(4):
                    bank, tbk = ps[pbank % 4], tps[pbank % 4]
                    pbank += 1
                    for k in range(5):
                        P.op("pe", MM(bank, dg[b][:, k, :], pre[b][:, tb * 512 + k:tb * 512 + k + 512], start=(k == 0), stop=(k == 4)),
                             reads=[t_dg[b]] + allpre, writes=[tbk])
                    P.op("act", ACT(dstT[:, tb * 512:(tb + 1) * 512], bank, AF.Silu, bias=cwT[:, ci, 5:6], scale=1.0),
                         reads=[tbk, t_cwT], writes=[t_dstT[tb]])
            if kind != "C":
                for q in range(4):
                    bank, tbk = ps[pbank % 4], tps[pbank % 4]
                    pbank += 1
                    for j in range(4):
                        tt = q * 4 + j
                        for k in range(5):
                            P.op("pe", MM(bank[:, j * 128:(j + 1) * 128], pre[b][:, tt * 128 + k:tt * 128 + k + 128], dg[b][:, k, :],
                                          start=(k == 0 and j == 0), stop=(k == 4), skip=True),
                                 reads=[t_dg[b]] + allpre, writes=[tbk])
                    ct, t_ct = ctmp[q % 2], t_ctmp[q % 2]
                    P.op("dve", TT(ct.rearrange("p (j c) -> p j c", c=128), bank.rearrange("p (j c) -> p j c", c=128),
                                   bc(cbb[:, ci * 128:(ci + 1) * 128].unsqueeze(1), [128, 4, 128]), ALU.add),
                         reads=[tbk, t_cbb], writes=[t_ct])
                    if kind == "B":
                        P.op("act", ACT(Btok[:, q * 4:(q + 1) * 4, :], ct.rearrange("p (j c) -> p j c", c=128), AF.Silu),
                             reads=[t_ct], writes=t_Btok[q * 4:(q + 1) * 4])
                    else:
                        jx = int(kind[1])
                        P.op("act", ACT(xs[:, q * 4:(q + 1) * 4, jx * 128:(jx + 1) * 128], ct.rearrange("p (j c) -> p j c", c=128), AF.Silu),
                             reads=[t_ct], writes=t_xs[q * 4:(q + 1) * 4])

        for st_ in stages:
            st_()
        stages = []
        P.barrier()
        A.release(m_conv)
        if STOP <= 2:
            A.release(mg)
            continue
        hb0 = 16 + 8 * g
        hf0 = 8 * g
        m_passA = A.mark()
        Sst = [A.alloc([128, 512], F32) for _ in range(2)]
        t_Sst = [T(), T()]
        for d in range(2):
            P.op("pool", MEMSET(Sst[d], 0.0), writes=[t_Sst[d]])
        stg = [[A.alloc([128, 512], BF16) for _ in range(2)] for _ in range(2)]
        t_stg = [[T(), T()], [T(), T()]]
        Xd = [[A.alloc([128, 512], BF16) for _ in range(2)] for _ in range(2)]
        t_Xd = [[T(), T()], [T(), T()]]
        t_sd = [[T() for _ in range(NT)] for _ in range(2)]
        sdram = [c.sf_d, c.sb_d]
        h0s = [hf0, hb0]

        def bh(ap3, h0):
            return bc(ap3[:, h0:h0 + 8].unsqueeze(2), [128, 8, 64])

        def v8(ap):
            return ap.rearrange("p (h d) -> p h d", d=64)
        for k in range(NT):
            for d in range(2):
                ci_ = k if d == 0 else NT - 1 - k
                last = (ci_ == NT - 1) if d == 0 else (ci_ == 0)
                sg, t_sg = stg[d][k % 2], t_stg[d][k % 2]
                P.op("act", ACT(sg, Sst[d], AF.Copy), reads=[t_Sst[d]], writes=[t_sg])
                P.op("sp", DMA(sdram[d][g, ci_], sg), reads=[t_sg], writes=[t_sd[d][ci_]])
                if not last:
                    xd, t_xd = Xd[d][k % 2], t_Xd[d][k % 2]
                    P.op("pool", TT(v8(xd), v8(xs[:, ci_, :]), bh(cXd[:, ci_, :], h0s[d]), ALU.mult), reads=[t_xs[ci_], t_cXd], writes=[t_xd])
                    bank, tbk = ps[4 + d], tps[4 + d]
                    P.op("pe", MM(bank, Btok[:, ci_, :], xd), reads=[t_Btok[ci_], t_xd], writes=[tbk])
                    P.op("dve", TT(v8(Sst[d]), v8(Sst[d]), bh(dcb[:, ci_, :], h0s[d]), ALU.mult), reads=[t_Sst[d], t_dcb], writes=[t_Sst[d]])
                    P.op("dve", TT(Sst[d], Sst[d], bank, ALU.add), reads=[t_Sst[d], tbk], writes=[t_Sst[d]])
        P.barrier()
        A.release(m_passA)
        if STOP <= 3:
            A.release(mg)
            continue
        R = [A.alloc([128, 2, 8, 128], F32) for _ in range(2)]
        t_R = [T(), T()]
        SFc = [A.alloc([128, 512], BF16) for _ in range(2)]
        t_SFc = [T(), T()]
        SBc = [A.alloc([128, 512], BF16) for _ in range(2)]
        t_SBc = [T(), T()]
        E = A.alloc([128, 2, 8, 128], BF16)
        t_E = T()
        Mt = [A.alloc([128, 2, 8, 128], BF16) for _ in range(2)]
        t_Mt = [T(), T()]
        Gm = [A.alloc([128, 2, 128], BF16) for _ in range(2)]
        t_Gm = [T(), T()]
        Xt = [A.alloc([128, 3, 512], BF16) for _ in range(2)]
        t_Xt = [T(), T()]
        sz = [A.alloc([128, 512], F32) for _ in range(4)]
        t_sz = [T() for _ in range(4)]
        ya = A.alloc([128, 512], F32)
        yb_ = A.alloc([128, 512], F32)
        yy = A.alloc([128, 512], F32)
        t_ya, t_yb, t_yy = T(), T(), T()
        ssn = A.alloc([128, 2], F32)
        t_ssn = T()
        yo = [A.alloc([128, 512], BF16) for _ in range(2)]
        t_yo = [T(), T()]
        yst = [A.alloc([128, 4, 128], BF16) for _ in range(2)]
        t_yst = [T(), T()]

        def loadR(ci_):
            b = ci_ % 2
            P.op("sp", DMA(R[b].rearrange("p d h l -> p d (h l)"),
                           csT_v[ci_, :, 8 * g:8 * g + 8, :].rearrange("d h l -> d (h l)").partition_broadcast(128)),
                 reads=[t_csd], writes=[t_R[b]])

        def loadS(ci_):
            b = ci_ % 2
            P.op("sp", DMA(SFc[b], c.sf_d[g, ci_]), reads=[t_sd[0][ci_]], writes=[t_SFc[b]])
            P.op("sp", DMA(SBc[b], c.sb_d[g, ci_]), reads=[t_sd[1][ci_]], writes=[t_SBc[b]])

        def iteration(cn, cc_):
            if cn is not None:
                bn = cn % 2
                tokn = slice(cn * 128, (cn + 1) * 128)
                P.op("pe", MM(ps[4][:, 0:128], BT[:, tokn], CT[:, tokn]), reads=[t_BT[cn // 4], t_CT[cn // 4]], writes=[tps[4]])
                if cn % 2 == 0:
                    for c2 in (cn, cn + 1):
                        if c2 < NT:
                            zb, t_zb = (ps[3], tps[3]) if c2 % 2 == 0 else (ps[6], tps[6])
                            tok2 = slice(c2 * 128, (c2 + 1) * 128)
                            for k in range(8):
                                P.op("pe", MM(zb, c.hdnT[:, k, tok2], Wz[:, k, :], start=(k == 0), stop=(k == 7)),
                                     reads=[c.t_hdnT[c2], tWz[0], t_Wzq], writes=[t_zb])
                P.op("dve", TT(Gm[bn][:, 0, :], ps[4][:, 0:128], c.U, ALU.mult), reads=[tps[4], c.t_cst], writes=[t_Gm[bn]])
                P.op("dve", TT(Gm[bn][:, 1, :], ps[4][:, 0:128], c.Ur, ALU.mult), reads=[tps[4], c.t_cst], writes=[t_Gm[bn]])
                csv = cs[:, cn, :].rearrange("p (d h) -> p d h", d=2)[:, :, 8 * g:8 * g + 8]
                Dd, t_D = R[bn], t_R[bn]
                P.op("dve", TT(Dd, Dd, bc(csv.unsqueeze(3), [128, 2, 8, 128]), ALU.subtract), reads=[t_cs], writes=[t_D])
                P.op("act", ACT(Dd, Dd, AF.Relu, scale=-1.0), reads=[], writes=[t_D])
                P.op("act", ACT(E, Dd, AF.Exp, scale=-1.0), reads=[t_D], writes=[t_E])
                P.op("pool", TT(v8(Xt[bn][:, 0, :]), v8(xs[:, cn, :]), bh(dt[:, cn, :], hf0), ALU.mult), reads=[t_xs[cn], t_dt], writes=[t_Xt[bn]])
                P.op("pool", TT(v8(Xt[bn][:, 1, :]), v8(xs[:, cn, :]), bh(dt[:, cn, :], hb0), ALU.mult), reads=[t_xs[cn], t_dt], writes=[t_Xt[bn]])
                P.op("pool", TT(v8(Xt[bn][:, 2, :]), v8(xs[:, cn, :]), bh(dsum, 8 * g), ALU.mult), reads=[t_xs[cn], t_dsk], writes=[t_Xt[bn]])
            if cc_ is not None:
                b = cc_ % 2
                tok = slice(cc_ * 128, (cc_ + 1) * 128)
                Yb_, t_Y = (ps[0], tps[0]) if b == 0 else (ps[5], tps[5])
                P.op("pe", MM(Yb_, c.ident, Xt[b][:, 2, :], start=True, stop=False, skip=True), reads=[c.t_ident, t_Xt[b]], writes=[t_Y])
                for h in range(8):
                    for d in range(2):
                        P.op("pe", MM(Yb_[:, h * 64:(h + 1) * 64], Mt[b][:, d, h, :], Xt[b][:, d, h * 64:(h + 1) * 64],
                                      start=False, stop=(d == 1), skip=True),
                             reads=[t_Mt[b], t_Xt[b]], writes=[t_Y])
                P.op("pe", MM(ps[1], CT[:, tok], SFc[b]), reads=[t_CT[cc_ // 4], t_SFc[b]], writes=[tps[1]])
                P.op("pe", MM(ps[2], CT[:, tok], SBc[b]), reads=[t_CT[cc_ // 4], t_SBc[b]], writes=[tps[2]])
                P.op("dve", TT(v8(ya), v8(ps[1]), bh(dout[:, cc_, :], hf0), ALU.mult), reads=[tps[1], t_dout], writes=[t_ya])
                P.op("dve", TT(v8(yb_), v8(ps[2]), bh(dout[:, cc_, :], hb0), ALU.mult), reads=[tps[2], t_dout], writes=[t_yb])
                P.op("dve", TT(ya, ya, yb_, ALU.add), reads=[t_ya, t_yb], writes=[t_ya])
                P.op("dve", TT(yy, Yb_, ya, ALU.add), reads=[t_Y, t_ya], writes=[t_yy])
                P.op("pool", TT(yy, yy, sz[cc_ % 4], ALU.mult), reads=[t_yy, t_sz[cc_ % 4]], writes=[t_yy])
            if cn is not None:
                for d in range(2):
                    P.op("dve", TT(Mt[bn][:, d], E[:, d], bc(Gm[bn][:, d, :].unsqueeze(1), [128, 8, 128]), ALU.mult),
                         reads=[t_E, t_Gm[bn]], writes=[t_Mt[bn]])
                if cn % 2 == 0:
                    for c2 in (cn, cn + 1):
                        if c2 < NT:
                            zb, t_zb = (ps[3], tps[3]) if c2 % 2 == 0 else (ps[6], tps[6])
                            P.op("act", ACT(sz[c2 % 4], zb, AF.Silu), reads=[t_zb], writes=[t_sz[c2 % 4]])
            if cc_ is not None:
                P.op("act", ACT(ya, yy, AF.Square, accum_out=ssn[:, 0:1]), reads=[t_yy], writes=[t_ya, t_ssn])
                P.op("act", ACT(ssn[:, 0:1], ssn[:, 0:1], AF.Ln, scale=1.0 / 512, bias=c.epsc), reads=[t_ssn, c.t_eps], writes=[t_ssn])
                P.op("act", ACT(ssn[:, 0:1], ssn[:, 0:1], AF.Exp, scale=-0.5), reads=[t_ssn], writes=[t_ssn])
                P.op("dve", STT(yo[b], yy, ssn[:, 0:1], snw[:, g * 512:(g + 1) * 512], ALU.mult, ALU.mult),
                     reads=[t_yy, t_ssn, t_snw], writes=[t_yo[b]])

        def stageC(ci_):
            b = ci_ % 2
            tok = slice(ci_ * 128, (ci_ + 1) * 128)
            for j in range(4):
                P.op("pe", TR(c.psb[:, j * 128:(j + 1) * 128], yo[b][:, j * 128:(j + 1) * 128], c.ident),
                     reads=[t_yo[b], c.t_ident], writes=[c.tpsb[0]])
            P.op("dve", CP(yst[b], c.psb[:, 0:512].rearrange("p (j t) -> p j t", t=128)), reads=[c.tpsb[0]], writes=[t_yst[b]])
            P.op("sp", DMA(c.yT[l][g * 512:(g + 1) * 512, tok].rearrange("(j p) t -> p j t", p=128), yst[b]), reads=[t_yst[b]])

        loadR(0)
        loadS(0)
        loadR(1)
        iteration(0, None)
        for ci_ in range(NT):
            if ci_ + 1 < NT:
                loadS(ci_ + 1)
                if ci_ + 2 < NT:
                    loadR(ci_ + 2)
            iteration(ci_ + 1 if ci_ + 1 < NT else None, ci_)
            if ci_ >= 1:
                stageC(ci_ - 1)
        stageC(NT - 1)
        A.release(mg)
    A.release(m)


def phase_out(c, l, xsrc, xdst, fuse_next=False):
    P, A = c.P, c.A
    m = A.mark()
    if fuse_next:
        nwb = A.alloc([128, DM], F32)
        t_nwb = T()
        P.op("sp", DMA(nwb, c.norm_w[l + 1:l + 2, :].partition_broadcast(128)), writes=[t_nwb])
        sqj = A.alloc([128, DM], F32)
        t_sqj = T()
        ssx = A.alloc([128, NT], F32)
        t_ssx = [T() for _ in range(NT)]
        hb = [A.alloc([128, DM], BF16) for _ in range(2)]
        t_hb = [T(), T()]
    Wo = c.Wo
    tWo = pf_out(c, l, [0, 1, 2, 3])
    yt = [A.alloc([128, 16, 512], BF16) for _ in range(2)]
    t_yt = [T(), T()]
    xt = [A.alloc([128, DM], F32) for _ in range(3)]
    t_xt = [T(), T(), T()]
    ot = [A.alloc([128, DM], F32) for _ in range(2)]
    t_ot = [T(), T()]
    ps, tps = c.ps, c.tps
    fin = []

    def load_y(qb):
        P.op("sp", DMA(yt[qb % 2], c.yT[l][:, qb * 512:(qb + 1) * 512].rearrange("(k p) t -> p k t", p=128)), writes=[t_yt[qb % 2]])

    def load_x(tt):
        P.op("sp", DMA(xt[tt % 3], xsrc[tt * 128:(tt + 1) * 128, :]), writes=[t_xt[tt % 3]])
    load_y(0)
    load_x(0)
    load_x(1)
    nb = 0
    for tt in range(NT):
        b = tt % 2
        qb, j = tt // 4, tt % 4
        tok = slice(tt * 128, (tt + 1) * 128)
        if j == 0 and qb + 1 < 4:
            load_y(qb + 1)
        if tt + 2 < NT:
            load_x(tt + 2)
        for n in range(2):
            bank, tb = ps[nb % 6], tps[nb % 6]
            nb += 1
            for k in range(16):
                P.op("pe", MM(bank, yt[qb % 2][:, k, j * 128:(j + 1) * 128], Wo[:, k, n * 512:(n + 1) * 512], start=(k == 0), stop=(k == 15)),
                     reads=[t_yt[qb % 2], tWo[k // 4], c.t_Wq[k // 4]], writes=[tb])
            P.op("dve", TT(ot[b][:, n * 512:(n + 1) * 512], bank, xt[tt % 3][:, n * 512:(n + 1) * 512], ALU.add),
                 reads=[tb, t_xt[tt % 3]], writes=[t_ot[b]])
        fin.append(P.op("sp", DMA(xdst[tok, :], ot[b]), reads=[t_ot[b]]))
        if fuse_next:
            s1 = ssx[:, tt:tt + 1]
            P.op("act", ACT(sqj, ot[b], AF.Square, accum_out=s1), reads=[t_ot[b]], writes=[t_sqj, t_ssx[tt]])
            P.op("act", ACT(s1, s1, AF.Ln, scale=1.0 / DM, bias=c.epsc), reads=[t_ssx[tt], c.t_eps], writes=[t_ssx[tt]])
            P.op("act", ACT(s1, s1, AF.Exp, scale=-0.5), reads=[t_ssx[tt]], writes=[t_ssx[tt]])
            P.op("dve", STT(hb[b], ot[b], s1, nwb, ALU.mult, ALU.mult), reads=[t_ot[b], t_ssx[tt], t_nwb], writes=[t_hb[b]])
            for k in range(8):
                P.op("pe", TR(c.psb[:, k * 128:(k + 1) * 128], hb[b][:, k * 128:(k + 1) * 128], c.ident),
                     reads=[t_hb[b], c.t_ident], writes=[c.tpsb[0]])
            P.op("act", ACT(c.hdnT[:, :, tok], c.psb[:, :].rearrange("p (k t) -> p k t", t=128), AF.Copy),
                 reads=[c.tpsb[0]], writes=[c.t_hdnT[tt]])
    A.release(m)
    return fin


def _host_consts():
    inv_freq = (10000.0 ** (-(np.arange(0, 64, 2, dtype=np.float32)) / np.float32(64))).astype(np.float32)
    ang = (np.arange(L, dtype=np.float32)[:, None] * inv_freq[None, :]).astype(np.float32)
    cos, sin = np.cos(ang).astype(np.float32), np.sin(ang).astype(np.float32)
    cs = np.concatenate([cos, cos, -sin, sin], axis=1).astype(np.float32)
    k = np.arange(128)
    ident = np.eye(128, dtype=np.float32)
    U = (k[:, None] <= k[None, :]).astype(np.float32)
    Ur = (k[:, None] >= k[None, :]).astype(np.float32)
    ones = np.ones((128, 128), np.float32)
    consts = np.concatenate([ident, U, Ur, ones], axis=1)
    return np.ascontiguousarray(cs), np.ascontiguousarray(consts)


_CACHE = {}


def make_in_maps(inputs, n_cores=8):
    cs, consts = _host_consts()
    f = lambda a: np.ascontiguousarray(np.asarray(a, dtype=np.float32))
    shared = {
        "norm_w": f(inputs["norm_w"]), "w_in": f(inputs["w_in"]), "conv_w": f(inputs["conv_w"]),
        "conv_b": f(inputs["conv_b"]), "a_log": f(inputs["a_log"]).reshape(2, 32),
        "dt_bias": f(inputs["dt_bias"]).reshape(2, 32), "d_skip": f(inputs["d_skip"]).reshape(2, 32),
        "ssd_norm_w": f(inputs["ssd_norm_w"]), "diff_qk_norm": f(inputs["diff_qk_norm"]).reshape(2, 128),
        "diff_lambda": f(inputs["diff_lambda"]).reshape(2, 256), "diff_subln": f(inputs["diff_subln"]),
        "na_qk_norm": f(inputs["na_qk_norm"]).reshape(2, 128), "na_tab": _na_tables(f(inputs["na_rpb"])),
        "w_out": f(inputs["w_out"]), "cs_tab": cs, "consts": consts,
    }
    x = f(inputs["x"])
    return [dict(shared, x=x[b]) for b in range(n_cores)]


def kernel(**inputs):
    if "nc" not in _CACHE:
        _CACHE["nc"] = build()[0]
    nc = _CACHE["nc"]
    in_maps = make_in_maps(inputs)
    res = run_bass_kernel_spmd(nc, in_maps, core_ids=list(range(8)))
    return np.stack([np.asarray(r["out"], dtype=np.float32) for r in res.results], axis=0)
```

```python
import contextlib
import math
import numpy as np
import concourse.bass as bass
import concourse.mybir as mybir
from concourse.bass_utils import run_bass_kernel_spmd

F32 = mybir.dt.float32
BF16 = mybir.dt.bfloat16
AF = mybir.ActivationFunctionType
ALU = mybir.AluOpType
AX = mybir.AxisListType

L = 2048
DM = 1024
NT = 16
INW = 6688
EPS = 1e-6
NEG = -30000.0
C_Z, C_XBC, C_DT, C_DIFF, C_NA = 0, 1024, 2560, 2592, 4640


class T:
    __slots__ = ("w", "r", "excl")

    def __init__(self, excl=False):
        self.w = None
        self.r = []
        self.excl = excl


class Op:
    __slots__ = ("eng", "fn", "deps", "idx", "sig", "isdma", "semid", "semval", "prev_same_sem")


class Prog:
    COMPUTE = ("pe", "act", "dve", "pool")
    DMAQ = ("sp", "actq", "poolq")
    STREAM = {"pe": "pe", "act": "act", "dve": "dve", "pool": "pool", "sp": "sp", "actq": "act", "poolq": "pool"}
    STREAMS = ("pe", "act", "dve", "pool", "sp")

    def __init__(self, nc, n_dma_sems=12):
        self.nc = nc
        self.ops = []
        self.n_dma_sems = n_dma_sems
        self.last = {s: None for s in self.STREAMS}
        self.recent_dma = {q: [] for q in self.DMAQ}
        self.frontier = []
        self.synced = {s: True for s in self.STREAMS}

    def barrier(self):
        fr = [o for o in self.last.values() if o is not None]
        for q in self.DMAQ:
            fr.extend(self.recent_dma[q])
        self.frontier = fr
        self.synced = {s: False for s in self.STREAMS}

    def op(self, eng, fn, reads=(), writes=()):
        o = Op()
        o.eng = eng
        o.fn = fn
        o.isdma = eng in self.DMAQ
        o.idx = len(self.ops)
        o.prev_same_sem = None
        deps = {}
        if any(t.excl for t in reads):
            writes = list(writes) + [t for t in reads if t.excl and t not in writes]
            reads = [t for t in reads if not t.excl]
        for t in reads:
            if t.w is not None:
                deps[t.w.idx] = ("raw", t.w)
        for t in writes:
            if t.w is not None and t.w.idx not in deps:
                deps[t.w.idx] = ("waw", t.w)
            for r in t.r:
                if r.idx not in deps:
                    deps[r.idx] = ("war", r)
        st = self.STREAM[eng]
        if not self.synced[st]:
            for p in self.frontier:
                if p.idx not in deps:
                    deps[p.idx] = ("bar", p)
            self.synced[st] = True
        for t in writes:
            t.w = o
            t.r = []
        for t in reads:
            if t.w is not o:
                t.r.append(o)
        o.deps = deps
        o.sig = False
        self.ops.append(o)
        self.last[st] = o
        if o.isdma:
            lst = self.recent_dma[eng]
            lst.append(o)
            if len(lst) > self.n_dma_sems:
                lst.pop(0)
        return o

    def emit(self, final_waits=()):
        nc = self.nc
        streams = {s: [] for s in self.STREAMS}
        for o in self.ops:
            streams[self.STREAM[o.eng]].append(o)
        pos = {}
        for s, lst in streams.items():
            for i, o in enumerate(lst):
                pos[o.idx] = i
        need = {}
        for o in self.ops:
            lst = []
            so = self.STREAM[o.eng]
            for (kind, p) in o.deps.values():
                sp_ = self.STREAM[p.eng]
                if p.isdma or o.isdma:
                    lst.append(p)
                elif sp_ != so:
                    lst.append(p)
                else:
                    if so == "pe":
                        continue
                    lst.append(p)
            need[o.idx] = lst
            for p in lst:
                p.sig = True
        for o in final_waits:
            o.sig = True
        stack = contextlib.ExitStack()
        esem = {e: stack.enter_context(nc.semaphore("s_" + e)) for e in self.COMPUTE}
        ecount = {e: 0 for e in self.COMPUTE}
        dsems = {q: [stack.enter_context(nc.semaphore("d_%s_%d" % (q, i))) for i in range(self.n_dma_sems)]
                 for q in self.DMAQ}
        duse = {q: [0] * self.n_dma_sems for q in self.DMAQ}
        dlast = {q: [None] * self.n_dma_sems for q in self.DMAQ}
        dnext = {q: 0 for q in self.DMAQ}
        for s in self.STREAMS:
            for o in streams[s]:
                if o.isdma:
                    q = o.eng
                    j = dnext[q]
                    dnext[q] = (j + 1) % self.n_dma_sems
                    duse[q][j] += 1
                    o.semid = (q, j)
                    o.semval = 16 * duse[q][j]
                    o.prev_same_sem = dlast[q][j]
                    dlast[q][j] = o
                elif o.sig:
                    ecount[o.eng] += 1
                    o.semid = o.eng
                    o.semval = ecount[o.eng]
        self.n_waits = 0

        def sem_of(p):
            if p.isdma:
                return dsems[p.semid[0]][p.semid[1]]
            return esem[p.semid]

        def emit_stream(s, engobj):
            known = {}
            for o in streams[s]:
                waits = {}
                cand = list(need[o.idx])
                if o.isdma and o.prev_same_sem is not None:
                    cand.append(o.prev_same_sem)
                for p in cand:
                    k = p.semid
                    if known.get(k, 0) >= p.semval:
                        continue
                    if waits.get(k, (0, None))[0] < p.semval:
                        waits[k] = (p.semval, p)
                for k, (v, p) in waits.items():
                    engobj.wait_ge(sem_of(p), v)
                    known[k] = v
                    self.n_waits += 1
                ins = o.fn(engobj)
                if o.isdma:
                    ins.then_inc(dsems[o.semid[0]][o.semid[1]], 16)
                elif o.sig:
                    ins.then_inc(esem[o.semid], 1)
            if s == "sp":
                for o in final_waits:
                    engobj.wait_ge(sem_of(o), o.semval)

        with nc.Block() as block:
            @block.tensor
            def _(e):
                emit_stream("pe", e)

            @block.scalar
            def _(e):
                emit_stream("act", e)

            @block.vector
            def _(e):
                emit_stream("dve", e)

            @block.gpsimd
            def _(e):
                emit_stream("pool", e)

            @block.sync
            def _(e):
                emit_stream("sp", e)
        stack.close()


def ACT(out, in_, func, **kw):
    return lambda e: e.activation(out=out, in_=in_, func=func, **kw)


def TT(out, in0, in1, op):
    return lambda e: e.tensor_tensor(out=out, in0=in0, in1=in1, op=op)


def TS(out, in0, s1, op0, s2=None, op1=None):
    if op1 is None:
        return lambda e: e.tensor_scalar(out=out, in0=in0, scalar1=s1, scalar2=None, op0=op0)
    return lambda e: e.tensor_scalar(out=out, in0=in0, scalar1=s1, scalar2=s2, op0=op0, op1=op1)


def STT(out, in0, scalar, in1, op0, op1):
    return lambda e: e.scalar_tensor_tensor(out=out, in0=in0, scalar=scalar, in1=in1, op0=op0, op1=op1)


def CP(out, in_):
    return lambda e: e.tensor_copy(out=out, in_=in_)


def MM(out, lhsT, rhs, start=True, stop=True, skip=False):
    if skip:
        return lambda e: e.matmul(out, lhsT, rhs, start=start, stop=stop, skip_group_check=True)
    return lambda e: e.matmul(out, lhsT, rhs, start=start, stop=stop)


def TR(out, in_, ident):
    return lambda e: e.transpose(out, in_, ident)


def DMA(out, in_):
    return lambda e: e.dma_start(out=out, in_=in_)


def RED(out, in_, op=None):
    return lambda e: e.tensor_reduce(out=out, in_=in_, axis=AX.X, op=(op or ALU.add))


def RECIP(out, in_):
    return lambda e: e.reciprocal(out=out, in_=in_)


def MEMSET(ap, v):
    return lambda e: e.memset(ap, v)


class Arena:
    def __init__(self, nc, words):
        self.t = nc.alloc_sbuf_tensor("arena", [128, words], F32)
        self.words = words
        self.off = 0
        self.peak = 0

    def alloc(self, shape, dt):
        n = int(np.prod(shape[1:]))
        nw = n if dt == F32 else (n + 1) // 2
        nw = (nw + 7) // 8 * 8
        assert self.off + nw <= self.words, ("SBUF arena overflow", self.off, nw, self.words)
        v = self.t[:, self.off:self.off + nw]
        self.off += nw
        self.peak = max(self.peak, self.off)
        if dt != F32:
            v = v.bitcast(dt)
        v = v[:, 0:n]
        if len(shape) == 3:
            v = v.rearrange("p (a b) -> p a b", b=shape[2])
        elif len(shape) == 4:
            v = v.rearrange("p (a b c) -> p a b c", b=shape[2], c=shape[3])
        return v

    def mark(self):
        return self.off

    def release(self, m):
        self.off = m


def bc(ap, shape):
    return ap.to_broadcast(shape)


class Ctx:
    pass


def build(n_layers=2, phases=("diff", "na", "ssd"), dbg=False):
    nc = bass.Bass("TRN2", target_bir_lowering=False)
    c = Ctx()
    c.nc = nc
    c.dbg = dbg

    def din(name, shape, dt=F32):
        return nc.dram_tensor(name, shape, dt, kind="ExternalInput").ap()

    c.x_in = din("x", [L, DM])
    c.norm_w = din("norm_w", [2, DM])
    c.w_in = din("w_in", [2, DM, INW])
    c.conv_w = din("conv_w", [2, 5, 1536])
    c.conv_b = din("conv_b", [2, 1536])
    c.a_log = din("a_log", [2, 32])
    c.dt_bias = din("dt_bias", [2, 32])
    c.d_skip = din("d_skip", [2, 32])
    c.ssd_norm_w = din("ssd_norm_w", [2, 1024])
    c.diff_qk_norm = din("diff_qk_norm", [2, 128])
    c.diff_lambda = din("diff_lambda", [2, 256])
    c.diff_subln = din("diff_subln", [2, 128])
    c.na_qk_norm = din("na_qk_norm", [2, 128])
    c.na_tab = din("na_tab", [2, 8, 128, NA_NCLS * 128])
    c.w_out = din("w_out", [2, 2048, DM])
    c.cs_tab = din("cs_tab", [L, 128])
    c.consts = din("consts", [128, 4 * 128])
    c.out = nc.dram_tensor("out", [L, DM], F32, kind="ExternalOutput").ap()
    kind_scr = "ExternalOutput" if dbg else "Internal"
    c.x1 = nc.dram_tensor("x1", [L, DM], F32, kind=kind_scr).ap()
    c.yT = [nc.dram_tensor("yT%d" % i, [2048, L], BF16, kind=kind_scr).ap() for i in range(n_layers)]
    c.csT_d = nc.dram_tensor("csT_d", [NT, 32, 128], F32, kind="Internal").ap()
    c.sb_d = nc.dram_tensor("sb_d", [2, NT, 128, 512], BF16, kind="Internal").ap()
    c.sf_d = nc.dram_tensor("sf_d", [2, NT, 128, 512], BF16, kind="Internal").ap()

    c.dumps = []

    def dump(name, ap, shape, dt, tiles):
        if not dbg:
            return
        d = nc.dram_tensor("dbg_" + name, list(shape), dt, kind="ExternalOutput").ap()
        c.dumps.append(c.P.op("sp", DMA(d, ap), reads=tiles))
    c.dump = dump
    A = Arena(nc, 51000)
    c.A = A
    P = Prog(nc)
    c.P = P
    c.psall = nc.alloc_psum_tensor("psall", [128, 8 * 512], F32)[:, :]
    c.ps = [c.psall[:, i * 512:(i + 1) * 512] for i in range(7)]
    c.tps = [T(excl=True) for _ in range(7)]
    c.psb = c.psall[:, 7 * 512:8 * 512].bitcast(BF16)
    _tb = T(excl=True)
    c.tpsb = [_tb, _tb]

    c.cst = A.alloc([128, 512], F32)
    c.t_cst = T()
    P.op("sp", DMA(c.cst, c.consts), writes=[c.t_cst])
    c.identf = c.cst[:, 0:128]
    c.U = c.cst[:, 128:256]
    c.Ur = c.cst[:, 256:384]
    c.onesf = c.cst[:, 384:512]
    c.ident = A.alloc([128, 128], BF16)
    c.t_ident = T()
    P.op("dve", CP(c.ident, c.identf), reads=[c.t_cst], writes=[c.t_ident])
    c.epsc = A.alloc([128, 1], F32)
    c.t_eps = T()
    P.op("pool", MEMSET(c.epsc, EPS), writes=[c.t_eps])
    c.onec = A.alloc([128, 1], F32)
    P.op("pool", MEMSET(c.onec, 1.0), writes=[c.t_eps])
    c.hdnT = A.alloc([128, 8, L], BF16)
    c.t_hdnT = [T() for _ in range(NT)]
    c.Wbuf = A.alloc([128, 16384], BF16)
    c.t_Wq = [T() for _ in range(4)]
    c.W8 = c.Wbuf.rearrange("p (k c) -> p k c", c=2048)
    c.Wo = c.Wbuf.rearrange("p (k c) -> p k c", c=1024)
    c.Wz = [c.Wbuf[:, 3 * 4096:4 * 4096].rearrange("p (k c) -> p k c", c=512),
            c.Wbuf[:, 2 * 4096:3 * 4096].rearrange("p (k c) -> p k c", c=512)]
    c.pf = {}
    c.phases = phases

    finals = []
    for l in range(n_layers):
        xsrc = c.x_in if l == 0 else c.x1
        xdst = c.x1 if l < n_layers - 1 or dbg and n_layers == 1 else c.out
        if l == n_layers - 1:
            xdst = c.out
        P.barrier()
        if "diff" in phases:
            pf_qkvg(c, l, "diff")
        if l == 0:
            phase_hdn(c, l, xsrc)
        if "diff" in phases:
            P.barrier()
            phase_diff(c, l)
        if "na" in phases:
            P.barrier()
            phase_na(c, l)
        if "ssd" in phases:
            P.barrier()
            phase_ssd(c, l)
        P.barrier()
        fin = phase_out(c, l, xsrc, xdst, fuse_next=(l + 1 < n_layers))
        if l == n_layers - 1:
            finals = fin
    P.barrier()
    finals = list(finals)
    if dbg:
        finals += [o for o in P.ops if o.isdma][-36:] + c.dumps
    P.emit(final_waits=finals)
    c.peak = A.peak
    return nc, c


def phase_hdn(c, l, xsrc):
    P, A = c.P, c.A
    m = A.mark()
    nwb = A.alloc([128, DM], F32)
    t_nwb = T()
    P.op("sp", DMA(nwb, c.norm_w[l:l + 1, :].partition_broadcast(128)), writes=[t_nwb])
    NX = 4
    xt = [A.alloc([128, DM], F32) for _ in range(NX)]
    t_xt = [T() for _ in range(NX)]
    sq = A.alloc([128, DM], F32)
    t_sq = T()
    ss = A.alloc([128, NT], F32)
    t_ss = [T() for _ in range(NT)]
    hb = [A.alloc([128, DM], BF16) for _ in range(2)]
    t_hb = [T(), T()]

    def load_x(tt):
        P.op("sp", DMA(xt[tt % NX], xsrc[tt * 128:(tt + 1) * 128, :]), writes=[t_xt[tt % NX]])
    for tt in range(min(NX - 1, NT)):
        load_x(tt)
    for tt in range(NT):
        b = tt % 2
        xb, t_xb = xt[tt % NX], t_xt[tt % NX]
        s1 = ss[:, tt:tt + 1]
        if tt + NX - 1 < NT:
            load_x(tt + NX - 1)
        P.op("act", ACT(sq, xb, AF.Square, accum_out=s1), reads=[t_xb], writes=[t_sq, t_ss[tt]])
        P.op("act", ACT(s1, s1, AF.Ln, scale=1.0 / DM, bias=c.epsc), reads=[t_ss[tt], c.t_eps], writes=[t_ss[tt]])
        P.op("act", ACT(s1, s1, AF.Exp, scale=-0.5), reads=[t_ss[tt]], writes=[t_ss[tt]])
        P.op("dve", STT(hb[b], xb, s1, nwb, ALU.mult, ALU.mult), reads=[t_xb, t_ss[tt], t_nwb], writes=[t_hb[b]])
        for k in range(8):
            P.op("pe", TR(c.psb[:, k * 128:(k + 1) * 128], hb[b][:, k * 128:(k + 1) * 128], c.ident),
                 reads=[t_hb[b], c.t_ident], writes=[c.tpsb[k // 4]])
        P.op("act", ACT(c.hdnT[:, :, tt * 128:(tt + 1) * 128], c.psb[:, :].rearrange("p (k t) -> p k t", t=128), AF.Copy),
             reads=[c.tpsb[0], c.tpsb[1]], writes=[c.t_hdnT[tt]])
    A.release(m)


def load_w(c, l, dst, col0, ncols, tiles, step=512, extra=()):
    P = c.P
    j = 0
    for c0 in range(0, ncols, step):
        n = min(step, ncols - c0)
        src = c.w_in[l, :, col0 + c0:col0 + c0 + n].rearrange("(k p) c -> p k c", p=128)
        P.op("poolq", DMA(dst[:, :, c0:c0 + n], src), writes=[tiles[j]] + list(extra))
        j += 1


def pf_qkvg(c, l, which):
    key = (which, l)
    if key not in c.pf:
        tW = [T() for _ in range(4)]
        load_w(c, l, c.W8, C_DIFF if which == "diff" else C_NA, 2048, tW, extra=c.t_Wq)
        c.pf[key] = tW
    return c.pf[key]


def pf_ssd(c, l):
    key = ("ssd", l)
    if key not in c.pf:
        tWz = [[T()], [T()]]
        load_w(c, l, c.Wz[0], C_Z, 512, tWz[0], extra=[c.t_Wq[3]])
        load_w(c, l, c.Wz[1], C_Z + 512, 512, tWz[1], extra=[c.t_Wq[2]])
        c.pf[key] = tWz
    return c.pf[key]


def pf_out(c, l, quarters):
    key = ("out", l)
    if key not in c.pf:
        c.pf[key] = [None] * 4
    tWo = c.pf[key]
    for j in quarters:
        if tWo[j] is None:
            tWo[j] = T()
            src = c.w_out[l, j * 512:(j + 1) * 512, :].rearrange("(k p) c -> p k c", p=128)
            c.P.op("poolq", DMA(c.Wo[:, j * 4:(j + 1) * 4, :], src), writes=[tWo[j], c.t_Wq[j]])
    return tWo


def qk_norm_rope(c, src_ps, t_src, sq, t_sq, ss8, t_ss8, tbuf, t_tb, ubuf, t_ub, TC, TS_, t_tab, tt, dst, t_dst, rope):
    P = c.P
    P.op("act", ACT(sq, src_ps, AF.Square), reads=[t_src], writes=[t_sq])
    P.op("dve", RED(ss8, sq.rearrange("p (g d) -> p g d", d=64)), reads=[t_sq], writes=[t_ss8])
    P.op("act", ACT(ss8, ss8, AF.Ln, scale=1.0 / 64, bias=c.epsc), reads=[t_ss8, c.t_eps], writes=[t_ss8])
    P.op("act", ACT(ss8, ss8, AF.Exp, scale=-0.5), reads=[t_ss8], writes=[t_ss8])
    t3 = tbuf.rearrange("p (g d) -> p g d", d=64)
    P.op("dve", TT(t3, src_ps.rearrange("p (g d) -> p g d", d=64), bc(ss8.unsqueeze(2), [128, 8, 64]), ALU.mult),
         reads=[t_src, t_ss8], writes=[t_tb])
    if rope:
        u3 = ubuf.rearrange("p (g d) -> p g d", d=64)
        P.op("pool", TT(u3[:, :, 0:32], t3[:, :, 32:64], bc(TS_[:, tt, 0:32].unsqueeze(1), [128, 8, 32]), ALU.mult),
             reads=[t_tb, t_tab], writes=[t_ub])
        P.op("pool", TT(u3[:, :, 32:64], t3[:, :, 0:32], bc(TS_[:, tt, 32:64].unsqueeze(1), [128, 8, 32]), ALU.mult),
             reads=[t_tb, t_tab], writes=[t_ub])
        P.op("dve", TT(t3, t3, bc(TC[:, tt, :].unsqueeze(1), [128, 8, 64]), ALU.mult), reads=[t_tb, t_tab], writes=[t_tb])
        P.op("dve", TT(dst, tbuf, ubuf, ALU.add), reads=[t_tb, t_ub], writes=[t_dst])
    else:
        P.op("dve", TT(dst.rearrange("p (g d) -> p g d", d=64), t3, bc(TC.unsqueeze(1), [128, 8, 64]), ALU.mult),
             reads=[t_tb, t_tab], writes=[t_dst])


def prep_qkvg(c, W, tW, qT, t_qT, kT, t_kT, vcopy, t_V, G, t_G, tabs, t_tab, rope):
    P, A = c.P, c.A
    ps, tps = c.ps, c.tps
    NB = 3 if rope else 2
    LAG = NB - 1
    raw = [[A.alloc([128, 512], F32) for _ in range(NB)] for _ in range(2)]
    t_raw = [[T() for _ in range(NB)] for _ in range(2)]
    sq = [A.alloc([128, 512], F32) for _ in range(2)]
    t_sq = [T(), T()]
    ss8 = [[A.alloc([128, 8], F32) for _ in range(NB)] for _ in range(2)]
    t_ss8 = [[T() for _ in range(NB)] for _ in range(2)]
    tbuf = [[A.alloc([128, 512], F32) for _ in range(NB)] for _ in range(2)]
    t_tb = [[T() for _ in range(NB)] for _ in range(2)]
    ubuf = [[A.alloc([128, 512], F32) for _ in range(NB)] for _ in range(2)] if rope else None
    t_ub = [[T() for _ in range(NB)] for _ in range(2)]
    rbuf = [[A.alloc([128, 512], BF16) for _ in range(NB)] for _ in range(2)]
    t_rb = [[T() for _ in range(NB)] for _ in range(2)]
    dsts = ((qT, t_qT), (kT, t_kT))
    nbank = 0
    gbank = {}

    def chain_a(tt, i):
        b = tt % NB
        P.op("act", ACT(sq[i], raw[i][b], AF.Square), reads=[t_raw[i][b]], writes=[t_sq[i]])
        P.op("dve", RED(ss8[i][b], sq[i].rearrange("p (g d) -> p g d", d=64)), reads=[t_sq[i]], writes=[t_ss8[i][b]])

    def chain_b(tt, i):
        b = tt % NB
        s8, t_s8 = ss8[i][b], t_ss8[i][b]
        P.op("act", ACT(s8, s8, AF.Ln, scale=1.0 / 64, bias=c.epsc), reads=[t_s8, c.t_eps], writes=[t_s8])
        P.op("act", ACT(s8, s8, AF.Exp, scale=-0.5), reads=[t_s8], writes=[t_s8])

    def chain_c(tt, i):
        b = tt % NB
        rw, t_rw = raw[i][b], t_raw[i][b]
        s8, t_s8 = ss8[i][b], t_ss8[i][b]
        tb_, t_tb_ = tbuf[i][b], t_tb[i][b]
        t3 = tb_.rearrange("p (g d) -> p g d", d=64)
        P.op("dve", TT(t3, rw.rearrange("p (g d) -> p g d", d=64), bc(s8.unsqueeze(2), [128, 8, 64]), ALU.mult),
             reads=[t_rw, t_s8], writes=[t_tb_])
        dst, t_dst = rbuf[i][b], t_rb[i][b]
        TC, TS_ = tabs[2 * i], tabs[2 * i + 1]
        if rope:
            ub, t_ub_ = ubuf[i][b], t_ub[i][b]
            u3 = ub.rearrange("p (g d) -> p g d", d=64)
            eng_u = "pool" if i == 1 else "dve"
            P.op(eng_u, TT(u3[:, :, 0:32], t3[:, :, 32:64], bc(TS_[:, tt, 0:32].unsqueeze(1), [128, 8, 32]), ALU.mult),
                 reads=[t_tb_, t_tab], writes=[t_ub_])
            P.op(eng_u, TT(u3[:, :, 32:64], t3[:, :, 0:32], bc(TS_[:, tt, 32:64].unsqueeze(1), [128, 8, 32]), ALU.mult),
                 reads=[t_tb_, t_tab], writes=[t_ub_])
            P.op("dve", TT(t3, t3, bc(TC[:, tt, :].unsqueeze(1), [128, 8, 64]), ALU.mult), reads=[t_tb_, t_tab], writes=[t_tb_])
            P.op("dve", TT(dst, tb_, ub, ALU.add), reads=[t_tb_, t_ub_], writes=[t_dst])
        else:
            P.op("dve", TT(dst.rearrange("p (g d) -> p g d", d=64), t3, bc(TC.unsqueeze(1), [128, 8, 64]), ALU.mult),
                 reads=[t_tb_, t_tab], writes=[t_dst])

    def transposes(tt):
        b = tt % NB
        tok = slice(tt * 128, (tt + 1) * 128)
        for i in range(2):
            for h in range(4):
                P.op("pe", TR(c.psb[:, i * 512 + h * 128:i * 512 + (h + 1) * 128], rbuf[i][b][:, h * 128:(h + 1) * 128], c.ident),
                     reads=[t_rb[i][b], c.t_ident], writes=[c.tpsb[i]])
        for i in range(2):
            dstT, t_dstT = dsts[i]
            P.op("dve", CP(dstT[:, :, tok], c.psb[:, i * 512:(i + 1) * 512].rearrange("p (h t) -> p h t", t=128)),
                 reads=[c.tpsb[i]], writes=[t_dstT[tt]])

    for tt in range(NT + LAG):
        if tt < NT:
            tok = slice(tt * 128, (tt + 1) * 128)
            b = tt % NB
            banks = []
            for j in range(4):
                bk = nbank % 7
                nbank += 1
                banks.append(bk)
                for k in range(8):
                    P.op("pe", MM(ps[bk], c.hdnT[:, k, tok], W[:, k, j * 512:(j + 1) * 512], start=(k == 0), stop=(k == 7)),
                         reads=[c.t_hdnT[tt], tW[j]] + c.t_Wq, writes=[tps[bk]])
            for i in range(2):
                if rope:
                    P.op("act", ACT(raw[i][b], ps[banks[i]], AF.Copy), reads=[tps[banks[i]]], writes=[t_raw[i][b]])
                else:
                    P.op("dve", CP(raw[i][b], ps[banks[i]]), reads=[tps[banks[i]]], writes=[t_raw[i][b]])
            vcopy(tt, ps[banks[2]], tps[banks[2]])
            gbank[tt] = banks[3]
            for stage in (chain_a, chain_b, chain_c):
                for i in range(2):
                    stage(tt, i)
            if tt % 2 == 1 or tt == NT - 1:
                for t2 in ([tt - 1, tt] if tt % 2 == 1 else [tt]):
                    P.op("act", ACT(G[:, t2, :], ps[gbank[t2]], AF.Silu), reads=[tps[gbank[t2]]], writes=[t_G[t2]])
        if tt >= LAG:
            transposes(tt - LAG)


def phase_diff(c, l):
    P, A = c.P, c.A
    m = A.mark()
    lam_init = 0.8 - 0.6 * math.exp(-0.3 * l)
    W = c.W8
    tW = pf_qkvg(c, l, "diff")
    qT = A.alloc([128, 4, L], BF16)
    kT = A.alloc([128, 4, L], BF16)
    t_qT = [T() for _ in range(NT)]
    t_kT = [T() for _ in range(NT)]
    V = A.alloc([128, NT, 4, 129], BF16)
    t_V = [T() for _ in range(NT)]
    G = A.alloc([128, NT, 512], BF16)
    t_G = [T() for _ in range(NT)]
    t_misc = T()
    P.op("pool", MEMSET(V[:, :, :, 128:129], 1.0), writes=t_V)
    wqk = A.alloc([128, 128], F32)
    P.op("sp", DMA(wqk, c.diff_qk_norm[l:l + 1, :].partition_broadcast(128)), writes=[t_misc])
    P.op("dve", TS(wqk[:, 0:64], wqk[:, 0:64], 0.125, ALU.mult), reads=[t_misc], writes=[t_misc])
    tabs = A.alloc([128, 4, NT * 64], F32)
    t_tab = T()
    m_cs = A.mark()
    c.cs_sb = A.alloc([128, NT, 128], F32)
    c.t_cs = T()
    P.op("sp", DMA(c.cs_sb, c.cs_tab.rearrange("(t p) c -> p t c", p=128)), writes=[c.t_cs])
    for i, w in enumerate((wqk[:, 0:64], wqk[:, 64:128])):
        TCv = tabs[:, 2 * i, :].rearrange("p (t d) -> p t d", d=64)
        TSv = tabs[:, 2 * i + 1, :].rearrange("p (t d) -> p t d", d=64)
        P.op("dve", TT(TCv, c.cs_sb[:, :, 0:64], bc(w.unsqueeze(1), [128, NT, 64]), ALU.mult), reads=[c.t_cs, t_misc], writes=[t_tab])
        P.op("dve", TT(TSv[:, :, 0:32], c.cs_sb[:, :, 64:96], bc(w[:, 32:64].unsqueeze(1), [128, NT, 32]), ALU.mult),
             reads=[c.t_cs, t_misc], writes=[t_tab])
        P.op("dve", TT(TSv[:, :, 32:64], c.cs_sb[:, :, 96:128], bc(w[:, 0:32].unsqueeze(1), [128, NT, 32]), ALU.mult),
             reads=[c.t_cs, t_misc], writes=[t_tab])
    P.barrier()
    A.release(m_cs)
    TCq = tabs[:, 0, :].rearrange("p (t d) -> p t d", d=64)
    TSq = tabs[:, 1, :].rearrange("p (t d) -> p t d", d=64)
    TCk = tabs[:, 2, :].rearrange("p (t d) -> p t d", d=64)
    TSk = tabs[:, 3, :].rearrange("p (t d) -> p t d", d=64)
    lamb = A.alloc([128, 256], F32)
    t_lam = T()
    P.op("sp", DMA(lamb, c.diff_lambda[l:l + 1, :].partition_broadcast(128)), writes=[t_lam])
    lsc = A.alloc([128, 4], F32)
    lam3 = lamb.rearrange("p (a b d) -> p a b d", a=2, b=2)
    prod = A.alloc([128, 2, 64], F32)
    P.op("dve", TT(prod, lam3[:, :, 0, :], lam3[:, :, 1, :], ALU.mult), reads=[t_lam], writes=[t_lam])
    P.op("dve", RED(lsc[:, 0:2], prod), reads=[t_lam], writes=[t_lam])
    P.op("act", ACT(lsc[:, 0:2], lsc[:, 0:2], AF.Exp), reads=[t_lam], writes=[t_lam])
    P.op("dve", TT(lsc[:, 2:3], lsc[:, 1:2], lsc[:, 0:1], ALU.subtract), reads=[t_lam], writes=[t_lam])
    P.op("dve", TS(lsc[:, 3:4], lsc[:, 2:3], -lam_init, ALU.add), reads=[t_lam], writes=[t_lam])
    neglam = lsc[:, 3:4]
    swb = A.alloc([128, 128], F32)
    t_swb = T()
    P.op("sp", DMA(swb, c.diff_subln[l:l + 1, :].partition_broadcast(128)), writes=[t_swb])
    P.op("dve", TS(swb, swb, 1.0 - lam_init, ALU.mult), reads=[t_swb], writes=[t_swb])

    ps, tps = c.ps, c.tps
    m_prep = A.mark()

    def vcopy(tt, bank, t_bank):
        P.op("act", ACT(V[:, tt, :, 0:128], bank.rearrange("p (h d) -> p h d", d=128), AF.Copy), reads=[t_bank], writes=[t_V[tt]])
    prep_qkvg(c, W, tW, qT, t_qT, kT, t_kT, vcopy, t_V, G, t_G, (TCq, TSq, TCk, TSk), t_tab, True)
    P.barrier()
    A.release(m_prep)
    if "na" in c.phases:
        pf_qkvg(c, l, "na")
    elif "ssd" in c.phases:
        pf_ssd(c, l)
    c.dump("qT%d" % l, qT, [128, 4, L], BF16, t_qT)
    c.dump("kT%d" % l, kT, [128, 4, L], BF16, t_kT)
    c.dump("V%d" % l, V, [128, NT, 4, 129], BF16, t_V)
    c.dump("G%d" % l, G, [128, NT, 512], BF16, t_G)
    c.dump("hdnT%d" % l, c.hdnT, [128, 8, L], BF16, c.t_hdnT)
    NPT = 4
    Pt = [A.alloc([128, 1024], BF16) for _ in range(NPT)]
    t_Pt = [T() for _ in range(NPT)]
    SP = [c.psall[:, 0:1024], c.psall[:, 1024:2048]]
    t_SP = [[tps[0], tps[1]], [tps[2], tps[3]]]
    Ob3 = [ps[4], ps[5], ps[6]]
    t_O3 = [tps[4], tps[5], tps[6]]

    def acc(a):
        return a // 3, (a % 3) * 129
    Osb = A.alloc([128, 3, 512], F32)
    t_Osb = T()
    rr = [A.alloc([128, 16], F32) for _ in range(2)]
    t_rr = [T(), T()]
    ob4 = [A.alloc([128, 4, 128], F32) for _ in range(2)]
    t_ob4 = [T(), T()]
    junk = A.alloc([128, 128], F32)
    t_junk = T()
    yb4 = [A.alloc([128, 512], BF16) for _ in range(2)]
    t_yb4 = [T(), T()]
    yst = [A.alloc([128, 512], BF16) for _ in range(2)]
    t_yst = [T(), T()]
    iters = [(h, qb, kt) for h in range(4) for qb in range(4) for kt in range(NT)]
    NI = len(iters)
    deferred = []

    def emit_S(i):
        h, qb, kt = iters[i]
        for cc in range(2):
            pr = slice(cc * 64, (cc + 1) * 64)
            P.op("pe", MM(SP[i % 2][:, cc * 512:(cc + 1) * 512], kT[pr, h, kt * 128:(kt + 1) * 128], qT[pr, h, qb * 512:(qb + 1) * 512]),
                 reads=[t_kT[kt]] + [t_qT[qb * 4 + j] for j in range(4)], writes=[t_SP[i % 2][cc]])

    def Oslice(a):
        bk, off = acc(a)
        return Osb[:, bk, off:off + 129]

    def fin_stage1(blk, h, qb):
        b = blk % 2
        r_, t_r = rr[b], t_rr[b]
        ob, t_ob = ob4[b], t_ob4[b]
        for j in range(4):
            O0, O1 = Oslice(j), Oslice(4 + j)
            P.op("dve", RECIP(r_[:, 4 * j:4 * j + 1], O0[:, 128:129]), reads=[t_Osb], writes=[t_r])
            P.op("dve", RECIP(r_[:, 4 * j + 1:4 * j + 2], O1[:, 128:129]), reads=[t_Osb], writes=[t_r])
            P.op("dve", TT(r_[:, 4 * j + 2:4 * j + 3], r_[:, 4 * j + 1:4 * j + 2], neglam, ALU.mult), reads=[t_r, t_lam], writes=[t_r])
            P.op("dve", TS(ob[:, j, :], O0[:, 0:128], r_[:, 4 * j:4 * j + 1], ALU.mult), reads=[t_Osb, t_r], writes=[t_ob])
            P.op("dve", STT(ob[:, j, :], O1[:, 0:128], r_[:, 4 * j + 2:4 * j + 3], ob[:, j, :], ALU.mult, ALU.add),
                 reads=[t_Osb, t_r, t_ob], writes=[t_ob])

    def fin_stage2(blk, h, qb):
        b = blk % 2
        r_, t_r = rr[b], t_rr[b]
        ob, t_ob = ob4[b], t_ob4[b]
        for j in range(4):
            P.op("act", ACT(junk, ob[:, j, :], AF.Square, accum_out=r_[:, 4 * j + 3:4 * j + 4]), reads=[t_ob], writes=[t_junk, t_r])
        r3 = r_.rearrange("p (j k) -> p j k", k=4)[:, :, 3]
        P.op("act", ACT(r3, r3, AF.Ln, scale=1.0 / 128, bias=c.epsc), reads=[t_r, c.t_eps], writes=[t_r])
        P.op("act", ACT(r3, r3, AF.Exp, scale=-0.5), reads=[t_r], writes=[t_r])

    def fin_stage3(blk, h, qb):
        b = blk % 2
        r_, t_r = rr[b], t_rr[b]
        ob, t_ob = ob4[b], t_ob4[b]
        yb, t_yb = yb4[b], t_yb4[b]
        for j in range(4):
            tt = qb * 4 + j
            P.op("dve", STT(ob[:, j, :], ob[:, j, :], r_[:, 4 * j + 3:4 * j + 4], swb, ALU.mult, ALU.mult), reads=[t_ob, t_r, t_swb], writes=[t_ob])
            P.op("dve", TT(yb[:, j * 128:(j + 1) * 128], ob[:, j, :], G[:, tt, h * 128:(h + 1) * 128], ALU.mult),
                 reads=[t_ob, t_G[tt]], writes=[t_yb])
        for j in range(4):
            P.op("pe", TR(c.psb[:, j * 128:(j + 1) * 128], yb[:, j * 128:(j + 1) * 128], c.ident),
                 reads=[t_yb, c.t_ident], writes=[c.tpsb[0]])

    def fin_stage4(blk, h, qb):
        b = blk % 2
        ys, t_ys = yst[b], t_yst[b]
        P.op("dve", CP(ys, c.psb[:, 0:512]), reads=[c.tpsb[0]], writes=[t_ys])
        P.op("sp", DMA(c.yT[l][1024 + h * 128:1024 + (h + 1) * 128, qb * 512:(qb + 1) * 512], ys), reads=[t_ys])

    emit_S(0)
    blk = 0
    for i in range(NI):
        h, qb, kt = iters[i]
        if i + 1 < NI:
            emit_S(i + 1)
        pt, t_pt = Pt[i % NPT], t_Pt[i % NPT]
        P.op("act", ACT(pt, SP[i % 2], AF.Exp), reads=t_SP[i % 2], writes=[t_pt])
        for cc in range(2):
            for j in range(4):
                bk, off = acc(cc * 4 + j)
                P.op("pe", MM(Ob3[bk][:, off:off + 129], pt[:, cc * 512 + j * 128:cc * 512 + (j + 1) * 128],
                              V[:, kt, h, :], start=(kt == 0 and off == 0), stop=(kt == NT - 1), skip=True),
                     reads=[t_pt, t_V[kt]], writes=[t_O3[bk]])
        for (due, fn) in [d for d in deferred if d[0] <= i]:
            fn()
        deferred = [d for d in deferred if d[0] > i]
        if kt == NT - 1:
            for k3 in range(3):
                P.op("dve", CP(Osb[:, k3, 0:387], Ob3[k3][:, 0:387]), reads=[t_O3[k3]], writes=[t_Osb])
            fin_stage1(blk, h, qb)
            deferred.append((i + 2, (lambda b_=blk, h_=h, q_=qb: fin_stage2(b_, h_, q_))))
            deferred.append((i + 4, (lambda b_=blk, h_=h, q_=qb: fin_stage3(b_, h_, q_))))
            deferred.append((i + 6, (lambda b_=blk, h_=h, q_=qb: fin_stage4(b_, h_, q_))))
            blk += 1
    for (due, fn) in deferred:
        fn()
    A.release(m)


def _na_classes():
    rows, W, kh, kw = 32, 64, 8, 16
    rs = lambda r: min(max(r - kh // 2, 0), rows - kh)
    types = {}
    keys = []
    plan = []
    for i in range(16):
        qrows = [2 * i, 2 * i + 1]
        lo = min(rs(r) for r in qrows)
        hi = max(rs(r) + kh - 1 for r in qrows)
        tkeys = []
        kbs = list(range(lo // 2, hi // 2 + 1))
        for kb in kbs:
            key = []
            for b in range(2):
                for a in range(2):
                    kr, qr = 2 * kb + b, 2 * i + a
                    ok = rs(qr) <= kr < rs(qr) + kh
                    key.append((kr - qr + 7) if ok else -1)
            tkeys.append(tuple(key))
        tkeys = tuple(tkeys)
        if tkeys not in types:
            types[tkeys] = len(keys)
            keys.extend(tkeys)
        base = types[tkeys]
        plan.append([(kb, base + bi) for bi, kb in enumerate(kbs)])
    return keys, plan


NA_KEYS, NA_PLAN = _na_classes()
NA_NCLS = len(NA_KEYS)


def _na_tables(rpb):
    W, kw = 64, 16
    cidx = np.arange(W)
    col_start = np.clip(cidx - kw // 2, 0, W - kw)
    col_ok = (cidx[None, :] >= col_start[:, None]) & (cidx[None, :] < col_start[:, None] + kw)
    dc = np.clip(cidx[None, :] - cidx[:, None] + (kw - 1), 0, 2 * kw - 2)
    out = np.full((2, 8, 128, NA_NCLS, 128), NEG, dtype=np.float32)
    for ci, key in enumerate(NA_KEYS):
        n = 0
        for b in range(2):
            for a in range(2):
                dr = key[n]
                n += 1
                if dr < 0:
                    continue
                blkv = rpb[:, :, dr, :][:, :, dc]
                blkv = np.where(col_ok[None, None], blkv, np.float32(NEG))
                out[:, :, b * 64:(b + 1) * 64, ci, a * 64:(a + 1) * 64] = np.transpose(blkv, (0, 1, 3, 2))
    return np.ascontiguousarray(out.reshape(2, 8, 128, NA_NCLS * 128))


def phase_na(c, l):
    P, A = c.P, c.A
    m = A.mark()
    W = c.W8
    tW = pf_qkvg(c, l, "na")
    qT = A.alloc([128, 4, L], BF16)
    kT = A.alloc([128, 4, L], BF16)
    t_qT = [T() for _ in range(NT)]
    t_kT = [T() for _ in range(NT)]
    V = A.alloc([128, NT, 8, 65], BF16)
    t_V = [T() for _ in range(NT)]
    G = A.alloc([128, NT, 512], BF16)
    t_G = [T() for _ in range(NT)]
    Y = A.alloc([128, NT, 512], BF16)
    t_Y = [T() for _ in range(NT)]
    P.op("pool", MEMSET(V[:, :, :, 64:65], 1.0), writes=t_V)
    t_misc = T()
    wqk = A.alloc([128, 128], F32)
    P.op("sp", DMA(wqk, c.na_qk_norm[l:l + 1, :].partition_broadcast(128)), writes=[t_misc])
    P.op("dve", TS(wqk[:, 0:64], wqk[:, 0:64], 0.125, ALU.mult), reads=[t_misc], writes=[t_misc])
    ps, tps = c.ps, c.tps
    m_prep = A.mark()

    def vcopy(tt, bank, t_bank):
        P.op("act", ACT(V[:, tt, :, 0:64], bank.rearrange("p (h d) -> p h d", d=64), AF.Copy), reads=[t_bank], writes=[t_V[tt]])
    prep_qkvg(c, W, tW, qT, t_qT, kT, t_kT, vcopy, t_V, G, t_G, (wqk[:, 0:64], None, wqk[:, 64:128], None), t_misc, False)
    P.barrier()
    A.release(m_prep)
    if "ssd" in c.phases:
        pf_ssd(c, l)
    pf_out(c, l, [0, 1])
    tab = [A.alloc([128, NA_NCLS * 128], F32) for _ in range(2)]
    t_tab = [T(), T()]
    Tb = [A.alloc([128, 640], F32) for _ in range(2)]
    t_Tb = [T(), T()]
    Pb = [A.alloc([128, 640], BF16) for _ in range(3)]
    t_Pb = [T(), T(), T()]
    rr = [A.alloc([128, 2], F32) for _ in range(2)]
    t_rr = [T(), T()]
    its = [(hh, i) for hh in range(8) for i in range(NT)]
    NI = len(its)
    loaded = set()

    def load_tab(hh):
        if hh in loaded or hh >= 8:
            return
        loaded.add(hh)
        P.op("sp", DMA(tab[hh % 2], c.na_tab[l, hh, :, :]), writes=[t_tab[hh % 2]])
        P.op("act", ACT(tab[hh % 2], tab[hh % 2], AF.Exp), reads=[], writes=[t_tab[hh % 2]])

    def emit_S(n):
        hh, i = its[n]
        jb, e = hh // 2, hh % 2
        pr = slice(e * 64, (e + 1) * 64)
        S0, S1 = ps[(n % 2) * 2], ps[(n % 2) * 2 + 1]
        tS = [tps[(n % 2) * 2], tps[(n % 2) * 2 + 1]]
        for bi, (kb, cls) in enumerate(NA_PLAN[i]):
            dstS = (S0 if bi < 4 else S1)[:, (bi % 4) * 128:(bi % 4 + 1) * 128]
            P.op("pe", MM(dstS, kT[pr, jb, kb * 128:(kb + 1) * 128], qT[pr, jb, i * 128:(i + 1) * 128]),
                 reads=[t_kT[kb], t_qT[i]], writes=[tS[bi // 4]])

    def emit_exp_mul(n):
        hh, i = its[n]
        plan = NA_PLAN[i]
        nb = len(plan)
        base = plan[0][1]
        tS = [tps[(n % 2) * 2], tps[(n % 2) * 2 + 1]]
        Sboth = c.psall[:, (n % 2) * 1024:(n % 2) * 1024 + nb * 128]
        T_, t_T = Tb[n % 2], t_Tb[n % 2]
        P_, t_P = Pb[n % 3], t_Pb[n % 3]
        P.op("act", ACT(T_[:, 0:nb * 128], Sboth, AF.Exp), reads=(tS if nb > 4 else tS[0:1]), writes=[t_T])
        P.op("dve", TT(P_[:, 0:nb * 128], T_[:, 0:nb * 128], tab[hh % 2][:, base * 128:(base + nb) * 128], ALU.mult),
             reads=[t_T, t_tab[hh % 2]], writes=[t_P])

    yst = [A.alloc([128, 512], BF16) for _ in range(2)]
    t_yst = [T(), T()]
    ycnt = [0]

    def emit_ytrans(j4):
        for qb in range(4):
            for j in range(4):
                tt = qb * 4 + j
                P.op("pe", TR(c.psb[:, j * 128:(j + 1) * 128], Y[:, tt, j4 * 128:(j4 + 1) * 128], c.ident),
                     reads=[t_Y[tt], c.t_ident], writes=[c.tpsb[0]])
            ys, t_ys = yst[ycnt[0] % 2], t_yst[ycnt[0] % 2]
            ycnt[0] += 1
            P.op("dve", CP(ys, c.psb[:, 0:512]), reads=[c.tpsb[0]], writes=[t_ys])
            P.op("sp", DMA(c.yT[l][1536 + j4 * 128:1536 + (j4 + 1) * 128, qb * 512:(qb + 1) * 512], ys), reads=[t_ys])
    pending_tr = []
    load_tab(0)
    emit_S(0)
    if NI > 1:
        emit_S(1)
    emit_exp_mul(0)
    for n, (hh, i) in enumerate(its):
        if i == 2:
            load_tab(hh + 1)
        plan = NA_PLAN[i]
        nb = len(plan)
        Ob, t_Ob = ps[4 + n % 2], tps[4 + n % 2]
        P_, t_P = Pb[n % 3], t_Pb[n % 3]
        for bi, (kb, cls) in enumerate(plan):
            P.op("pe", MM(Ob[:, 0:65], P_[:, bi * 128:(bi + 1) * 128], V[:, kb, hh, :], start=(bi == 0), stop=(bi == nb - 1)),
                 reads=[t_P, t_V[kb]], writes=[t_Ob])
        if n + 2 < NI:
            emit_S(n + 2)
        if n + 1 < NI:
            emit_exp_mul(n + 1)
        r_, t_r = rr[n % 2], t_rr[n % 2]
        P.op("dve", RECIP(r_[:, 0:1], Ob[:, 64:65]), reads=[t_Ob], writes=[t_r])
        P.op("dve", STT(Y[:, i, hh * 64:(hh + 1) * 64], Ob[:, 0:64], r_[:, 0:1], G[:, i, hh * 64:(hh + 1) * 64], ALU.mult, ALU.mult),
             reads=[t_Ob, t_r, t_G[i]], writes=[t_Y[i]])
        if hh % 2 == 1 and i == NT - 1:
            pending_tr.append((n + 3, hh // 2))
        for (due, j4_) in [p_ for p_ in pending_tr if p_[0] <= n]:
            emit_ytrans(j4_)
        pending_tr = [p_ for p_ in pending_tr if p_[0] > n]
    for (due, j4_) in pending_tr:
        emit_ytrans(j4_)
    A.release(m)


def phase_ssd(c, l):
    STOP = 9
    P, A = c.P, c.A
    m = A.mark()
    ps, tps = c.ps, c.tps
    nc = c.nc
    Wdt = A.alloc([128, 8, 32], BF16)
    tWdt = [T()]
    load_w(c, l, Wdt, C_DT, 32, tWdt)
    t_sm = T()
    dtb = A.alloc([128, 32], F32)
    alog = A.alloc([128, 32], F32)
    dsk = A.alloc([128, 32], F32)
    P.op("sp", DMA(dtb, c.dt_bias[l:l + 1, :].partition_broadcast(128)), writes=[t_sm])
    t_al = T()
    P.op("sp", DMA(alog, c.a_log[l:l + 1, :].partition_broadcast(128)), writes=[t_al])
    t_dsk = T()
    P.op("sp", DMA(dsk, c.d_skip[l:l + 1, :].partition_broadcast(128)), writes=[t_dsk])
    dsum = A.alloc([128, 16], F32)
    P.op("dve", TT(dsum, dsk[:, 0:16], dsk[:, 16:32], ALU.add), reads=[t_dsk], writes=[t_dsk])
    P.op("act", ACT(alog, alog, AF.Exp), reads=[t_al], writes=[t_al])
    P.op("dve", TS(alog, alog, -1.0, ALU.mult), reads=[t_al], writes=[t_al])
    snw = A.alloc([128, 1024], F32)
    t_snw = T()
    P.op("sp", DMA(snw, c.ssd_norm_w[l:l + 1, :].partition_broadcast(128)), writes=[t_snw])
    cbb = A.alloc([128, 1536], F32)
    t_cbb = T()
    P.op("sp", DMA(cbb, c.conv_b[l:l + 1, :].partition_broadcast(128)), writes=[t_cbb])
    cwT = A.alloc([128, 12, 6], F32)
    t_cwT = T()

    def a3(shape=(128, NT, 32)):
        return A.alloc(list(shape), F32)
    dt = a3()
    dta = a3()
    cs = a3()
    dcb = a3()
    dout = a3()
    dend = a3()
    cXd = a3()
    t_dt, t_dta, t_cs, t_dcb, t_dout, t_dend, t_cXd, t_v, t_tmp3 = [T() for _ in range(9)]
    t_csd = T()
    csT_v = c.csT_d.rearrange("t (d h) l -> t d h l", d=2)
    m_tmp = A.mark()
    cw6 = A.alloc([128, 1536], F32)
    t_cw6 = T()
    P.op("sp", DMA(cw6[0:5, :], c.conv_w[l, :, :]), writes=[t_cw6])
    P.op("sp", DMA(cw6[5:6, :], c.conv_b[l:l + 1, :]), writes=[t_cw6])
    for j in range(12):
        P.op("pe", TR(ps[6][:, j * 6:(j + 1) * 6], cw6[0:6, j * 128:(j + 1) * 128], c.identf[0:6, 0:6]),
             reads=[t_cw6, c.t_cst], writes=[tps[6]])
    P.op("dve", CP(cwT, ps[6][:, 0:72].rearrange("p (j k) -> p j k", k=6)), reads=[tps[6]], writes=[t_cwT])
    P.barrier()
    A.release(m_tmp)

    def make_decay_stages(v, tmp3, csT):
        t_csT = T()

        def s1():
            for tt in range(NT):
                for k in range(8):
                    P.op("pe", MM(ps[4][:, tt * 32:(tt + 1) * 32], c.hdnT[:, k, tt * 128:(tt + 1) * 128], Wdt[:, k, :],
                                  start=(k == 0), stop=(k == 7), skip=True),
                         reads=[c.t_hdnT[tt], tWdt[0]], writes=[tps[4]])
            p0 = ps[4].rearrange("p (t h) -> p t h", h=32)
            P.op("dve", TT(v, p0, bc(dtb.unsqueeze(1), [128, NT, 32]), ALU.add), reads=[tps[4], t_sm], writes=[t_v])
            P.op("act", ACT(tmp3, v, AF.Abs), reads=[t_v], writes=[t_tmp3])
            P.op("act", ACT(tmp3, tmp3, AF.Exp, scale=-1.0), reads=[t_tmp3], writes=[t_tmp3])
            P.op("act", ACT(tmp3, tmp3, AF.Ln, bias=c.onec, scale=1.0), reads=[t_tmp3, c.t_eps], writes=[t_tmp3])

        def s2():
            P.op("dve", STT(dt, v, 0.0, tmp3, ALU.max, ALU.add), reads=[t_v, t_tmp3], writes=[t_dt])
            P.op("dve", TT(dta, dt, bc(alog.unsqueeze(1), [128, NT, 32]), ALU.mult), reads=[t_dt, t_al], writes=[t_dta])
            for tt in range(NT):
                P.op("pe", MM(ps[5][:, tt * 32:tt * 32 + 16], c.U, dta[:, tt, 0:16], skip=True), reads=[t_dta, c.t_cst], writes=[tps[5]])
                P.op("pe", MM(ps[5][:, tt * 32 + 16:tt * 32 + 32], c.Ur, dta[:, tt, 16:32], skip=True), reads=[t_dta, c.t_cst], writes=[tps[5]])
            for tt in range(NT):
                P.op("pe", MM(ps[6][:, tt * 32:(tt + 1) * 32], c.onesf, dta[:, tt, :], skip=True), reads=[t_dta, c.t_cst], writes=[tps[6]])
            P.op("act", ACT(cs, ps[5].rearrange("p (t h) -> p t h", h=32), AF.Copy), reads=[tps[5]], writes=[t_cs])
            P.op("act", ACT(tmp3, ps[6].rearrange("p (t h) -> p t h", h=32), AF.Copy), reads=[tps[6]], writes=[t_tmp3])

        def s3():
            P.op("act", ACT(dcb, tmp3, AF.Exp), reads=[t_tmp3], writes=[t_dcb])
            P.op("act", ACT(dout, cs, AF.Exp), reads=[t_cs], writes=[t_dout])
            P.op("dve", TT(dend, tmp3, cs, ALU.subtract), reads=[t_tmp3, t_cs], writes=[t_dend])
            P.op("act", ACT(dend, dend, AF.Exp), reads=[t_dend], writes=[t_dend])
            P.op("dve", TT(cXd, dt, dend, ALU.mult), reads=[t_dt, t_dend], writes=[t_cXd])

        def s4():
            for q in range(4):
                for j in range(4):
                    tt = q * 4 + j
                    P.op("pe", TR(ps[4][0:32, j * 128:(j + 1) * 128], cs[:, tt, :], c.identf), reads=[t_cs, c.t_cst], writes=[tps[4]])
                P.op("act", ACT(csT[0:32, q * 4:(q + 1) * 4, :], ps[4][0:32, :].rearrange("p (j t) -> p j t", t=128), AF.Copy),
                     reads=[tps[4]], writes=[t_csT])
            P.op("sp", DMA(c.csT_d.rearrange("t h l -> h t l"), csT[0:32, :, :]), reads=[t_csT], writes=[t_csd])
        return [s1, s2, s3, s4]

    for g in range(2):
        mg = A.mark()
        Wz = c.Wz[g]
        tWz = pf_ssd(c, l)[g]
        t_Wzq = c.t_Wq[3 - g]
        xs = A.alloc([128, NT, 512], F32)
        t_xs = [T() for _ in range(NT)]
        Btok = A.alloc([128, NT, 128], BF16)
        t_Btok = [T() for _ in range(NT)]
        BT = A.alloc([128, L], BF16)
        t_BT = [T() for _ in range(4)]
        CT = A.alloc([128, L], BF16)
        t_CT = [T() for _ in range(4)]
        m_conv = A.mark()
        stages = []
        if g == 0:
            stages = make_decay_stages(A.alloc([128, NT, 32], F32), A.alloc([128, NT, 32], F32), A.alloc([128, NT, 128], F32))
        pre = [A.alloc([128, L + 4], BF16) for _ in range(2)]
        t_pre = [[T() for _ in range(4)] for _ in range(2)]
        t_halo = [T(), T()]
        for b in range(2):
            P.op("pool", MEMSET(pre[b][:, 0:2], 0.0), writes=[t_halo[b]])
            P.op("pool", MEMSET(pre[b][:, L + 2:L + 4], 0.0), writes=[t_halo[b]])
        Wc = [A.alloc([128, 8, 128], BF16) for _ in range(2)]
        tWc = [[T()], [T()]]
        dg = [A.alloc([128, 5, 128], BF16) for _ in range(2)]
        t_dg = [T(), T()]
        ctmp = [A.alloc([128, 512], F32) for _ in range(2)]
        t_ctmp = [T(), T()]
        chunks = [("B", 8 + g), ("C", 10 + g)] + [("x%d" % j, 4 * g + j) for j in range(4)]
        pbank = 0
        for n, (kind, ci) in enumerate(chunks):
            b = n % 2
            if n >= 1 and stages:
                stages.pop(0)()
            load_w(c, l, Wc[b], C_XBC + ci * 128, 128, tWc[b])
            P.op("dve", TT(dg[b], bc(c.identf.unsqueeze(1), [128, 5, 128]), bc(cwT[:, ci, 0:5].unsqueeze(2), [128, 5, 128]), ALU.mult),
                 reads=[c.t_cst, t_cwT], writes=[t_dg[b]])
            for tb in range(4):
                bank, tbk = ps[pbank % 4], tps[pbank % 4]
                pbank += 1
                for k in range(8):
                    P.op("pe", MM(bank, Wc[b][:, k, :], c.hdnT[:, k, tb * 512:(tb + 1) * 512], start=(k == 0), stop=(k == 7)),
                         reads=[tWc[b][0]] + c.t_hdnT[tb * 4:(tb + 1) * 4], writes=[tbk])
                P.op("act", ACT(pre[b][:, 2 + tb * 512:2 + (tb + 1) * 512], bank, AF.Copy), reads=[tbk], writes=[t_pre[b][tb]])
            allpre = t_pre[b] + [t_halo[b]]
            if kind in ("B", "C"):
                dstT, t_dstT = (BT, t_BT) if kind == "B" else (CT, t_CT)
                for tb in range(4):
                    bank, tbk = ps[pbank % 4], tps[pbank % 4]
                    pbank += 1
                    for k in range(5):
                        P.op("pe", MM(bank, dg[b][:, k, :], pre[b][:, tb * 512 + k:tb * 512 + k + 512], start=(k == 0), stop=(k == 4)),
                             reads=[t_dg[b]] + allpre, writes=[tbk])
                    P.op("act", ACT(dstT[:, tb * 512:(tb + 1) * 512], bank, AF.Silu, bias=cwT[:, ci, 5:6], scale=1.0),
                         reads=[tbk, t_cwT], writes=[t_dstT[tb]])
            if kind != "C":
                for q in range(4):
                    bank, tbk = ps[pbank % 4], tps[pbank % 4]
                    pbank += 1
                    for j in range(4):
                        tt = q * 4 + j
                        for k in range(5):
                            P.op("pe", MM(bank[:, j * 128:(j + 1) * 128], pre[b][:, tt * 128 + k:tt * 128 + k + 128], dg[b][:, k, :],
                                          start=(k == 0 and j == 0), stop=(k == 4), skip=True),
                                 reads=[t_dg[b]] + allpre, writes=[tbk])
                    ct, t_ct = ctmp[q % 2], t_ctmp[q % 2]
                    P.op("dve", TT(ct.rearrange("p (j c) -> p j c", c=128), bank.rearrange("p (j c) -> p j c", c=128),
                                   bc(cbb[:, ci * 128:(ci + 1) * 128].unsqueeze(1), [128, 4, 128]), ALU.add),
                         reads=[tbk, t_cbb], writes=[t_ct])
                    if kind == "B":
                        P.op("act", ACT(Btok[:, q * 4:(q + 1) * 4, :], ct.rearrange("p (j c) -> p j c", c=128), AF.Silu),
                             reads=[t_ct], writes=t_Btok[q * 4:(q + 1) * 4])
                    else:
                        jx = int(kind[1])
                        P.op("act", ACT(xs[:, q * 4:(q + 1) * 4, jx * 128:(jx + 1) * 128], ct.rearrange("p (j c) -> p j c", c=128), AF.Silu),
                             reads=[t_ct], writes=t_xs[q * 4:(q + 1) * 4])

        for st_ in stages:
            st_()
        stages = []
        P.barrier()
        A.release(m_conv)
        if STOP <= 2:
            A.release(mg)
            continue
        hb0 = 16 + 8 * g
        hf0 = 8 * g
        m_passA = A.mark()
        Sst = [A.alloc([128, 512], F32) for _ in range(2)]
        t_Sst = [T(), T()]
        for d in range(2):
            P.op("pool", MEMSET(Sst[d], 0.0), writes=[t_Sst[d]])
        stg = [[A.alloc([128, 512], BF16) for _ in range(2)] for _ in range(2)]
        t_stg = [[T(), T()], [T(), T()]]
        Xd = [[A.alloc([128, 512], BF16) for _ in range(2)] for _ in range(2)]
        t_Xd = [[T(), T()], [T(), T()]]
        t_sd = [[T() for _ in range(NT)] for _ in range(2)]
        sdram = [c.sf_d, c.sb_d]
        h0s = [hf0, hb0]

        def bh(ap3, h0):
            return bc(ap3[:, h0:h0 + 8].unsqueeze(2), [128, 8, 64])

        def v8(ap):
            return ap.rearrange("p (h d) -> p h d", d=64)
        for k in range(NT):
            for d in range(2):
                ci_ = k if d == 0 else NT - 1 - k
                last = (ci_ == NT - 1) if d == 0 else (ci_ == 0)
                sg, t_sg = stg[d][k % 2], t_stg[d][k % 2]
                P.op("act", ACT(sg, Sst[d], AF.Copy), reads=[t_Sst[d]], writes=[t_sg])
                P.op("sp", DMA(sdram[d][g, ci_], sg), reads=[t_sg], writes=[t_sd[d][ci_]])
                if not last:
                    xd, t_xd = Xd[d][k % 2], t_Xd[d][k % 2]
                    P.op("pool", TT(v8(xd), v8(xs[:, ci_, :]), bh(cXd[:, ci_, :], h0s[d]), ALU.mult), reads=[t_xs[ci_], t_cXd], writes=[t_xd])
                    bank, tbk = ps[4 + d], tps[4 + d]
                    P.op("pe", MM(bank, Btok[:, ci_, :], xd), reads=[t_Btok[ci_], t_xd], writes=[tbk])
                    P.op("dve", TT(v8(Sst[d]), v8(Sst[d]), bh(dcb[:, ci_, :], h0s[d]), ALU.mult), reads=[t_Sst[d], t_dcb], writes=[t_Sst[d]])
                    P.op("dve", TT(Sst[d], Sst[d], bank, ALU.add), reads=[t_Sst[d], tbk], writes=[t_Sst[d]])
        P.barrier()
        A.release(m_passA)
        if STOP <= 3:
            A.release(mg)
            continue
        R = [A.alloc([128, 2, 8, 128], F32) for _ in range(2)]
        t_R = [T(), T()]
        SFc = [A.alloc([128, 512], BF16) for _ in range(2)]
        t_SFc = [T(), T()]
        SBc = [A.alloc([128, 512], BF16) for _ in range(2)]
        t_SBc = [T(), T()]
        E = A.alloc([128, 2, 8, 128], BF16)
        t_E = T()
        Mt = [A.alloc([128, 2, 8, 128], BF16) for _ in range(2)]
        t_Mt = [T(), T()]
        Gm = [A.alloc([128, 2, 128], BF16) for _ in range(2)]
        t_Gm = [T(), T()]
        Xt = [A.alloc([128, 3, 512], BF16) for _ in range(2)]
        t_Xt = [T(), T()]
        sz = [A.alloc([128, 512], F32) for _ in range(4)]
        t_sz = [T() for _ in range(4)]
        ya = A.alloc([128, 512], F32)
        yb_ = A.alloc([128, 512], F32)
        yy = A.alloc([128, 512], F32)
        t_ya, t_yb, t_yy = T(), T(), T()
        ssn = A.alloc([128, 2], F32)
        t_ssn = T()
        yo = [A.alloc([128, 512], BF16) for _ in range(2)]
        t_yo = [T(), T()]
        yst = [A.alloc([128, 4, 128], BF16) for _ in range(2)]
        t_yst = [T(), T()]

        def loadR(ci_):
            b = ci_ % 2
            P.op("sp", DMA(R[b].rearrange("p d h l -> p d (h l)"),
                           csT_v[ci_, :, 8 * g:8 * g + 8, :].rearrange("d h l -> d (h l)").partition_broadcast(128)),
                 reads=[t_csd], writes=[t_R[b]])

        def loadS(ci_):
            b = ci_ % 2
            P.op("sp", DMA(SFc[b], c.sf_d[g, ci_]), reads=[t_sd[0][ci_]], writes=[t_SFc[b]])
            P.op("sp", DMA(SBc[b], c.sb_d[g, ci_]), reads=[t_sd[1][ci_]], writes=[t_SBc[b]])

        def iteration(cn, cc_):
            if cn is not None:
                bn = cn % 2
                tokn = slice(cn * 128, (cn + 1) * 128)
                P.op("pe", MM(ps[4][:, 0:128], BT[:, tokn], CT[:, tokn]), reads=[t_BT[cn // 4], t_CT[cn // 4]], writes=[tps[4]])
                if cn % 2 == 0:
                    for c2 in (cn, cn + 1):
                        if c2 < NT:
                            zb, t_zb = (ps[3], tps[3]) if c2 % 2 == 0 else (ps[6], tps[6])
                            tok2 = slice(c2 * 128, (c2 + 1) * 128)
                            for k in range(8):
                                P.op("pe", MM(zb, c.hdnT[:, k, tok2], Wz[:, k, :], start=(k == 0), stop=(k == 7)),
                                     reads=[c.t_hdnT[c2], tWz[0], t_Wzq], writes=[t_zb])
                P.op("dve", TT(Gm[bn][:, 0, :], ps[4][:, 0:128], c.U, ALU.mult), reads=[tps[4], c.t_cst], writes=[t_Gm[bn]])
                P.op("dve", TT(Gm[bn][:, 1, :], ps[4][:, 0:128], c.Ur, ALU.mult), reads=[tps[4], c.t_cst], writes=[t_Gm[bn]])
                csv = cs[:, cn, :].rearrange("p (d h) -> p d h", d=2)[:, :, 8 * g:8 * g + 8]
                Dd, t_D = R[bn], t_R[bn]
                P.op("dve", TT(Dd, Dd, bc(csv.unsqueeze(3), [128, 2, 8, 128]), ALU.subtract), reads=[t_cs], writes=[t_D])
                P.op("act", ACT(Dd, Dd, AF.Relu, scale=-1.0), reads=[], writes=[t_D])
                P.op("act", ACT(E, Dd, AF.Exp, scale=-1.0), reads=[t_D], writes=[t_E])
                P.op("pool", TT(v8(Xt[bn][:, 0, :]), v8(xs[:, cn, :]), bh(dt[:, cn, :], hf0), ALU.mult), reads=[t_xs[cn], t_dt], writes=[t_Xt[bn]])
                P.op("pool", TT(v8(Xt[bn][:, 1, :]), v8(xs[:, cn, :]), bh(dt[:, cn, :], hb0), ALU.mult), reads=[t_xs[cn], t_dt], writes=[t_Xt[bn]])
                P.op("pool", TT(v8(Xt[bn][:, 2, :]), v8(xs[:, cn, :]), bh(dsum, 8 * g), ALU.mult), reads=[t_xs[cn], t_dsk], writes=[t_Xt[bn]])
            if cc_ is not None:
                b = cc_ % 2
                tok = slice(cc_ * 128, (cc_ + 1) * 128)
                Yb_, t_Y = (ps[0], tps[0]) if b == 0 else (ps[5], tps[5])
                P.op("pe", MM(Yb_, c.ident, Xt[b][:, 2, :], start=True, stop=False, skip=True), reads=[c.t_ident, t_Xt[b]], writes=[t_Y])
                for h in range(8):
                    for d in range(2):
                        P.op("pe", MM(Yb_[:, h * 64:(h + 1) * 64], Mt[b][:, d, h, :], Xt[b][:, d, h * 64:(h + 1) * 64],
                                      start=False, stop=(d == 1), skip=True),
                             reads=[t_Mt[b], t_Xt[b]], writes=[t_Y])
                P.op("pe", MM(ps[1], CT[:, tok], SFc[b]), reads=[t_CT[cc_ // 4], t_SFc[b]], writes=[tps[1]])
                P.op("pe", MM(ps[2], CT[:, tok], SBc[b]), reads=[t_CT[cc_ // 4], t_SBc[b]], writes=[tps[2]])
                P.op("dve", TT(v8(ya), v8(ps[1]), bh(dout[:, cc_, :], hf0), ALU.mult), reads=[tps[1], t_dout], writes=[t_ya])
                P.op("dve", TT(v8(yb_), v8(ps[2]), bh(dout[:, cc_, :], hb0), ALU.mult), reads=[tps[2], t_dout], writes=[t_yb])
                P.op("dve", TT(ya, ya, yb_, ALU.add), reads=[t_ya, t_yb], writes=[t_ya])
                P.op("dve", TT(yy, Yb_, ya, ALU.add), reads=[t_Y, t_ya], writes=[t_yy])
                P.op("pool", TT(yy, yy, sz[cc_ % 4], ALU.mult), reads=[t_yy, t_sz[cc_ % 4]], writes=[t_yy])
            if cn is not None:
                for d in range(2):
                    P.op("dve", TT(Mt[bn][:, d], E[:, d], bc(Gm[bn][:, d, :].unsqueeze(1), [128, 8, 128]), ALU.mult),
                         reads=[t_E, t_Gm[bn]], writes=[t_Mt[bn]])
                if cn % 2 == 0:
                    for c2 in (cn, cn + 1):
                        if c2 < NT:
                            zb, t_zb = (ps[3], tps[3]) if c2 % 2 == 0 else (ps[6], tps[6])
                            P.op("act", ACT(sz[c2 % 4], zb, AF.Silu), reads=[t_zb], writes=[t_sz[c2 % 4]])
            if cc_ is not None:
                P.op("act", ACT(ya, yy, AF.Square, accum_out=ssn[:, 0:1]), reads=[t_yy], writes=[t_ya, t_ssn])
                P.op("act", ACT(ssn[:, 0:1], ssn[:, 0:1], AF.Ln, scale=1.0 / 512, bias=c.epsc), reads=[t_ssn, c.t_eps], writes=[t_ssn])
                P.op("act", ACT(ssn[:, 0:1], ssn[:, 0:1], AF.Exp, scale=-0.5), reads=[t_ssn], writes=[t_ssn])
                P.op("dve", STT(yo[b], yy, ssn[:, 0:1], snw[:, g * 512:(g + 1) * 512], ALU.mult, ALU.mult),
                     reads=[t_yy, t_ssn, t_snw], writes=[t_yo[b]])

        def stageC(ci_):
            b = ci_ % 2
            tok = slice(ci_ * 128, (ci_ + 1) * 128)
            for j in range(4):
                P.op("pe", TR(c.psb[:, j * 128:(j + 1) * 128], yo[b][:, j * 128:(j + 1) * 128], c.ident),
                     reads=[t_yo[b], c.t_ident], writes=[c.tpsb[0]])
            P.op("dve", CP(yst[b], c.psb[:, 0:512].rearrange("p (j t) -> p j t", t=128)), reads=[c.tpsb[0]], writes=[t_yst[b]])
            P.op("sp", DMA(c.yT[l][g * 512:(g + 1) * 512, tok].rearrange("(j p) t -> p j t", p=128), yst[b]), reads=[t_yst[b]])

        loadR(0)
        loadS(0)
        loadR(1)
        iteration(0, None)
        for ci_ in range(NT):
            if ci_ + 1 < NT:
                loadS(ci_ + 1)
                if ci_ + 2 < NT:
                    loadR(ci_ + 2)
            iteration(ci_ + 1 if ci_ + 1 < NT else None, ci_)
            if ci_ >= 1:
                stageC(ci_ - 1)
        stageC(NT - 1)
        A.release(mg)
    A.release(m)


def phase_out(c, l, xsrc, xdst, fuse_next=False):
    P, A = c.P, c.A
    m = A.mark()
    if fuse_next:
        nwb = A.alloc([128, DM], F32)
        t_nwb = T()
        P.op("sp", DMA(nwb, c.norm_w[l + 1:l + 2, :].partition_broadcast(128)), writes=[t_nwb])
        sqj = A.alloc([128, DM], F32)
        t_sqj = T()
        ssx = A.alloc([128, NT], F32)
        t_ssx = [T() for _ in range(NT)]
        hb = [A.alloc([128, DM], BF16) for _ in range(2)]
        t_hb = [T(), T()]
    Wo = c.Wo
    tWo = pf_out(c, l, [0, 1, 2, 3])
    yt = [A.alloc([128, 16, 512], BF16) for _ in range(2)]
    t_yt = [T(), T()]
    xt = [A.alloc([128, DM], F32) for _ in range(3)]
    t_xt = [T(), T(), T()]
    ot = [A.alloc([128, DM], F32) for _ in range(2)]
    t_ot = [T(), T()]
    ps, tps = c.ps, c.tps
    fin = []

    def load_y(qb):
        P.op("sp", DMA(yt[qb % 2], c.yT[l][:, qb * 512:(qb + 1) * 512].rearrange("(k p) t -> p k t", p=128)), writes=[t_yt[qb % 2]])

    def load_x(tt):
        P.op("sp", DMA(xt[tt % 3], xsrc[tt * 128:(tt + 1) * 128, :]), writes=[t_xt[tt % 3]])
    load_y(0)
    load_x(0)
    load_x(1)
    nb = 0
    for tt in range(NT):
        b = tt % 2
        qb, j = tt // 4, tt % 4
        tok = slice(tt * 128, (tt + 1) * 128)
        if j == 0 and qb + 1 < 4:
            load_y(qb + 1)
        if tt + 2 < NT:
            load_x(tt + 2)
        for n in range(2):
            bank, tb = ps[nb % 6], tps[nb % 6]
            nb += 1
            for k in range(16):
                P.op("pe", MM(bank, yt[qb % 2][:, k, j * 128:(j + 1) * 128], Wo[:, k, n * 512:(n + 1) * 512], start=(k == 0), stop=(k == 15)),
                     reads=[t_yt[qb % 2], tWo[k // 4], c.t_Wq[k // 4]], writes=[tb])
            P.op("dve", TT(ot[b][:, n * 512:(n + 1) * 512], bank, xt[tt % 3][:, n * 512:(n + 1) * 512], ALU.add),
                 reads=[tb, t_xt[tt % 3]], writes=[t_ot[b]])
        fin.append(P.op("sp", DMA(xdst[tok, :], ot[b]), reads=[t_ot[b]]))
        if fuse_next:
            s1 = ssx[:, tt:tt + 1]
            P.op("act", ACT(sqj, ot[b], AF.Square, accum_out=s1), reads=[t_ot[b]], writes=[t_sqj, t_ssx[tt]])
            P.op("act", ACT(s1, s1, AF.Ln, scale=1.0 / DM, bias=c.epsc), reads=[t_ssx[tt], c.t_eps], writes=[t_ssx[tt]])
            P.op("act", ACT(s1, s1, AF.Exp, scale=-0.5), reads=[t_ssx[tt]], writes=[t_ssx[tt]])
            P.op("dve", STT(hb[b], ot[b], s1, nwb, ALU.mult, ALU.mult), reads=[t_ot[b], t_ssx[tt], t_nwb], writes=[t_hb[b]])
            for k in range(8):
                P.op("pe", TR(c.psb[:, k * 128:(k + 1) * 128], hb[b][:, k * 128:(k + 1) * 128], c.ident),
                     reads=[t_hb[b], c.t_ident], writes=[c.tpsb[0]])
            P.op("act", ACT(c.hdnT[:, :, tok], c.psb[:, :].rearrange("p (k t) -> p k t", t=128), AF.Copy),
                 reads=[c.tpsb[0]], writes=[c.t_hdnT[tt]])
    A.release(m)
    return fin


def _host_consts():
    inv_freq = (10000.0 ** (-(np.arange(0, 64, 2, dtype=np.float32)) / np.float32(64))).astype(np.float32)
    ang = (np.arange(L, dtype=np.float32)[:, None] * inv_freq[None, :]).astype(np.float32)
    cos, sin = np.cos(ang).astype(np.float32), np.sin(ang).astype(np.float32)
    cs = np.concatenate([cos, cos, -sin, sin], axis=1).astype(np.float32)
    k = np.arange(128)
    ident = np.eye(128, dtype=np.float32)
    U = (k[:, None] <= k[None, :]).astype(np.float32)
    Ur = (k[:, None] >= k[None, :]).astype(np.float32)
    ones = np.ones((128, 128), np.float32)
    consts = np.concatenate([ident, U, Ur, ones], axis=1)
    return np.ascontiguousarray(cs), np.ascontiguousarray(consts)


_CACHE = {}


def make_in_maps(inputs, n_cores=8):
    cs, consts = _host_consts()
    f = lambda a: np.ascontiguousarray(np.asarray(a, dtype=np.float32))
    shared = {
        "norm_w": f(inputs["norm_w"]), "w_in": f(inputs["w_in"]), "conv_w": f(inputs["conv_w"]),
        "conv_b": f(inputs["conv_b"]), "a_log": f(inputs["a_log"]).reshape(2, 32),
        "dt_bias": f(inputs["dt_bias"]).reshape(2, 32), "d_skip": f(inputs["d_skip"]).reshape(2, 32),
        "ssd_norm_w": f(inputs["ssd_norm_w"]), "diff_qk_norm": f(inputs["diff_qk_norm"]).reshape(2, 128),
        "diff_lambda": f(inputs["diff_lambda"]).reshape(2, 256), "diff_subln": f(inputs["diff_subln"]),
        "na_qk_norm": f(inputs["na_qk_norm"]).reshape(2, 128), "na_tab": _na_tables(f(inputs["na_rpb"])),
        "w_out": f(inputs["w_out"]), "cs_tab": cs, "consts": consts,
    }
    x = f(inputs["x"])
    return [dict(shared, x=x[b]) for b in range(n_cores)]


def kernel(**inputs):
    if "nc" not in _CACHE:
        _CACHE["nc"] = build()[0]
    nc = _CACHE["nc"]
    in_maps = make_in_maps(inputs)
    res = run_bass_kernel_spmd(nc, in_maps, core_ids=list(range(8)))
    return np.stack([np.asarray(r["out"], dtype=np.float32) for r in res.results], axis=0)
```

```python
import contextlib
import math
import numpy as np
import concourse.bass as bass
import concourse.mybir as mybir
from concourse.bass_utils import run_bass_kernel_spmd

F32 = mybir.dt.float32
BF16 = mybir.dt.bfloat16
AF = mybir.ActivationFunctionType
ALU = mybir.AluOpType
AX = mybir.AxisListType

L = 2048
DM = 1024
NT = 16
INW = 6688
EPS = 1e-6
NEG = -30000.0
C_Z, C_XBC, C_DT, C_DIFF, C_NA = 0, 1024, 2560, 2592, 4640


class T:
    __slots__ = ("w", "r", "excl")

    def __init__(self, excl=False):
        self.w = None
        self.r = []
        self.excl = excl


class Op:
    __slots__ = ("eng", "fn", "deps", "idx", "sig", "isdma", "semid", "semval", "prev_same_sem")


class Prog:
    COMPUTE = ("pe", "act", "dve", "pool")
    DMAQ = ("sp", "actq", "poolq")
    STREAM = {"pe": "pe", "act": "act", "dve": "dve", "pool": "pool", "sp": "sp", "actq": "act", "poolq": "pool"}
    STREAMS = ("pe", "act", "dve", "pool", "sp")

    def __init__(self, nc, n_dma_sems=12):
        self.nc = nc
        self.ops = []
        self.n_dma_sems = n_dma_sems
        self.last = {s: None for s in self.STREAMS}
        self.recent_dma = {q: [] for q in self.DMAQ}
        self.frontier = []
        self.synced = {s: True for s in self.STREAMS}

    def barrier(self):
        fr = [o for o in self.last.values() if o is not None]
        for q in self.DMAQ:
            fr.extend(self.recent_dma[q])
        self.frontier = fr
        self.synced = {s: False for s in self.STREAMS}

    def op(self, eng, fn, reads=(), writes=()):
        o = Op()
        o.eng = eng
        o.fn = fn
        o.isdma = eng in self.DMAQ
        o.idx = len(self.ops)
        o.prev_same_sem = None
        deps = {}
        if any(t.excl for t in reads):
            writes = list(writes) + [t for t in reads if t.excl and t not in writes]
            reads = [t for t in reads if not t.excl]
        for t in reads:
            if t.w is not None:
                deps[t.w.idx] = ("raw", t.w)
        for t in writes:
            if t.w is not None and t.w.idx not in deps:
                deps[t.w.idx] = ("waw", t.w)
            for r in t.r:
                if r.idx not in deps:
                    deps[r.idx] = ("war", r)
        st = self.STREAM[eng]
        if not self.synced[st]:
            for p in self.frontier:
                if p.idx not in deps:
                    deps[p.idx] = ("bar", p)
            self.synced[st] = True
        for t in writes:
            t.w = o
            t.r = []
        for t in reads:
            if t.w is not o:
                t.r.append(o)
        o.deps = deps
        o.sig = False
        self.ops.append(o)
        self.last[st] = o
        if o.isdma:
            lst = self.recent_dma[eng]
            lst.append(o)
            if len(lst) > self.n_dma_sems:
                lst.pop(0)
        return o

    def emit(self, final_waits=()):
        nc = self.nc
        streams = {s: [] for s in self.STREAMS}
        for o in self.ops:
            streams[self.STREAM[o.eng]].append(o)
        pos = {}
        for s, lst in streams.items():
            for i, o in enumerate(lst):
                pos[o.idx] = i
        need = {}
        for o in self.ops:
            lst = []
            so = self.STREAM[o.eng]
            for (kind, p) in o.deps.values():
                sp_ = self.STREAM[p.eng]
                if p.isdma or o.isdma:
                    lst.append(p)
                elif sp_ != so:
                    lst.append(p)
                else:
                    if so == "pe":
                        continue
                    lst.append(p)
            need[o.idx] = lst
            for p in lst:
                p.sig = True
        for o in final_waits:
            o.sig = True
        stack = contextlib.ExitStack()
        esem = {e: stack.enter_context(nc.semaphore("s_" + e)) for e in self.COMPUTE}
        ecount = {e: 0 for e in self.COMPUTE}
        dsems = {q: [stack.enter_context(nc.semaphore("d_%s_%d" % (q, i))) for i in range(self.n_dma_sems)]
                 for q in self.DMAQ}
        duse = {q: [0] * self.n_dma_sems for q in self.DMAQ}
        dlast = {q: [None] * self.n_dma_sems for q in self.DMAQ}
        dnext = {q: 0 for q in self.DMAQ}
        for s in self.STREAMS:
            for o in streams[s]:
                if o.isdma:
                    q = o.eng
                    j = dnext[q]
                    dnext[q] = (j + 1) % self.n_dma_sems
                    duse[q][j] += 1
                    o.semid = (q, j)
                    o.semval = 16 * duse[q][j]
                    o.prev_same_sem = dlast[q][j]
                    dlast[q][j] = o
                elif o.sig:
                    ecount[o.eng] += 1
                    o.semid = o.eng
                    o.semval = ecount[o.eng]
        self.n_waits = 0

        def sem_of(p):
            if p.isdma:
                return dsems[p.semid[0]][p.semid[1]]
            return esem[p.semid]

        def emit_stream(s, engobj):
            known = {}
            for o in streams[s]:
                waits = {}
                cand = list(need[o.idx])
                if o.isdma and o.prev_same_sem is not None:
                    cand.append(o.prev_same_sem)
                for p in cand:
                    k = p.semid
                    if known.get(k, 0) >= p.semval:
                        continue
                    if waits.get(k, (0, None))[0] < p.semval:
                        waits[k] = (p.semval, p)
                for k, (v, p) in waits.items():
                    engobj.wait_ge(sem_of(p), v)
                    known[k] = v
                    self.n_waits += 1
                ins = o.fn(engobj)
                if o.isdma:
                    ins.then_inc(dsems[o.semid[0]][o.semid[1]], 16)
                elif o.sig:
                    ins.then_inc(esem[o.semid], 1)
            if s == "sp":
                for o in final_waits:
                    engobj.wait_ge(sem_of(o), o.semval)

        with nc.Block() as block:
            @block.tensor
            def _(e):
                emit_stream("pe", e)

            @block.scalar
            def _(e):
                emit_stream("act", e)

            @block.vector
            def _(e):
                emit_stream("dve", e)

            @block.gpsimd
            def _(e):
                emit_stream("pool", e)

            @block.sync
            def _(e):
                emit_stream("sp", e)
        stack.close()


def ACT(out, in_, func, **kw):
    return lambda e: e.activation(out=out, in_=in_, func=func, **kw)


def TT(out, in0, in1, op):
    return lambda e: e.tensor_tensor(out=out, in0=in0, in1=in1, op=op)


def TS(out, in0, s1, op0, s2=None, op1=None):
    if op1 is None:
        return lambda e: e.tensor_scalar(out=out, in0=in0, scalar1=s1, scalar2=None, op0=op0)
    return lambda e: e.tensor_scalar(out=out, in0=in0, scalar1=s1, scalar2=s2, op0=op0, op1=op1)


def STT(out, in0, scalar, in1, op0, op1):
    return lambda e: e.scalar_tensor_tensor(out=out, in0=in0, scalar=scalar, in1=in1, op0=op0, op1=op1)


def CP(out, in_):
    return lambda e: e.tensor_copy(out=out, in_=in_)


def MM(out, lhsT, rhs, start=True, stop=True, skip=False):
    if skip:
        return lambda e: e.matmul(out, lhsT, rhs, start=start, stop=stop, skip_group_check=True)
    return lambda e: e.matmul(out, lhsT, rhs, start=start, stop=stop)


def TR(out, in_, ident):
    return lambda e: e.transpose(out, in_, ident)


def DMA(out, in_):
    return lambda e: e.dma_start(out=out, in_=in_)


def RED(out, in_, op=None):
    return lambda e: e.tensor_reduce(out=out, in_=in_, axis=AX.X, op=(op or ALU.add))


def RECIP(out, in_):
    return lambda e: e.reciprocal(out=out, in_=in_)


def MEMSET(ap, v):
    return lambda e: e.memset(ap, v)


class Arena:
    def __init__(self, nc, words):
        self.t = nc.alloc_sbuf_tensor("arena", [128, words], F32)
        self.words = words
        self.off = 0
        self.peak = 0

    def alloc(self, shape, dt):
        n = int(np.prod(shape[1:]))
        nw = n if dt == F32 else (n + 1) // 2
        nw = (nw + 7) // 8 * 8
        assert self.off + nw <= self.words, ("SBUF arena overflow", self.off, nw, self.words)
        v = self.t[:, self.off:self.off + nw]
        self.off += nw
        self.peak = max(self.peak, self.off)
        if dt != F32:
            v = v.bitcast(dt)
        v = v[:, 0:n]
        if len(shape) == 3:
            v = v.rearrange("p (a b) -> p a b", b=shape[2])
        elif len(shape) == 4:
            v = v.rearrange("p (a b c) -> p a b c", b=shape[2], c=shape[3])
        return v

    def mark(self):
        return self.off

    def release(self, m):
        self.off = m


def bc(ap, shape):
    return ap.to_broadcast(shape)


class Ctx:
    pass


def build(n_layers=2, phases=("diff", "na", "ssd"), dbg=False):
    nc = bass.Bass("TRN2", target_bir_lowering=False)
    c = Ctx()
    c.nc = nc
    c.dbg = dbg

    def din(name, shape, dt=F32):
        return nc.dram_tensor(name, shape, dt, kind="ExternalInput").ap()

    c.x_in = din("x", [L, DM])
    c.norm_w = din("norm_w", [2, DM])
    c.w_in = din("w_in", [2, DM, INW])
    c.conv_w = din("conv_w", [2, 5, 1536])
    c.conv_b = din("conv_b", [2, 1536])
    c.a_log = din("a_log", [2, 32])
    c.dt_bias = din("dt_bias", [2, 32])
    c.d_skip = din("d_skip", [2, 32])
    c.ssd_norm_w = din("ssd_norm_w", [2, 1024])
    c.diff_qk_norm = din("diff_qk_norm", [2, 128])
    c.diff_lambda = din("diff_lambda", [2, 256])
    c.diff_subln = din("diff_subln", [2, 128])
    c.na_qk_norm = din("na_qk_norm", [2, 128])
    c.na_tab = din("na_tab", [2, 8, 128, NA_NCLS * 128])
    c.w_out = din("w_out", [2, 2048, DM])
    c.cs_tab = din("cs_tab", [L, 128])
    c.consts = din("consts", [128, 4 * 128])
    c.out = nc.dram_tensor("out", [L, DM], F32, kind="ExternalOutput").ap()
    kind_scr = "ExternalOutput" if dbg else "Internal"
    c.x1 = nc.dram_tensor("x1", [L, DM], F32, kind=kind_scr).ap()
    c.yT = [nc.dram_tensor("yT%d" % i, [2048, L], BF16, kind=kind_scr).ap() for i in range(n_layers)]
    c.csT_d = nc.dram_tensor("csT_d", [NT, 32, 128], F32, kind="Internal").ap()
    c.sb_d = nc.dram_tensor("sb_d", [2, NT, 128, 512], BF16, kind="Internal").ap()
    c.sf_d = nc.dram_tensor("sf_d", [2, NT, 128, 512], BF16, kind="Internal").ap()

    c.dumps = []

    def dump(name, ap, shape, dt, tiles):
        if not dbg:
            return
        d = nc.dram_tensor("dbg_" + name, list(shape), dt, kind="ExternalOutput").ap()
        c.dumps.append(c.P.op("sp", DMA(d, ap), reads=tiles))
    c.dump = dump
    A = Arena(nc, 51000)
    c.A = A
    P = Prog(nc)
    c.P = P
    c.psall = nc.alloc_psum_tensor("psall", [128, 8 * 512], F32)[:, :]
    c.ps = [c.psall[:, i * 512:(i + 1) * 512] for i in range(7)]
    c.tps = [T(excl=True) for _ in range(7)]
    c.psb = c.psall[:, 7 * 512:8 * 512].bitcast(BF16)
    _tb = T(excl=True)
    c.tpsb = [_tb, _tb]

    c.cst = A.alloc([128, 512], F32)
    c.t_cst = T()
    P.op("sp", DMA(c.cst, c.consts), writes=[c.t_cst])
    c.identf = c.cst[:, 0:128]
    c.U = c.cst[:, 128:256]
    c.Ur = c.cst[:, 256:384]
    c.onesf = c.cst[:, 384:512]
    c.ident = A.alloc([128, 128], BF16)
    c.t_ident = T()
    P.op("dve", CP(c.ident, c.identf), reads=[c.t_cst], writes=[c.t_ident])
    c.epsc = A.alloc([128, 1], F32)
    c.t_eps = T()
    P.op("pool", MEMSET(c.epsc, EPS), writes=[c.t_eps])
    c.onec = A.alloc([128, 1], F32)
    P.op("pool", MEMSET(c.onec, 1.0), writes=[c.t_eps])
    c.hdnT = A.alloc([128, 8, L], BF16)
    c.t_hdnT = [T() for _ in range(NT)]
    c.Wbuf = A.alloc([128, 16384], BF16)
    c.t_Wq = [T() for _ in range(4)]
    c.W8 = c.Wbuf.rearrange("p (k c) -> p k c", c=2048)
    c.Wo = c.Wbuf.rearrange("p (k c) -> p k c", c=1024)
    c.Wz = [c.Wbuf[:, 3 * 4096:4 * 4096].rearrange("p (k c) -> p k c", c=512),
            c.Wbuf[:, 2 * 4096:3 * 4096].rearrange("p (k c) -> p k c", c=512)]
    c.pf = {}
    c.phases = phases

    finals = []
    for l in range(n_layers):
        xsrc = c.x_in if l == 0 else c.x1
        xdst = c.x1 if l < n_layers - 1 or dbg and n_layers == 1 else c.out
        if l == n_layers - 1:
            xdst = c.out
        P.barrier()
        if "diff" in phases:
            pf_qkvg(c, l, "diff")
        if l == 0:
            phase_hdn(c, l, xsrc)
        if "diff" in phases:
            P.barrier()
            phase_diff(c, l)
        if "na" in phases:
            P.barrier()
            phase_na(c, l)
        if "ssd" in phases:
            P.barrier()
            phase_ssd(c, l)
        P.barrier()
        fin = phase_out(c, l, xsrc, xdst, fuse_next=(l + 1 < n_layers))
        if l == n_layers - 1:
            finals = fin
    P.barrier()
    finals = list(finals)
    if dbg:
        finals += [o for o in P.ops if o.isdma][-36:] + c.dumps
    P.emit(final_waits=finals)
    c.peak = A.peak
    return nc, c


def phase_hdn(c, l, xsrc):
    P, A = c.P, c.A
    m = A.mark()
    nwb = A.alloc([128, DM], F32)
    t_nwb = T()
    P.op("sp", DMA(nwb, c.norm_w[l:l + 1, :].partition_broadcast(128)), writes=[t_nwb])
    NX = 4
    xt = [A.alloc([128, DM], F32) for _ in range(NX)]
    t_xt = [T() for _ in range(NX)]
    sq = A.alloc([128, DM], F32)
    t_sq = T()
    ss = A.alloc([128, NT], F32)
    t_ss = [T() for _ in range(NT)]
    hb = [A.alloc([128, DM], BF16) for _ in range(2)]
    t_hb = [T(), T()]

    def load_x(tt):
        P.op("sp", DMA(xt[tt % NX], xsrc[tt * 128:(tt + 1) * 128, :]), writes=[t_xt[tt % NX]])
    for tt in range(min(NX - 1, NT)):
        load_x(tt)
    for tt in range(NT):
        b = tt % 2
        xb, t_xb = xt[tt % NX], t_xt[tt % NX]
        s1 = ss[:, tt:tt + 1]
        if tt + NX - 1 < NT:
            load_x(tt + NX - 1)
        P.op("act", ACT(sq, xb, AF.Square, accum_out=s1), reads=[t_xb], writes=[t_sq, t_ss[tt]])
        P.op("act", ACT(s1, s1, AF.Ln, scale=1.0 / DM, bias=c.epsc), reads=[t_ss[tt], c.t_eps], writes=[t_ss[tt]])
        P.op("act", ACT(s1, s1, AF.Exp, scale=-0.5), reads=[t_ss[tt]], writes=[t_ss[tt]])
        P.op("dve", STT(hb[b], xb, s1, nwb, ALU.mult, ALU.mult), reads=[t_xb, t_ss[tt], t_nwb], writes=[t_hb[b]])
        for k in range(8):
            P.op("pe", TR(c.psb[:, k * 128:(k + 1) * 128], hb[b][:, k * 128:(k + 1) * 128], c.ident),
                 reads=[t_hb[b], c.t_ident], writes=[c.tpsb[k // 4]])
        P.op("act", ACT(c.hdnT[:, :, tt * 128:(tt + 1) * 128], c.psb[:, :].rearrange("p (k t) -> p k t", t=128), AF.Copy),
             reads=[c.tpsb[0], c.tpsb[1]], writes=[c.t_hdnT[tt]])
    A.release(m)


def load_w(c, l, dst, col0, ncols, tiles, step=512, extra=()):
    P = c.P
    j = 0
    for c0 in range(0, ncols, step):
        n = min(step, ncols - c0)
        src = c.w_in[l, :, col0 + c0:col0 + c0 + n].rearrange("(k p) c -> p k c", p=128)
        P.op("poolq", DMA(dst[:, :, c0:c0 + n], src), writes=[tiles[j]] + list(extra))
        j += 1


def pf_qkvg(c, l, which):
    key = (which, l)
    if key not in c.pf:
        tW = [T() for _ in range(4)]
        load_w(c, l, c.W8, C_DIFF if which == "diff" else C_NA, 2048, tW, extra=c.t_Wq)
        c.pf[key] = tW
    return c.pf[key]


def pf_ssd(c, l):
    key = ("ssd", l)
    if key not in c.pf:
        tWz = [[T()], [T()]]
        load_w(c, l, c.Wz[0], C_Z, 512, tWz[0], extra=[c.t_Wq[3]])
        load_w(c, l, c.Wz[1], C_Z + 512, 512, tWz[1], extra=[c.t_Wq[2]])
        c.pf[key] = tWz
    return c.pf[key]


def pf_out(c, l, quarters):
    key = ("out", l)
    if key not in c.pf:
        c.pf[key] = [None] * 4
    tWo = c.pf[key]
    for j in quarters:
        if tWo[j] is None:
            tWo[j] = T()
            src = c.w_out[l, j * 512:(j + 1) * 512, :].rearrange("(k p) c -> p k c", p=128)
            c.P.op("poolq", DMA(c.Wo[:, j * 4:(j + 1) * 4, :], src), writes=[tWo[j], c.t_Wq[j]])
    return tWo


def qk_norm_rope(c, src_ps, t_src, sq, t_sq, ss8, t_ss8, tbuf, t_tb, ubuf, t_ub, TC, TS_, t_tab, tt, dst, t_dst, rope):
    P = c.P
    P.op("act", ACT(sq, src_ps, AF.Square), reads=[t_src], writes=[t_sq])
    P.op("dve", RED(ss8, sq.rearrange("p (g d) -> p g d", d=64)), reads=[t_sq], writes=[t_ss8])
    P.op("act", ACT(ss8, ss8, AF.Ln, scale=1.0 / 64, bias=c.epsc), reads=[t_ss8, c.t_eps], writes=[t_ss8])
    P.op("act", ACT(ss8, ss8, AF.Exp, scale=-0.5), reads=[t_ss8], writes=[t_ss8])
    t3 = tbuf.rearrange("p (g d) -> p g d", d=64)
    P.op("dve", TT(t3, src_ps.rearrange("p (g d) -> p g d", d=64), bc(ss8.unsqueeze(2), [128, 8, 64]), ALU.mult),
         reads=[t_src, t_ss8], writes=[t_tb])
    if rope:
        u3 = ubuf.rearrange("p (g d) -> p g d", d=64)
        P.op("pool", TT(u3[:, :, 0:32], t3[:, :, 32:64], bc(TS_[:, tt, 0:32].unsqueeze(1), [128, 8, 32]), ALU.mult),
             reads=[t_tb, t_tab], writes=[t_ub])
        P.op("pool", TT(u3[:, :, 32:64], t3[:, :, 0:32], bc(TS_[:, tt, 32:64].unsqueeze(1), [128, 8, 32]), ALU.mult),
             reads=[t_tb, t_tab], writes=[t_ub])
        P.op("dve", TT(t3, t3, bc(TC[:, tt, :].unsqueeze(1), [128, 8, 64]), ALU.mult), reads=[t_tb, t_tab], writes=[t_tb])
        P.op("dve", TT(dst, tbuf, ubuf, ALU.add), reads=[t_tb, t_ub], writes=[t_dst])
    else:
        P.op("dve", TT(dst.rearrange("p (g d) -> p g d", d=64), t3, bc(TC.unsqueeze(1), [128, 8, 64]), ALU.mult),
             reads=[t_tb, t_tab], writes=[t_dst])


def prep_qkvg(c, W, tW, qT, t_qT, kT, t_kT, vcopy, t_V, G, t_G, tabs, t_tab, rope):
    P, A = c.P, c.A
    ps, tps = c.ps, c.tps
    NB = 3 if rope else 2
    LAG = NB - 1
    raw = [[A.alloc([128, 512], F32) for _ in range(NB)] for _ in range(2)]
    t_raw = [[T() for _ in range(NB)] for _ in range(2)]
    sq = [A.alloc([128, 512], F32) for _ in range(2)]
    t_sq = [T(), T()]
    ss8 = [[A.alloc([128, 8], F32) for _ in range(NB)] for _ in range(2)]
    t_ss8 = [[T() for _ in range(NB)] for _ in range(2)]
    tbuf = [[A.alloc([128, 512], F32) for _ in range(NB)] for _ in range(2)]
    t_tb = [[T() for _ in range(NB)] for _ in range(2)]
    ubuf = [[A.alloc([128, 512], F32) for _ in range(NB)] for _ in range(2)] if rope else None
    t_ub = [[T() for _ in range(NB)] for _ in range(2)]
    rbuf = [[A.alloc([128, 512], BF16) for _ in range(NB)] for _ in range(2)]
    t_rb = [[T() for _ in range(NB)] for _ in range(2)]
    dsts = ((qT, t_qT), (kT, t_kT))
    nbank = 0
    gbank = {}

    def chain_a(tt, i):
        b = tt % NB
        P.op("act", ACT(sq[i], raw[i][b], AF.Square), reads=[t_raw[i][b]], writes=[t_sq[i]])
        P.op("dve", RED(ss8[i][b], sq[i].rearrange("p (g d) -> p g d", d=64)), reads=[t_sq[i]], writes=[t_ss8[i][b]])

    def chain_b(tt, i):
        b = tt % NB
        s8, t_s8 = ss8[i][b], t_ss8[i][b]
        P.op("act", ACT(s8, s8, AF.Ln, scale=1.0 / 64, bias=c.epsc), reads=[t_s8, c.t_eps], writes=[t_s8])
        P.op("act", ACT(s8, s8, AF.Exp, scale=-0.5), reads=[t_s8], writes=[t_s8])

    def chain_c(tt, i):
        b = tt % NB
        rw, t_rw = raw[i][b], t_raw[i][b]
        s8, t_s8 = ss8[i][b], t_ss8[i][b]
        tb_, t_tb_ = tbuf[i][b], t_tb[i][b]
        t3 = tb_.rearrange("p (g d) -> p g d", d=64)
        P.op("dve", TT(t3, rw.rearrange("p (g d) -> p g d", d=64), bc(s8.unsqueeze(2), [128, 8, 64]), ALU.mult),
             reads=[t_rw, t_s8], writes=[t_tb_])
        dst, t_dst = rbuf[i][b], t_rb[i][b]
        TC, TS_ = tabs[2 * i], tabs[2 * i + 1]
        if rope:
            ub, t_ub_ = ubuf[i][b], t_ub[i][b]
            u3 = ub.rearrange("p (g d) -> p g d", d=64)
            eng_u = "pool" if i == 1 else "dve"
            P.op(eng_u, TT(u3[:, :, 0:32], t3[:, :, 32:64], bc(TS_[:, tt, 0:32].unsqueeze(1), [128, 8, 32]), ALU.mult),
                 reads=[t_tb_, t_tab], writes=[t_ub_])
            P.op(eng_u, TT(u3[:, :, 32:64], t3[:, :, 0:32], bc(TS_[:, tt, 32:64].unsqueeze(1), [128, 8, 32]), ALU.mult),
                 reads=[t_tb_, t_tab], writes=[t_ub_])
            P.op("dve", TT(t3, t3, bc(TC[:, tt, :].unsqueeze(1), [128, 8, 64]), ALU.mult), reads=[t_tb_, t_tab], writes=[t_tb_])
            P.op("dve", TT(dst, tb_, ub, ALU.add), reads=[t_tb_, t_ub_], writes=[t_dst])
        else:
            P.op("dve", TT(dst.rearrange("p (g d) -> p g d", d=64), t3, bc(TC.unsqueeze(1), [128, 8, 64]), ALU.mult),
                 reads=[t_tb_, t_tab], writes=[t_dst])

    def transposes(tt):
        b = tt % NB
        tok = slice(tt * 128, (tt + 1) * 128)
        for i in range(2):
            for h in range(4):
                P.op("pe", TR(c.psb[:, i * 512 + h * 128:i * 512 + (h + 1) * 128], rbuf[i][b][:, h * 128:(h + 1) * 128], c.ident),
                     reads=[t_rb[i][b], c.t_ident], writes=[c.tpsb[i]])
        for i in range(2):
            dstT, t_dstT = dsts[i]
            P.op("dve", CP(dstT[:, :, tok], c.psb[:, i * 512:(i + 1) * 512].rearrange("p (h t) -> p h t", t=128)),
                 reads=[c.tpsb[i]], writes=[t_dstT[tt]])

    for tt in range(NT + LAG):
        if tt < NT:
            tok = slice(tt * 128, (tt + 1) * 128)
            b = tt % NB
            banks = []
            for j in range(4):
                bk = nbank % 7
                nbank += 1
                banks.append(bk)
                for k in range(8):
                    P.op("pe", MM(ps[bk], c.hdnT[:, k, tok], W[:, k, j * 512:(j + 1) * 512], start=(k == 0), stop=(k == 7)),
                         reads=[c.t_hdnT[tt], tW[j]] + c.t_Wq, writes=[tps[bk]])
            for i in range(2):
                if rope:
                    P.op("act", ACT(raw[i][b], ps[banks[i]], AF.Copy), reads=[tps[banks[i]]], writes=[t_raw[i][b]])
                else:
                    P.op("dve", CP(raw[i][b], ps[banks[i]]), reads=[tps[banks[i]]], writes=[t_raw[i][b]])
            vcopy(tt, ps[banks[2]], tps[banks[2]])
            gbank[tt] = banks[3]
            for stage in (chain_a, chain_b, chain_c):
                for i in range(2):
                    stage(tt, i)
            if tt % 2 == 1 or tt == NT - 1:
                for t2 in ([tt - 1, tt] if tt % 2 == 1 else [tt]):
                    P.op("act", ACT(G[:, t2, :], ps[gbank[t2]], AF.Silu), reads=[tps[gbank[t2]]], writes=[t_G[t2]])
        if tt >= LAG:
            transposes(tt - LAG)


def phase_diff(c, l):
    P, A = c.P, c.A
    m = A.mark()
    lam_init = 0.8 - 0.6 * math.exp(-0.3 * l)
    W = c.W8
    tW = pf_qkvg(c, l, "diff")
    qT = A.alloc([128, 4, L], BF16)
    kT = A.alloc([128, 4, L], BF16)
    t_qT = [T() for _ in range(NT)]
    t_kT = [T() for _ in range(NT)]
    V = A.alloc([128, NT, 4, 129], BF16)
    t_V = [T() for _ in range(NT)]
    G = A.alloc([128, NT, 512], BF16)
    t_G = [T() for _ in range(NT)]
    t_misc = T()
    P.op("pool", MEMSET(V[:, :, :, 128:129], 1.0), writes=t_V)
    wqk = A.alloc([128, 128], F32)
    P.op("sp", DMA(wqk, c.diff_qk_norm[l:l + 1, :].partition_broadcast(128)), writes=[t_misc])
    P.op("dve", TS(wqk[:, 0:64], wqk[:, 0:64], 0.125, ALU.mult), reads=[t_misc], writes=[t_misc])
    tabs = A.alloc([128, 4, NT * 64], F32)
    t_tab = T()
    m_cs = A.mark()
    c.cs_sb = A.alloc([128, NT, 128], F32)
    c.t_cs = T()
    P.op("sp", DMA(c.cs_sb, c.cs_tab.rearrange("(t p) c -> p t c", p=128)), writes=[c.t_cs])
    for i, w in enumerate((wqk[:, 0:64], wqk[:, 64:128])):
        TCv = tabs[:, 2 * i, :].rearrange("p (t d) -> p t d", d=64)
        TSv = tabs[:, 2 * i + 1, :].rearrange("p (t d) -> p t d", d=64)
        P.op("dve", TT(TCv, c.cs_sb[:, :, 0:64], bc(w.unsqueeze(1), [128, NT, 64]), ALU.mult), reads=[c.t_cs, t_misc], writes=[t_tab])
        P.op("dve", TT(TSv[:, :, 0:32], c.cs_sb[:, :, 64:96], bc(w[:, 32:64].unsqueeze(1), [128, NT, 32]), ALU.mult),
             reads=[c.t_cs, t_misc], writes=[t_tab])
        P.op("dve", TT(TSv[:, :, 32:64], c.cs_sb[:, :, 96:128], bc(w[:, 0:32].unsqueeze(1), [128, NT, 32]), ALU.mult),
             reads=[c.t_cs, t_misc], writes=[t_tab])
    P.barrier()
    A.release(m_cs)
    TCq = tabs[:, 0, :].rearrange("p (t d) -> p t d", d=64)
    TSq = tabs[:, 1, :].rearrange("p (t d) -> p t d", d=64)
    TCk = tabs[:, 2, :].rearrange("p (t d) -> p t d", d=64)
    TSk = tabs[:, 3, :].rearrange("p (t d) -> p t d", d=64)
    lamb = A.alloc([128, 256], F32)
    t_lam = T()
    P.op("sp", DMA(lamb, c.diff_lambda[l:l + 1, :].partition_broadcast(128)), writes=[t_lam])
    lsc = A.alloc([128, 4], F32)
    lam3 = lamb.rearrange("p (a b d) -> p a b d", a=2, b=2)
    prod = A.alloc([128, 2, 64], F32)
    P.op("dve", TT(prod, lam3[:, :, 0, :], lam3[:, :, 1, :], ALU.mult), reads=[t_lam], writes=[t_lam])
    P.op("dve", RED(lsc[:, 0:2], prod), reads=[t_lam], writes=[t_lam])
    P.op("act", ACT(lsc[:, 0:2], lsc[:, 0:2], AF.Exp), reads=[t_lam], writes=[t_lam])
    P.op("dve", TT(lsc[:, 2:3], lsc[:, 1:2], lsc[:, 0:1], ALU.subtract), reads=[t_lam], writes=[t_lam])
    P.op("dve", TS(lsc[:, 3:4], lsc[:, 2:3], -lam_init, ALU.add), reads=[t_lam], writes=[t_lam])
    neglam = lsc[:, 3:4]
    swb = A.alloc([128, 128], F32)
    t_swb = T()
    P.op("sp", DMA(swb, c.diff_subln[l:l + 1, :].partition_broadcast(128)), writes=[t_swb])
    P.op("dve", TS(swb, swb, 1.0 - lam_init, ALU.mult), reads=[t_swb], writes=[t_swb])

    ps, tps = c.ps, c.tps
    m_prep = A.mark()

    def vcopy(tt, bank, t_bank):
        P.op("act", ACT(V[:, tt, :, 0:128], bank.rearrange("p (h d) -> p h d", d=128), AF.Copy), reads=[t_bank], writes=[t_V[tt]])
    prep_qkvg(c, W, tW, qT, t_qT, kT, t_kT, vcopy, t_V, G, t_G, (TCq, TSq, TCk, TSk), t_tab, True)
    P.barrier()
    A.release(m_prep)
    if "na" in c.phases:
        pf_qkvg(c, l, "na")
    elif "ssd" in c.phases:
        pf_ssd(c, l)
    c.dump("qT%d" % l, qT, [128, 4, L], BF16, t_qT)
    c.dump("kT%d" % l, kT, [128, 4, L], BF16, t_kT)
    c.dump("V%d" % l, V, [128, NT, 4, 129], BF16, t_V)
    c.dump("G%d" % l, G, [128, NT, 512], BF16, t_G)
    c.dump("hdnT%d" % l, c.hdnT, [128, 8, L], BF16, c.t_hdnT)
    NPT = 4
    Pt = [A.alloc([128, 1024], BF16) for _ in range(NPT)]
    t_Pt = [T() for _ in range(NPT)]
    t_Pth = [[T(), T()] for _ in range(NPT)]
    SP = [c.psall[:, 0:1024], c.psall[:, 1024:2048]]
    t_SP = [[tps[0], tps[1]], [tps[2], tps[3]]]
    Ob3 = [ps[4], ps[5], ps[6]]
    t_O3 = [tps[4], tps[5], tps[6]]

    def acc(a):
        return a // 3, (a % 3) * 129
    Osb = A.alloc([128, 3, 512], F32)
    t_Osb = T()
    rr = [A.alloc([128, 16], F32) for _ in range(2)]
    t_rr = [T(), T()]
    ob4 = [A.alloc([128, 4, 128], F32) for _ in range(2)]
    t_ob4 = [T(), T()]
    junk = A.alloc([128, 128], F32)
    t_junk = T()
    yb4 = [A.alloc([128, 512], BF16) for _ in range(2)]
    t_yb4 = [T(), T()]
    yst = [A.alloc([128, 512], BF16) for _ in range(2)]
    t_yst = [T(), T()]
    iters = [(h, qb, kt) for h in range(4) for qb in range(4) for kt in range(NT)]
    NI = len(iters)
    deferred = []

    def emit_S(i):
        h, qb, kt = iters[i]
        for cc in range(2):
            pr = slice(cc * 64, (cc + 1) * 64)
            P.op("pe", MM(SP[i % 2][:, cc * 512:(cc + 1) * 512], kT[pr, h, kt * 128:(kt + 1) * 128], qT[pr, h, qb * 512:(qb + 1) * 512]),
                 reads=[t_kT[kt]] + [t_qT[qb * 4 + j] for j in range(4)], writes=[t_SP[i % 2][cc]])

    def Oslice(a):
        bk, off = acc(a)
        return Osb[:, bk, off:off + 129]

    def fin_stage1(blk, h, qb):
        b = blk % 2
        r_, t_r = rr[b], t_rr[b]
        ob, t_ob = ob4[b], t_ob4[b]
        for j in range(4):
            O0, O1 = Oslice(j), Oslice(4 + j)
            P.op("dve", RECIP(r_[:, 4 * j:4 * j + 1], O0[:, 128:129]), reads=[t_Osb], writes=[t_r])
            P.op("dve", RECIP(r_[:, 4 * j + 1:4 * j + 2], O1[:, 128:129]), reads=[t_Osb], writes=[t_r])
            P.op("dve", TT(r_[:, 4 * j + 2:4 * j + 3], r_[:, 4 * j + 1:4 * j + 2], neglam, ALU.mult), reads=[t_r, t_lam], writes=[t_r])
            P.op("dve", TS(ob[:, j, :], O0[:, 0:128], r_[:, 4 * j:4 * j + 1], ALU.mult), reads=[t_Osb, t_r], writes=[t_ob])
            P.op("dve", STT(ob[:, j, :], O1[:, 0:128], r_[:, 4 * j + 2:4 * j + 3], ob[:, j, :], ALU.mult, ALU.add),
                 reads=[t_Osb, t_r, t_ob], writes=[t_ob])

    def fin_stage2(blk, h, qb):
        b = blk % 2
        r_, t_r = rr[b], t_rr[b]
        ob, t_ob = ob4[b], t_ob4[b]
        for j in range(4):
            P.op("act", ACT(junk, ob[:, j, :], AF.Square, accum_out=r_[:, 4 * j + 3:4 * j + 4]), reads=[t_ob], writes=[t_junk, t_r])
        r3 = r_.rearrange("p (j k) -> p j k", k=4)[:, :, 3]
        P.op("act", ACT(r3, r3, AF.Ln, scale=1.0 / 128, bias=c.epsc), reads=[t_r, c.t_eps], writes=[t_r])
        P.op("act", ACT(r3, r3, AF.Exp, scale=-0.5), reads=[t_r], writes=[t_r])

    def fin_stage3(blk, h, qb):
        b = blk % 2
        r_, t_r = rr[b], t_rr[b]
        ob, t_ob = ob4[b], t_ob4[b]
        yb, t_yb = yb4[b], t_yb4[b]
        for j in range(4):
            tt = qb * 4 + j
            P.op("dve", STT(ob[:, j, :], ob[:, j, :], r_[:, 4 * j + 3:4 * j + 4], swb, ALU.mult, ALU.mult), reads=[t_ob, t_r, t_swb], writes=[t_ob])
            P.op("dve", TT(yb[:, j * 128:(j + 1) * 128], ob[:, j, :], G[:, tt, h * 128:(h + 1) * 128], ALU.mult),
                 reads=[t_ob, t_G[tt]], writes=[t_yb])
        for j in range(4):
            P.op("pe", TR(c.psb[:, j * 128:(j + 1) * 128], yb[:, j * 128:(j + 1) * 128], c.ident),
                 reads=[t_yb, c.t_ident], writes=[c.tpsb[0]])

    def fin_stage4(blk, h, qb):
        b = blk % 2
        ys, t_ys = yst[b], t_yst[b]
        P.op("dve", CP(ys, c.psb[:, 0:512]), reads=[c.tpsb[0]], writes=[t_ys])
        P.op("sp", DMA(c.yT[l][1024 + h * 128:1024 + (h + 1) * 128, qb * 512:(qb + 1) * 512], ys), reads=[t_ys])

    emit_S(0)
    blk = 0
    for i in range(NI):
        h, qb, kt = iters[i]
        if i + 1 < NI:
            emit_S(i + 1)
        pt = Pt[i % NPT]
        t_pth = t_Pth[i % NPT]
        for cc in range(2):
            P.op("act", ACT(pt[:, cc * 512:(cc + 1) * 512], SP[i % 2][:, cc * 512:(cc + 1) * 512], AF.Exp),
                 reads=[t_SP[i % 2][cc]], writes=[t_pth[cc]])
        for cc in range(2):
            for j in range(4):
                bk, off = acc(cc * 4 + j)
                P.op("pe", MM(Ob3[bk][:, off:off + 129], pt[:, cc * 512 + j * 128:cc * 512 + (j + 1) * 128],
                              V[:, kt, h, :], start=(kt == 0 and off == 0), stop=(kt == NT - 1), skip=True),
                     reads=[t_pth[cc], t_V[kt]], writes=[t_O3[bk]])
        for (due, fn) in [d for d in deferred if d[0] <= i]:
            fn()
        deferred = [d for d in deferred if d[0] > i]
        if kt == NT - 1:
            for k3 in range(3):
                P.op("dve", CP(Osb[:, k3, 0:387], Ob3[k3][:, 0:387]), reads=[t_O3[k3]], writes=[t_Osb])
            fin_stage1(blk, h, qb)
            deferred.append((i + 2, (lambda b_=blk, h_=h, q_=qb: fin_stage2(b_, h_, q_))))
            deferred.append((i + 4, (lambda b_=blk, h_=h, q_=qb: fin_stage3(b_, h_, q_))))
            deferred.append((i + 6, (lambda b_=blk, h_=h, q_=qb: fin_stage4(b_, h_, q_))))
            blk += 1
    for (due, fn) in deferred:
        fn()
    A.release(m)


def _na_classes():
    rows, W, kh, kw = 32, 64, 8, 16
    rs = lambda r: min(max(r - kh // 2, 0), rows - kh)
    types = {}
    keys = []
    plan = []
    for i in range(16):
        qrows = [2 * i, 2 * i + 1]
        lo = min(rs(r) for r in qrows)
        hi = max(rs(r) + kh - 1 for r in qrows)
        tkeys = []
        kbs = list(range(lo // 2, hi // 2 + 1))
        for kb in kbs:
            key = []
            for b in range(2):
                for a in range(2):
                    kr, qr = 2 * kb + b, 2 * i + a
                    ok = rs(qr) <= kr < rs(qr) + kh
                    key.append((kr - qr + 7) if ok else -1)
            tkeys.append(tuple(key))
        tkeys = tuple(tkeys)
        if tkeys not in types:
            types[tkeys] = len(keys)
            keys.extend(tkeys)
        base = types[tkeys]
        plan.append([(kb, base + bi) for bi, kb in enumerate(kbs)])
    return keys, plan


NA_KEYS, NA_PLAN = _na_classes()
NA_NCLS = len(NA_KEYS)


def _na_tables(rpb):
    W, kw = 64, 16
    cidx = np.arange(W)
    col_start = np.clip(cidx - kw // 2, 0, W - kw)
    col_ok = (cidx[None, :] >= col_start[:, None]) & (cidx[None, :] < col_start[:, None] + kw)
    dc = np.clip(cidx[None, :] - cidx[:, None] + (kw - 1), 0, 2 * kw - 2)
    out = np.full((2, 8, 128, NA_NCLS, 128), NEG, dtype=np.float32)
    for ci, key in enumerate(NA_KEYS):
        n = 0
        for b in range(2):
            for a in range(2):
                dr = key[n]
                n += 1
                if dr < 0:
                    continue
                blkv = rpb[:, :, dr, :][:, :, dc]
                blkv = np.where(col_ok[None, None], blkv, np.float32(NEG))
                out[:, :, b * 64:(b + 1) * 64, ci, a * 64:(a + 1) * 64] = np.transpose(blkv, (0, 1, 3, 2))
    return np.ascontiguousarray(out.reshape(2, 8, 128, NA_NCLS * 128))


def phase_na(c, l):
    P, A = c.P, c.A
    m = A.mark()
    W = c.W8
    tW = pf_qkvg(c, l, "na")
    qT = A.alloc([128, 4, L], BF16)
    kT = A.alloc([128, 4, L], BF16)
    t_qT = [T() for _ in range(NT)]
    t_kT = [T() for _ in range(NT)]
    V = A.alloc([128, NT, 8, 65], BF16)
    t_V = [T() for _ in range(NT)]
    G = A.alloc([128, NT, 512], BF16)
    t_G = [T() for _ in range(NT)]
    Y = A.alloc([128, NT, 512], BF16)
    t_Y = [T() for _ in range(NT)]
    P.op("pool", MEMSET(V[:, :, :, 64:65], 1.0), writes=t_V)
    t_misc = T()
    wqk = A.alloc([128, 128], F32)
    P.op("sp", DMA(wqk, c.na_qk_norm[l:l + 1, :].partition_broadcast(128)), writes=[t_misc])
    P.op("dve", TS(wqk[:, 0:64], wqk[:, 0:64], 0.125, ALU.mult), reads=[t_misc], writes=[t_misc])
    ps, tps = c.ps, c.tps
    m_prep = A.mark()

    def vcopy(tt, bank, t_bank):
        P.op("act", ACT(V[:, tt, :, 0:64], bank.rearrange("p (h d) -> p h d", d=64), AF.Copy), reads=[t_bank], writes=[t_V[tt]])
    prep_qkvg(c, W, tW, qT, t_qT, kT, t_kT, vcopy, t_V, G, t_G, (wqk[:, 0:64], None, wqk[:, 64:128], None), t_misc, False)
    P.barrier()
    A.release(m_prep)
    if "ssd" in c.phases:
        pf_ssd(c, l)
    pf_out(c, l, [0, 1])
    tab = [A.alloc([128, NA_NCLS * 128], F32) for _ in range(2)]
    t_tab = [T(), T()]
    Tb = [A.alloc([128, 640], F32) for _ in range(2)]
    t_Tb = [T(), T()]
    Pb = [A.alloc([128, 640], BF16) for _ in range(3)]
    t_Pb = [T(), T(), T()]
    rr = [A.alloc([128, 2], F32) for _ in range(2)]
    t_rr = [T(), T()]
    its = [(hh, i) for hh in range(8) for i in range(NT)]
    NI = len(its)
    loaded = set()

    def load_tab(hh):
        if hh in loaded or hh >= 8:
            return
        loaded.add(hh)
        P.op("sp", DMA(tab[hh % 2], c.na_tab[l, hh, :, :]), writes=[t_tab[hh % 2]])
        P.op("act", ACT(tab[hh % 2], tab[hh % 2], AF.Exp), reads=[], writes=[t_tab[hh % 2]])

    def emit_S(n):
        hh, i = its[n]
        jb, e = hh // 2, hh % 2
        pr = slice(e * 64, (e + 1) * 64)
        S0, S1 = ps[(n % 2) * 2], ps[(n % 2) * 2 + 1]
        tS = [tps[(n % 2) * 2], tps[(n % 2) * 2 + 1]]
        for bi, (kb, cls) in enumerate(NA_PLAN[i]):
            dstS = (S0 if bi < 4 else S1)[:, (bi % 4) * 128:(bi % 4 + 1) * 128]
            P.op("pe", MM(dstS, kT[pr, jb, kb * 128:(kb + 1) * 128], qT[pr, jb, i * 128:(i + 1) * 128]),
                 reads=[t_kT[kb], t_qT[i]], writes=[tS[bi // 4]])

    def emit_exp_mul(n):
        hh, i = its[n]
        plan = NA_PLAN[i]
        nb = len(plan)
        base = plan[0][1]
        tS = [tps[(n % 2) * 2], tps[(n % 2) * 2 + 1]]
        Sboth = c.psall[:, (n % 2) * 1024:(n % 2) * 1024 + nb * 128]
        T_, t_T = Tb[n % 2], t_Tb[n % 2]
        P_, t_P = Pb[n % 3], t_Pb[n % 3]
        P.op("act", ACT(T_[:, 0:nb * 128], Sboth, AF.Exp), reads=(tS if nb > 4 else tS[0:1]), writes=[t_T])
        P.op("dve", TT(P_[:, 0:nb * 128], T_[:, 0:nb * 128], tab[hh % 2][:, base * 128:(base + nb) * 128], ALU.mult),
             reads=[t_T, t_tab[hh % 2]], writes=[t_P])

    yst = [A.alloc([128, 512], BF16) for _ in range(2)]
    t_yst = [T(), T()]
    ycnt = [0]

    def emit_ytrans(j4):
        for qb in range(4):
            for j in range(4):
                tt = qb * 4 + j
                P.op("pe", TR(c.psb[:, j * 128:(j + 1) * 128], Y[:, tt, j4 * 128:(j4 + 1) * 128], c.ident),
                     reads=[t_Y[tt], c.t_ident], writes=[c.tpsb[0]])
            ys, t_ys = yst[ycnt[0] % 2], t_yst[ycnt[0] % 2]
            ycnt[0] += 1
            P.op("dve", CP(ys, c.psb[:, 0:512]), reads=[c.tpsb[0]], writes=[t_ys])
            P.op("sp", DMA(c.yT[l][1536 + j4 * 128:1536 + (j4 + 1) * 128, qb * 512:(qb + 1) * 512], ys), reads=[t_ys])
    pending_tr = []
    load_tab(0)
    emit_S(0)
    if NI > 1:
        emit_S(1)
    emit_exp_mul(0)
    for n, (hh, i) in enumerate(its):
        if i == 2:
            load_tab(hh + 1)
        plan = NA_PLAN[i]
        nb = len(plan)
        Ob, t_Ob = ps[4 + n % 2], tps[4 + n % 2]
        P_, t_P = Pb[n % 3], t_Pb[n % 3]
        for bi, (kb, cls) in enumerate(plan):
            P.op("pe", MM(Ob[:, 0:65], P_[:, bi * 128:(bi + 1) * 128], V[:, kb, hh, :], start=(bi == 0), stop=(bi == nb - 1)),
                 reads=[t_P, t_V[kb]], writes=[t_Ob])
        if n + 2 < NI:
            emit_S(n + 2)
        if n + 1 < NI:
            emit_exp_mul(n + 1)
        r_, t_r = rr[n % 2], t_rr[n % 2]
        P.op("dve", RECIP(r_[:, 0:1], Ob[:, 64:65]), reads=[t_Ob], writes=[t_r])
        P.op("dve", STT(Y[:, i, hh * 64:(hh + 1) * 64], Ob[:, 0:64], r_[:, 0:1], G[:, i, hh * 64:(hh + 1) * 64], ALU.mult, ALU.mult),
             reads=[t_Ob, t_r, t_G[i]], writes=[t_Y[i]])
        if hh % 2 == 1 and i == NT - 1:
            pending_tr.append((n + 3, hh // 2))
        for (due, j4_) in [p_ for p_ in pending_tr if p_[0] <= n]:
            emit_ytrans(j4_)
        pending_tr = [p_ for p_ in pending_tr if p_[0] > n]
    for (due, j4_) in pending_tr:
        emit_ytrans(j4_)
    A.release(m)


def phase_ssd(c, l):
    STOP = 9
    P, A = c.P, c.A
    m = A.mark()
    ps, tps = c.ps, c.tps
    nc = c.nc
    Wdt = A.alloc([128, 8, 32], BF16)
    tWdt = [T()]
    load_w(c, l, Wdt, C_DT, 32, tWdt)
    t_sm = T()
    dtb = A.alloc([128, 32], F32)
    alog = A.alloc([128, 32], F32)
    dsk = A.alloc([128, 32], F32)
    P.op("sp", DMA(dtb, c.dt_bias[l:l + 1, :].partition_broadcast(128)), writes=[t_sm])
    t_al = T()
    P.op("sp", DMA(alog, c.a_log[l:l + 1, :].partition_broadcast(128)), writes=[t_al])
    t_dsk = T()
    P.op("sp", DMA(dsk, c.d_skip[l:l + 1, :].partition_broadcast(128)), writes=[t_dsk])
    dsum = A.alloc([128, 16], F32)
    P.op("dve", TT(dsum, dsk[:, 0:16], dsk[:, 16:32], ALU.add), reads=[t_dsk], writes=[t_dsk])
    P.op("act", ACT(alog, alog, AF.Exp), reads=[t_al], writes=[t_al])
    P.op("dve", TS(alog, alog, -1.0, ALU.mult), reads=[t_al], writes=[t_al])
    snw = A.alloc([128, 1024], F32)
    t_snw = T()
    P.op("sp", DMA(snw, c.ssd_norm_w[l:l + 1, :].partition_broadcast(128)), writes=[t_snw])
    cbb = A.alloc([128, 1536], F32)
    t_cbb = T()
    P.op("sp", DMA(cbb, c.conv_b[l:l + 1, :].partition_broadcast(128)), writes=[t_cbb])
    cwT = A.alloc([128, 12, 6], F32)
    t_cwT = T()

    def a3(shape=(128, NT, 32)):
        return A.alloc(list(shape), F32)
    dt = a3()
    dta = a3()
    cs = a3()
    dcb = a3()
    dout = a3()
    dend = a3()
    cXd = a3()
    t_dt, t_dta, t_cs, t_dcb, t_dout, t_dend, t_cXd, t_v, t_tmp3 = [T() for _ in range(9)]
    t_csd = T()
    csT_v = c.csT_d.rearrange("t (d h) l -> t d h l", d=2)
    m_tmp = A.mark()
    cw6 = A.alloc([128, 1536], F32)
    t_cw6 = T()
    P.op("sp", DMA(cw6[0:5, :], c.conv_w[l, :, :]), writes=[t_cw6])
    P.op("sp", DMA(cw6[5:6, :], c.conv_b[l:l + 1, :]), writes=[t_cw6])
    for j in range(12):
        P.op("pe", TR(ps[6][:, j * 6:(j + 1) * 6], cw6[0:6, j * 128:(j + 1) * 128], c.identf[0:6, 0:6]),
             reads=[t_cw6, c.t_cst], writes=[tps[6]])
    P.op("dve", CP(cwT, ps[6][:, 0:72].rearrange("p (j k) -> p j k", k=6)), reads=[tps[6]], writes=[t_cwT])
    P.barrier()
    A.release(m_tmp)

    def make_decay_stages(v, tmp3, csT):
        t_csT = T()

        def s1():
            for tt in range(NT):
                for k in range(8):
                    P.op("pe", MM(ps[4][:, tt * 32:(tt + 1) * 32], c.hdnT[:, k, tt * 128:(tt + 1) * 128], Wdt[:, k, :],
                                  start=(k == 0), stop=(k == 7), skip=True),
                         reads=[c.t_hdnT[tt], tWdt[0]], writes=[tps[4]])
            p0 = ps[4].rearrange("p (t h) -> p t h", h=32)
            P.op("dve", TT(v, p0, bc(dtb.unsqueeze(1), [128, NT, 32]), ALU.add), reads=[tps[4], t_sm], writes=[t_v])
            P.op("act", ACT(tmp3, v, AF.Abs), reads=[t_v], writes=[t_tmp3])
            P.op("act", ACT(tmp3, tmp3, AF.Exp, scale=-1.0), reads=[t_tmp3], writes=[t_tmp3])
            P.op("act", ACT(tmp3, tmp3, AF.Ln, bias=c.onec, scale=1.0), reads=[t_tmp3, c.t_eps], writes=[t_tmp3])

        def s2():
            P.op("dve", STT(dt, v, 0.0, tmp3, ALU.max, ALU.add), reads=[t_v, t_tmp3], writes=[t_dt])
            P.op("dve", TT(dta, dt, bc(alog.unsqueeze(1), [128, NT, 32]), ALU.mult), reads=[t_dt, t_al], writes=[t_dta])
            for tt in range(NT):
                P.op("pe", MM(ps[5][:, tt * 32:tt * 32 + 16], c.U, dta[:, tt, 0:16], skip=True), reads=[t_dta, c.t_cst], writes=[tps[5]])
                P.op("pe", MM(ps[5][:, tt * 32 + 16:tt * 32 + 32], c.Ur, dta[:, tt, 16:32], skip=True), reads=[t_dta, c.t_cst], writes=[tps[5]])
            for tt in range(NT):
                P.op("pe", MM(ps[6][:, tt * 32:(tt + 1) * 32], c.onesf, dta[:, tt, :], skip=True), reads=[t_dta, c.t_cst], writes=[tps[6]])
            P.op("act", ACT(cs, ps[5].rearrange("p (t h) -> p t h", h=32), AF.Copy), reads=[tps[5]], writes=[t_cs])
            P.op("act", ACT(tmp3, ps[6].rearrange("p (t h) -> p t h", h=32), AF.Copy), reads=[tps[6]], writes=[t_tmp3])

        def s3():
            P.op("act", ACT(dcb, tmp3, AF.Exp), reads=[t_tmp3], writes=[t_dcb])
            P.op("act", ACT(dout, cs, AF.Exp), reads=[t_cs], writes=[t_dout])
            P.op("dve", TT(dend, tmp3, cs, ALU.subtract), reads=[t_tmp3, t_cs], writes=[t_dend])
            P.op("act", ACT(dend, dend, AF.Exp), reads=[t_dend], writes=[t_dend])
            P.op("dve", TT(cXd, dt, dend, ALU.mult), reads=[t_dt, t_dend], writes=[t_cXd])

        def s4():
            for q in range(4):
                for j in range(4):
                    tt = q * 4 + j
                    P.op("pe", TR(ps[4][0:32, j * 128:(j + 1) * 128], cs[:, tt, :], c.identf), reads=[t_cs, c.t_cst], writes=[tps[4]])
                P.op("act", ACT(csT[0:32, q * 4:(q + 1) * 4, :], ps[4][0:32, :].rearrange("p (j t) -> p j t", t=128), AF.Copy),
                     reads=[tps[4]], writes=[t_csT])
            P.op("sp", DMA(c.csT_d.rearrange("t h l -> h t l"), csT[0:32, :, :]), reads=[t_csT], writes=[t_csd])
        return [s1, s2, s3, s4]

    for g in range(2):
        mg = A.mark()
        Wz = c.Wz[g]
        tWz = pf_ssd(c, l)[g]
        t_Wzq = c.t_Wq[3 - g]
        xs = A.alloc([128, NT, 512], F32)
        t_xs = [T() for _ in range(NT)]
        Btok = A.alloc([128, NT, 128], BF16)
        t_Btok = [T() for _ in range(NT)]
        BT = A.alloc([128, L], BF16)
        t_BT = [T() for _ in range(4)]
        CT = A.alloc([128, L], BF16)
        t_CT = [T() for _ in range(4)]
        m_conv = A.mark()
        stages = []
        if g == 0:
            stages = make_decay_stages(A.alloc([128, NT, 32], F32), A.alloc([128, NT, 32], F32), A.alloc([128, NT, 128], F32))
        pre = [A.alloc([128, L + 4], BF16) for _ in range(2)]
        t_pre = [[T() for _ in range(4)] for _ in range(2)]
        t_halo = [T(), T()]
        for b in range(2):
            P.op("pool", MEMSET(pre[b][:, 0:2], 0.0), writes=[t_halo[b]])
            P.op("pool", MEMSET(pre[b][:, L + 2:L + 4], 0.0), writes=[t_halo[b]])
        Wc = [A.alloc([128, 8, 128], BF16) for _ in range(2)]
        tWc = [[T()], [T()]]
        dg = [A.alloc([128, 5, 128], BF16) for _ in range(2)]
        t_dg = [T(), T()]
        ctmp = [A.alloc([128, 512], F32) for _ in range(2)]
        t_ctmp = [T(), T()]
        chunks = [("B", 8 + g), ("C", 10 + g)] + [("x%d" % j, 4 * g + j) for j in range(4)]
        pbank = 0
        for n, (kind, ci) in enumerate(chunks):
            b = n % 2
            if n >= 1 and stages:
                stages.pop(0)()
            load_w(c, l, Wc[b], C_XBC + ci * 128, 128, tWc[b])
            P.op("dve", TT(dg[b], bc(c.identf.unsqueeze(1), [128, 5, 128]), bc(cwT[:, ci, 0:5].unsqueeze(2), [128, 5, 128]), ALU.mult),
                 reads=[c.t_cst, t_cwT], writes=[t_dg[b]])
            for tb in range(4):
                bank, tbk = ps[pbank % 4], tps[pbank % 4]
                pbank += 1
                for k in range(8):
                    P.op("pe", MM(bank, Wc[b][:, k, :], c.hdnT[:, k, tb * 512:(tb + 1) * 512], start=(k == 0), stop=(k == 7)),
                         reads=[tWc[b][0]] + c.t_hdnT[tb * 4:(tb + 1) * 4], writes=[tbk])
                P.op("act", ACT(pre[b][:, 2 + tb * 512:2 + (tb + 1) * 512], bank, AF.Copy), reads=[tbk], writes=[t_pre[b][tb]])
            allpre = t_pre[b] + [t_halo[b]]
            if kind in ("B", "C"):
                dstT, t_dstT = (BT, t_BT) if kind == "B" else (CT, t_CT)
                for tb in range(4):
                    bank, tbk = ps[pbank % 4], tps[pbank % 4]
                    pbank += 1
                    for k in range(5):
                        P.op("pe", MM(bank, dg[b][:, k, :], pre[b][:, tb * 512 + k:tb * 512 + k + 512], start=(k == 0), stop=(k == 4)),
                             reads=[t_dg[b]] + allpre, writes=[tbk])
                    P.op("act", ACT(dstT[:, tb * 512:(tb + 1) * 512], bank, AF.Silu, bias=cwT[:, ci, 5:6], scale=1.0),
                         reads=[tbk, t_cwT], writes=[t_dstT[tb]])
            if kind != "C":
                for q in range(4):
                    bank, tbk = ps[pbank % 4], tps[pbank % 4]
                    pbank += 1
                    for j in range(4):
                        tt = q * 4 + j
                        for k in range(5):
                            P.op("pe", MM(bank[:, j * 128:(j + 1) * 128], pre[b][:, tt * 128 + k:tt * 128 + k + 128], dg[b][:, k, :],
                                          start=(k == 0 and j == 0), stop=(k == 4), skip=True),
                                 reads=[t_dg[b]] + allpre, writes=[tbk])
                    ct, t_ct = ctmp[q % 2], t_ctmp[q % 2]
                    P.op("dve", TT(ct.rearrange("p (j c) -> p j c", c=128), bank.rearrange("p (j c) -> p j c", c=128),
                                   bc(cbb[:, ci * 128:(ci + 1) * 128].unsqueeze(1), [128, 4, 128]), ALU.add),
                         reads=[tbk, t_cbb], writes=[t_ct])
                    if kind == "B":
                        P.op("act", ACT(Btok[:, q * 4:(q + 1) * 4, :], ct.rearrange("p (j c) -> p j c", c=128), AF.Silu),
                             reads=[t_ct], writes=t_Btok[q * 4:(q + 1) * 4])
                    else:
                        jx = int(kind[1])
                        P.op("act", ACT(xs[:, q * 4:(q + 1) * 4, jx * 128:(jx + 1) * 128], ct.rearrange("p (j c) -> p j c", c=128), AF.Silu),
                             reads=[t_ct], writes=t_xs[q * 4:(q + 1) * 4])

        for st_ in stages:
            st_()
        stages = []
        P.barrier()
        A.release(m_conv)
        if STOP <= 2:
            A.release(mg)
            continue
        hb0 = 16 + 8 * g
        hf0 = 8 * g
        m_passA = A.mark()
        Sst = [A.alloc([128, 512], F32) for _ in range(2)]
        t_Sst = [T(), T()]
        for d in range(2):
            P.op("pool", MEMSET(Sst[d], 0.0), writes=[t_Sst[d]])
        stg = [[A.alloc([128, 512], BF16) for _ in range(2)] for _ in range(2)]
        t_stg = [[T(), T()], [T(), T()]]
        Xd = [[A.alloc([128, 512], BF16) for _ in range(2)] for _ in range(2)]
        t_Xd = [[T(), T()], [T(), T()]]
        t_sd = [[T() for _ in range(NT)] for _ in range(2)]
        sdram = [c.sf_d, c.sb_d]
        h0s = [hf0, hb0]

        def bh(ap3, h0):
            return bc(ap3[:, h0:h0 + 8].unsqueeze(2), [128, 8, 64])

        def v8(ap):
            return ap.rearrange("p (h d) -> p h d", d=64)
        for k in range(NT):
            for d in range(2):
                ci_ = k if d == 0 else NT - 1 - k
                last = (ci_ == NT - 1) if d == 0 else (ci_ == 0)
                sg, t_sg = stg[d][k % 2], t_stg[d][k % 2]
                P.op("act", ACT(sg, Sst[d], AF.Copy), reads=[t_Sst[d]], writes=[t_sg])
                P.op("sp", DMA(sdram[d][g, ci_], sg), reads=[t_sg], writes=[t_sd[d][ci_]])
                if not last:
                    xd, t_xd = Xd[d][k % 2], t_Xd[d][k % 2]
                    P.op("pool", TT(v8(xd), v8(xs[:, ci_, :]), bh(cXd[:, ci_, :], h0s[d]), ALU.mult), reads=[t_xs[ci_], t_cXd], writes=[t_xd])
                    bank, tbk = ps[4 + d], tps[4 + d]
                    P.op("pe", MM(bank, Btok[:, ci_, :], xd), reads=[t_Btok[ci_], t_xd], writes=[tbk])
                    P.op("dve", TT(v8(Sst[d]), v8(Sst[d]), bh(dcb[:, ci_, :], h0s[d]), ALU.mult), reads=[t_Sst[d], t_dcb], writes=[t_Sst[d]])
                    P.op("dve", TT(Sst[d], Sst[d], bank, ALU.add), reads=[t_Sst[d], tbk], writes=[t_Sst[d]])
        P.barrier()
        A.release(m_passA)
        if STOP <= 3:
            A.release(mg)
            continue
        R = [A.alloc([128, 2, 8, 128], F32) for _ in range(2)]
        t_R = [T(), T()]
        SFc = [A.alloc([128, 512], BF16) for _ in range(2)]
        t_SFc = [T(), T()]
        SBc = [A.alloc([128, 512], BF16) for _ in range(2)]
        t_SBc = [T(), T()]
        E = A.alloc([128, 2, 8, 128], BF16)
        t_E = T()
        Mt = [A.alloc([128, 2, 8, 128], BF16) for _ in range(2)]
        t_Mt = [T(), T()]
        Gm = [A.alloc([128, 2, 128], BF16) for _ in range(2)]
        t_Gm = [T(), T()]
        Xt = [A.alloc([128, 3, 512], BF16) for _ in range(2)]
        t_Xt = [T(), T()]
        sz = [A.alloc([128, 512], F32) for _ in range(4)]
        t_sz = [T() for _ in range(4)]
        ya = A.alloc([128, 512], F32)
        yb_ = A.alloc([128, 512], F32)
        yy = A.alloc([128, 512], F32)
        t_ya, t_yb, t_yy = T(), T(), T()
        ssn = A.alloc([128, 2], F32)
        t_ssn = T()
        yo = [A.alloc([128, 512], BF16) for _ in range(2)]
        t_yo = [T(), T()]
        yst = [A.alloc([128, 4, 128], BF16) for _ in range(2)]
        t_yst = [T(), T()]

        def loadR(ci_):
            b = ci_ % 2
            P.op("sp", DMA(R[b].rearrange("p d h l -> p d (h l)"),
                           csT_v[ci_, :, 8 * g:8 * g + 8, :].rearrange("d h l -> d (h l)").partition_broadcast(128)),
                 reads=[t_csd], writes=[t_R[b]])

        def loadS(ci_):
            b = ci_ % 2
            P.op("sp", DMA(SFc[b], c.sf_d[g, ci_]), reads=[t_sd[0][ci_]], writes=[t_SFc[b]])
            P.op("sp", DMA(SBc[b], c.sb_d[g, ci_]), reads=[t_sd[1][ci_]], writes=[t_SBc[b]])

        def iteration(cn, cc_):
            if cn is not None:
                bn = cn % 2
                tokn = slice(cn * 128, (cn + 1) * 128)
                P.op("pe", MM(ps[4][:, 0:128], BT[:, tokn], CT[:, tokn]), reads=[t_BT[cn // 4], t_CT[cn // 4]], writes=[tps[4]])
                if cn % 2 == 0:
                    for c2 in (cn, cn + 1):
                        if c2 < NT:
                            zb, t_zb = (ps[3], tps[3]) if c2 % 2 == 0 else (ps[6], tps[6])
                            tok2 = slice(c2 * 128, (c2 + 1) * 128)
                            for k in range(8):
                                P.op("pe", MM(zb, c.hdnT[:, k, tok2], Wz[:, k, :], start=(k == 0), stop=(k == 7)),
                                     reads=[c.t_hdnT[c2], tWz[0], t_Wzq], writes=[t_zb])
                P.op("dve", TT(Gm[bn][:, 0, :], ps[4][:, 0:128], c.U, ALU.mult), reads=[tps[4], c.t_cst], writes=[t_Gm[bn]])
                P.op("dve", TT(Gm[bn][:, 1, :], ps[4][:, 0:128], c.Ur, ALU.mult), reads=[tps[4], c.t_cst], writes=[t_Gm[bn]])
                csv = cs[:, cn, :].rearrange("p (d h) -> p d h", d=2)[:, :, 8 * g:8 * g + 8]
                Dd, t_D = R[bn], t_R[bn]
                P.op("dve", TT(Dd, Dd, bc(csv.unsqueeze(3), [128, 2, 8, 128]), ALU.subtract), reads=[t_cs], writes=[t_D])
                P.op("act", ACT(Dd, Dd, AF.Relu, scale=-1.0), reads=[], writes=[t_D])
                P.op("act", ACT(E, Dd, AF.Exp, scale=-1.0), reads=[t_D], writes=[t_E])
                P.op("pool", TT(v8(Xt[bn][:, 0, :]), v8(xs[:, cn, :]), bh(dt[:, cn, :], hf0), ALU.mult), reads=[t_xs[cn], t_dt], writes=[t_Xt[bn]])
                P.op("pool", TT(v8(Xt[bn][:, 1, :]), v8(xs[:, cn, :]), bh(dt[:, cn, :], hb0), ALU.mult), reads=[t_xs[cn], t_dt], writes=[t_Xt[bn]])
                P.op("pool", TT(v8(Xt[bn][:, 2, :]), v8(xs[:, cn, :]), bh(dsum, 8 * g), ALU.mult), reads=[t_xs[cn], t_dsk], writes=[t_Xt[bn]])
            if cc_ is not None:
                b = cc_ % 2
                tok = slice(cc_ * 128, (cc_ + 1) * 128)
                Yb_, t_Y = (ps[0], tps[0]) if b == 0 else (ps[5], tps[5])
                P.op("pe", MM(Yb_, c.ident, Xt[b][:, 2, :], start=True, stop=False, skip=True), reads=[c.t_ident, t_Xt[b]], writes=[t_Y])
                for h in range(8):
                    for d in range(2):
                        P.op("pe", MM(Yb_[:, h * 64:(h + 1) * 64], Mt[b][:, d, h, :], Xt[b][:, d, h * 64:(h + 1) * 64],
                                      start=False, stop=(d == 1), skip=True),
                             reads=[t_Mt[b], t_Xt[b]], writes=[t_Y])
                P.op("pe", MM(ps[1], CT[:, tok], SFc[b]), reads=[t_CT[cc_ // 4], t_SFc[b]], writes=[tps[1]])
                P.op("pe", MM(ps[2], CT[:, tok], SBc[b]), reads=[t_CT[cc_ // 4], t_SBc[b]], writes=[tps[2]])
                P.op("dve", TT(v8(ya), v8(ps[1]), bh(dout[:, cc_, :], hf0), ALU.mult), reads=[tps[1], t_dout], writes=[t_ya])
                P.op("dve", TT(v8(yb_), v8(ps[2]), bh(dout[:, cc_, :], hb0), ALU.mult), reads=[tps[2], t_dout], writes=[t_yb])
                P.op("dve", TT(ya, ya, yb_, ALU.add), reads=[t_ya, t_yb], writes=[t_ya])
                P.op("dve", TT(yy, Yb_, ya, ALU.add), reads=[t_Y, t_ya], writes=[t_yy])
                P.op("pool", TT(yy, yy, sz[cc_ % 4], ALU.mult), reads=[t_yy, t_sz[cc_ % 4]], writes=[t_yy])
            if cn is not None:
                for d in range(2):
                    P.op("dve", TT(Mt[bn][:, d], E[:, d], bc(Gm[bn][:, d, :].unsqueeze(1), [128, 8, 128]), ALU.mult),
                         reads=[t_E, t_Gm[bn]], writes=[t_Mt[bn]])
                if cn % 2 == 0:
                    for c2 in (cn, cn + 1):
                        if c2 < NT:
                            zb, t_zb = (ps[3], tps[3]) if c2 % 2 == 0 else (ps[6], tps[6])
                            P.op("act", ACT(sz[c2 % 4], zb, AF.Silu), reads=[t_zb], writes=[t_sz[c2 % 4]])
            if cc_ is not None:
                P.op("act", ACT(ya, yy, AF.Square, accum_out=ssn[:, 0:1]), reads=[t_yy], writes=[t_ya, t_ssn])
                P.op("act", ACT(ssn[:, 0:1], ssn[:, 0:1], AF.Ln, scale=1.0 / 512, bias=c.epsc), reads=[t_ssn, c.t_eps], writes=[t_ssn])
                P.op("act", ACT(ssn[:, 0:1], ssn[:, 0:1], AF.Exp, scale=-0.5), reads=[t_ssn], writes=[t_ssn])
                P.op("dve", STT(yo[b], yy, ssn[:, 0:1], snw[:, g * 512:(g + 1) * 512], ALU.mult, ALU.mult),
                     reads=[t_yy, t_ssn, t_snw], writes=[t_yo[b]])

        def stageC(ci_):
            b = ci_ % 2
            tok = slice(ci_ * 128, (ci_ + 1) * 128)
            for j in range(4):
                P.op("pe", TR(c.psb[:, j * 128:(j + 1) * 128], yo[b][:, j * 128:(j + 1) * 128], c.ident),
                     reads=[t_yo[b], c.t_ident], writes=[c.tpsb[0]])
            P.op("dve", CP(yst[b], c.psb[:, 0:512].rearrange("p (j t) -> p j t", t=128)), reads=[c.tpsb[0]], writes=[t_yst[b]])
            P.op("sp", DMA(c.yT[l][g * 512:(g + 1) * 512, tok].rearrange("(j p) t -> p j t", p=128), yst[b]), reads=[t_yst[b]])

        loadR(0)
        loadS(0)
        loadR(1)
        iteration(0, None)
        for ci_ in range(NT):
            if ci_ + 1 < NT:
                loadS(ci_ + 1)
                if ci_ + 2 < NT:
                    loadR(ci_ + 2)
            iteration(ci_ + 1 if ci_ + 1 < NT else None, ci_)
            if ci_ >= 1:
                stageC(ci_ - 1)
        stageC(NT - 1)
        A.release(mg)
    A.release(m)


def phase_out(c, l, xsrc, xdst, fuse_next=False):
    P, A = c.P, c.A
    m = A.mark()
    if fuse_next:
        nwb = A.alloc([128, DM], F32)
        t_nwb = T()
        P.op("sp", DMA(nwb, c.norm_w[l + 1:l + 2, :].partition_broadcast(128)), writes=[t_nwb])
        sqj = A.alloc([128, DM], F32)
        t_sqj = T()
        ssx = A.alloc([128, NT], F32)
        t_ssx = [T() for _ in range(NT)]
        hb = [A.alloc([128, DM], BF16) for _ in range(2)]
        t_hb = [T(), T()]
    Wo = c.Wo
    tWo = pf_out(c, l, [0, 1, 2, 3])
    yt = [A.alloc([128, 16, 512], BF16) for _ in range(2)]
    t_yt = [T(), T()]
    xt = [A.alloc([128, DM], F32) for _ in range(3)]
    t_xt = [T(), T(), T()]
    ot = [A.alloc([128, DM], F32) for _ in range(2)]
    t_ot = [T(), T()]
    ps, tps = c.ps, c.tps
    fin = []

    def load_y(qb):
        P.op("sp", DMA(yt[qb % 2], c.yT[l][:, qb * 512:(qb + 1) * 512].rearrange("(k p) t -> p k t", p=128)), writes=[t_yt[qb % 2]])

    def load_x(tt):
        P.op("sp", DMA(xt[tt % 3], xsrc[tt * 128:(tt + 1) * 128, :]), writes=[t_xt[tt % 3]])
    load_y(0)
    load_x(0)
    load_x(1)
    nb = 0
    for tt in range(NT):
        b = tt % 2
        qb, j = tt // 4, tt % 4
        tok = slice(tt * 128, (tt + 1) * 128)
        if j == 0 and qb + 1 < 4:
            load_y(qb + 1)
        if tt + 2 < NT:
            load_x(tt + 2)
        for n in range(2):
            bank, tb = ps[nb % 6], tps[nb % 6]
            nb += 1
            for k in range(16):
                P.op("pe", MM(bank, yt[qb % 2][:, k, j * 128:(j + 1) * 128], Wo[:, k, n * 512:(n + 1) * 512], start=(k == 0), stop=(k == 15)),
                     reads=[t_yt[qb % 2], tWo[k // 4], c.t_Wq[k // 4]], writes=[tb])
            P.op("dve", TT(ot[b][:, n * 512:(n + 1) * 512], bank, xt[tt % 3][:, n * 512:(n + 1) * 512], ALU.add),
                 reads=[tb, t_xt[tt % 3]], writes=[t_ot[b]])
        fin.append(P.op("sp", DMA(xdst[tok, :], ot[b]), reads=[t_ot[b]]))
        if fuse_next:
            s1 = ssx[:, tt:tt + 1]
            P.op("act", ACT(sqj, ot[b], AF.Square, accum_out=s1), reads=[t_ot[b]], writes=[t_sqj, t_ssx[tt]])
            P.op("act", ACT(s1, s1, AF.Ln, scale=1.0 / DM, bias=c.epsc), reads=[t_ssx[tt], c.t_eps], writes=[t_ssx[tt]])
            P.op("act", ACT(s1, s1, AF.Exp, scale=-0.5), reads=[t_ssx[tt]], writes=[t_ssx[tt]])
            P.op("dve", STT(hb[b], ot[b], s1, nwb, ALU.mult, ALU.mult), reads=[t_ot[b], t_ssx[tt], t_nwb], writes=[t_hb[b]])
            for k in range(8):
                P.op("pe", TR(c.psb[:, k * 128:(k + 1) * 128], hb[b][:, k * 128:(k + 1) * 128], c.ident),
                     reads=[t_hb[b], c.t_ident], writes=[c.tpsb[0]])
            P.op("act", ACT(c.hdnT[:, :, tok], c.psb[:, :].rearrange("p (k t) -> p k t", t=128), AF.Copy),
                 reads=[c.tpsb[0]], writes=[c.t_hdnT[tt]])
    A.release(m)
    return fin


def _host_consts():
    inv_freq = (10000.0 ** (-(np.arange(0, 64, 2, dtype=np.float32)) / np.float32(64))).astype(np.float32)
    ang = (np.arange(L, dtype=np.float32)[:, None] * inv_freq[None, :]).astype(np.float32)
    cos, sin = np.cos(ang).astype(np.float32), np.sin(ang).astype(np.float32)
    cs = np.concatenate([cos, cos, -sin, sin], axis=1).astype(np.float32)
    k = np.arange(128)
    ident = np.eye(128, dtype=np.float32)
    U = (k[:, None] <= k[None, :]).astype(np.float32)
    Ur = (k[:, None] >= k[None, :]).astype(np.float32)
    ones = np.ones((128, 128), np.float32)
    consts = np.concatenate([ident, U, Ur, ones], axis=1)
    return np.ascontiguousarray(cs), np.ascontiguousarray(consts)


_CACHE = {}


def make_in_maps(inputs, n_cores=8):
    cs, consts = _host_consts()
    f = lambda a: np.ascontiguousarray(np.asarray(a, dtype=np.float32))
    shared = {
        "norm_w": f(inputs["norm_w"]), "w_in": f(inputs["w_in"]), "conv_w": f(inputs["conv_w"]),
        "conv_b": f(inputs["conv_b"]), "a_log": f(inputs["a_log"]).reshape(2, 32),
        "dt_bias": f(inputs["dt_bias"]).reshape(2, 32), "d_skip": f(inputs["d_skip"]).reshape(2, 32),
        "ssd_norm_w": f(inputs["ssd_norm_w"]), "diff_qk_norm": f(inputs["diff_qk_norm"]).reshape(2, 128),
        "diff_lambda": f(inputs["diff_lambda"]).reshape(2, 256), "diff_subln": f(inputs["diff_subln"]),
        "na_qk_norm": f(inputs["na_qk_norm"]).reshape(2, 128), "na_tab": _na_tables(f(inputs["na_rpb"])),
        "w_out": f(inputs["w_out"]), "cs_tab": cs, "consts": consts,
    }
    x = f(inputs["x"])
    return [dict(shared, x=x[b]) for b in range(n_cores)]


def kernel(**inputs):
    if "nc" not in _CACHE:
        _CACHE["nc"] = build()[0]
    nc = _CACHE["nc"]
    in_maps = make_in_maps(inputs)
    res = run_bass_kernel_spmd(nc, in_maps, core_ids=list(range(8)))
    return np.stack([np.asarray(r["out"], dtype=np.float32) for r in res.results], axis=0)
```

```python
import contextlib
import math
import numpy as np
import concourse.bass as bass
import concourse.mybir as mybir
from concourse.bass_utils import run_bass_kernel_spmd

F32 = mybir.dt.float32
BF16 = mybir.dt.bfloat16
AF = mybir.ActivationFunctionType
ALU = mybir.AluOpType
AX = mybir.AxisListType

L = 2048
DM = 1024
NT = 16
INW = 6688
EPS = 1e-6
NEG = -30000.0
C_Z, C_XBC, C_DT, C_DIFF, C_NA = 0, 1024, 2560, 2592, 4640


class T:
    __slots__ = ("w", "r", "excl")

    def __init__(self, excl=False):
        self.w = None
        self.r = []
        self.excl = excl


class Op:
    __slots__ = ("eng", "fn", "deps", "idx", "sig", "isdma", "semid", "semval", "prev_same_sem")


class Prog:
    COMPUTE = ("pe", "act", "dve", "pool")
    DMAQ = ("sp", "actq", "poolq")
    STREAM = {"pe": "pe", "act": "act", "dve": "dve", "pool": "pool", "sp": "sp", "actq": "act", "poolq": "pool"}
    STREAMS = ("pe", "act", "dve", "pool", "sp")

    def __init__(self, nc, n_dma_sems=12):
        self.nc = nc
        self.ops = []
        self.n_dma_sems = n_dma_sems
        self.last = {s: None for s in self.STREAMS}
        self.recent_dma = {q: [] for q in self.DMAQ}
        self.frontier = []
        self.synced = {s: True for s in self.STREAMS}

    def barrier(self):
        fr = [o for o in self.last.values() if o is not None]
        for q in self.DMAQ:
            fr.extend(self.recent_dma[q])
        self.frontier = fr
        self.synced = {s: False for s in self.STREAMS}

    def op(self, eng, fn, reads=(), writes=()):
        o = Op()
        o.eng = eng
        o.fn = fn
        o.isdma = eng in self.DMAQ
        o.idx = len(self.ops)
        o.prev_same_sem = None
        deps = {}
        if any(t.excl for t in reads):
            writes = list(writes) + [t for t in reads if t.excl and t not in writes]
            reads = [t for t in reads if not t.excl]
        for t in reads:
            if t.w is not None:
                deps[t.w.idx] = ("raw", t.w)
        for t in writes:
            if t.w is not None and t.w.idx not in deps:
                deps[t.w.idx] = ("waw", t.w)
            for r in t.r:
                if r.idx not in deps:
                    deps[r.idx] = ("war", r)
        st = self.STREAM[eng]
        if not self.synced[st]:
            for p in self.frontier:
                if p.idx not in deps:
                    deps[p.idx] = ("bar", p)
            self.synced[st] = True
        for t in writes:
            t.w = o
            t.r = []
        for t in reads:
            if t.w is not o:
                t.r.append(o)
        o.deps = deps
        o.sig = False
        self.ops.append(o)
        self.last[st] = o
        if o.isdma:
            lst = self.recent_dma[eng]
            lst.append(o)
            if len(lst) > self.n_dma_sems:
                lst.pop(0)
        return o

    def emit(self, final_waits=()):
        nc = self.nc
        streams = {s: [] for s in self.STREAMS}
        for o in self.ops:
            streams[self.STREAM[o.eng]].append(o)
        pos = {}
        for s, lst in streams.items():
            for i, o in enumerate(lst):
                pos[o.idx] = i
        need = {}
        for o in self.ops:
            lst = []
            so = self.STREAM[o.eng]
            for (kind, p) in o.deps.values():
                sp_ = self.STREAM[p.eng]
                if p.isdma or o.isdma:
                    lst.append(p)
                elif sp_ != so:
                    lst.append(p)
                else:
                    if so == "pe":
                        continue
                    lst.append(p)
            need[o.idx] = lst
            for p in lst:
                p.sig = True
        for o in final_waits:
            o.sig = True
        stack = contextlib.ExitStack()
        esem = {e: stack.enter_context(nc.semaphore("s_" + e)) for e in self.COMPUTE}
        ecount = {e: 0 for e in self.COMPUTE}
        dsems = {q: [stack.enter_context(nc.semaphore("d_%s_%d" % (q, i))) for i in range(self.n_dma_sems)]
                 for q in self.DMAQ}
        duse = {q: [0] * self.n_dma_sems for q in self.DMAQ}
        dlast = {q: [None] * self.n_dma_sems for q in self.DMAQ}
        dnext = {q: 0 for q in self.DMAQ}
        for s in self.STREAMS:
            for o in streams[s]:
                if o.isdma:
                    q = o.eng
                    j = dnext[q]
                    dnext[q] = (j + 1) % self.n_dma_sems
                    duse[q][j] += 1
                    o.semid = (q, j)
                    o.semval = 16 * duse[q][j]
                    o.prev_same_sem = dlast[q][j]
                    dlast[q][j] = o
                elif o.sig:
                    ecount[o.eng] += 1
                    o.semid = o.eng
                    o.semval = ecount[o.eng]
        self.n_waits = 0

        def sem_of(p):
            if p.isdma:
                return dsems[p.semid[0]][p.semid[1]]
            return esem[p.semid]

        def emit_stream(s, engobj):
            known = {}
            for o in streams[s]:
                waits = {}
                cand = list(need[o.idx])
                if o.isdma and o.prev_same_sem is not None:
                    cand.append(o.prev_same_sem)
                for p in cand:
                    k = p.semid
                    if known.get(k, 0) >= p.semval:
                        continue
                    if waits.get(k, (0, None))[0] < p.semval:
                        waits[k] = (p.semval, p)
                for k, (v, p) in waits.items():
                    engobj.wait_ge(sem_of(p), v)
                    known[k] = v
                    self.n_waits += 1
                ins = o.fn(engobj)
                if o.isdma:
                    ins.then_inc(dsems[o.semid[0]][o.semid[1]], 16)
                elif o.sig:
                    ins.then_inc(esem[o.semid], 1)
            if s == "sp":
                for o in final_waits:
                    engobj.wait_ge(sem_of(o), o.semval)

        with nc.Block() as block:
            @block.tensor
            def _(e):
                emit_stream("pe", e)

            @block.scalar
            def _(e):
                emit_stream("act", e)

            @block.vector
            def _(e):
                emit_stream("dve", e)

            @block.gpsimd
            def _(e):
                emit_stream("pool", e)

            @block.sync
            def _(e):
                emit_stream("sp", e)
        stack.close()


def ACT(out, in_, func, **kw):
    return lambda e: e.activation(out=out, in_=in_, func=func, **kw)


def TT(out, in0, in1, op):
    return lambda e: e.tensor_tensor(out=out, in0=in0, in1=in1, op=op)


def TS(out, in0, s1, op0, s2=None, op1=None):
    if op1 is None:
        return lambda e: e.tensor_scalar(out=out, in0=in0, scalar1=s1, scalar2=None, op0=op0)
    return lambda e: e.tensor_scalar(out=out, in0=in0, scalar1=s1, scalar2=s2, op0=op0, op1=op1)


def STT(out, in0, scalar, in1, op0, op1):
    return lambda e: e.scalar_tensor_tensor(out=out, in0=in0, scalar=scalar, in1=in1, op0=op0, op1=op1)


def CP(out, in_):
    return lambda e: e.tensor_copy(out=out, in_=in_)


def MM(out, lhsT, rhs, start=True, stop=True, skip=False):
    if skip:
        return lambda e: e.matmul(out, lhsT, rhs, start=start, stop=stop, skip_group_check=True)
    return lambda e: e.matmul(out, lhsT, rhs, start=start, stop=stop)


def TR(out, in_, ident):
    return lambda e: e.transpose(out, in_, ident)


def DMA(out, in_):
    return lambda e: e.dma_start(out=out, in_=in_)


def RED(out, in_, op=None):
    return lambda e: e.tensor_reduce(out=out, in_=in_, axis=AX.X, op=(op or ALU.add))


def RECIP(out, in_):
    return lambda e: e.reciprocal(out=out, in_=in_)


def MEMSET(ap, v):
    return lambda e: e.memset(ap, v)


class Arena:
    def __init__(self, nc, words):
        self.t = nc.alloc_sbuf_tensor("arena", [128, words], F32)
        self.words = words
        self.off = 0
        self.peak = 0

    def alloc(self, shape, dt):
        n = int(np.prod(shape[1:]))
        nw = n if dt == F32 else (n + 1) // 2
        nw = (nw + 7) // 8 * 8
        assert self.off + nw <= self.words, ("SBUF arena overflow", self.off, nw, self.words)
        v = self.t[:, self.off:self.off + nw]
        self.off += nw
        self.peak = max(self.peak, self.off)
        if dt != F32:
            v = v.bitcast(dt)
        v = v[:, 0:n]
        if len(shape) == 3:
            v = v.rearrange("p (a b) -> p a b", b=shape[2])
        elif len(shape) == 4:
            v = v.rearrange("p (a b c) -> p a b c", b=shape[2], c=shape[3])
        return v

    def mark(self):
        return self.off

    def release(self, m):
        self.off = m


def bc(ap, shape):
    return ap.to_broadcast(shape)


class Ctx:
    pass


def build(n_layers=2, phases=("diff", "na", "ssd"), dbg=False):
    nc = bass.Bass("TRN2", target_bir_lowering=False)
    c = Ctx()
    c.nc = nc
    c.dbg = dbg

    def din(name, shape, dt=F32):
        return nc.dram_tensor(name, shape, dt, kind="ExternalInput").ap()

    c.x_in = din("x", [L, DM])
    c.norm_w = din("norm_w", [2, DM])
    c.w_in = din("w_in", [2, DM, INW])
    c.conv_w = din("conv_w", [2, 5, 1536])
    c.conv_b = din("conv_b", [2, 1536])
    c.a_log = din("a_log", [2, 32])
    c.dt_bias = din("dt_bias", [2, 32])
    c.d_skip = din("d_skip", [2, 32])
    c.ssd_norm_w = din("ssd_norm_w", [2, 1024])
    c.diff_qk_norm = din("diff_qk_norm", [2, 128])
    c.diff_lambda = din("diff_lambda", [2, 256])
    c.diff_subln = din("diff_subln", [2, 128])
    c.na_qk_norm = din("na_qk_norm", [2, 128])
    c.na_tab = din("na_tab", [2, 8, 128, NA_NCLS * 128])
    c.w_out = din("w_out", [2, 2048, DM])
    c.cs_tab = din("cs_tab", [L, 128])
    c.consts = din("consts", [128, 4 * 128])
    c.out = nc.dram_tensor("out", [L, DM], F32, kind="ExternalOutput").ap()
    kind_scr = "ExternalOutput" if dbg else "Internal"
    c.x1 = nc.dram_tensor("x1", [L, DM], F32, kind=kind_scr).ap()
    c.yT = [nc.dram_tensor("yT%d" % i, [2048, L], BF16, kind=kind_scr).ap() for i in range(n_layers)]
    c.csT_d = nc.dram_tensor("csT_d", [NT, 32, 128], F32, kind="Internal").ap()
    c.sb_d = nc.dram_tensor("sb_d", [2, NT, 128, 512], BF16, kind="Internal").ap()
    c.sf_d = nc.dram_tensor("sf_d", [2, NT, 128, 512], BF16, kind="Internal").ap()

    c.dumps = []

    def dump(name, ap, shape, dt, tiles):
        if not dbg:
            return
        d = nc.dram_tensor("dbg_" + name, list(shape), dt, kind="ExternalOutput").ap()
        c.dumps.append(c.P.op("sp", DMA(d, ap), reads=tiles))
    c.dump = dump
    A = Arena(nc, 51000)
    c.A = A
    P = Prog(nc)
    c.P = P
    c.psall = nc.alloc_psum_tensor("psall", [128, 8 * 512], F32)[:, :]
    c.ps = [c.psall[:, i * 512:(i + 1) * 512] for i in range(7)]
    c.tps = [T(excl=True) for _ in range(7)]
    c.psb = c.psall[:, 7 * 512:8 * 512].bitcast(BF16)
    _tb = T(excl=True)
    c.tpsb = [_tb, _tb]

    c.cst = A.alloc([128, 512], F32)
    c.t_cst = T()
    P.op("sp", DMA(c.cst, c.consts), writes=[c.t_cst])
    c.identf = c.cst[:, 0:128]
    c.U = c.cst[:, 128:256]
    c.Ur = c.cst[:, 256:384]
    c.onesf = c.cst[:, 384:512]
    c.ident = A.alloc([128, 128], BF16)
    c.t_ident = T()
    P.op("dve", CP(c.ident, c.identf), reads=[c.t_cst], writes=[c.t_ident])
    c.epsc = A.alloc([128, 1], F32)
    c.t_eps = T()
    P.op("pool", MEMSET(c.epsc, EPS), writes=[c.t_eps])
    c.onec = A.alloc([128, 1], F32)
    P.op("pool", MEMSET(c.onec, 1.0), writes=[c.t_eps])
    c.hdnT = A.alloc([128, 8, L], BF16)
    c.t_hdnT = [T() for _ in range(NT)]
    c.Wbuf = A.alloc([128, 16384], BF16)
    c.t_Wq = [T() for _ in range(4)]
    c.W8 = c.Wbuf.rearrange("p (k c) -> p k c", c=2048)
    c.Wo = c.Wbuf.rearrange("p (k c) -> p k c", c=1024)
    c.Wz = [c.Wbuf[:, 3 * 4096:4 * 4096].rearrange("p (k c) -> p k c", c=512),
            c.Wbuf[:, 2 * 4096:3 * 4096].rearrange("p (k c) -> p k c", c=512)]
    c.pf = {}
    c.phases = phases

    finals = []
    for l in range(n_layers):
        xsrc = c.x_in if l == 0 else c.x1
        xdst = c.x1 if l < n_layers - 1 or dbg and n_layers == 1 else c.out
        if l == n_layers - 1:
            xdst = c.out
        P.barrier()
        if "diff" in phases:
            pf_qkvg(c, l, "diff")
        if l == 0:
            phase_hdn(c, l, xsrc)
        if "diff" in phases:
            P.barrier()
            phase_diff(c, l)
        if "na" in phases:
            P.barrier()
            phase_na(c, l)
        if "ssd" in phases:
            P.barrier()
            phase_ssd(c, l)
        P.barrier()
        fin = phase_out(c, l, xsrc, xdst, fuse_next=(l + 1 < n_layers))
        if l == n_layers - 1:
            finals = fin
    P.barrier()
    finals = list(finals)
    if dbg:
        finals += [o for o in P.ops if o.isdma][-36:] + c.dumps
    P.emit(final_waits=finals)
    c.peak = A.peak
    return nc, c


def phase_hdn(c, l, xsrc):
    P, A = c.P, c.A
    m = A.mark()
    nwb = A.alloc([128, DM], F32)
    t_nwb = T()
    P.op("sp", DMA(nwb, c.norm_w[l:l + 1, :].partition_broadcast(128)), writes=[t_nwb])
    NX = 4
    xt = [A.alloc([128, DM], F32) for _ in range(NX)]
    t_xt = [T() for _ in range(NX)]
    sq = A.alloc([128, DM], F32)
    t_sq = T()
    ss = A.alloc([128, NT], F32)
    t_ss = [T() for _ in range(NT)]
    hb = [A.alloc([128, DM], BF16) for _ in range(2)]
    t_hb = [T(), T()]

    def load_x(tt):
        P.op("sp", DMA(xt[tt % NX], xsrc[tt * 128:(tt + 1) * 128, :]), writes=[t_xt[tt % NX]])
    for tt in range(min(NX - 1, NT)):
        load_x(tt)
    for tt in range(NT):
        b = tt % 2
        xb, t_xb = xt[tt % NX], t_xt[tt % NX]
        s1 = ss[:, tt:tt + 1]
        if tt + NX - 1 < NT:
            load_x(tt + NX - 1)
        P.op("act", ACT(sq, xb, AF.Square, accum_out=s1), reads=[t_xb], writes=[t_sq, t_ss[tt]])
        P.op("act", ACT(s1, s1, AF.Ln, scale=1.0 / DM, bias=c.epsc), reads=[t_ss[tt], c.t_eps], writes=[t_ss[tt]])
        P.op("act", ACT(s1, s1, AF.Exp, scale=-0.5), reads=[t_ss[tt]], writes=[t_ss[tt]])
        P.op("dve", STT(hb[b], xb, s1, nwb, ALU.mult, ALU.mult), reads=[t_xb, t_ss[tt], t_nwb], writes=[t_hb[b]])
        for k in range(8):
            P.op("pe", TR(c.psb[:, k * 128:(k + 1) * 128], hb[b][:, k * 128:(k + 1) * 128], c.ident),
                 reads=[t_hb[b], c.t_ident], writes=[c.tpsb[k // 4]])
        P.op("act", ACT(c.hdnT[:, :, tt * 128:(tt + 1) * 128], c.psb[:, :].rearrange("p (k t) -> p k t", t=128), AF.Copy),
             reads=[c.tpsb[0], c.tpsb[1]], writes=[c.t_hdnT[tt]])
    A.release(m)


def load_w(c, l, dst, col0, ncols, tiles, step=512, extra=()):
    P = c.P
    j = 0
    for c0 in range(0, ncols, step):
        n = min(step, ncols - c0)
        src = c.w_in[l, :, col0 + c0:col0 + c0 + n].rearrange("(k p) c -> p k c", p=128)
        P.op("poolq", DMA(dst[:, :, c0:c0 + n], src), writes=[tiles[j]] + list(extra))
        j += 1


def pf_qkvg(c, l, which):
    key = (which, l)
    if key not in c.pf:
        tW = [T() for _ in range(4)]
        load_w(c, l, c.W8, C_DIFF if which == "diff" else C_NA, 2048, tW, extra=c.t_Wq)
        c.pf[key] = tW
    return c.pf[key]


def pf_ssd(c, l):
    key = ("ssd", l)
    if key not in c.pf:
        tWz = [[T()], [T()]]
        load_w(c, l, c.Wz[0], C_Z, 512, tWz[0], extra=[c.t_Wq[3]])
        load_w(c, l, c.Wz[1], C_Z + 512, 512, tWz[1], extra=[c.t_Wq[2]])
        c.pf[key] = tWz
    return c.pf[key]


def pf_out(c, l, quarters):
    key = ("out", l)
    if key not in c.pf:
        c.pf[key] = [None] * 4
    tWo = c.pf[key]
    for j in quarters:
        if tWo[j] is None:
            tWo[j] = T()
            src = c.w_out[l, j * 512:(j + 1) * 512, :].rearrange("(k p) c -> p k c", p=128)
            c.P.op("poolq", DMA(c.Wo[:, j * 4:(j + 1) * 4, :], src), writes=[tWo[j], c.t_Wq[j]])
    return tWo


def qk_norm_rope(c, src_ps, t_src, sq, t_sq, ss8, t_ss8, tbuf, t_tb, ubuf, t_ub, TC, TS_, t_tab, tt, dst, t_dst, rope):
    P = c.P
    P.op("act", ACT(sq, src_ps, AF.Square), reads=[t_src], writes=[t_sq])
    P.op("dve", RED(ss8, sq.rearrange("p (g d) -> p g d", d=64)), reads=[t_sq], writes=[t_ss8])
    P.op("act", ACT(ss8, ss8, AF.Ln, scale=1.0 / 64, bias=c.epsc), reads=[t_ss8, c.t_eps], writes=[t_ss8])
    P.op("act", ACT(ss8, ss8, AF.Exp, scale=-0.5), reads=[t_ss8], writes=[t_ss8])
    t3 = tbuf.rearrange("p (g d) -> p g d", d=64)
    P.op("dve", TT(t3, src_ps.rearrange("p (g d) -> p g d", d=64), bc(ss8.unsqueeze(2), [128, 8, 64]), ALU.mult),
         reads=[t_src, t_ss8], writes=[t_tb])
    if rope:
        u3 = ubuf.rearrange("p (g d) -> p g d", d=64)
        P.op("pool", TT(u3[:, :, 0:32], t3[:, :, 32:64], bc(TS_[:, tt, 0:32].unsqueeze(1), [128, 8, 32]), ALU.mult),
             reads=[t_tb, t_tab], writes=[t_ub])
        P.op("pool", TT(u3[:, :, 32:64], t3[:, :, 0:32], bc(TS_[:, tt, 32:64].unsqueeze(1), [128, 8, 32]), ALU.mult),
             reads=[t_tb, t_tab], writes=[t_ub])
        P.op("dve", TT(t3, t3, bc(TC[:, tt, :].unsqueeze(1), [128, 8, 64]), ALU.mult), reads=[t_tb, t_tab], writes=[t_tb])
        P.op("dve", TT(dst, tbuf, ubuf, ALU.add), reads=[t_tb, t_ub], writes=[t_dst])
    else:
        P.op("dve", TT(dst.rearrange("p (g d) -> p g d", d=64), t3, bc(TC.unsqueeze(1), [128, 8, 64]), ALU.mult),
             reads=[t_tb, t_tab], writes=[t_dst])


def prep_qkvg(c, W, tW, qT, t_qT, kT, t_kT, vcopy, t_V, G, t_G, tabs, t_tab, rope):
    P, A = c.P, c.A
    ps, tps = c.ps, c.tps
    NB = 3 if rope else 2
    LAG = NB - 1
    raw = [[A.alloc([128, 512], F32) for _ in range(NB)] for _ in range(2)]
    t_raw = [[T() for _ in range(NB)] for _ in range(2)]
    sq = [A.alloc([128, 512], F32) for _ in range(2)]
    t_sq = [T(), T()]
    ss8 = [[A.alloc([128, 8], F32) for _ in range(NB)] for _ in range(2)]
    t_ss8 = [[T() for _ in range(NB)] for _ in range(2)]
    tbuf = [[A.alloc([128, 512], F32) for _ in range(NB)] for _ in range(2)]
    t_tb = [[T() for _ in range(NB)] for _ in range(2)]
    ubuf = [[A.alloc([128, 512], F32) for _ in range(NB)] for _ in range(2)] if rope else None
    t_ub = [[T() for _ in range(NB)] for _ in range(2)]
    rbuf = [[A.alloc([128, 512], BF16) for _ in range(NB)] for _ in range(2)]
    t_rb = [[T() for _ in range(NB)] for _ in range(2)]
    dsts = ((qT, t_qT), (kT, t_kT))
    nbank = 0
    gbank = {}

    def chain_a(tt, i):
        b = tt % NB
        P.op("act", ACT(sq[i], raw[i][b], AF.Square), reads=[t_raw[i][b]], writes=[t_sq[i]])
        P.op("dve", RED(ss8[i][b], sq[i].rearrange("p (g d) -> p g d", d=64)), reads=[t_sq[i]], writes=[t_ss8[i][b]])

    def chain_b(tt, i):
        b = tt % NB
        s8, t_s8 = ss8[i][b], t_ss8[i][b]
        P.op("act", ACT(s8, s8, AF.Ln, scale=1.0 / 64, bias=c.epsc), reads=[t_s8, c.t_eps], writes=[t_s8])
        P.op("act", ACT(s8, s8, AF.Exp, scale=-0.5), reads=[t_s8], writes=[t_s8])

    def chain_c(tt, i):
        b = tt % NB
        rw, t_rw = raw[i][b], t_raw[i][b]
        s8, t_s8 = ss8[i][b], t_ss8[i][b]
        tb_, t_tb_ = tbuf[i][b], t_tb[i][b]
        t3 = tb_.rearrange("p (g d) -> p g d", d=64)
        P.op("dve", TT(t3, rw.rearrange("p (g d) -> p g d", d=64), bc(s8.unsqueeze(2), [128, 8, 64]), ALU.mult),
             reads=[t_rw, t_s8], writes=[t_tb_])
        dst, t_dst = rbuf[i][b], t_rb[i][b]
        TC, TS_ = tabs[2 * i], tabs[2 * i + 1]
        if rope:
            ub, t_ub_ = ubuf[i][b], t_ub[i][b]
            u3 = ub.rearrange("p (g d) -> p g d", d=64)
            eng_u = "pool" if i == 1 else "dve"
            P.op(eng_u, TT(u3[:, :, 0:32], t3[:, :, 32:64], bc(TS_[:, tt, 0:32].unsqueeze(1), [128, 8, 32]), ALU.mult),
                 reads=[t_tb_, t_tab], writes=[t_ub_])
            P.op(eng_u, TT(u3[:, :, 32:64], t3[:, :, 0:32], bc(TS_[:, tt, 32:64].unsqueeze(1), [128, 8, 32]), ALU.mult),
                 reads=[t_tb_, t_tab], writes=[t_ub_])
            P.op("dve", TT(t3, t3, bc(TC[:, tt, :].unsqueeze(1), [128, 8, 64]), ALU.mult), reads=[t_tb_, t_tab], writes=[t_tb_])
            P.op("dve", TT(dst, tb_, ub, ALU.add), reads=[t_tb_, t_ub_], writes=[t_dst])
        else:
            P.op("dve", TT(dst.rearrange("p (g d) -> p g d", d=64), t3, bc(TC.unsqueeze(1), [128, 8, 64]), ALU.mult),
                 reads=[t_tb_, t_tab], writes=[t_dst])

    def transposes(tt):
        b = tt % NB
        tok = slice(tt * 128, (tt + 1) * 128)
        for i in range(2):
            for h in range(4):
                P.op("pe", TR(c.psb[:, i * 512 + h * 128:i * 512 + (h + 1) * 128], rbuf[i][b][:, h * 128:(h + 1) * 128], c.ident),
                     reads=[t_rb[i][b], c.t_ident], writes=[c.tpsb[i]])
        for i in range(2):
            dstT, t_dstT = dsts[i]
            P.op("dve", CP(dstT[:, :, tok], c.psb[:, i * 512:(i + 1) * 512].rearrange("p (h t) -> p h t", t=128)),
                 reads=[c.tpsb[i]], writes=[t_dstT[tt]])

    for tt in range(NT + LAG):
        if tt < NT:
            tok = slice(tt * 128, (tt + 1) * 128)
            b = tt % NB
            banks = []
            for j in range(4):
                bk = nbank % 7
                nbank += 1
                banks.append(bk)
                for k in range(8):
                    P.op("pe", MM(ps[bk], c.hdnT[:, k, tok], W[:, k, j * 512:(j + 1) * 512], start=(k == 0), stop=(k == 7)),
                         reads=[c.t_hdnT[tt], tW[j]] + c.t_Wq, writes=[tps[bk]])
            for i in range(2):
                if rope:
                    P.op("act", ACT(raw[i][b], ps[banks[i]], AF.Copy), reads=[tps[banks[i]]], writes=[t_raw[i][b]])
                else:
                    P.op("dve", CP(raw[i][b], ps[banks[i]]), reads=[tps[banks[i]]], writes=[t_raw[i][b]])
            vcopy(tt, ps[banks[2]], tps[banks[2]])
            gbank[tt] = banks[3]
            for stage in (chain_a, chain_b, chain_c):
                for i in range(2):
                    stage(tt, i)
            if tt % 2 == 1 or tt == NT - 1:
                for t2 in ([tt - 1, tt] if tt % 2 == 1 else [tt]):
                    P.op("act", ACT(G[:, t2, :], ps[gbank[t2]], AF.Silu), reads=[tps[gbank[t2]]], writes=[t_G[t2]])
        if tt >= LAG:
            transposes(tt - LAG)


def phase_diff(c, l):
    P, A = c.P, c.A
    m = A.mark()
    lam_init = 0.8 - 0.6 * math.exp(-0.3 * l)
    W = c.W8
    tW = pf_qkvg(c, l, "diff")
    qT = A.alloc([128, 4, L], BF16)
    kT = A.alloc([128, 4, L], BF16)
    t_qT = [T() for _ in range(NT)]
    t_kT = [T() for _ in range(NT)]
    V = A.alloc([128, NT, 4, 129], BF16)
    t_V = [T() for _ in range(NT)]
    G = A.alloc([128, NT, 512], BF16)
    t_G = [T() for _ in range(NT)]
    t_misc = T()
    P.op("pool", MEMSET(V[:, :, :, 128:129], 1.0), writes=t_V)
    wqk = A.alloc([128, 128], F32)
    P.op("sp", DMA(wqk, c.diff_qk_norm[l:l + 1, :].partition_broadcast(128)), writes=[t_misc])
    P.op("dve", TS(wqk[:, 0:64], wqk[:, 0:64], 0.125, ALU.mult), reads=[t_misc], writes=[t_misc])
    tabs = A.alloc([128, 4, NT * 64], F32)
    t_tab = T()
    m_cs = A.mark()
    c.cs_sb = A.alloc([128, NT, 128], F32)
    c.t_cs = T()
    P.op("sp", DMA(c.cs_sb, c.cs_tab.rearrange("(t p) c -> p t c", p=128)), writes=[c.t_cs])
    for i, w in enumerate((wqk[:, 0:64], wqk[:, 64:128])):
        TCv = tabs[:, 2 * i, :].rearrange("p (t d) -> p t d", d=64)
        TSv = tabs[:, 2 * i + 1, :].rearrange("p (t d) -> p t d", d=64)
        P.op("dve", TT(TCv, c.cs_sb[:, :, 0:64], bc(w.unsqueeze(1), [128, NT, 64]), ALU.mult), reads=[c.t_cs, t_misc], writes=[t_tab])
        P.op("dve", TT(TSv[:, :, 0:32], c.cs_sb[:, :, 64:96], bc(w[:, 32:64].unsqueeze(1), [128, NT, 32]), ALU.mult),
             reads=[c.t_cs, t_misc], writes=[t_tab])
        P.op("dve", TT(TSv[:, :, 32:64], c.cs_sb[:, :, 96:128], bc(w[:, 0:32].unsqueeze(1), [128, NT, 32]), ALU.mult),
             reads=[c.t_cs, t_misc], writes=[t_tab])
    P.barrier()
    A.release(m_cs)
    TCq = tabs[:, 0, :].rearrange("p (t d) -> p t d", d=64)
    TSq = tabs[:, 1, :].rearrange("p (t d) -> p t d", d=64)
    TCk = tabs[:, 2, :].rearrange("p (t d) -> p t d", d=64)
    TSk = tabs[:, 3, :].rearrange("p (t d) -> p t d", d=64)
    lamb = A.alloc([128, 256], F32)
    t_lam = T()
    P.op("sp", DMA(lamb, c.diff_lambda[l:l + 1, :].partition_broadcast(128)), writes=[t_lam])
    lsc = A.alloc([128, 4], F32)
    lam3 = lamb.rearrange("p (a b d) -> p a b d", a=2, b=2)
    prod = A.alloc([128, 2, 64], F32)
    P.op("dve", TT(prod, lam3[:, :, 0, :], lam3[:, :, 1, :], ALU.mult), reads=[t_lam], writes=[t_lam])
    P.op("dve", RED(lsc[:, 0:2], prod), reads=[t_lam], writes=[t_lam])
    P.op("act", ACT(lsc[:, 0:2], lsc[:, 0:2], AF.Exp), reads=[t_lam], writes=[t_lam])
    P.op("dve", TT(lsc[:, 2:3], lsc[:, 1:2], lsc[:, 0:1], ALU.subtract), reads=[t_lam], writes=[t_lam])
    P.op("dve", TS(lsc[:, 3:4], lsc[:, 2:3], -lam_init, ALU.add), reads=[t_lam], writes=[t_lam])
    neglam = lsc[:, 3:4]
    swb = A.alloc([128, 128], F32)
    t_swb = T()
    P.op("sp", DMA(swb, c.diff_subln[l:l + 1, :].partition_broadcast(128)), writes=[t_swb])
    P.op("dve", TS(swb, swb, 1.0 - lam_init, ALU.mult), reads=[t_swb], writes=[t_swb])

    ps, tps = c.ps, c.tps
    m_prep = A.mark()

    def vcopy(tt, bank, t_bank):
        P.op("act", ACT(V[:, tt, :, 0:128], bank.rearrange("p (h d) -> p h d", d=128), AF.Copy), reads=[t_bank], writes=[t_V[tt]])
    prep_qkvg(c, W, tW, qT, t_qT, kT, t_kT, vcopy, t_V, G, t_G, (TCq, TSq, TCk, TSk), t_tab, True)
    P.barrier()
    A.release(m_prep)
    if "na" in c.phases:
        pf_qkvg(c, l, "na")
    elif "ssd" in c.phases:
        pf_ssd(c, l)
    c.dump("qT%d" % l, qT, [128, 4, L], BF16, t_qT)
    c.dump("kT%d" % l, kT, [128, 4, L], BF16, t_kT)
    c.dump("V%d" % l, V, [128, NT, 4, 129], BF16, t_V)
    c.dump("G%d" % l, G, [128, NT, 512], BF16, t_G)
    c.dump("hdnT%d" % l, c.hdnT, [128, 8, L], BF16, c.t_hdnT)
    NPT = 4
    Pt = [A.alloc([128, 1024], BF16) for _ in range(NPT)]
    t_Pt = [T() for _ in range(NPT)]
    t_Pth = [[T(), T()] for _ in range(NPT)]
    SP = [c.psall[:, 0:1024], c.psall[:, 1024:2048]]
    t_SP = [[tps[0], tps[1]], [tps[2], tps[3]]]
    Ob3 = [ps[4], ps[5], ps[6]]
    t_O3 = [tps[4], tps[5], tps[6]]

    def acc(a):
        return a // 3, (a % 3) * 129
    Osb = A.alloc([128, 3, 512], F32)
    t_Osb = T()
    rr = [A.alloc([128, 16], F32) for _ in range(2)]
    t_rr = [T(), T()]
    ob4 = [A.alloc([128, 4, 128], F32) for _ in range(2)]
    t_ob4 = [T(), T()]
    junk = A.alloc([128, 128], F32)
    t_junk = T()
    yb4 = [A.alloc([128, 512], BF16) for _ in range(2)]
    t_yb4 = [T(), T()]
    yst = [A.alloc([128, 512], BF16) for _ in range(2)]
    t_yst = [T(), T()]
    iters = [(h, qb, kt) for h in range(4) for qb in range(4) for kt in range(NT)]
    NI = len(iters)
    deferred = []

    def emit_S(i):
        h, qb, kt = iters[i]
        for cc in range(2):
            pr = slice(cc * 64, (cc + 1) * 64)
            P.op("pe", MM(SP[i % 2][:, cc * 512:(cc + 1) * 512], kT[pr, h, kt * 128:(kt + 1) * 128], qT[pr, h, qb * 512:(qb + 1) * 512]),
                 reads=[t_kT[kt]] + [t_qT[qb * 4 + j] for j in range(4)], writes=[t_SP[i % 2][cc]])

    def Oslice(a):
        bk, off = acc(a)
        return Osb[:, bk, off:off + 129]

    def fin_stage1(blk, h, qb):
        b = blk % 2
        r_, t_r = rr[b], t_rr[b]
        ob, t_ob = ob4[b], t_ob4[b]
        for j in range(4):
            O0, O1 = Oslice(j), Oslice(4 + j)
            P.op("dve", RECIP(r_[:, 4 * j:4 * j + 1], O0[:, 128:129]), reads=[t_Osb], writes=[t_r])
            P.op("dve", RECIP(r_[:, 4 * j + 1:4 * j + 2], O1[:, 128:129]), reads=[t_Osb], writes=[t_r])
            P.op("dve", TT(r_[:, 4 * j + 2:4 * j + 3], r_[:, 4 * j + 1:4 * j + 2], neglam, ALU.mult), reads=[t_r, t_lam], writes=[t_r])
            P.op("dve", TS(ob[:, j, :], O0[:, 0:128], r_[:, 4 * j:4 * j + 1], ALU.mult), reads=[t_Osb, t_r], writes=[t_ob])
            P.op("dve", STT(ob[:, j, :], O1[:, 0:128], r_[:, 4 * j + 2:4 * j + 3], ob[:, j, :], ALU.mult, ALU.add),
                 reads=[t_Osb, t_r, t_ob], writes=[t_ob])

    def fin_stage2(blk, h, qb):
        b = blk % 2
        r_, t_r = rr[b], t_rr[b]
        ob, t_ob = ob4[b], t_ob4[b]
        for j in range(4):
            P.op("act", ACT(junk, ob[:, j, :], AF.Square, accum_out=r_[:, 4 * j + 3:4 * j + 4]), reads=[t_ob], writes=[t_junk, t_r])
        r3 = r_.rearrange("p (j k) -> p j k", k=4)[:, :, 3]
        P.op("act", ACT(r3, r3, AF.Ln, scale=1.0 / 128, bias=c.epsc), reads=[t_r, c.t_eps], writes=[t_r])
        P.op("act", ACT(r3, r3, AF.Exp, scale=-0.5), reads=[t_r], writes=[t_r])

    def fin_stage3(blk, h, qb):
        b = blk % 2
        r_, t_r = rr[b], t_rr[b]
        ob, t_ob = ob4[b], t_ob4[b]
        yb, t_yb = yb4[b], t_yb4[b]
        for j in range(4):
            tt = qb * 4 + j
            P.op("dve", STT(ob[:, j, :], ob[:, j, :], r_[:, 4 * j + 3:4 * j + 4], swb, ALU.mult, ALU.mult), reads=[t_ob, t_r, t_swb], writes=[t_ob])
            P.op("dve", TT(yb[:, j * 128:(j + 1) * 128], ob[:, j, :], G[:, tt, h * 128:(h + 1) * 128], ALU.mult),
                 reads=[t_ob, t_G[tt]], writes=[t_yb])
        for j in range(4):
            P.op("pe", TR(c.psb[:, j * 128:(j + 1) * 128], yb[:, j * 128:(j + 1) * 128], c.ident),
                 reads=[t_yb, c.t_ident], writes=[c.tpsb[0]])

    def fin_stage4(blk, h, qb):
        b = blk % 2
        ys, t_ys = yst[b], t_yst[b]
        P.op("dve", CP(ys, c.psb[:, 0:512]), reads=[c.tpsb[0]], writes=[t_ys])
        P.op("sp", DMA(c.yT[l][1024 + h * 128:1024 + (h + 1) * 128, qb * 512:(qb + 1) * 512], ys), reads=[t_ys])

    emit_S(0)
    blk = 0
    for i in range(NI):
        h, qb, kt = iters[i]
        if i + 1 < NI:
            emit_S(i + 1)
        pt = Pt[i % NPT]
        t_pth = t_Pth[i % NPT]
        for cc in range(2):
            P.op("act", ACT(pt[:, cc * 512:(cc + 1) * 512], SP[i % 2][:, cc * 512:(cc + 1) * 512], AF.Exp),
                 reads=[t_SP[i % 2][cc]], writes=[t_pth[cc]])
        for cc in range(2):
            for j in range(4):
                bk, off = acc(cc * 4 + j)
                P.op("pe", MM(Ob3[bk][:, off:off + 129], pt[:, cc * 512 + j * 128:cc * 512 + (j + 1) * 128],
                              V[:, kt, h, :], start=(kt == 0 and off == 0), stop=(kt == NT - 1), skip=True),
                     reads=[t_pth[cc], t_V[kt]], writes=[t_O3[bk]])
        for (due, fn) in [d for d in deferred if d[0] <= i]:
            fn()
        deferred = [d for d in deferred if d[0] > i]
        if kt == NT - 1:
            for k3 in range(3):
                P.op("dve", CP(Osb[:, k3, 0:387], Ob3[k3][:, 0:387]), reads=[t_O3[k3]], writes=[t_Osb])
            fin_stage1(blk, h, qb)
            deferred.append((i + 2, (lambda b_=blk, h_=h, q_=qb: fin_stage2(b_, h_, q_))))
            deferred.append((i + 4, (lambda b_=blk, h_=h, q_=qb: fin_stage3(b_, h_, q_))))
            deferred.append((i + 6, (lambda b_=blk, h_=h, q_=qb: fin_stage4(b_, h_, q_))))
            blk += 1
    for (due, fn) in deferred:
        fn()
    A.release(m)


def _na_classes():
    rows, W, kh, kw = 32, 64, 8, 16
    rs = lambda r: min(max(r - kh // 2, 0), rows - kh)
    types = {}
    keys = []
    plan = []
    for i in range(16):
        qrows = [2 * i, 2 * i + 1]
        lo = min(rs(r) for r in qrows)
        hi = max(rs(r) + kh - 1 for r in qrows)
        tkeys = []
        kbs = list(range(lo // 2, hi // 2 + 1))
        for kb in kbs:
            key = []
            for b in range(2):
                for a in range(2):
                    kr, qr = 2 * kb + b, 2 * i + a
                    ok = rs(qr) <= kr < rs(qr) + kh
                    key.append((kr - qr + 7) if ok else -1)
            tkeys.append(tuple(key))
        tkeys = tuple(tkeys)
        if tkeys not in types:
            types[tkeys] = len(keys)
            keys.extend(tkeys)
        base = types[tkeys]
        plan.append([(kb, base + bi) for bi, kb in enumerate(kbs)])
    return keys, plan


NA_KEYS, NA_PLAN = _na_classes()
NA_NCLS = len(NA_KEYS)


def _na_tables(rpb):
    W, kw = 64, 16
    cidx = np.arange(W)
    col_start = np.clip(cidx - kw // 2, 0, W - kw)
    col_ok = (cidx[None, :] >= col_start[:, None]) & (cidx[None, :] < col_start[:, None] + kw)
    dc = np.clip(cidx[None, :] - cidx[:, None] + (kw - 1), 0, 2 * kw - 2)
    out = np.full((2, 8, 128, NA_NCLS, 128), NEG, dtype=np.float32)
    for ci, key in enumerate(NA_KEYS):
        n = 0
        for b in range(2):
            for a in range(2):
                dr = key[n]
                n += 1
                if dr < 0:
                    continue
                blkv = rpb[:, :, dr, :][:, :, dc]
                blkv = np.where(col_ok[None, None], blkv, np.float32(NEG))
                out[:, :, b * 64:(b + 1) * 64, ci, a * 64:(a + 1) * 64] = np.transpose(blkv, (0, 1, 3, 2))
    return np.ascontiguousarray(out.reshape(2, 8, 128, NA_NCLS * 128))


def phase_na(c, l):
    P, A = c.P, c.A
    m = A.mark()
    W = c.W8
    tW = pf_qkvg(c, l, "na")
    qT = A.alloc([128, 4, L], BF16)
    kT = A.alloc([128, 4, L], BF16)
    t_qT = [T() for _ in range(NT)]
    t_kT = [T() for _ in range(NT)]
    V = A.alloc([128, NT, 8, 65], BF16)
    t_V = [T() for _ in range(NT)]
    G = A.alloc([128, NT, 512], BF16)
    t_G = [T() for _ in range(NT)]
    Y = A.alloc([128, NT, 512], BF16)
    t_Y = [T() for _ in range(NT)]
    P.op("pool", MEMSET(V[:, :, :, 64:65], 1.0), writes=t_V)
    t_misc = T()
    wqk = A.alloc([128, 128], F32)
    P.op("sp", DMA(wqk, c.na_qk_norm[l:l + 1, :].partition_broadcast(128)), writes=[t_misc])
    P.op("dve", TS(wqk[:, 0:64], wqk[:, 0:64], 0.125, ALU.mult), reads=[t_misc], writes=[t_misc])
    ps, tps = c.ps, c.tps
    m_prep = A.mark()

    def vcopy(tt, bank, t_bank):
        P.op("act", ACT(V[:, tt, :, 0:64], bank.rearrange("p (h d) -> p h d", d=64), AF.Copy), reads=[t_bank], writes=[t_V[tt]])
    prep_qkvg(c, W, tW, qT, t_qT, kT, t_kT, vcopy, t_V, G, t_G, (wqk[:, 0:64], None, wqk[:, 64:128], None), t_misc, False)
    P.barrier()
    A.release(m_prep)
    if "ssd" in c.phases:
        pf_ssd(c, l)
    pf_out(c, l, [0, 1])
    tab = [A.alloc([128, NA_NCLS * 128], F32) for _ in range(2)]
    t_tab = [T(), T()]
    Tb = [A.alloc([128, 640], F32) for _ in range(2)]
    t_Tb = [T(), T()]
    Pb = [A.alloc([128, 640], BF16) for _ in range(3)]
    t_Pb = [T(), T(), T()]
    rr = [A.alloc([128, 2], F32) for _ in range(2)]
    t_rr = [T(), T()]
    its = [(hh, i) for hh in range(8) for i in range(NT)]
    NI = len(its)
    loaded = set()

    def load_tab(hh):
        if hh in loaded or hh >= 8:
            return
        loaded.add(hh)
        P.op("sp", DMA(tab[hh % 2], c.na_tab[l, hh, :, :]), writes=[t_tab[hh % 2]])
        P.op("act", ACT(tab[hh % 2], tab[hh % 2], AF.Exp), reads=[], writes=[t_tab[hh % 2]])

    def emit_S(n):
        hh, i = its[n]
        jb, e = hh // 2, hh % 2
        pr = slice(e * 64, (e + 1) * 64)
        S0, S1 = ps[(n % 2) * 2], ps[(n % 2) * 2 + 1]
        tS = [tps[(n % 2) * 2], tps[(n % 2) * 2 + 1]]
        for bi, (kb, cls) in enumerate(NA_PLAN[i]):
            dstS = (S0 if bi < 4 else S1)[:, (bi % 4) * 128:(bi % 4 + 1) * 128]
            P.op("pe", MM(dstS, kT[pr, jb, kb * 128:(kb + 1) * 128], qT[pr, jb, i * 128:(i + 1) * 128]),
                 reads=[t_kT[kb], t_qT[i]], writes=[tS[bi // 4]])

    def emit_exp_mul(n):
        hh, i = its[n]
        plan = NA_PLAN[i]
        nb = len(plan)
        base = plan[0][1]
        tS = [tps[(n % 2) * 2], tps[(n % 2) * 2 + 1]]
        Sboth = c.psall[:, (n % 2) * 1024:(n % 2) * 1024 + nb * 128]
        T_, t_T = Tb[n % 2], t_Tb[n % 2]
        P_, t_P = Pb[n % 3], t_Pb[n % 3]
        P.op("act", ACT(T_[:, 0:nb * 128], Sboth, AF.Exp), reads=(tS if nb > 4 else tS[0:1]), writes=[t_T])
        P.op("dve", TT(P_[:, 0:nb * 128], T_[:, 0:nb * 128], tab[hh % 2][:, base * 128:(base + nb) * 128], ALU.mult),
             reads=[t_T, t_tab[hh % 2]], writes=[t_P])

    yst = [A.alloc([128, 512], BF16) for _ in range(2)]
    t_yst = [T(), T()]
    ycnt = [0]

    def emit_ytrans(j4):
        for qb in range(4):
            for j in range(4):
                tt = qb * 4 + j
                P.op("pe", TR(c.psb[:, j * 128:(j + 1) * 128], Y[:, tt, j4 * 128:(j4 + 1) * 128], c.ident),
                     reads=[t_Y[tt], c.t_ident], writes=[c.tpsb[0]])
            ys, t_ys = yst[ycnt[0] % 2], t_yst[ycnt[0] % 2]
            ycnt[0] += 1
            P.op("dve", CP(ys, c.psb[:, 0:512]), reads=[c.tpsb[0]], writes=[t_ys])
            P.op("sp", DMA(c.yT[l][1536 + j4 * 128:1536 + (j4 + 1) * 128, qb * 512:(qb + 1) * 512], ys), reads=[t_ys])
    pending_tr = []
    load_tab(0)
    emit_S(0)
    if NI > 1:
        emit_S(1)
    emit_exp_mul(0)
    for n, (hh, i) in enumerate(its):
        if i == 2:
            load_tab(hh + 1)
        plan = NA_PLAN[i]
        nb = len(plan)
        Ob, t_Ob = ps[4 + n % 2], tps[4 + n % 2]
        P_, t_P = Pb[n % 3], t_Pb[n % 3]
        for bi, (kb, cls) in enumerate(plan):
            P.op("pe", MM(Ob[:, 0:65], P_[:, bi * 128:(bi + 1) * 128], V[:, kb, hh, :], start=(bi == 0), stop=(bi == nb - 1)),
                 reads=[t_P, t_V[kb]], writes=[t_Ob])
        if n + 2 < NI:
            emit_S(n + 2)
        if n + 1 < NI:
            emit_exp_mul(n + 1)
        r_, t_r = rr[n % 2], t_rr[n % 2]
        P.op("dve", RECIP(r_[:, 0:1], Ob[:, 64:65]), reads=[t_Ob], writes=[t_r])
        P.op("dve", STT(Y[:, i, hh * 64:(hh + 1) * 64], Ob[:, 0:64], r_[:, 0:1], G[:, i, hh * 64:(hh + 1) * 64], ALU.mult, ALU.mult),
             reads=[t_Ob, t_r, t_G[i]], writes=[t_Y[i]])
        if hh % 2 == 1 and i == NT - 1:
            pending_tr.append((n + 3, hh // 2))
        for (due, j4_) in [p_ for p_ in pending_tr if p_[0] <= n]:
            emit_ytrans(j4_)
        pending_tr = [p_ for p_ in pending_tr if p_[0] > n]
    for (due, j4_) in pending_tr:
        emit_ytrans(j4_)
    A.release(m)


def phase_ssd(c, l):
    STOP = 9
    P, A = c.P, c.A
    m = A.mark()
    ps, tps = c.ps, c.tps
    nc = c.nc
    Wdt = A.alloc([128, 8, 32], BF16)
    tWdt = [T()]
    load_w(c, l, Wdt, C_DT, 32, tWdt)
    t_sm = T()
    dtb = A.alloc([128, 32], F32)
    alog = A.alloc([128, 32], F32)
    dsk = A.alloc([128, 32], F32)
    P.op("sp", DMA(dtb, c.dt_bias[l:l + 1, :].partition_broadcast(128)), writes=[t_sm])
    t_al = T()
    P.op("sp", DMA(alog, c.a_log[l:l + 1, :].partition_broadcast(128)), writes=[t_al])
    t_dsk = T()
    P.op("sp", DMA(dsk, c.d_skip[l:l + 1, :].partition_broadcast(128)), writes=[t_dsk])
    dsum = A.alloc([128, 16], F32)
    P.op("dve", TT(dsum, dsk[:, 0:16], dsk[:, 16:32], ALU.add), reads=[t_dsk], writes=[t_dsk])
    P.op("act", ACT(alog, alog, AF.Exp), reads=[t_al], writes=[t_al])
    P.op("dve", TS(alog, alog, -1.0, ALU.mult), reads=[t_al], writes=[t_al])
    snw = A.alloc([128, 1024], F32)
    t_snw = T()
    P.op("sp", DMA(snw, c.ssd_norm_w[l:l + 1, :].partition_broadcast(128)), writes=[t_snw])
    cbb = A.alloc([128, 1536], F32)
    t_cbb = T()
    P.op("sp", DMA(cbb, c.conv_b[l:l + 1, :].partition_broadcast(128)), writes=[t_cbb])
    cwT = A.alloc([128, 12, 6], F32)
    t_cwT = T()

    def a3(shape=(128, NT, 32)):
        return A.alloc(list(shape), F32)
    dt = a3()
    dta = a3()
    cs = a3()
    dcb = a3()
    dout = a3()
    dend = a3()
    cXd = a3()
    t_dt, t_dta, t_cs, t_dcb, t_dout, t_dend, t_cXd, t_v, t_tmp3 = [T() for _ in range(9)]
    t_csd = T()
    csT_v = c.csT_d.rearrange("t (d h) l -> t d h l", d=2)
    m_tmp = A.mark()
    cw6 = A.alloc([128, 1536], F32)
    t_cw6 = T()
    P.op("sp", DMA(cw6[0:5, :], c.conv_w[l, :, :]), writes=[t_cw6])
    P.op("sp", DMA(cw6[5:6, :], c.conv_b[l:l + 1, :]), writes=[t_cw6])
    for j in range(12):
        P.op("pe", TR(ps[6][:, j * 6:(j + 1) * 6], cw6[0:6, j * 128:(j + 1) * 128], c.identf[0:6, 0:6]),
             reads=[t_cw6, c.t_cst], writes=[tps[6]])
    P.op("dve", CP(cwT, ps[6][:, 0:72].rearrange("p (j k) -> p j k", k=6)), reads=[tps[6]], writes=[t_cwT])
    P.barrier()
    A.release(m_tmp)

    def make_decay_stages(v, tmp3, csT):
        t_csT = T()

        def s1():
            for tt in range(NT):
                for k in range(8):
                    P.op("pe", MM(ps[4][:, tt * 32:(tt + 1) * 32], c.hdnT[:, k, tt * 128:(tt + 1) * 128], Wdt[:, k, :],
                                  start=(k == 0), stop=(k == 7), skip=True),
                         reads=[c.t_hdnT[tt], tWdt[0]], writes=[tps[4]])
            p0 = ps[4].rearrange("p (t h) -> p t h", h=32)
            P.op("dve", TT(v, p0, bc(dtb.unsqueeze(1), [128, NT, 32]), ALU.add), reads=[tps[4], t_sm], writes=[t_v])
            P.op("act", ACT(tmp3, v, AF.Abs), reads=[t_v], writes=[t_tmp3])
            P.op("act", ACT(tmp3, tmp3, AF.Exp, scale=-1.0), reads=[t_tmp3], writes=[t_tmp3])
            P.op("act", ACT(tmp3, tmp3, AF.Ln, bias=c.onec, scale=1.0), reads=[t_tmp3, c.t_eps], writes=[t_tmp3])

        def s2():
            P.op("dve", STT(dt, v, 0.0, tmp3, ALU.max, ALU.add), reads=[t_v, t_tmp3], writes=[t_dt])
            P.op("dve", TT(dta, dt, bc(alog.unsqueeze(1), [128, NT, 32]), ALU.mult), reads=[t_dt, t_al], writes=[t_dta])
            for tt in range(NT):
                P.op("pe", MM(ps[5][:, tt * 32:tt * 32 + 16], c.U, dta[:, tt, 0:16], skip=True), reads=[t_dta, c.t_cst], writes=[tps[5]])
                P.op("pe", MM(ps[5][:, tt * 32 + 16:tt * 32 + 32], c.Ur, dta[:, tt, 16:32], skip=True), reads=[t_dta, c.t_cst], writes=[tps[5]])
            for tt in range(NT):
                P.op("pe", MM(ps[6][:, tt * 32:(tt + 1) * 32], c.onesf, dta[:, tt, :], skip=True), reads=[t_dta, c.t_cst], writes=[tps[6]])
            P.op("act", ACT(cs, ps[5].rearrange("p (t h) -> p t h", h=32), AF.Copy), reads=[tps[5]], writes=[t_cs])
            P.op("act", ACT(tmp3, ps[6].rearrange("p (t h) -> p t h", h=32), AF.Copy), reads=[tps[6]], writes=[t_tmp3])

        def s3():
            P.op("act", ACT(dcb, tmp3, AF.Exp), reads=[t_tmp3], writes=[t_dcb])
            P.op("act", ACT(dout, cs, AF.Exp), reads=[t_cs], writes=[t_dout])
            P.op("dve", TT(dend, tmp3, cs, ALU.subtract), reads=[t_tmp3, t_cs], writes=[t_dend])
            P.op("act", ACT(dend, dend, AF.Exp), reads=[t_dend], writes=[t_dend])
            P.op("dve", TT(cXd, dt, dend, ALU.mult), reads=[t_dt, t_dend], writes=[t_cXd])

        def s4():
            for q in range(4):
                for j in range(4):
                    tt = q * 4 + j
                    P.op("pe", TR(ps[4][0:32, j * 128:(j + 1) * 128], cs[:, tt, :], c.identf), reads=[t_cs, c.t_cst], writes=[tps[4]])
                P.op("act", ACT(csT[0:32, q * 4:(q + 1) * 4, :], ps[4][0:32, :].rearrange("p (j t) -> p j t", t=128), AF.Copy),
                     reads=[tps[4]], writes=[t_csT])
            P.op("sp", DMA(c.csT_d.rearrange("t h l -> h t l"), csT[0:32, :, :]), reads=[t_csT], writes=[t_csd])
        return [s1, s2, s3, s4]

    for g in range(2):
        mg = A.mark()
        Wz = c.Wz[g]
        tWz = pf_ssd(c, l)[g]
        t_Wzq = c.t_Wq[3 - g]
        xs = A.alloc([128, NT, 512], F32)
        t_xs = [T() for _ in range(NT)]
        Btok = A.alloc([128, NT, 128], BF16)
        t_Btok = [T() for _ in range(NT)]
        BT = A.alloc([128, L], BF16)
        t_BT = [T() for _ in range(4)]
        CT = A.alloc([128, L], BF16)
        t_CT = [T() for _ in range(4)]
        m_conv = A.mark()
        stages = []
        if g == 0:
            stages = make_decay_stages(A.alloc([128, NT, 32], F32), A.alloc([128, NT, 32], F32), A.alloc([128, NT, 128], F32))
        pre = [A.alloc([128, L + 4], BF16) for _ in range(2)]
        t_pre = [[T() for _ in range(4)] for _ in range(2)]
        t_halo = [T(), T()]
        for b in range(2):
            P.op("pool", MEMSET(pre[b][:, 0:2], 0.0), writes=[t_halo[b]])
            P.op("pool", MEMSET(pre[b][:, L + 2:L + 4], 0.0), writes=[t_halo[b]])
        Wc = [A.alloc([128, 8, 128], BF16) for _ in range(2)]
        tWc = [[T()], [T()]]
        dg = [A.alloc([128, 5, 128], BF16) for _ in range(2)]
        t_dg = [T(), T()]
        ctmp = [A.alloc([128, 512], F32) for _ in range(2)]
        t_ctmp = [T(), T()]
        chunks = [("B", 8 + g), ("C", 10 + g)] + [("x%d" % j, 4 * g + j) for j in range(4)]
        pbank = 0
        for n, (kind, ci) in enumerate(chunks):
            b = n % 2
            if n >= 1 and stages:
                stages.pop(0)()
            load_w(c, l, Wc[b], C_XBC + ci * 128, 128, tWc[b])
            P.op("dve", TT(dg[b], bc(c.identf.unsqueeze(1), [128, 5, 128]), bc(cwT[:, ci, 0:5].unsqueeze(2), [128, 5, 128]), ALU.mult),
                 reads=[c.t_cst, t_cwT], writes=[t_dg[b]])
            for tb in range(4):
                bank, tbk = ps[pbank % 4], tps[pbank % 4]
                pbank += 1
                for k in range(8):
                    P.op("pe", MM(bank, Wc[b][:, k, :], c.hdnT[:, k, tb * 512:(tb + 1) * 512], start=(k == 0), stop=(k == 7)),
                         reads=[tWc[b][0]] + c.t_hdnT[tb * 4:(tb + 1) * 4], writes=[tbk])
                P.op("act", ACT(pre[b][:, 2 + tb * 512:2 + (tb + 1) * 512], bank, AF.Copy), reads=[tbk], writes=[t_pre[b][tb]])
            allpre = t_pre[b] + [t_halo[b]]
            if kind in ("B", "C"):
                dstT, t_dstT = (BT, t_BT) if kind == "B" else (CT, t_CT)
                for tb in range(4):
                    bank, tbk = ps[pbank % 4], tps[pbank % 4]
                    pbank += 1
                    for k in range(5):
                        P.op("pe", MM(bank, dg[b][:, k, :], pre[b][:, tb * 512 + k:tb * 512 + k + 512], start=(k == 0), stop=(k == 4)),
                             reads=[t_dg[b]] + allpre, writes=[tbk])
                    P.op("act", ACT(dstT[:, tb * 512:(tb + 1) * 512], bank, AF.Silu, bias=cwT[:, ci, 5:6], scale=1.0),
                         reads=[tbk, t_cwT], writes=[t_dstT[tb]])
            if kind != "C":
                for q in range(4):
                    bank, tbk = ps[pbank % 4], tps[pbank % 4]
                    pbank += 1
                    for j in range(4):
                        tt = q * 4 + j
                        for k in range(5):
                            P.op("pe", MM(bank[:, j * 128:(j + 1) * 128], pre[b][:, tt * 128 + k:tt * 128 + k + 128], dg[b][:, k, :],
                                          start=(k == 0 and j == 0), stop=(k == 4), skip=True),
                                 reads=[t_dg[b]] + allpre, writes=[tbk])
                    ct, t_ct = ctmp[q % 2], t_ctmp[q % 2]
                    P.op("dve", TT(ct.rearrange("p (j c) -> p j c", c=128), bank.rearrange("p (j c) -> p j c", c=128),
                                   bc(cbb[:, ci * 128:(ci + 1) * 128].unsqueeze(1), [128, 4, 128]), ALU.add),
                         reads=[tbk, t_cbb], writes=[t_ct])
                    if kind == "B":
                        P.op("act", ACT(Btok[:, q * 4:(q + 1) * 4, :], ct.rearrange("p (j c) -> p j c", c=128), AF.Silu),
                             reads=[t_ct], writes=t_Btok[q * 4:(q + 1) * 4])
                    else:
                        jx = int(kind[1])
                        P.op("act", ACT(xs[:, q * 4:(q + 1) * 4, jx * 128:(jx + 1) * 128], ct.rearrange("p (j c) -> p j c", c=128), AF.Silu),
                             reads=[t_ct], writes=t_xs[q * 4:(q + 1) * 4])

        for st_ in stages:
            st_()
        stages = []
        P.barrier()
        A.release(m_conv)
        if STOP <= 2:
            A.release(mg)
            continue
        hb0 = 16 + 8 * g
        hf0 = 8 * g
        m_passA = A.mark()
        Sst = [A.alloc([128, 512], F32) for _ in range(2)]
        t_Sst = [T(), T()]
        for d in range(2):
            P.op("pool", MEMSET(Sst[d], 0.0), writes=[t_Sst[d]])
        stg = [[A.alloc([128, 512], BF16) for _ in range(2)] for _ in range(2)]
        t_stg = [[T(), T()], [T(), T()]]
        Xd = [[A.alloc([128, 512], BF16) for _ in range(2)] for _ in range(2)]
        t_Xd = [[T(), T()], [T(), T()]]
        t_sd = [[T() for _ in range(NT)] for _ in range(2)]
        sdram = [c.sf_d, c.sb_d]
        h0s = [hf0, hb0]

        def bh(ap3, h0):
            return bc(ap3[:, h0:h0 + 8].unsqueeze(2), [128, 8, 64])

        def v8(ap):
            return ap.rearrange("p (h d) -> p h d", d=64)
        for k in range(NT):
            for d in range(2):
                ci_ = k if d == 0 else NT - 1 - k
                last = (ci_ == NT - 1) if d == 0 else (ci_ == 0)
                sg, t_sg = stg[d][k % 2], t_stg[d][k % 2]
                P.op("act", ACT(sg, Sst[d], AF.Copy), reads=[t_Sst[d]], writes=[t_sg])
                P.op("sp", DMA(sdram[d][g, ci_], sg), reads=[t_sg], writes=[t_sd[d][ci_]])
                if not last:
                    xd, t_xd = Xd[d][k % 2], t_Xd[d][k % 2]
                    P.op("pool", TT(v8(xd), v8(xs[:, ci_, :]), bh(cXd[:, ci_, :], h0s[d]), ALU.mult), reads=[t_xs[ci_], t_cXd], writes=[t_xd])
                    bank, tbk = ps[4 + d], tps[4 + d]
                    P.op("pe", MM(bank, Btok[:, ci_, :], xd), reads=[t_Btok[ci_], t_xd], writes=[tbk])
                    P.op("dve", TT(v8(Sst[d]), v8(Sst[d]), bh(dcb[:, ci_, :], h0s[d]), ALU.mult), reads=[t_Sst[d], t_dcb], writes=[t_Sst[d]])
                    P.op("dve", TT(Sst[d], Sst[d], bank, ALU.add), reads=[t_Sst[d], tbk], writes=[t_Sst[d]])
        P.barrier()
        A.release(m_passA)
        if STOP <= 3:
            A.release(mg)
            continue
        R = [A.alloc([128, 2, 8, 128], F32) for _ in range(2)]
        t_R = [T(), T()]
        SFc = [A.alloc([128, 512], BF16) for _ in range(2)]
        t_SFc = [T(), T()]
        SBc = [A.alloc([128, 512], BF16) for _ in range(2)]
        t_SBc = [T(), T()]
        E = A.alloc([128, 2, 8, 128], BF16)
        t_E = T()
        Mt = [A.alloc([128, 2, 8, 128], BF16) for _ in range(2)]
        t_Mt = [T(), T()]
        Gm = [A.alloc([128, 2, 128], BF16) for _ in range(2)]
        t_Gm = [T(), T()]
        Xt = [A.alloc([128, 3, 512], BF16) for _ in range(2)]
        t_Xt = [T(), T()]
        sz = [A.alloc([128, 512], F32) for _ in range(4)]
        t_sz = [T() for _ in range(4)]
        ya = A.alloc([128, 512], F32)
        yb_ = A.alloc([128, 512], F32)
        yy = A.alloc([128, 512], F32)
        t_ya, t_yb, t_yy = T(), T(), T()
        ssn = A.alloc([128, 2], F32)
        t_ssn = T()
        yo = [A.alloc([128, 512], BF16) for _ in range(2)]
        t_yo = [T(), T()]
        yst = [A.alloc([128, 4, 128], BF16) for _ in range(2)]
        t_yst = [T(), T()]

        def loadR(ci_):
            b = ci_ % 2
            P.op("sp", DMA(R[b].rearrange("p d h l -> p d (h l)"),
                           csT_v[ci_, :, 8 * g:8 * g + 8, :].rearrange("d h l -> d (h l)").partition_broadcast(128)),
                 reads=[t_csd], writes=[t_R[b]])

        def loadS(ci_):
            b = ci_ % 2
            P.op("sp", DMA(SFc[b], c.sf_d[g, ci_]), reads=[t_sd[0][ci_]], writes=[t_SFc[b]])
            P.op("sp", DMA(SBc[b], c.sb_d[g, ci_]), reads=[t_sd[1][ci_]], writes=[t_SBc[b]])

        def iteration(cn, cc_):
            if cn is not None:
                bn = cn % 2
                tokn = slice(cn * 128, (cn + 1) * 128)
                P.op("pe", MM(ps[4][:, 0:128], BT[:, tokn], CT[:, tokn]), reads=[t_BT[cn // 4], t_CT[cn // 4]], writes=[tps[4]])
                if cn % 2 == 0:
                    for c2 in (cn, cn + 1):
                        if c2 < NT:
                            zb, t_zb = (ps[3], tps[3]) if c2 % 2 == 0 else (ps[6], tps[6])
                            tok2 = slice(c2 * 128, (c2 + 1) * 128)
                            for k in range(8):
                                P.op("pe", MM(zb, c.hdnT[:, k, tok2], Wz[:, k, :], start=(k == 0), stop=(k == 7)),
                                     reads=[c.t_hdnT[c2], tWz[0], t_Wzq], writes=[t_zb])
                P.op("dve", TT(Gm[bn], bc(ps[4][:, 0:128].unsqueeze(1), [128, 2, 128]),
                               c.cst[:, 128:384].rearrange("p (d l) -> p d l", d=2), ALU.mult),
                     reads=[tps[4], c.t_cst], writes=[t_Gm[bn]])
                csv = cs[:, cn, :].rearrange("p (d h) -> p d h", d=2)[:, :, 8 * g:8 * g + 8]
                Dd, t_D = R[bn], t_R[bn]
                P.op("dve", TT(Dd, Dd, bc(csv.unsqueeze(3), [128, 2, 8, 128]), ALU.subtract), reads=[t_cs], writes=[t_D])
                P.op("act", ACT(Dd, Dd, AF.Relu, scale=-1.0), reads=[], writes=[t_D])
                P.op("act", ACT(E, Dd, AF.Exp, scale=-1.0), reads=[t_D], writes=[t_E])
                P.op("pool", TT(v8(Xt[bn][:, 0, :]), v8(xs[:, cn, :]), bh(dt[:, cn, :], hf0), ALU.mult), reads=[t_xs[cn], t_dt], writes=[t_Xt[bn]])
                P.op("pool", TT(v8(Xt[bn][:, 1, :]), v8(xs[:, cn, :]), bh(dt[:, cn, :], hb0), ALU.mult), reads=[t_xs[cn], t_dt], writes=[t_Xt[bn]])
                P.op("pool", TT(v8(Xt[bn][:, 2, :]), v8(xs[:, cn, :]), bh(dsum, 8 * g), ALU.mult), reads=[t_xs[cn], t_dsk], writes=[t_Xt[bn]])
            if cc_ is not None:
                b = cc_ % 2
                tok = slice(cc_ * 128, (cc_ + 1) * 128)
                Yb_, t_Y = (ps[0], tps[0]) if b == 0 else (ps[5], tps[5])
                P.op("pe", MM(Yb_, c.ident, Xt[b][:, 2, :], start=True, stop=False, skip=True), reads=[c.t_ident, t_Xt[b]], writes=[t_Y])
                for h in range(8):
                    for d in range(2):
                        P.op("pe", MM(Yb_[:, h * 64:(h + 1) * 64], Mt[b][:, d, h, :], Xt[b][:, d, h * 64:(h + 1) * 64],
                                      start=False, stop=(d == 1), skip=True),
                             reads=[t_Mt[b], t_Xt[b]], writes=[t_Y])
                P.op("pe", MM(ps[1], CT[:, tok], SFc[b]), reads=[t_CT[cc_ // 4], t_SFc[b]], writes=[tps[1]])
                P.op("pe", MM(ps[2], CT[:, tok], SBc[b]), reads=[t_CT[cc_ // 4], t_SBc[b]], writes=[tps[2]])
                P.op("dve", TT(v8(ya), v8(ps[1]), bh(dout[:, cc_, :], hf0), ALU.mult), reads=[tps[1], t_dout], writes=[t_ya])
                P.op("dve", TT(v8(yb_), v8(ps[2]), bh(dout[:, cc_, :], hb0), ALU.mult), reads=[tps[2], t_dout], writes=[t_yb])
                P.op("dve", TT(ya, ya, yb_, ALU.add), reads=[t_ya, t_yb], writes=[t_ya])
                P.op("dve", TT(yy, Yb_, ya, ALU.add), reads=[t_Y, t_ya], writes=[t_yy])
                P.op("pool", TT(yy, yy, sz[cc_ % 4], ALU.mult), reads=[t_yy, t_sz[cc_ % 4]], writes=[t_yy])
            if cn is not None:
                for d in range(2):
                    P.op("dve", TT(Mt[bn][:, d], E[:, d], bc(Gm[bn][:, d, :].unsqueeze(1), [128, 8, 128]), ALU.mult),
                         reads=[t_E, t_Gm[bn]], writes=[t_Mt[bn]])
                if cn % 2 == 0:
                    for c2 in (cn, cn + 1):
                        if c2 < NT:
                            zb, t_zb = (ps[3], tps[3]) if c2 % 2 == 0 else (ps[6], tps[6])
                            P.op("act", ACT(sz[c2 % 4], zb, AF.Silu), reads=[t_zb], writes=[t_sz[c2 % 4]])
            if cc_ is not None:
                P.op("act", ACT(ya, yy, AF.Square, accum_out=ssn[:, 0:1]), reads=[t_yy], writes=[t_ya, t_ssn])
                P.op("act", ACT(ssn[:, 0:1], ssn[:, 0:1], AF.Ln, scale=1.0 / 512, bias=c.epsc), reads=[t_ssn, c.t_eps], writes=[t_ssn])
                P.op("act", ACT(ssn[:, 0:1], ssn[:, 0:1], AF.Exp, scale=-0.5), reads=[t_ssn], writes=[t_ssn])
                P.op("dve", STT(yo[b], yy, ssn[:, 0:1], snw[:, g * 512:(g + 1) * 512], ALU.mult, ALU.mult),
                     reads=[t_yy, t_ssn, t_snw], writes=[t_yo[b]])

        def stageC(ci_):
            b = ci_ % 2
            tok = slice(ci_ * 128, (ci_ + 1) * 128)
            for j in range(4):
                P.op("pe", TR(c.psb[:, j * 128:(j + 1) * 128], yo[b][:, j * 128:(j + 1) * 128], c.ident),
                     reads=[t_yo[b], c.t_ident], writes=[c.tpsb[0]])
            P.op("dve", CP(yst[b], c.psb[:, 0:512].rearrange("p (j t) -> p j t", t=128)), reads=[c.tpsb[0]], writes=[t_yst[b]])
            P.op("sp", DMA(c.yT[l][g * 512:(g + 1) * 512, tok].rearrange("(j p) t -> p j t", p=128), yst[b]), reads=[t_yst[b]])

        loadR(0)
        loadS(0)
        loadR(1)
        iteration(0, None)
        for ci_ in range(NT):
            if ci_ + 1 < NT:
                loadS(ci_ + 1)
                if ci_ + 2 < NT:
                    loadR(ci_ + 2)
            iteration(ci_ + 1 if ci_ + 1 < NT else None, ci_)
            if ci_ >= 1:
                stageC(ci_ - 1)
        stageC(NT - 1)
        A.release(mg)
    A.release(m)


def phase_out(c, l, xsrc, xdst, fuse_next=False):
    P, A = c.P, c.A
    m = A.mark()
    if fuse_next:
        nwb = A.alloc([128, DM], F32)
        t_nwb = T()
        P.op("sp", DMA(nwb, c.norm_w[l + 1:l + 2, :].partition_broadcast(128)), writes=[t_nwb])
        sqj = A.alloc([128, DM], F32)
        t_sqj = T()
        ssx = A.alloc([128, NT], F32)
        t_ssx = [T() for _ in range(NT)]
        hb = [A.alloc([128, DM], BF16) for _ in range(2)]
        t_hb = [T(), T()]
    Wo = c.Wo
    tWo = pf_out(c, l, [0, 1, 2, 3])
    yt = [A.alloc([128, 16, 512], BF16) for _ in range(2)]
    t_yt = [T(), T()]
    xt = [A.alloc([128, DM], F32) for _ in range(3)]
    t_xt = [T(), T(), T()]
    ot = [A.alloc([128, DM], F32) for _ in range(2)]
    t_ot = [T(), T()]
    ps, tps = c.ps, c.tps
    fin = []

    def load_y(qb):
        P.op("sp", DMA(yt[qb % 2], c.yT[l][:, qb * 512:(qb + 1) * 512].rearrange("(k p) t -> p k t", p=128)), writes=[t_yt[qb % 2]])

    def load_x(tt):
        P.op("sp", DMA(xt[tt % 3], xsrc[tt * 128:(tt + 1) * 128, :]), writes=[t_xt[tt % 3]])
    load_y(0)
    load_x(0)
    load_x(1)
    nb = 0
    for tt in range(NT):
        b = tt % 2
        qb, j = tt // 4, tt % 4
        tok = slice(tt * 128, (tt + 1) * 128)
        if j == 0 and qb + 1 < 4:
            load_y(qb + 1)
        if tt + 2 < NT:
            load_x(tt + 2)
        for n in range(2):
            bank, tb = ps[nb % 6], tps[nb % 6]
            nb += 1
            for k in range(16):
                P.op("pe", MM(bank, yt[qb % 2][:, k, j * 128:(j + 1) * 128], Wo[:, k, n * 512:(n + 1) * 512], start=(k == 0), stop=(k == 15)),
                     reads=[t_yt[qb % 2], tWo[k // 4], c.t_Wq[k // 4]], writes=[tb])
            P.op("dve", TT(ot[b][:, n * 512:(n + 1) * 512], bank, xt[tt % 3][:, n * 512:(n + 1) * 512], ALU.add),
                 reads=[tb, t_xt[tt % 3]], writes=[t_ot[b]])
        fin.append(P.op("sp", DMA(xdst[tok, :], ot[b]), reads=[t_ot[b]]))
        if fuse_next:
            s1 = ssx[:, tt:tt + 1]
            P.op("act", ACT(sqj, ot[b], AF.Square, accum_out=s1), reads=[t_ot[b]], writes=[t_sqj, t_ssx[tt]])
            P.op("act", ACT(s1, s1, AF.Ln, scale=1.0 / DM, bias=c.epsc), reads=[t_ssx[tt], c.t_eps], writes=[t_ssx[tt]])
            P.op("act", ACT(s1, s1, AF.Exp, scale=-0.5), reads=[t_ssx[tt]], writes=[t_ssx[tt]])
            P.op("dve", STT(hb[b], ot[b], s1, nwb, ALU.mult, ALU.mult), reads=[t_ot[b], t_ssx[tt], t_nwb], writes=[t_hb[b]])
            for k in range(8):
                P.op("pe", TR(c.psb[:, k * 128:(k + 1) * 128], hb[b][:, k * 128:(k + 1) * 128], c.ident),
                     reads=[t_hb[b], c.t_ident], writes=[c.tpsb[0]])
            P.op("act", ACT(c.hdnT[:, :, tok], c.psb[:, :].rearrange("p (k t) -> p k t", t=128), AF.Copy),
                 reads=[c.tpsb[0]], writes=[c.t_hdnT[tt]])
    A.release(m)
    return fin


def _host_consts():
    inv_freq = (10000.0 ** (-(np.arange(0, 64, 2, dtype=np.float32)) / np.float32(64))).astype(np.float32)
    ang = (np.arange(L, dtype=np.float32)[:, None] * inv_freq[None, :]).astype(np.float32)
    cos, sin = np.cos(ang).astype(np.float32), np.sin(ang).astype(np.float32)
    cs = np.concatenate([cos, cos, -sin, sin], axis=1).astype(np.float32)
    k = np.arange(128)
    ident = np.eye(128, dtype=np.float32)
    U = (k[:, None] <= k[None, :]).astype(np.float32)
    Ur = (k[:, None] >= k[None, :]).astype(np.float32)
    ones = np.ones((128, 128), np.float32)
    consts = np.concatenate([ident, U, Ur, ones], axis=1)
    return np.ascontiguousarray(cs), np.ascontiguousarray(consts)


_CACHE = {}


def make_in_maps(inputs, n_cores=8):
    cs, consts = _host_consts()
    f = lambda a: np.ascontiguousarray(np.asarray(a, dtype=np.float32))
    shared = {
        "norm_w": f(inputs["norm_w"]), "w_in": f(inputs["w_in"]), "conv_w": f(inputs["conv_w"]),
        "conv_b": f(inputs["conv_b"]), "a_log": f(inputs["a_log"]).reshape(2, 32),
        "dt_bias": f(inputs["dt_bias"]).reshape(2, 32), "d_skip": f(inputs["d_skip"]).reshape(2, 32),
        "ssd_norm_w": f(inputs["ssd_norm_w"]), "diff_qk_norm": f(inputs["diff_qk_norm"]).reshape(2, 128),
        "diff_lambda": f(inputs["diff_lambda"]).reshape(2, 256), "diff_subln": f(inputs["diff_subln"]),
        "na_qk_norm": f(inputs["na_qk_norm"]).reshape(2, 128), "na_tab": _na_tables(f(inputs["na_rpb"])),
        "w_out": f(inputs["w_out"]), "cs_tab": cs, "consts": consts,
    }
    x = f(inputs["x"])
    return [dict(shared, x=x[b]) for b in range(n_cores)]


def kernel(**inputs):
    if "nc" not in _CACHE:
        _CACHE["nc"] = build()[0]
    nc = _CACHE["nc"]
    in_maps = make_in_maps(inputs)
    res = run_bass_kernel_spmd(nc, in_maps, core_ids=list(range(8)))
    return np.stack([np.asarray(r["out"], dtype=np.float32) for r in res.results], axis=0)
```

```python
import contextlib
import math
import numpy as np
import concourse.bass as bass
import concourse.mybir as mybir
from concourse.bass_utils import run_bass_kernel_spmd

F32 = mybir.dt.float32
BF16 = mybir.dt.bfloat16
AF = mybir.ActivationFunctionType
ALU = mybir.AluOpType
AX = mybir.AxisListType

L = 2048
DM = 1024
NT = 16
INW = 6688
EPS = 1e-6
NEG = -30000.0
C_Z, C_XBC, C_DT, C_DIFF, C_NA = 0, 1024, 2560, 2592, 4640


class T:
    __slots__ = ("w", "r", "excl")

    def __init__(self, excl=False):
        self.w = None
        self.r = []
        self.excl = excl


class Op:
    __slots__ = ("eng", "fn", "deps", "idx", "sig", "isdma", "semid", "semval", "prev_same_sem")


class Prog:
    COMPUTE = ("pe", "act", "dve", "pool")
    DMAQ = ("sp", "actq", "poolq")
    STREAM = {"pe": "pe", "act": "act", "dve": "dve", "pool": "pool", "sp": "sp", "actq": "act", "poolq": "pool"}
    STREAMS = ("pe", "act", "dve", "pool", "sp")

    def __init__(self, nc, n_dma_sems=12):
        self.nc = nc
        self.ops = []
        self.n_dma_sems = n_dma_sems
        self.last = {s: None for s in self.STREAMS}
        self.recent_dma = {q: [] for q in self.DMAQ}
        self.frontier = []
        self.synced = {s: True for s in self.STREAMS}

    def barrier(self):
        fr = [o for o in self.last.values() if o is not None]
        for q in self.DMAQ:
            fr.extend(self.recent_dma[q])
        self.frontier = fr
        self.synced = {s: False for s in self.STREAMS}

    def op(self, eng, fn, reads=(), writes=()):
        o = Op()
        o.eng = eng
        o.fn = fn
        o.isdma = eng in self.DMAQ
        o.idx = len(self.ops)
        o.prev_same_sem = None
        deps = {}
        if any(t.excl for t in reads):
            writes = list(writes) + [t for t in reads if t.excl and t not in writes]
            reads = [t for t in reads if not t.excl]
        for t in reads:
            if t.w is not None:
                deps[t.w.idx] = ("raw", t.w)
        for t in writes:
            if t.w is not None and t.w.idx not in deps:
                deps[t.w.idx] = ("waw", t.w)
            for r in t.r:
                if r.idx not in deps:
                    deps[r.idx] = ("war", r)
        st = self.STREAM[eng]
        if not self.synced[st]:
            for p in self.frontier:
                if p.idx not in deps:
                    deps[p.idx] = ("bar", p)
            self.synced[st] = True
        for t in writes:
            t.w = o
            t.r = []
        for t in reads:
            if t.w is not o:
                t.r.append(o)
        o.deps = deps
        o.sig = False
        self.ops.append(o)
        self.last[st] = o
        if o.isdma:
            lst = self.recent_dma[eng]
            lst.append(o)
            if len(lst) > self.n_dma_sems:
                lst.pop(0)
        return o

    def emit(self, final_waits=()):
        nc = self.nc
        streams = {s: [] for s in self.STREAMS}
        for o in self.ops:
            streams[self.STREAM[o.eng]].append(o)
        pos = {}
        for s, lst in streams.items():
            for i, o in enumerate(lst):
                pos[o.idx] = i
        need = {}
        for o in self.ops:
            lst = []
            so = self.STREAM[o.eng]
            for (kind, p) in o.deps.values():
                sp_ = self.STREAM[p.eng]
                if p.isdma or o.isdma:
                    lst.append(p)
                elif sp_ != so:
                    lst.append(p)
                else:
                    if so == "pe":
                        continue
                    lst.append(p)
            need[o.idx] = lst
            for p in lst:
                p.sig = True
        for o in final_waits:
            o.sig = True
        stack = contextlib.ExitStack()
        esem = {e: stack.enter_context(nc.semaphore("s_" + e)) for e in self.COMPUTE}
        ecount = {e: 0 for e in self.COMPUTE}
        dsems = {q: [stack.enter_context(nc.semaphore("d_%s_%d" % (q, i))) for i in range(self.n_dma_sems)]
                 for q in self.DMAQ}
        duse = {q: [0] * self.n_dma_sems for q in self.DMAQ}
        dlast = {q: [None] * self.n_dma_sems for q in self.DMAQ}
        dnext = {q: 0 for q in self.DMAQ}
        for s in self.STREAMS:
            for o in streams[s]:
                if o.isdma:
                    q = o.eng
                    j = dnext[q]
                    dnext[q] = (j + 1) % self.n_dma_sems
                    duse[q][j] += 1
                    o.semid = (q, j)
                    o.semval = 16 * duse[q][j]
                    o.prev_same_sem = dlast[q][j]
                    dlast[q][j] = o
                elif o.sig:
                    ecount[o.eng] += 1
                    o.semid = o.eng
                    o.semval = ecount[o.eng]
        self.n_waits = 0

        def sem_of(p):
            if p.isdma:
                return dsems[p.semid[0]][p.semid[1]]
            return esem[p.semid]

        def emit_stream(s, engobj):
            known = {}
            for o in streams[s]:
                waits = {}
                cand = list(need[o.idx])
                if o.isdma and o.prev_same_sem is not None:
                    cand.append(o.prev_same_sem)
                for p in cand:
                    k = p.semid
                    if known.get(k, 0) >= p.semval:
                        continue
                    if waits.get(k, (0, None))[0] < p.semval:
                        waits[k] = (p.semval, p)
                for k, (v, p) in waits.items():
                    engobj.wait_ge(sem_of(p), v)
                    known[k] = v
                    self.n_waits += 1
                ins = o.fn(engobj)
                if o.isdma:
                    ins.then_inc(dsems[o.semid[0]][o.semid[1]], 16)
                elif o.sig:
                    ins.then_inc(esem[o.semid], 1)
            if s == "sp":
                for o in final_waits:
                    engobj.wait_ge(sem_of(o), o.semval)

        with nc.Block() as block:
            @block.tensor
            def _(e):
                emit_stream("pe", e)

            @block.scalar
            def _(e):
                emit_stream("act", e)

            @block.vector
            def _(e):
                emit_stream("dve", e)

            @block.gpsimd
            def _(e):
                emit_stream("pool", e)

            @block.sync
            def _(e):
                emit_stream("sp", e)
        stack.close()


def ACT(out, in_, func, **kw):
    return lambda e: e.activation(out=out, in_=in_, func=func, **kw)


def TT(out, in0, in1, op):
    return lambda e: e.tensor_tensor(out=out, in0=in0, in1=in1, op=op)


def TS(out, in0, s1, op0, s2=None, op1=None):
    if op1 is None:
        return lambda e: e.tensor_scalar(out=out, in0=in0, scalar1=s1, scalar2=None, op0=op0)
    return lambda e: e.tensor_scalar(out=out, in0=in0, scalar1=s1, scalar2=s2, op0=op0, op1=op1)


def STT(out, in0, scalar, in1, op0, op1):
    return lambda e: e.scalar_tensor_tensor(out=out, in0=in0, scalar=scalar, in1=in1, op0=op0, op1=op1)


def CP(out, in_):
    return lambda e: e.tensor_copy(out=out, in_=in_)


def MM(out, lhsT, rhs, start=True, stop=True, skip=False):
    if skip:
        return lambda e: e.matmul(out, lhsT, rhs, start=start, stop=stop, skip_group_check=True)
    return lambda e: e.matmul(out, lhsT, rhs, start=start, stop=stop)


def TR(out, in_, ident):
    return lambda e: e.transpose(out, in_, ident)


def DMA(out, in_):
    return lambda e: e.dma_start(out=out, in_=in_)


def RED(out, in_, op=None):
    return lambda e: e.tensor_reduce(out=out, in_=in_, axis=AX.X, op=(op or ALU.add))


def RECIP(out, in_):
    return lambda e: e.reciprocal(out=out, in_=in_)


def MEMSET(ap, v):
    return lambda e: e.memset(ap, v)


class Arena:
    def __init__(self, nc, words):
        self.t = nc.alloc_sbuf_tensor("arena", [128, words], F32)
        self.words = words
        self.off = 0
        self.peak = 0

    def alloc(self, shape, dt):
        n = int(np.prod(shape[1:]))
        nw = n if dt == F32 else (n + 1) // 2
        nw = (nw + 7) // 8 * 8
        assert self.off + nw <= self.words, ("SBUF arena overflow", self.off, nw, self.words)
        v = self.t[:, self.off:self.off + nw]
        self.off += nw
        self.peak = max(self.peak, self.off)
        if dt != F32:
            v = v.bitcast(dt)
        v = v[:, 0:n]
        if len(shape) == 3:
            v = v.rearrange("p (a b) -> p a b", b=shape[2])
        elif len(shape) == 4:
            v = v.rearrange("p (a b c) -> p a b c", b=shape[2], c=shape[3])
        return v

    def mark(self):
        return self.off

    def release(self, m):
        self.off = m


def bc(ap, shape):
    return ap.to_broadcast(shape)


class Ctx:
    pass


def build(n_layers=2, phases=("diff", "na", "ssd"), dbg=False):
    nc = bass.Bass("TRN2", target_bir_lowering=False)
    c = Ctx()
    c.nc = nc
    c.dbg = dbg

    def din(name, shape, dt=F32):
        return nc.dram_tensor(name, shape, dt, kind="ExternalInput").ap()

    c.x_in = din("x", [L, DM])
    c.norm_w = din("norm_w", [2, DM])
    c.w_in = din("w_in", [2, DM, INW])
    c.conv_w = din("conv_w", [2, 5, 1536])
    c.conv_b = din("conv_b", [2, 1536])
    c.a_log = din("a_log", [2, 32])
    c.dt_bias = din("dt_bias", [2, 32])
    c.d_skip = din("d_skip", [2, 32])
    c.ssd_norm_w = din("ssd_norm_w", [2, 1024])
    c.diff_qk_norm = din("diff_qk_norm", [2, 128])
    c.diff_lambda = din("diff_lambda", [2, 256])
    c.diff_subln = din("diff_subln", [2, 128])
    c.na_qk_norm = din("na_qk_norm", [2, 128])
    c.na_tab = din("na_tab", [2, 8, 128, NA_NCLS * 128])
    c.w_out = din("w_out", [2, 2048, DM])
    c.cs_tab = din("cs_tab", [L, 128])
    c.consts = din("consts", [128, 4 * 128])
    c.out = nc.dram_tensor("out", [L, DM], F32, kind="ExternalOutput").ap()
    kind_scr = "ExternalOutput" if dbg else "Internal"
    c.x1 = nc.dram_tensor("x1", [L, DM], F32, kind=kind_scr).ap()
    c.yT = [nc.dram_tensor("yT%d" % i, [2048, L], BF16, kind=kind_scr).ap() for i in range(n_layers)]
    c.csT_d = nc.dram_tensor("csT_d", [NT, 32, 128], F32, kind="Internal").ap()
    c.sb_d = nc.dram_tensor("sb_d", [2, NT, 128, 512], BF16, kind="Internal").ap()
    c.sf_d = nc.dram_tensor("sf_d", [2, NT, 128, 512], BF16, kind="Internal").ap()

    c.dumps = []

    def dump(name, ap, shape, dt, tiles):
        if not dbg:
            return
        d = nc.dram_tensor("dbg_" + name, list(shape), dt, kind="ExternalOutput").ap()
        c.dumps.append(c.P.op("sp", DMA(d, ap), reads=tiles))
    c.dump = dump
    A = Arena(nc, 51000)
    c.A = A
    P = Prog(nc)
    c.P = P
    c.psall = nc.alloc_psum_tensor("psall", [128, 8 * 512], F32)[:, :]
    c.ps = [c.psall[:, i * 512:(i + 1) * 512] for i in range(7)]
    c.tps = [T(excl=True) for _ in range(7)]
    c.psb = c.psall[:, 7 * 512:8 * 512].bitcast(BF16)
    _tb = T(excl=True)
    c.tpsb = [_tb, _tb]

    c.cst = A.alloc([128, 512], F32)
    c.t_cst = T()
    P.op("sp", DMA(c.cst, c.consts), writes=[c.t_cst])
    c.identf = c.cst[:, 0:128]
    c.U = c.cst[:, 128:256]
    c.Ur = c.cst[:, 256:384]
    c.onesf = c.cst[:, 384:512]
    c.ident = A.alloc([128, 128], BF16)
    c.t_ident = T()
    P.op("dve", CP(c.ident, c.identf), reads=[c.t_cst], writes=[c.t_ident])
    c.epsc = A.alloc([128, 1], F32)
    c.t_eps = T()
    P.op("pool", MEMSET(c.epsc, EPS), writes=[c.t_eps])
    c.onec = A.alloc([128, 1], F32)
    P.op("pool", MEMSET(c.onec, 1.0), writes=[c.t_eps])
    c.hdnT = A.alloc([128, 8, L], BF16)
    c.t_hdnT = [T() for _ in range(NT)]
    c.Wbuf = A.alloc([128, 16384], BF16)
    c.t_Wq = [T() for _ in range(4)]
    c.W8 = c.Wbuf.rearrange("p (k c) -> p k c", c=2048)
    c.Wo = c.Wbuf.rearrange("p (k c) -> p k c", c=1024)
    c.Wz = [c.Wbuf[:, 3 * 4096:4 * 4096].rearrange("p (k c) -> p k c", c=512),
            c.Wbuf[:, 2 * 4096:3 * 4096].rearrange("p (k c) -> p k c", c=512)]
    c.pf = {}
    c.phases = phases

    finals = []
    for l in range(n_layers):
        xsrc = c.x_in if l == 0 else c.x1
        xdst = c.x1 if l < n_layers - 1 or dbg and n_layers == 1 else c.out
        if l == n_layers - 1:
            xdst = c.out
        P.barrier()
        if "diff" in phases:
            pf_qkvg(c, l, "diff")
        if l == 0:
            phase_hdn(c, l, xsrc)
        if "diff" in phases:
            P.barrier()
            phase_diff(c, l)
        if "na" in phases:
            P.barrier()
            phase_na(c, l)
        if "ssd" in phases:
            P.barrier()
            phase_ssd(c, l)
        P.barrier()
        fin = phase_out(c, l, xsrc, xdst, fuse_next=(l + 1 < n_layers))
        if l == n_layers - 1:
            finals = fin
    P.barrier()
    finals = list(finals)
    if dbg:
        finals += [o for o in P.ops if o.isdma][-36:] + c.dumps
    P.emit(final_waits=finals)
    c.peak = A.peak
    return nc, c


def phase_hdn(c, l, xsrc):
    P, A = c.P, c.A
    m = A.mark()
    nwb = A.alloc([128, DM], F32)
    t_nwb = T()
    P.op("sp", DMA(nwb, c.norm_w[l:l + 1, :].partition_broadcast(128)), writes=[t_nwb])
    NX = 4
    xt = [A.alloc([128, DM], F32) for _ in range(NX)]
    t_xt = [T() for _ in range(NX)]
    sq = A.alloc([128, DM], F32)
    t_sq = T()
    ss = A.alloc([128, NT], F32)
    t_ss = [T() for _ in range(NT)]
    hb = [A.alloc([128, DM], BF16) for _ in range(2)]
    t_hb = [T(), T()]

    def load_x(tt):
        P.op("sp", DMA(xt[tt % NX], xsrc[tt * 128:(tt + 1) * 128, :]), writes=[t_xt[tt % NX]])
    for tt in range(min(NX - 1, NT)):
        load_x(tt)
    for tt in range(NT):
        b = tt % 2
        xb, t_xb = xt[tt % NX], t_xt[tt % NX]
        s1 = ss[:, tt:tt + 1]
        if tt + NX - 1 < NT:
            load_x(tt + NX - 1)
        P.op("act", ACT(sq, xb, AF.Square, accum_out=s1), reads=[t_xb], writes=[t_sq, t_ss[tt]])
        P.op("act", ACT(s1, s1, AF.Ln, scale=1.0 / DM, bias=c.epsc), reads=[t_ss[tt], c.t_eps], writes=[t_ss[tt]])
        P.op("act", ACT(s1, s1, AF.Exp, scale=-0.5), reads=[t_ss[tt]], writes=[t_ss[tt]])
        P.op("dve", STT(hb[b], xb, s1, nwb, ALU.mult, ALU.mult), reads=[t_xb, t_ss[tt], t_nwb], writes=[t_hb[b]])
        for k in range(8):
            P.op("pe", TR(c.psb[:, k * 128:(k + 1) * 128], hb[b][:, k * 128:(k + 1) * 128], c.ident),
                 reads=[t_hb[b], c.t_ident], writes=[c.tpsb[k // 4]])
        P.op("act", ACT(c.hdnT[:, :, tt * 128:(tt + 1) * 128], c.psb[:, :].rearrange("p (k t) -> p k t", t=128), AF.Copy),
             reads=[c.tpsb[0], c.tpsb[1]], writes=[c.t_hdnT[tt]])
    A.release(m)


def load_w(c, l, dst, col0, ncols, tiles, step=512, extra=()):
    P = c.P
    j = 0
    for c0 in range(0, ncols, step):
        n = min(step, ncols - c0)
        src = c.w_in[l, :, col0 + c0:col0 + c0 + n].rearrange("(k p) c -> p k c", p=128)
        P.op("poolq", DMA(dst[:, :, c0:c0 + n], src), writes=[tiles[j]] + list(extra))
        j += 1


def pf_qkvg(c, l, which):
    key = (which, l)
    if key not in c.pf:
        tW = [T() for _ in range(4)]
        load_w(c, l, c.W8, C_DIFF if which == "diff" else C_NA, 2048, tW, extra=c.t_Wq)
        c.pf[key] = tW
    return c.pf[key]


def pf_ssd(c, l):
    key = ("ssd", l)
    if key not in c.pf:
        tWz = [[T()], [T()]]
        load_w(c, l, c.Wz[0], C_Z, 512, tWz[0], extra=[c.t_Wq[3]])
        load_w(c, l, c.Wz[1], C_Z + 512, 512, tWz[1], extra=[c.t_Wq[2]])
        c.pf[key] = tWz
    return c.pf[key]


def pf_out(c, l, quarters):
    key = ("out", l)
    if key not in c.pf:
        c.pf[key] = [None] * 4
    tWo = c.pf[key]
    for j in quarters:
        if tWo[j] is None:
            tWo[j] = T()
            src = c.w_out[l, j * 512:(j + 1) * 512, :].rearrange("(k p) c -> p k c", p=128)
            c.P.op("poolq", DMA(c.Wo[:, j * 4:(j + 1) * 4, :], src), writes=[tWo[j], c.t_Wq[j]])
    return tWo


def qk_norm_rope(c, src_ps, t_src, sq, t_sq, ss8, t_ss8, tbuf, t_tb, ubuf, t_ub, TC, TS_, t_tab, tt, dst, t_dst, rope):
    P = c.P
    P.op("act", ACT(sq, src_ps, AF.Square), reads=[t_src], writes=[t_sq])
    P.op("dve", RED(ss8, sq.rearrange("p (g d) -> p g d", d=64)), reads=[t_sq], writes=[t_ss8])
    P.op("act", ACT(ss8, ss8, AF.Ln, scale=1.0 / 64, bias=c.epsc), reads=[t_ss8, c.t_eps], writes=[t_ss8])
    P.op("act", ACT(ss8, ss8, AF.Exp, scale=-0.5), reads=[t_ss8], writes=[t_ss8])
    t3 = tbuf.rearrange("p (g d) -> p g d", d=64)
    P.op("dve", TT(t3, src_ps.rearrange("p (g d) -> p g d", d=64), bc(ss8.unsqueeze(2), [128, 8, 64]), ALU.mult),
         reads=[t_src, t_ss8], writes=[t_tb])
    if rope:
        u3 = ubuf.rearrange("p (g d) -> p g d", d=64)
        P.op("pool", TT(u3[:, :, 0:32], t3[:, :, 32:64], bc(TS_[:, tt, 0:32].unsqueeze(1), [128, 8, 32]), ALU.mult),
             reads=[t_tb, t_tab], writes=[t_ub])
        P.op("pool", TT(u3[:, :, 32:64], t3[:, :, 0:32], bc(TS_[:, tt, 32:64].unsqueeze(1), [128, 8, 32]), ALU.mult),
             reads=[t_tb, t_tab], writes=[t_ub])
        P.op("dve", TT(t3, t3, bc(TC[:, tt, :].unsqueeze(1), [128, 8, 64]), ALU.mult), reads=[t_tb, t_tab], writes=[t_tb])
        P.op("dve", TT(dst, tbuf, ubuf, ALU.add), reads=[t_tb, t_ub], writes=[t_dst])
    else:
        P.op("dve", TT(dst.rearrange("p (g d) -> p g d", d=64), t3, bc(TC.unsqueeze(1), [128, 8, 64]), ALU.mult),
             reads=[t_tb, t_tab], writes=[t_dst])


def prep_qkvg(c, W, tW, qT, t_qT, kT, t_kT, vcopy, t_V, G, t_G, tabs, t_tab, rope):
    P, A = c.P, c.A
    ps, tps = c.ps, c.tps
    NB = 3 if rope else 2
    LAG = NB - 1
    raw = [[A.alloc([128, 512], F32) for _ in range(NB)] for _ in range(2)]
    t_raw = [[T() for _ in range(NB)] for _ in range(2)]
    sq = [A.alloc([128, 512], F32) for _ in range(2)]
    t_sq = [T(), T()]
    ss8 = [[A.alloc([128, 8], F32) for _ in range(NB)] for _ in range(2)]
    t_ss8 = [[T() for _ in range(NB)] for _ in range(2)]
    tbuf = [[A.alloc([128, 512], F32) for _ in range(NB)] for _ in range(2)]
    t_tb = [[T() for _ in range(NB)] for _ in range(2)]
    ubuf = [[A.alloc([128, 512], F32) for _ in range(NB)] for _ in range(2)] if rope else None
    t_ub = [[T() for _ in range(NB)] for _ in range(2)]
    rbuf = [[A.alloc([128, 512], BF16) for _ in range(NB)] for _ in range(2)]
    t_rb = [[T() for _ in range(NB)] for _ in range(2)]
    dsts = ((qT, t_qT), (kT, t_kT))
    nbank = 0
    gbank = {}

    def chain_a(tt, i):
        b = tt % NB
        P.op("act", ACT(sq[i], raw[i][b], AF.Square), reads=[t_raw[i][b]], writes=[t_sq[i]])
        P.op("dve", RED(ss8[i][b], sq[i].rearrange("p (g d) -> p g d", d=64)), reads=[t_sq[i]], writes=[t_ss8[i][b]])

    def chain_b(tt, i):
        b = tt % NB
        s8, t_s8 = ss8[i][b], t_ss8[i][b]
        P.op("act", ACT(s8, s8, AF.Ln, scale=1.0 / 64, bias=c.epsc), reads=[t_s8, c.t_eps], writes=[t_s8])
        P.op("act", ACT(s8, s8, AF.Exp, scale=-0.5), reads=[t_s8], writes=[t_s8])

    def chain_c(tt, i):
        b = tt % NB
        rw, t_rw = raw[i][b], t_raw[i][b]
        s8, t_s8 = ss8[i][b], t_ss8[i][b]
        tb_, t_tb_ = tbuf[i][b], t_tb[i][b]
        t3 = tb_.rearrange("p (g d) -> p g d", d=64)
        P.op("dve", TT(t3, rw.rearrange("p (g d) -> p g d", d=64), bc(s8.unsqueeze(2), [128, 8, 64]), ALU.mult),
             reads=[t_rw, t_s8], writes=[t_tb_])
        dst, t_dst = rbuf[i][b], t_rb[i][b]
        TC, TS_ = tabs[2 * i], tabs[2 * i + 1]
        if rope:
            ub, t_ub_ = ubuf[i][b], t_ub[i][b]
            u3 = ub.rearrange("p (g d) -> p g d", d=64)
            eng_u = "pool" if i == 1 else "dve"
            P.op(eng_u, TT(u3[:, :, 0:32], t3[:, :, 32:64], bc(TS_[:, tt, 0:32].unsqueeze(1), [128, 8, 32]), ALU.mult),
                 reads=[t_tb_, t_tab], writes=[t_ub_])
            P.op(eng_u, TT(u3[:, :, 32:64], t3[:, :, 0:32], bc(TS_[:, tt, 32:64].unsqueeze(1), [128, 8, 32]), ALU.mult),
                 reads=[t_tb_, t_tab], writes=[t_ub_])
            P.op("dve", TT(t3, t3, bc(TC[:, tt, :].unsqueeze(1), [128, 8, 64]), ALU.mult), reads=[t_tb_, t_tab], writes=[t_tb_])
            P.op("dve", TT(dst, tb_, ub, ALU.add), reads=[t_tb_, t_ub_], writes=[t_dst])
        else:
            P.op("dve", TT(dst.rearrange("p (g d) -> p g d", d=64), t3, bc(TC.unsqueeze(1), [128, 8, 64]), ALU.mult),
                 reads=[t_tb_, t_tab], writes=[t_dst])

    def transposes(tt):
        b = tt % NB
        tok = slice(tt * 128, (tt + 1) * 128)
        for i in range(2):
            for h in range(4):
                P.op("pe", TR(c.psb[:, i * 512 + h * 128:i * 512 + (h + 1) * 128], rbuf[i][b][:, h * 128:(h + 1) * 128], c.ident),
                     reads=[t_rb[i][b], c.t_ident], writes=[c.tpsb[i]])
        for i in range(2):
            dstT, t_dstT = dsts[i]
            P.op("dve", CP(dstT[:, :, tok], c.psb[:, i * 512:(i + 1) * 512].rearrange("p (h t) -> p h t", t=128)),
                 reads=[c.tpsb[i]], writes=[t_dstT[tt]])

    for tt in range(NT + LAG):
        if tt < NT:
            tok = slice(tt * 128, (tt + 1) * 128)
            b = tt % NB
            banks = []
            for j in range(4):
                bk = nbank % 7
                nbank += 1
                banks.append(bk)
                for k in range(8):
                    P.op("pe", MM(ps[bk], c.hdnT[:, k, tok], W[:, k, j * 512:(j + 1) * 512], start=(k == 0), stop=(k == 7)),
                         reads=[c.t_hdnT[tt], tW[j]] + c.t_Wq, writes=[tps[bk]])
            for i in range(2):
                if rope:
                    P.op("act", ACT(raw[i][b], ps[banks[i]], AF.Copy), reads=[tps[banks[i]]], writes=[t_raw[i][b]])
                else:
                    P.op("dve", CP(raw[i][b], ps[banks[i]]), reads=[tps[banks[i]]], writes=[t_raw[i][b]])
            vcopy(tt, ps[banks[2]], tps[banks[2]])
            gbank[tt] = banks[3]
            for stage in (chain_a, chain_b, chain_c):
                for i in range(2):
                    stage(tt, i)
            if tt % 2 == 1 or tt == NT - 1:
                for t2 in ([tt - 1, tt] if tt % 2 == 1 else [tt]):
                    P.op("act", ACT(G[:, t2, :], ps[gbank[t2]], AF.Silu), reads=[tps[gbank[t2]]], writes=[t_G[t2]])
        if tt >= LAG:
            transposes(tt - LAG)


def phase_diff(c, l):
    P, A = c.P, c.A
    m = A.mark()
    lam_init = 0.8 - 0.6 * math.exp(-0.3 * l)
    W = c.W8
    tW = pf_qkvg(c, l, "diff")
    qT = A.alloc([128, 4, L], BF16)
    kT = A.alloc([128, 4, L], BF16)
    t_qT = [T() for _ in range(NT)]
    t_kT = [T() for _ in range(NT)]
    V = A.alloc([128, NT, 4, 129], BF16)
    t_V = [T() for _ in range(NT)]
    G = A.alloc([128, NT, 512], BF16)
    t_G = [T() for _ in range(NT)]
    t_misc = T()
    P.op("pool", MEMSET(V[:, :, :, 128:129], 1.0), writes=t_V)
    wqk = A.alloc([128, 128], F32)
    P.op("sp", DMA(wqk, c.diff_qk_norm[l:l + 1, :].partition_broadcast(128)), writes=[t_misc])
    P.op("dve", TS(wqk[:, 0:64], wqk[:, 0:64], 0.125, ALU.mult), reads=[t_misc], writes=[t_misc])
    tabs = A.alloc([128, 4, NT * 64], F32)
    t_tab = T()
    m_cs = A.mark()
    c.cs_sb = A.alloc([128, NT, 128], F32)
    c.t_cs = T()
    P.op("sp", DMA(c.cs_sb, c.cs_tab.rearrange("(t p) c -> p t c", p=128)), writes=[c.t_cs])
    for i, w in enumerate((wqk[:, 0:64], wqk[:, 64:128])):
        TCv = tabs[:, 2 * i, :].rearrange("p (t d) -> p t d", d=64)
        TSv = tabs[:, 2 * i + 1, :].rearrange("p (t d) -> p t d", d=64)
        P.op("dve", TT(TCv, c.cs_sb[:, :, 0:64], bc(w.unsqueeze(1), [128, NT, 64]), ALU.mult), reads=[c.t_cs, t_misc], writes=[t_tab])
        P.op("dve", TT(TSv[:, :, 0:32], c.cs_sb[:, :, 64:96], bc(w[:, 32:64].unsqueeze(1), [128, NT, 32]), ALU.mult),
             reads=[c.t_cs, t_misc], writes=[t_tab])
        P.op("dve", TT(TSv[:, :, 32:64], c.cs_sb[:, :, 96:128], bc(w[:, 0:32].unsqueeze(1), [128, NT, 32]), ALU.mult),
             reads=[c.t_cs, t_misc], writes=[t_tab])
    P.barrier()
    A.release(m_cs)
    TCq = tabs[:, 0, :].rearrange("p (t d) -> p t d", d=64)
    TSq = tabs[:, 1, :].rearrange("p (t d) -> p t d", d=64)
    TCk = tabs[:, 2, :].rearrange("p (t d) -> p t d", d=64)
    TSk = tabs[:, 3, :].rearrange("p (t d) -> p t d", d=64)
    lamb = A.alloc([128, 256], F32)
    t_lam = T()
    P.op("sp", DMA(lamb, c.diff_lambda[l:l + 1, :].partition_broadcast(128)), writes=[t_lam])
    lsc = A.alloc([128, 4], F32)
    lam3 = lamb.rearrange("p (a b d) -> p a b d", a=2, b=2)
    prod = A.alloc([128, 2, 64], F32)
    P.op("dve", TT(prod, lam3[:, :, 0, :], lam3[:, :, 1, :], ALU.mult), reads=[t_lam], writes=[t_lam])
    P.op("dve", RED(lsc[:, 0:2], prod), reads=[t_lam], writes=[t_lam])
    P.op("act", ACT(lsc[:, 0:2], lsc[:, 0:2], AF.Exp), reads=[t_lam], writes=[t_lam])
    P.op("dve", TT(lsc[:, 2:3], lsc[:, 1:2], lsc[:, 0:1], ALU.subtract), reads=[t_lam], writes=[t_lam])
    P.op("dve", TS(lsc[:, 3:4], lsc[:, 2:3], -lam_init, ALU.add), reads=[t_lam], writes=[t_lam])
    neglam = lsc[:, 3:4]
    swb = A.alloc([128, 128], F32)
    t_swb = T()
    P.op("sp", DMA(swb, c.diff_subln[l:l + 1, :].partition_broadcast(128)), writes=[t_swb])
    P.op("dve", TS(swb, swb, 1.0 - lam_init, ALU.mult), reads=[t_swb], writes=[t_swb])

    ps, tps = c.ps, c.tps
    m_prep = A.mark()

    def vcopy(tt, bank, t_bank):
        P.op("act", ACT(V[:, tt, :, 0:128], bank.rearrange("p (h d) -> p h d", d=128), AF.Copy), reads=[t_bank], writes=[t_V[tt]])
    prep_qkvg(c, W, tW, qT, t_qT, kT, t_kT, vcopy, t_V, G, t_G, (TCq, TSq, TCk, TSk), t_tab, True)
    P.barrier()
    A.release(m_prep)
    if "na" in c.phases:
        pf_qkvg(c, l, "na")
    elif "ssd" in c.phases:
        pf_ssd(c, l)
    c.dump("qT%d" % l, qT, [128, 4, L], BF16, t_qT)
    c.dump("kT%d" % l, kT, [128, 4, L], BF16, t_kT)
    c.dump("V%d" % l, V, [128, NT, 4, 129], BF16, t_V)
    c.dump("G%d" % l, G, [128, NT, 512], BF16, t_G)
    c.dump("hdnT%d" % l, c.hdnT, [128, 8, L], BF16, c.t_hdnT)
    NPT = 4
    Pt = [A.alloc([128, 1024], BF16) for _ in range(NPT)]
    t_Pt = [T() for _ in range(NPT)]
    t_Pth = [[T(), T()] for _ in range(NPT)]
    SP = [c.psall[:, 0:1024], c.psall[:, 1024:2048]]
    t_SP = [[tps[0], tps[1]], [tps[2], tps[3]]]
    Ob3 = [ps[4], ps[5], ps[6]]
    t_O3 = [tps[4], tps[5], tps[6]]

    def acc(a):
        return a // 3, (a % 3) * 129
    Osb = A.alloc([128, 3, 512], F32)
    t_Osb = T()
    rr = [A.alloc([128, 16], F32) for _ in range(2)]
    t_rr = [T(), T()]
    ob4 = [A.alloc([128, 4, 128], F32) for _ in range(2)]
    t_ob4 = [T(), T()]
    junk = A.alloc([128, 128], F32)
    t_junk = T()
    yb4 = [A.alloc([128, 512], BF16) for _ in range(2)]
    t_yb4 = [T(), T()]
    yst = [A.alloc([128, 512], BF16) for _ in range(2)]
    t_yst = [T(), T()]
    iters = [(h, qb, kt) for h in range(4) for qb in range(4) for kt in range(NT)]
    NI = len(iters)
    deferred = []

    def emit_S(i):
        h, qb, kt = iters[i]
        for cc in range(2):
            pr = slice(cc * 64, (cc + 1) * 64)
            P.op("pe", MM(SP[i % 2][:, cc * 512:(cc + 1) * 512], kT[pr, h, kt * 128:(kt + 1) * 128], qT[pr, h, qb * 512:(qb + 1) * 512]),
                 reads=[t_kT[kt]] + [t_qT[qb * 4 + j] for j in range(4)], writes=[t_SP[i % 2][cc]])

    def Oslice(a):
        bk, off = acc(a)
        return Osb[:, bk, off:off + 129]

    def fin_stage1(blk, h, qb):
        b = blk % 2
        r_, t_r = rr[b], t_rr[b]
        ob, t_ob = ob4[b], t_ob4[b]
        for j in range(4):
            O0, O1 = Oslice(j), Oslice(4 + j)
            P.op("dve", RECIP(r_[:, 4 * j:4 * j + 1], O0[:, 128:129]), reads=[t_Osb], writes=[t_r])
            P.op("dve", RECIP(r_[:, 4 * j + 1:4 * j + 2], O1[:, 128:129]), reads=[t_Osb], writes=[t_r])
            P.op("dve", TT(r_[:, 4 * j + 2:4 * j + 3], r_[:, 4 * j + 1:4 * j + 2], neglam, ALU.mult), reads=[t_r, t_lam], writes=[t_r])
            P.op("dve", TS(ob[:, j, :], O0[:, 0:128], r_[:, 4 * j:4 * j + 1], ALU.mult), reads=[t_Osb, t_r], writes=[t_ob])
            P.op("dve", STT(ob[:, j, :], O1[:, 0:128], r_[:, 4 * j + 2:4 * j + 3], ob[:, j, :], ALU.mult, ALU.add),
                 reads=[t_Osb, t_r, t_ob], writes=[t_ob])

    def fin_stage2(blk, h, qb):
        b = blk % 2
        r_, t_r = rr[b], t_rr[b]
        ob, t_ob = ob4[b], t_ob4[b]
        for j in range(4):
            P.op("act", ACT(junk, ob[:, j, :], AF.Square, accum_out=r_[:, 4 * j + 3:4 * j + 4]), reads=[t_ob], writes=[t_junk, t_r])
        r3 = r_.rearrange("p (j k) -> p j k", k=4)[:, :, 3]
        P.op("act", ACT(r3, r3, AF.Ln, scale=1.0 / 128, bias=c.epsc), reads=[t_r, c.t_eps], writes=[t_r])
        P.op("act", ACT(r3, r3, AF.Exp, scale=-0.5), reads=[t_r], writes=[t_r])

    def fin_stage3(blk, h, qb):
        b = blk % 2
        r_, t_r = rr[b], t_rr[b]
        ob, t_ob = ob4[b], t_ob4[b]
        yb, t_yb = yb4[b], t_yb4[b]
        for j in range(4):
            tt = qb * 4 + j
            P.op("dve", STT(ob[:, j, :], ob[:, j, :], r_[:, 4 * j + 3:4 * j + 4], swb, ALU.mult, ALU.mult), reads=[t_ob, t_r, t_swb], writes=[t_ob])
            P.op("dve", TT(yb[:, j * 128:(j + 1) * 128], ob[:, j, :], G[:, tt, h * 128:(h + 1) * 128], ALU.mult),
                 reads=[t_ob, t_G[tt]], writes=[t_yb])
        for j in range(4):
            P.op("pe", TR(c.psb[:, j * 128:(j + 1) * 128], yb[:, j * 128:(j + 1) * 128], c.ident),
                 reads=[t_yb, c.t_ident], writes=[c.tpsb[0]])

    def fin_stage4(blk, h, qb):
        b = blk % 2
        ys, t_ys = yst[b], t_yst[b]
        P.op("dve", CP(ys, c.psb[:, 0:512]), reads=[c.tpsb[0]], writes=[t_ys])
        P.op("sp", DMA(c.yT[l][1024 + h * 128:1024 + (h + 1) * 128, qb * 512:(qb + 1) * 512], ys), reads=[t_ys])

    emit_S(0)
    blk = 0
    for i in range(NI):
        h, qb, kt = iters[i]
        if i + 1 < NI:
            emit_S(i + 1)
        pt = Pt[i % NPT]
        t_pth = t_Pth[i % NPT]
        for cc in range(2):
            P.op("act", ACT(pt[:, cc * 512:(cc + 1) * 512], SP[i % 2][:, cc * 512:(cc + 1) * 512], AF.Exp),
                 reads=[t_SP[i % 2][cc]], writes=[t_pth[cc]])
        for cc in range(2):
            for j in range(4):
                bk, off = acc(cc * 4 + j)
                P.op("pe", MM(Ob3[bk][:, off:off + 129], pt[:, cc * 512 + j * 128:cc * 512 + (j + 1) * 128],
                              V[:, kt, h, :], start=(kt == 0 and off == 0), stop=(kt == NT - 1), skip=True),
                     reads=[t_pth[cc], t_V[kt]], writes=[t_O3[bk]])
        for (due, fn) in [d for d in deferred if d[0] <= i]:
            fn()
        deferred = [d for d in deferred if d[0] > i]
        if kt == NT - 1:
            for k3 in range(3):
                P.op("dve", CP(Osb[:, k3, 0:387], Ob3[k3][:, 0:387]), reads=[t_O3[k3]], writes=[t_Osb])
            fin_stage1(blk, h, qb)
            deferred.append((i + 2, (lambda b_=blk, h_=h, q_=qb: fin_stage2(b_, h_, q_))))
            deferred.append((i + 4, (lambda b_=blk, h_=h, q_=qb: fin_stage3(b_, h_, q_))))
            deferred.append((i + 6, (lambda b_=blk, h_=h, q_=qb: fin_stage4(b_, h_, q_))))
            blk += 1
    for (due, fn) in deferred:
        fn()
    A.release(m)


def _na_classes():
    rows, W, kh, kw = 32, 64, 8, 16
    rs = lambda r: min(max(r - kh // 2, 0), rows - kh)
    types = {}
    keys = []
    plan = []
    for i in range(16):
        qrows = [2 * i, 2 * i + 1]
        lo = min(rs(r) for r in qrows)
        hi = max(rs(r) + kh - 1 for r in qrows)
        tkeys = []
        kbs = list(range(lo // 2, hi // 2 + 1))
        for kb in kbs:
            key = []
            for b in range(2):
                for a in range(2):
                    kr, qr = 2 * kb + b, 2 * i + a
                    ok = rs(qr) <= kr < rs(qr) + kh
                    key.append((kr - qr + 7) if ok else -1)
            tkeys.append(tuple(key))
        tkeys = tuple(tkeys)
        if tkeys not in types:
            types[tkeys] = len(keys)
            keys.extend(tkeys)
        base = types[tkeys]
        plan.append([(kb, base + bi) for bi, kb in enumerate(kbs)])
    return keys, plan


NA_KEYS, NA_PLAN = _na_classes()
NA_NCLS = len(NA_KEYS)


def _na_tables(rpb):
    W, kw = 64, 16
    cidx = np.arange(W)
    col_start = np.clip(cidx - kw // 2, 0, W - kw)
    col_ok = (cidx[None, :] >= col_start[:, None]) & (cidx[None, :] < col_start[:, None] + kw)
    dc = np.clip(cidx[None, :] - cidx[:, None] + (kw - 1), 0, 2 * kw - 2)
    out = np.full((2, 8, 128, NA_NCLS, 128), NEG, dtype=np.float32)
    for ci, key in enumerate(NA_KEYS):
        n = 0
        for b in range(2):
            for a in range(2):
                dr = key[n]
                n += 1
                if dr < 0:
                    continue
                blkv = rpb[:, :, dr, :][:, :, dc]
                blkv = np.where(col_ok[None, None], blkv, np.float32(NEG))
                out[:, :, b * 64:(b + 1) * 64, ci, a * 64:(a + 1) * 64] = np.transpose(blkv, (0, 1, 3, 2))
    return np.ascontiguousarray(out.reshape(2, 8, 128, NA_NCLS * 128))


def phase_na(c, l):
    P, A = c.P, c.A
    m = A.mark()
    W = c.W8
    tW = pf_qkvg(c, l, "na")
    qT = A.alloc([128, 4, L], BF16)
    kT = A.alloc([128, 4, L], BF16)
    t_qT = [T() for _ in range(NT)]
    t_kT = [T() for _ in range(NT)]
    V = A.alloc([128, NT, 8, 65], BF16)
    t_V = [T() for _ in range(NT)]
    G = A.alloc([128, NT, 512], BF16)
    t_G = [T() for _ in range(NT)]
    Y = A.alloc([128, NT, 512], BF16)
    t_Y = [T() for _ in range(NT)]
    P.op("pool", MEMSET(V[:, :, :, 64:65], 1.0), writes=t_V)
    t_misc = T()
    wqk = A.alloc([128, 128], F32)
    P.op("sp", DMA(wqk, c.na_qk_norm[l:l + 1, :].partition_broadcast(128)), writes=[t_misc])
    P.op("dve", TS(wqk[:, 0:64], wqk[:, 0:64], 0.125, ALU.mult), reads=[t_misc], writes=[t_misc])
    ps, tps = c.ps, c.tps
    m_prep = A.mark()

    def vcopy(tt, bank, t_bank):
        P.op("act", ACT(V[:, tt, :, 0:64], bank.rearrange("p (h d) -> p h d", d=64), AF.Copy), reads=[t_bank], writes=[t_V[tt]])
    prep_qkvg(c, W, tW, qT, t_qT, kT, t_kT, vcopy, t_V, G, t_G, (wqk[:, 0:64], None, wqk[:, 64:128], None), t_misc, False)
    P.barrier()
    A.release(m_prep)
    if "ssd" in c.phases:
        pf_ssd(c, l)
    pf_out(c, l, [0, 1])
    tab = [A.alloc([128, NA_NCLS * 128], F32) for _ in range(2)]
    t_tab = [T(), T()]
    Tb = [A.alloc([128, 640], F32) for _ in range(2)]
    t_Tb = [T(), T()]
    Pb = [A.alloc([128, 640], BF16) for _ in range(3)]
    t_Pb = [T(), T(), T()]
    rr = [A.alloc([128, 2], F32) for _ in range(2)]
    t_rr = [T(), T()]
    its = [(hh, i) for hh in range(8) for i in range(NT)]
    NI = len(its)
    loaded = set()

    def load_tab(hh):
        if hh in loaded or hh >= 8:
            return
        loaded.add(hh)
        P.op("sp", DMA(tab[hh % 2], c.na_tab[l, hh, :, :]), writes=[t_tab[hh % 2]])
        P.op("act", ACT(tab[hh % 2], tab[hh % 2], AF.Exp), reads=[], writes=[t_tab[hh % 2]])

    def emit_S(n):
        hh, i = its[n]
        jb, e = hh // 2, hh % 2
        pr = slice(e * 64, (e + 1) * 64)
        S0, S1 = ps[(n % 2) * 2], ps[(n % 2) * 2 + 1]
        tS = [tps[(n % 2) * 2], tps[(n % 2) * 2 + 1]]
        for bi, (kb, cls) in enumerate(NA_PLAN[i]):
            dstS = (S0 if bi < 4 else S1)[:, (bi % 4) * 128:(bi % 4 + 1) * 128]
            P.op("pe", MM(dstS, kT[pr, jb, kb * 128:(kb + 1) * 128], qT[pr, jb, i * 128:(i + 1) * 128]),
                 reads=[t_kT[kb], t_qT[i]], writes=[tS[bi // 4]])

    def emit_exp_mul(n):
        hh, i = its[n]
        plan = NA_PLAN[i]
        nb = len(plan)
        base = plan[0][1]
        tS = [tps[(n % 2) * 2], tps[(n % 2) * 2 + 1]]
        Sboth = c.psall[:, (n % 2) * 1024:(n % 2) * 1024 + nb * 128]
        T_, t_T = Tb[n % 2], t_Tb[n % 2]
        P_, t_P = Pb[n % 3], t_Pb[n % 3]
        P.op("act", ACT(T_[:, 0:nb * 128], Sboth, AF.Exp), reads=(tS if nb > 4 else tS[0:1]), writes=[t_T])
        P.op("dve", TT(P_[:, 0:nb * 128], T_[:, 0:nb * 128], tab[hh % 2][:, base * 128:(base + nb) * 128], ALU.mult),
             reads=[t_T, t_tab[hh % 2]], writes=[t_P])

    yst = [A.alloc([128, 512], BF16) for _ in range(2)]
    t_yst = [T(), T()]
    ycnt = [0]

    def emit_ytrans(j4):
        for qb in range(4):
            for j in range(4):
                tt = qb * 4 + j
                P.op("pe", TR(c.psb[:, j * 128:(j + 1) * 128], Y[:, tt, j4 * 128:(j4 + 1) * 128], c.ident),
                     reads=[t_Y[tt], c.t_ident], writes=[c.tpsb[0]])
            ys, t_ys = yst[ycnt[0] % 2], t_yst[ycnt[0] % 2]
            ycnt[0] += 1
            P.op("dve", CP(ys, c.psb[:, 0:512]), reads=[c.tpsb[0]], writes=[t_ys])
            P.op("sp", DMA(c.yT[l][1536 + j4 * 128:1536 + (j4 + 1) * 128, qb * 512:(qb + 1) * 512], ys), reads=[t_ys])
    pending_tr = []
    load_tab(0)
    emit_S(0)
    if NI > 1:
        emit_S(1)
    emit_exp_mul(0)
    for n, (hh, i) in enumerate(its):
        if i == 2:
            load_tab(hh + 1)
        plan = NA_PLAN[i]
        nb = len(plan)
        Ob, t_Ob = ps[4 + n % 2], tps[4 + n % 2]
        P_, t_P = Pb[n % 3], t_Pb[n % 3]
        for bi, (kb, cls) in enumerate(plan):
            P.op("pe", MM(Ob[:, 0:65], P_[:, bi * 128:(bi + 1) * 128], V[:, kb, hh, :], start=(bi == 0), stop=(bi == nb - 1)),
                 reads=[t_P, t_V[kb]], writes=[t_Ob])
        if n + 2 < NI:
            emit_S(n + 2)
        if n + 1 < NI:
            emit_exp_mul(n + 1)
        r_, t_r = rr[n % 2], t_rr[n % 2]
        P.op("dve", RECIP(r_[:, 0:1], Ob[:, 64:65]), reads=[t_Ob], writes=[t_r])
        P.op("dve", STT(Y[:, i, hh * 64:(hh + 1) * 64], Ob[:, 0:64], r_[:, 0:1], G[:, i, hh * 64:(hh + 1) * 64], ALU.mult, ALU.mult),
             reads=[t_Ob, t_r, t_G[i]], writes=[t_Y[i]])
        if hh % 2 == 1 and i == NT - 1:
            pending_tr.append((n + 3, hh // 2))
        for (due, j4_) in [p_ for p_ in pending_tr if p_[0] <= n]:
            emit_ytrans(j4_)
        pending_tr = [p_ for p_ in pending_tr if p_[0] > n]
    for (due, j4_) in pending_tr:
        emit_ytrans(j4_)
    A.release(m)


def phase_ssd(c, l):
    STOP = 9
    P, A = c.P, c.A
    m = A.mark()
    ps, tps = c.ps, c.tps
    nc = c.nc
    Wdt = A.alloc([128, 8, 32], BF16)
    tWdt = [T()]
    load_w(c, l, Wdt, C_DT, 32, tWdt)
    t_sm = T()
    dtb = A.alloc([128, 32], F32)
    alog = A.alloc([128, 32], F32)
    dsk = A.alloc([128, 32], F32)
    P.op("sp", DMA(dtb, c.dt_bias[l:l + 1, :].partition_broadcast(128)), writes=[t_sm])
    t_al = T()
    P.op("sp", DMA(alog, c.a_log[l:l + 1, :].partition_broadcast(128)), writes=[t_al])
    t_dsk = T()
    P.op("sp", DMA(dsk, c.d_skip[l:l + 1, :].partition_broadcast(128)), writes=[t_dsk])
    dsum = A.alloc([128, 16], F32)
    P.op("dve", TT(dsum, dsk[:, 0:16], dsk[:, 16:32], ALU.add), reads=[t_dsk], writes=[t_dsk])
    P.op("act", ACT(alog, alog, AF.Exp), reads=[t_al], writes=[t_al])
    P.op("dve", TS(alog, alog, -1.0, ALU.mult), reads=[t_al], writes=[t_al])
    snw = A.alloc([128, 1024], F32)
    t_snw = T()
    P.op("sp", DMA(snw, c.ssd_norm_w[l:l + 1, :].partition_broadcast(128)), writes=[t_snw])
    cbb = A.alloc([128, 1536], F32)
    t_cbb = T()
    P.op("sp", DMA(cbb, c.conv_b[l:l + 1, :].partition_broadcast(128)), writes=[t_cbb])
    cwT = A.alloc([128, 12, 6], F32)
    t_cwT = T()

    def a3(shape=(128, NT, 32)):
        return A.alloc(list(shape), F32)
    dt = a3()
    dta = a3()
    cs = a3()
    dcb = a3()
    dout = a3()
    dend = a3()
    cXd = a3()
    t_dt, t_dta, t_cs, t_dcb, t_dout, t_dend, t_cXd, t_v, t_tmp3 = [T() for _ in range(9)]
    t_csd = T()
    csT_v = c.csT_d.rearrange("t (d h) l -> t d h l", d=2)
    m_tmp = A.mark()
    cw6 = A.alloc([128, 1536], F32)
    t_cw6 = T()
    P.op("sp", DMA(cw6[0:5, :], c.conv_w[l, :, :]), writes=[t_cw6])
    P.op("sp", DMA(cw6[5:6, :], c.conv_b[l:l + 1, :]), writes=[t_cw6])
    for j in range(12):
        P.op("pe", TR(ps[6][:, j * 6:(j + 1) * 6], cw6[0:6, j * 128:(j + 1) * 128], c.identf[0:6, 0:6]),
             reads=[t_cw6, c.t_cst], writes=[tps[6]])
    P.op("dve", CP(cwT, ps[6][:, 0:72].rearrange("p (j k) -> p j k", k=6)), reads=[tps[6]], writes=[t_cwT])
    P.barrier()
    A.release(m_tmp)

    def make_decay_stages(v, tmp3, csT):
        t_csT = T()

        def s1():
            for tt in range(NT):
                for k in range(8):
                    P.op("pe", MM(ps[4][:, tt * 32:(tt + 1) * 32], c.hdnT[:, k, tt * 128:(tt + 1) * 128], Wdt[:, k, :],
                                  start=(k == 0), stop=(k == 7), skip=True),
                         reads=[c.t_hdnT[tt], tWdt[0]], writes=[tps[4]])
            p0 = ps[4].rearrange("p (t h) -> p t h", h=32)
            P.op("dve", TT(v, p0, bc(dtb.unsqueeze(1), [128, NT, 32]), ALU.add), reads=[tps[4], t_sm], writes=[t_v])
            P.op("act", ACT(tmp3, v, AF.Abs), reads=[t_v], writes=[t_tmp3])
            P.op("act", ACT(tmp3, tmp3, AF.Exp, scale=-1.0), reads=[t_tmp3], writes=[t_tmp3])
            P.op("act", ACT(tmp3, tmp3, AF.Ln, bias=c.onec, scale=1.0), reads=[t_tmp3, c.t_eps], writes=[t_tmp3])

        def s2():
            P.op("dve", STT(dt, v, 0.0, tmp3, ALU.max, ALU.add), reads=[t_v, t_tmp3], writes=[t_dt])
            P.op("dve", TT(dta, dt, bc(alog.unsqueeze(1), [128, NT, 32]), ALU.mult), reads=[t_dt, t_al], writes=[t_dta])
            for tt in range(NT):
                P.op("pe", MM(ps[5][:, tt * 32:tt * 32 + 16], c.U, dta[:, tt, 0:16], skip=True), reads=[t_dta, c.t_cst], writes=[tps[5]])
                P.op("pe", MM(ps[5][:, tt * 32 + 16:tt * 32 + 32], c.Ur, dta[:, tt, 16:32], skip=True), reads=[t_dta, c.t_cst], writes=[tps[5]])
            for tt in range(NT):
                P.op("pe", MM(ps[6][:, tt * 32:(tt + 1) * 32], c.onesf, dta[:, tt, :], skip=True), reads=[t_dta, c.t_cst], writes=[tps[6]])
            P.op("act", ACT(cs, ps[5].rearrange("p (t h) -> p t h", h=32), AF.Copy), reads=[tps[5]], writes=[t_cs])
            P.op("act", ACT(tmp3, ps[6].rearrange("p (t h) -> p t h", h=32), AF.Copy), reads=[tps[6]], writes=[t_tmp3])

        def s3():
            P.op("act", ACT(dcb, tmp3, AF.Exp), reads=[t_tmp3], writes=[t_dcb])
            P.op("act", ACT(dout, cs, AF.Exp), reads=[t_cs], writes=[t_dout])
            P.op("dve", TT(dend, tmp3, cs, ALU.subtract), reads=[t_tmp3, t_cs], writes=[t_dend])
            P.op("act", ACT(dend, dend, AF.Exp), reads=[t_dend], writes=[t_dend])
            P.op("dve", TT(cXd, dt, dend, ALU.mult), reads=[t_dt, t_dend], writes=[t_cXd])

        def s4():
            for q in range(4):
                for j in range(4):
                    tt = q * 4 + j
                    P.op("pe", TR(ps[4][0:32, j * 128:(j + 1) * 128], cs[:, tt, :], c.identf), reads=[t_cs, c.t_cst], writes=[tps[4]])
                P.op("act", ACT(csT[0:32, q * 4:(q + 1) * 4, :], ps[4][0:32, :].rearrange("p (j t) -> p j t", t=128), AF.Copy),
                     reads=[tps[4]], writes=[t_csT])
            P.op("sp", DMA(c.csT_d.rearrange("t h l -> h t l"), csT[0:32, :, :]), reads=[t_csT], writes=[t_csd])
        return [s1, s2, s3, s4]

    for g in range(2):
        mg = A.mark()
        Wz = c.Wz[g]
        tWz = pf_ssd(c, l)[g]
        t_Wzq = c.t_Wq[3 - g]
        xs = A.alloc([128, NT, 512], F32)
        t_xs = [T() for _ in range(NT)]
        Btok = A.alloc([128, NT, 128], BF16)
        t_Btok = [T() for _ in range(NT)]
        BT = A.alloc([128, L], BF16)
        t_BT = [T() for _ in range(4)]
        CT = A.alloc([128, L], BF16)
        t_CT = [T() for _ in range(4)]
        m_conv = A.mark()
        stages = []
        if g == 0:
            stages = make_decay_stages(A.alloc([128, NT, 32], F32), A.alloc([128, NT, 32], F32), A.alloc([128, NT, 128], F32))
        pre = [A.alloc([128, L + 4], BF16) for _ in range(2)]
        t_pre = [[T() for _ in range(4)] for _ in range(2)]
        t_halo = [T(), T()]
        for b in range(2):
            P.op("pool", MEMSET(pre[b][:, 0:2], 0.0), writes=[t_halo[b]])
            P.op("pool", MEMSET(pre[b][:, L + 2:L + 4], 0.0), writes=[t_halo[b]])
        Wc = [A.alloc([128, 8, 128], BF16) for _ in range(2)]
        tWc = [[T()], [T()]]
        dg = [A.alloc([128, 5, 128], BF16) for _ in range(2)]
        t_dg = [T(), T()]
        ctmp = [A.alloc([128, 512], F32) for _ in range(2)]
        t_ctmp = [T(), T()]
        chunks = [("B", 8 + g), ("C", 10 + g)] + [("x%d" % j, 4 * g + j) for j in range(4)]
        pbank = 0
        for n, (kind, ci) in enumerate(chunks):
            b = n % 2
            if n >= 1 and stages:
                stages.pop(0)()
            load_w(c, l, Wc[b], C_XBC + ci * 128, 128, tWc[b])
            P.op("dve", TT(dg[b], bc(c.identf.unsqueeze(1), [128, 5, 128]), bc(cwT[:, ci, 0:5].unsqueeze(2), [128, 5, 128]), ALU.mult),
                 reads=[c.t_cst, t_cwT], writes=[t_dg[b]])
            for tb in range(4):
                bank, tbk = ps[pbank % 4], tps[pbank % 4]
                pbank += 1
                for k in range(8):
                    P.op("pe", MM(bank, Wc[b][:, k, :], c.hdnT[:, k, tb * 512:(tb + 1) * 512], start=(k == 0), stop=(k == 7)),
                         reads=[tWc[b][0]] + c.t_hdnT[tb * 4:(tb + 1) * 4], writes=[tbk])
                P.op("act", ACT(pre[b][:, 2 + tb * 512:2 + (tb + 1) * 512], bank, AF.Copy), reads=[tbk], writes=[t_pre[b][tb]])
            allpre = t_pre[b] + [t_halo[b]]
            if kind in ("B", "C"):
                dstT, t_dstT = (BT, t_BT) if kind == "B" else (CT, t_CT)
                for tb in range(4):
                    bank, tbk = ps[pbank % 4], tps[pbank % 4]
                    pbank += 1
                    for k in range(5):
                        P.op("pe", MM(bank, dg[b][:, k, :], pre[b][:, tb * 512 + k:tb * 512 + k + 512], start=(k == 0), stop=(k == 4)),
                             reads=[t_dg[b]] + allpre, writes=[tbk])
                    P.op("act", ACT(dstT[:, tb * 512:(tb + 1) * 512], bank, AF.Silu, bias=cwT[:, ci, 5:6], scale=1.0),
                         reads=[tbk, t_cwT], writes=[t_dstT[tb]])
            if kind != "C":
                for q in range(4):
                    bank, tbk = ps[pbank % 4], tps[pbank % 4]
                    pbank += 1
                    for j in range(4):
                        tt = q * 4 + j
                        for k in range(5):
                            P.op("pe", MM(bank[:, j * 128:(j + 1) * 128], pre[b][:, tt * 128 + k:tt * 128 + k + 128], dg[b][:, k, :],
                                          start=(k == 0 and j == 0), stop=(k == 4), skip=True),
                                 reads=[t_dg[b]] + allpre, writes=[tbk])
                    ct, t_ct = ctmp[q % 2], t_ctmp[q % 2]
                    P.op("dve", TT(ct.rearrange("p (j c) -> p j c", c=128), bank.rearrange("p (j c) -> p j c", c=128),
                                   bc(cbb[:, ci * 128:(ci + 1) * 128].unsqueeze(1), [128, 4, 128]), ALU.add),
                         reads=[tbk, t_cbb], writes=[t_ct])
                    if kind == "B":
                        P.op("act", ACT(Btok[:, q * 4:(q + 1) * 4, :], ct.rearrange("p (j c) -> p j c", c=128), AF.Silu),
                             reads=[t_ct], writes=t_Btok[q * 4:(q + 1) * 4])
                    else:
                        jx = int(kind[1])
                        P.op("act", ACT(xs[:, q * 4:(q + 1) * 4, jx * 128:(jx + 1) * 128], ct.rearrange("p (j c) -> p j c", c=128), AF.Silu),
                             reads=[t_ct], writes=t_xs[q * 4:(q + 1) * 4])

        for st_ in stages:
            st_()
        stages = []
        P.barrier()
        A.release(m_conv)
        if STOP <= 2:
            A.release(mg)
            continue
        hb0 = 16 + 8 * g
        hf0 = 8 * g
        m_passA = A.mark()
        Sst = [A.alloc([128, 512], F32) for _ in range(2)]
        t_Sst = [T(), T()]
        for d in range(2):
            P.op("pool", MEMSET(Sst[d], 0.0), writes=[t_Sst[d]])
        stg = [[A.alloc([128, 512], BF16) for _ in range(2)] for _ in range(2)]
        t_stg = [[T(), T()], [T(), T()]]
        Xd = [[A.alloc([128, 512], BF16) for _ in range(2)] for _ in range(2)]
        t_Xd = [[T(), T()], [T(), T()]]
        t_sd = [[T() for _ in range(NT)] for _ in range(2)]
        sdram = [c.sf_d, c.sb_d]
        h0s = [hf0, hb0]

        def bh(ap3, h0):
            return bc(ap3[:, h0:h0 + 8].unsqueeze(2), [128, 8, 64])

        def v8(ap):
            return ap.rearrange("p (h d) -> p h d", d=64)
        for k in range(NT):
            for d in range(2):
                ci_ = k if d == 0 else NT - 1 - k
                last = (ci_ == NT - 1) if d == 0 else (ci_ == 0)
                sg, t_sg = stg[d][k % 2], t_stg[d][k % 2]
                P.op("act", ACT(sg, Sst[d], AF.Copy), reads=[t_Sst[d]], writes=[t_sg])
                P.op("sp", DMA(sdram[d][g, ci_], sg), reads=[t_sg], writes=[t_sd[d][ci_]])
                if not last:
                    xd, t_xd = Xd[d][k % 2], t_Xd[d][k % 2]
                    P.op("pool", TT(v8(xd), v8(xs[:, ci_, :]), bh(cXd[:, ci_, :], h0s[d]), ALU.mult), reads=[t_xs[ci_], t_cXd], writes=[t_xd])
                    bank, tbk = ps[4 + d], tps[4 + d]
                    P.op("pe", MM(bank, Btok[:, ci_, :], xd), reads=[t_Btok[ci_], t_xd], writes=[tbk])
                    P.op("dve", TT(v8(Sst[d]), v8(Sst[d]), bh(dcb[:, ci_, :], h0s[d]), ALU.mult), reads=[t_Sst[d], t_dcb], writes=[t_Sst[d]])
                    P.op("dve", TT(Sst[d], Sst[d], bank, ALU.add), reads=[t_Sst[d], tbk], writes=[t_Sst[d]])
        P.barrier()
        A.release(m_passA)
        if STOP <= 3:
            A.release(mg)
            continue
        R = [A.alloc([128, 2, 8, 128], F32) for _ in range(2)]
        t_R = [T(), T()]
        SFc = [A.alloc([128, 512], BF16) for _ in range(2)]
        t_SFc = [T(), T()]
        SBc = [A.alloc([128, 512], BF16) for _ in range(2)]
        t_SBc = [T(), T()]
        E = A.alloc([128, 2, 8, 128], BF16)
        t_E = T()
        Mt = [A.alloc([128, 2, 8, 128], BF16) for _ in range(2)]
        t_Mt = [T(), T()]
        Gm = [A.alloc([128, 2, 128], BF16) for _ in range(2)]
        t_Gm = [T(), T()]
        Xt = [A.alloc([128, 3, 512], BF16) for _ in range(2)]
        t_Xt = [T(), T()]
        sz = [A.alloc([128, 512], F32) for _ in range(4)]
        t_sz = [T() for _ in range(4)]
        yab = A.alloc([128, 2, 512], F32)
        ya = yab[:, 0, :]
        yb_ = yab[:, 1, :]
        yy = A.alloc([128, 512], F32)
        t_ya, t_yb, t_yy = T(), T(), T()
        ssn = A.alloc([128, 2], F32)
        t_ssn = T()
        yo = [A.alloc([128, 512], BF16) for _ in range(2)]
        t_yo = [T(), T()]
        yst = [A.alloc([128, 4, 128], BF16) for _ in range(2)]
        t_yst = [T(), T()]

        def loadR(ci_):
            b = ci_ % 2
            P.op("sp", DMA(R[b].rearrange("p d h l -> p d (h l)"),
                           csT_v[ci_, :, 8 * g:8 * g + 8, :].rearrange("d h l -> d (h l)").partition_broadcast(128)),
                 reads=[t_csd], writes=[t_R[b]])

        def loadS(ci_):
            b = ci_ % 2
            P.op("sp", DMA(SFc[b], c.sf_d[g, ci_]), reads=[t_sd[0][ci_]], writes=[t_SFc[b]])
            P.op("sp", DMA(SBc[b], c.sb_d[g, ci_]), reads=[t_sd[1][ci_]], writes=[t_SBc[b]])

        def iteration(cn, cc_):
            if cn is not None:
                bn = cn % 2
                tokn = slice(cn * 128, (cn + 1) * 128)
                P.op("pe", MM(ps[4][:, 0:128], BT[:, tokn], CT[:, tokn]), reads=[t_BT[cn // 4], t_CT[cn // 4]], writes=[tps[4]])
                if cn % 2 == 0:
                    for c2 in (cn, cn + 1):
                        if c2 < NT:
                            zb, t_zb = (ps[3], tps[3]) if c2 % 2 == 0 else (ps[6], tps[6])
                            tok2 = slice(c2 * 128, (c2 + 1) * 128)
                            for k in range(8):
                                P.op("pe", MM(zb, c.hdnT[:, k, tok2], Wz[:, k, :], start=(k == 0), stop=(k == 7)),
                                     reads=[c.t_hdnT[c2], tWz[0], t_Wzq], writes=[t_zb])
                P.op("dve", TT(Gm[bn], bc(ps[4][:, 0:128].unsqueeze(1), [128, 2, 128]),
                               c.cst[:, 128:384].rearrange("p (d l) -> p d l", d=2), ALU.mult),
                     reads=[tps[4], c.t_cst], writes=[t_Gm[bn]])
                csv = cs[:, cn, :].rearrange("p (d h) -> p d h", d=2)[:, :, 8 * g:8 * g + 8]
                Dd, t_D = R[bn], t_R[bn]
                P.op("dve", TT(Dd, Dd, bc(csv.unsqueeze(3), [128, 2, 8, 128]), ALU.subtract), reads=[t_cs], writes=[t_D])
                P.op("act", ACT(Dd, Dd, AF.Relu, scale=-1.0), reads=[], writes=[t_D])
                P.op("act", ACT(E, Dd, AF.Exp, scale=-1.0), reads=[t_D], writes=[t_E])
                P.op("pool", TT(v8(Xt[bn][:, 0, :]), v8(xs[:, cn, :]), bh(dt[:, cn, :], hf0), ALU.mult), reads=[t_xs[cn], t_dt], writes=[t_Xt[bn]])
                P.op("pool", TT(v8(Xt[bn][:, 1, :]), v8(xs[:, cn, :]), bh(dt[:, cn, :], hb0), ALU.mult), reads=[t_xs[cn], t_dt], writes=[t_Xt[bn]])
                P.op("pool", TT(v8(Xt[bn][:, 2, :]), v8(xs[:, cn, :]), bh(dsum, 8 * g), ALU.mult), reads=[t_xs[cn], t_dsk], writes=[t_Xt[bn]])
            if cc_ is not None:
                b = cc_ % 2
                tok = slice(cc_ * 128, (cc_ + 1) * 128)
                Yb_, t_Y = (ps[0], tps[0]) if b == 0 else (ps[5], tps[5])
                P.op("pe", MM(Yb_, c.ident, Xt[b][:, 2, :], start=True, stop=False, skip=True), reads=[c.t_ident, t_Xt[b]], writes=[t_Y])
                for h in range(8):
                    for d in range(2):
                        P.op("pe", MM(Yb_[:, h * 64:(h + 1) * 64], Mt[b][:, d, h, :], Xt[b][:, d, h * 64:(h + 1) * 64],
                                      start=False, stop=(d == 1), skip=True),
                             reads=[t_Mt[b], t_Xt[b]], writes=[t_Y])
                P.op("pe", MM(ps[1], CT[:, tok], SFc[b]), reads=[t_CT[cc_ // 4], t_SFc[b]], writes=[tps[1]])
                P.op("pe", MM(ps[2], CT[:, tok], SBc[b]), reads=[t_CT[cc_ // 4], t_SBc[b]], writes=[tps[2]])
                doutv = dout[:, cc_, :].rearrange("p (d h) -> p d h", d=2)[:, :, 8 * g:8 * g + 8]
                P.op("dve", TT(yab.rearrange("p d (h e) -> p d h e", e=64),
                               c.psall[:, 512:1536].rearrange("p (d h e) -> p d h e", d=2, e=64),
                               bc(doutv.unsqueeze(3), [128, 2, 8, 64]), ALU.mult),
                     reads=[tps[1], tps[2], t_dout], writes=[t_ya, t_yb])
                P.op("dve", TT(ya, ya, yb_, ALU.add), reads=[t_ya, t_yb], writes=[t_ya])
                P.op("dve", TT(yy, Yb_, ya, ALU.add), reads=[t_Y, t_ya], writes=[t_yy])
                P.op("pool", TT(yy, yy, sz[cc_ % 4], ALU.mult), reads=[t_yy, t_sz[cc_ % 4]], writes=[t_yy])
            if cn is not None:
                P.op("dve", TT(Mt[bn], E, bc(Gm[bn].unsqueeze(2), [128, 2, 8, 128]), ALU.mult),
                     reads=[t_E, t_Gm[bn]], writes=[t_Mt[bn]])
                if cn % 2 == 0:
                    for c2 in (cn, cn + 1):
                        if c2 < NT:
                            zb, t_zb = (ps[3], tps[3]) if c2 % 2 == 0 else (ps[6], tps[6])
                            P.op("act", ACT(sz[c2 % 4], zb, AF.Silu), reads=[t_zb], writes=[t_sz[c2 % 4]])
            if cc_ is not None:
                P.op("act", ACT(ya, yy, AF.Square, accum_out=ssn[:, 0:1]), reads=[t_yy], writes=[t_ya, t_ssn])
                P.op("act", ACT(ssn[:, 0:1], ssn[:, 0:1], AF.Ln, scale=1.0 / 512, bias=c.epsc), reads=[t_ssn, c.t_eps], writes=[t_ssn])
                P.op("act", ACT(ssn[:, 0:1], ssn[:, 0:1], AF.Exp, scale=-0.5), reads=[t_ssn], writes=[t_ssn])
                P.op("dve", STT(yo[b], yy, ssn[:, 0:1], snw[:, g * 512:(g + 1) * 512], ALU.mult, ALU.mult),
                     reads=[t_yy, t_ssn, t_snw], writes=[t_yo[b]])

        def stageC(ci_):
            b = ci_ % 2
            tok = slice(ci_ * 128, (ci_ + 1) * 128)
            for j in range(4):
                P.op("pe", TR(c.psb[:, j * 128:(j + 1) * 128], yo[b][:, j * 128:(j + 1) * 128], c.ident),
                     reads=[t_yo[b], c.t_ident], writes=[c.tpsb[0]])
            P.op("dve", CP(yst[b], c.psb[:, 0:512].rearrange("p (j t) -> p j t", t=128)), reads=[c.tpsb[0]], writes=[t_yst[b]])
            P.op("sp", DMA(c.yT[l][g * 512:(g + 1) * 512, tok].rearrange("(j p) t -> p j t", p=128), yst[b]), reads=[t_yst[b]])

        loadR(0)
        loadS(0)
        loadR(1)
        iteration(0, None)
        for ci_ in range(NT):
            if ci_ + 1 < NT:
                loadS(ci_ + 1)
                if ci_ + 2 < NT:
                    loadR(ci_ + 2)
            iteration(ci_ + 1 if ci_ + 1 < NT else None, ci_)
            if ci_ >= 1:
                stageC(ci_ - 1)
        stageC(NT - 1)
        A.release(mg)
    A.release(m)


def phase_out(c, l, xsrc, xdst, fuse_next=False):
    P, A = c.P, c.A
    m = A.mark()
    if fuse_next:
        nwb = A.alloc([128, DM], F32)
        t_nwb = T()
        P.op("sp", DMA(nwb, c.norm_w[l + 1:l + 2, :].partition_broadcast(128)), writes=[t_nwb])
        sqj = A.alloc([128, DM], F32)
        t_sqj = T()
        ssx = A.alloc([128, NT], F32)
        t_ssx = [T() for _ in range(NT)]
        hb = [A.alloc([128, DM], BF16) for _ in range(2)]
        t_hb = [T(), T()]
    Wo = c.Wo
    tWo = pf_out(c, l, [0, 1, 2, 3])
    yt = [A.alloc([128, 16, 512], BF16) for _ in range(2)]
    t_yt = [T(), T()]
    xt = [A.alloc([128, DM], F32) for _ in range(3)]
    t_xt = [T(), T(), T()]
    ot = [A.alloc([128, DM], F32) for _ in range(2)]
    t_ot = [T(), T()]
    ps, tps = c.ps, c.tps
    fin = []

    def load_y(qb):
        P.op("sp", DMA(yt[qb % 2], c.yT[l][:, qb * 512:(qb + 1) * 512].rearrange("(k p) t -> p k t", p=128)), writes=[t_yt[qb % 2]])

    def load_x(tt):
        P.op("sp", DMA(xt[tt % 3], xsrc[tt * 128:(tt + 1) * 128, :]), writes=[t_xt[tt % 3]])
    load_y(0)
    load_x(0)
    load_x(1)
    nb = 0
    for tt in range(NT):
        b = tt % 2
        qb, j = tt // 4, tt % 4
        tok = slice(tt * 128, (tt + 1) * 128)
        if j == 0 and qb + 1 < 4:
            load_y(qb + 1)
        if tt + 2 < NT:
            load_x(tt + 2)
        for n in range(2):
            bank, tb = ps[nb % 6], tps[nb % 6]
            nb += 1
            for k in range(16):
                P.op("pe", MM(bank, yt[qb % 2][:, k, j * 128:(j + 1) * 128], Wo[:, k, n * 512:(n + 1) * 512], start=(k == 0), stop=(k == 15)),
                     reads=[t_yt[qb % 2], tWo[k // 4], c.t_Wq[k // 4]], writes=[tb])
            P.op("dve", TT(ot[b][:, n * 512:(n + 1) * 512], bank, xt[tt % 3][:, n * 512:(n + 1) * 512], ALU.add),
                 reads=[tb, t_xt[tt % 3]], writes=[t_ot[b]])
        fin.append(P.op("sp", DMA(xdst[tok, :], ot[b]), reads=[t_ot[b]]))
        if fuse_next:
            s1 = ssx[:, tt:tt + 1]
            P.op("act", ACT(sqj, ot[b], AF.Square, accum_out=s1), reads=[t_ot[b]], writes=[t_sqj, t_ssx[tt]])
            P.op("act", ACT(s1, s1, AF.Ln, scale=1.0 / DM, bias=c.epsc), reads=[t_ssx[tt], c.t_eps], writes=[t_ssx[tt]])
            P.op("act", ACT(s1, s1, AF.Exp, scale=-0.5), reads=[t_ssx[tt]], writes=[t_ssx[tt]])
            P.op("dve", STT(hb[b], ot[b], s1, nwb, ALU.mult, ALU.mult), reads=[t_ot[b], t_ssx[tt], t_nwb], writes=[t_hb[b]])
            for k in range(8):
                P.op("pe", TR(c.psb[:, k * 128:(k + 1) * 128], hb[b][:, k * 128:(k + 1) * 128], c.ident),
                     reads=[t_hb[b], c.t_ident], writes=[c.tpsb[0]])
            P.op("act", ACT(c.hdnT[:, :, tok], c.psb[:, :].rearrange("p (k t) -> p k t", t=128), AF.Copy),
                 reads=[c.tpsb[0]], writes=[c.t_hdnT[tt]])
    A.release(m)
    return fin


def _host_consts():
    inv_freq = (10000.0 ** (-(np.arange(0, 64, 2, dtype=np.float32)) / np.float32(64))).astype(np.float32)
    ang = (np.arange(L, dtype=np.float32)[:, None] * inv_freq[None, :]).astype(np.float32)
    cos, sin = np.cos(ang).astype(np.float32), np.sin(ang).astype(np.float32)
    cs = np.concatenate([cos, cos, -sin, sin], axis=1).astype(np.float32)
    k = np.arange(128)
    ident = np.eye(128, dtype=np.float32)
    U = (k[:, None] <= k[None, :]).astype(np.float32)
    Ur = (k[:, None] >= k[None, :]).astype(np.float32)
    ones = np.ones((128, 128), np.float32)
    consts = np.concatenate([ident, U, Ur, ones], axis=1)
    return np.ascontiguousarray(cs), np.ascontiguousarray(consts)


_CACHE = {}


def make_in_maps(inputs, n_cores=8):
    cs, consts = _host_consts()
    f = lambda a: np.ascontiguousarray(np.asarray(a, dtype=np.float32))
    shared = {
        "norm_w": f(inputs["norm_w"]), "w_in": f(inputs["w_in"]), "conv_w": f(inputs["conv_w"]),
        "conv_b": f(inputs["conv_b"]), "a_log": f(inputs["a_log"]).reshape(2, 32),
        "dt_bias": f(inputs["dt_bias"]).reshape(2, 32), "d_skip": f(inputs["d_skip"]).reshape(2, 32),
        "ssd_norm_w": f(inputs["ssd_norm_w"]), "diff_qk_norm": f(inputs["diff_qk_norm"]).reshape(2, 128),
        "diff_lambda": f(inputs["diff_lambda"]).reshape(2, 256), "diff_subln": f(inputs["diff_subln"]),
        "na_qk_norm": f(inputs["na_qk_norm"]).reshape(2, 128), "na_tab": _na_tables(f(inputs["na_rpb"])),
        "w_out": f(inputs["w_out"]), "cs_tab": cs, "consts": consts,
    }
    x = f(inputs["x"])
    return [dict(shared, x=x[b]) for b in range(n_cores)]


def kernel(**inputs):
    if "nc" not in _CACHE:
        _CACHE["nc"] = build()[0]
    nc = _CACHE["nc"]
    in_maps = make_in_maps(inputs)
    res = run_bass_kernel_spmd(nc, in_maps, core_ids=list(range(8)))
    return np.stack([np.asarray(r["out"], dtype=np.float32) for r in res.results], axis=0)
```
